# Optimizing a Trainium2 kernel written in Bass

```python
import math
import jax, jax.numpy as jnp
from jax import lax
import numpy as np

D_MODEL = 2048
BATCH = 4
SEQ = 2048
DEPTH = 4

N_EVEN = (DEPTH + 1) // 2
N_ODD = DEPTH // 2
EPS = 1e-6

POOL_WINDOWS = (2, 4, 8, 16)
POOL_GROUP = 128
POOL_WIDTH = len(POOL_WINDOWS) * POOL_GROUP
RET_HEADS = 6
RET_QK_DIM = 128
RET_V_DIM = 256
RET_CHUNK = 128
IN_EVEN = POOL_WIDTH + 2 * RET_HEADS * RET_QK_DIM + 2 * RET_HEADS * RET_V_DIM
MIX_EVEN = POOL_WIDTH + RET_HEADS * RET_V_DIM
SGU_CHUNK = 128
SGU_GROUPS = 8
SGU_GROUP_DIM = 128
SGU_WIDTH = SGU_GROUPS * SGU_GROUP_DIM
DIFF_HEADS = 4
DIFF_HEAD_DIM = 128
DIFF_V_DIM = 2 * DIFF_HEAD_DIM
Q_BLOCK = 128
N_BUCKETS = 32
MAX_DISTANCE = 128
IN_ODD = 2 * SGU_WIDTH + 2 * DIFF_HEADS * 2 * DIFF_HEAD_DIM + DIFF_HEADS * DIFF_V_DIM
MIX_ODD = SGU_WIDTH + DIFF_HEADS * DIFF_V_DIM
D_FF = 5632
CONV_WIDTH = 3

kernel_name = 'hybrid_pool_retention_sgu_diffattn'


def rms_norm(x, g):
    xf = x.astype(jnp.float32)
    y = xf * lax.rsqrt(jnp.mean(xf * xf, axis=-1, keepdims=True) + EPS)
    return (y * g.astype(jnp.float32)).astype(x.dtype)


def layer_norm(x, g):
    xf = x.astype(jnp.float32)
    mu = jnp.mean(xf, axis=-1, keepdims=True)
    var = jnp.mean(jnp.square(xf - mu), axis=-1, keepdims=True)
    return ((xf - mu) * lax.rsqrt(var + EPS) * g.astype(jnp.float32)).astype(x.dtype)


def rotary(x):
    s, d = x.shape[1], x.shape[-1]
    half = d // 2
    inv = 1.0 / (10000.0 ** (jnp.arange(half, dtype=jnp.float32) / half))
    ang = jnp.arange(s, dtype=jnp.float32)[:, None] * inv[None, :]
    cos = jnp.cos(ang)[None, :, None, :].astype(x.dtype)
    sin = jnp.sin(ang)[None, :, None, :].astype(x.dtype)
    x1, x2 = x[..., :half], x[..., half:]
    return jnp.concatenate([x1 * cos - x2 * sin, x1 * sin + x2 * cos], axis=-1)


def pool_mixer(u, pool_w, pool_scale):
    b, s, _ = u.shape
    ug = u.reshape(b, s, len(POOL_WINDOWS), POOL_GROUP)
    ugf = ug.astype(jnp.float32)
    cs = jnp.cumsum(ugf, axis=1)
    t = jnp.arange(1, s + 1, dtype=jnp.float32)
    pooled = []
    for gi, w in enumerate(POOL_WINDOWS):
        c = cs[:, :, gi]
        lag = jnp.pad(c, ((0, 0), (w, 0), (0, 0)))[:, :s]
        pooled.append((c - lag) / jnp.minimum(t, float(w))[None, :, None])
    pooled = jnp.stack(pooled, axis=2)
    y = (pooled - ugf).astype(u.dtype)
    y = jnp.einsum('bsgc,gcd->bsgd', y, pool_w)
    return y.reshape(b, s, POOL_WIDTH) * pool_scale


def retention(q, k, v):
    b, s, h, dk = q.shape
    dv = v.shape[-1]
    c = RET_CHUNK
    n = s // c
    dt = v.dtype
    log_g = jnp.log(1.0 - 2.0 ** (-5.0 - jnp.arange(h, dtype=jnp.float32)))
    idx = jnp.arange(c, dtype=jnp.float32)
    diff = idx[:, None] - idx[None, :]
    intra = jnp.where(diff >= 0, jnp.exp(log_g[:, None, None] * jnp.maximum(diff, 0.0)), 0.0)
    q_dec = jnp.exp(log_g[:, None] * (idx[None, :] + 1.0))
    k_dec = jnp.exp(log_g[:, None] * (c - 1.0 - idx[None, :]))
    chunk_dec = jnp.exp(log_g * c)
    qc = q.reshape(b, n, c, h, dk)
    kc = k.reshape(b, n, c, h, dk)
    vc = v.reshape(b, n, c, h, dv)
    scores = jnp.einsum('bnchd,bnmhd->bnhcm', qc, kc) * intra.astype(dt)
    inner = jnp.einsum('bnhcm,bnmhe->bnche', scores, vc)
    kv = jnp.einsum('bnmhd,hm,bnmhe->bnhde', kc, k_dec.astype(dt), vc)

    def step(state, kv_n):
        return state * chunk_dec[None, :, None, None] + kv_n, state

    _, prev = lax.scan(step, jnp.zeros((b, h, dk, dv), jnp.float32),
                       jnp.moveaxis(kv, 1, 0).astype(jnp.float32))
    prev = jnp.moveaxis(prev, 0, 1).astype(dt)
    cross = jnp.einsum('bnchd,hc,bnhde->bnche', qc, q_dec.astype(dt), prev)
    return (inner + cross).reshape(b, s, h, dv)


def pool_retention_mixer(hn, w_in, w_out, pool_w, pool_scale, ret_gn_g):
    b, s, _ = hn.shape
    z = hn @ w_in
    i0 = POOL_WIDTH
    i1 = i0 + RET_HEADS * RET_QK_DIM
    i2 = i1 + RET_HEADS * RET_QK_DIM
    i3 = i2 + RET_HEADS * RET_V_DIM
    a_out = pool_mixer(z[..., :i0], pool_w, pool_scale)
    q = rotary(z[..., i0:i1].reshape(b, s, RET_HEADS, RET_QK_DIM))
    k = rotary(z[..., i1:i2].reshape(b, s, RET_HEADS, RET_QK_DIM)) * (RET_QK_DIM ** -0.5)
    v = z[..., i2:i3].reshape(b, s, RET_HEADS, RET_V_DIM)
    g = z[..., i3:]
    r = retention(q, k, v)
    r = layer_norm(r, ret_gn_g.reshape(RET_HEADS, RET_V_DIM)).reshape(b, s, RET_HEADS * RET_V_DIM)
    b_out = r * jax.nn.silu(g)
    return jnp.concatenate([a_out, b_out], axis=-1) @ w_out


def spatial_gating(zu, zv, ln_g, w_s, b_s):
    b, s, _ = zu.shape
    n = s // SGU_CHUNK
    v = layer_norm(zv, ln_g).reshape(b, n, SGU_CHUNK, SGU_GROUPS, SGU_GROUP_DIM)
    causal = jnp.tril(jnp.ones((SGU_CHUNK, SGU_CHUNK), dtype=w_s.dtype))
    sv = jnp.einsum('gcm,bnmgd->bncgd', w_s * causal, v) + b_s.T[None, None, :, :, None]
    return zu * sv.reshape(b, s, SGU_WIDTH)


def t5_bucket(rel):
    n = jnp.maximum(rel, 0)
    max_exact = N_BUCKETS // 2
    large = max_exact + (jnp.log(jnp.maximum(n, 1).astype(jnp.float32) / max_exact)
                         / math.log(MAX_DISTANCE / max_exact) * (N_BUCKETS - max_exact)).astype(jnp.int32)
    large = jnp.minimum(large, N_BUCKETS - 1)
    return jnp.where(n < max_exact, n, large)


def diff_attention(q, k, v, lam, rel_bias):
    b, s, h, _, dh = q.shape
    nb = s // Q_BLOCK
    kpos = jnp.arange(s)
    qb = jnp.moveaxis((q * (dh ** -0.5)).reshape(b, nb, Q_BLOCK, h, 2, dh), 1, 0)

    def block(args):
        q_blk, i = args
        qpos = i * Q_BLOCK + jnp.arange(Q_BLOCK)
        rel = qpos[:, None] - kpos[None, :]
        bias = jnp.transpose(rel_bias[t5_bucket(rel)], (2, 0, 1))
        logits = jnp.einsum('bqhjd,bkhjd->bhjqk', q_blk, k).astype(jnp.float32)
        logits = logits + bias.astype(jnp.float32)[None, :, None]
        logits = jnp.where(rel >= 0, logits, -1e30)
        p = jax.nn.softmax(logits, axis=-1)
        attn = (p[:, :, 0] - lam * p[:, :, 1]).astype(v.dtype)
        return jnp.einsum('bhqk,bkhe->bqhe', attn, v)

    out = lax.map(block, (qb, jnp.arange(nb)))
    return jnp.moveaxis(out, 0, 1).reshape(b, s, h, v.shape[-1])


def sgu_diff_mixer(hn, w_in, w_out, sgu_ln_g, sgu_w, sgu_b, lq1, lk1, lq2, lk2, subln_g, rel_bias, layer_idx):
    b, s, _ = hn.shape
    z = hn @ w_in
    j0 = 2 * SGU_WIDTH
    j1 = j0 + DIFF_HEADS * 2 * DIFF_HEAD_DIM
    j2 = j1 + DIFF_HEADS * 2 * DIFF_HEAD_DIM
    zc = jax.nn.gelu(z[..., :j0])
    c_out = spatial_gating(zc[..., :SGU_WIDTH], zc[..., SGU_WIDTH:], sgu_ln_g, sgu_w, sgu_b)
    q = z[..., j0:j1].reshape(b, s, DIFF_HEADS, 2, DIFF_HEAD_DIM)
    k = z[..., j1:j2].reshape(b, s, DIFF_HEADS, 2, DIFF_HEAD_DIM)
    v = z[..., j2:].reshape(b, s, DIFF_HEADS, DIFF_V_DIM)
    lam_init = 0.8 - 0.6 * math.exp(-0.3 * layer_idx)
    lam = (jnp.exp(jnp.sum(lq1.astype(jnp.float32) * lk1.astype(jnp.float32)))
           - jnp.exp(jnp.sum(lq2.astype(jnp.float32) * lk2.astype(jnp.float32))) + lam_init)
    d = diff_attention(q, k, v, lam, rel_bias)
    d_out = (rms_norm(d, subln_g) * (1.0 - lam_init)).reshape(b, s, DIFF_HEADS * DIFF_V_DIM)
    return jnp.concatenate([c_out, d_out], axis=-1) @ w_out


def conv_ffn(hn, w_up, conv_w, conv_b, w_down):
    a = hn @ w_up
    ch = a.shape[-1]
    a = lax.conv_general_dilated(a, conv_w[:, None, :], window_strides=(1,),
                                 padding=[(CONV_WIDTH - 1, 0)],
                                 dimension_numbers=('NWC', 'WIO', 'NWC'),
                                 feature_group_count=ch) + conv_b
    gate, val = jnp.split(a, 2, axis=-1)
    return (jax.nn.silu(gate) * val) @ w_down


def setup_inputs(seed: int = 0) -> dict:
    key = jax.random.key(seed)
    ks = jax.random.split(key, 24)
    f32 = jnp.float32

    def nrm(k, shape, scale):
        return jax.random.normal(k, shape, f32) * scale

    return {
        'x': nrm(ks[0], (BATCH, SEQ, D_MODEL), 1.0),
        'w_in_even': nrm(ks[1], (N_EVEN, D_MODEL, IN_EVEN), D_MODEL ** -0.5),
        'w_out_even': nrm(ks[2], (N_EVEN, MIX_EVEN, D_MODEL), MIX_EVEN ** -0.5),
        'pool_w': nrm(ks[3], (N_EVEN, len(POOL_WINDOWS), POOL_GROUP, POOL_GROUP), POOL_GROUP ** -0.5),
        'pool_scale': 1.0 + nrm(ks[4], (N_EVEN, POOL_WIDTH), 0.02),
        'ret_gn_g': 1.0 + nrm(ks[5], (N_EVEN, RET_HEADS * RET_V_DIM), 0.02),
        'w_in_odd': nrm(ks[6], (N_ODD, D_MODEL, IN_ODD), D_MODEL ** -0.5),
        'w_out_odd': nrm(ks[7], (N_ODD, MIX_ODD, D_MODEL), MIX_ODD ** -0.5),
        'sgu_ln_g': 1.0 + nrm(ks[8], (N_ODD, SGU_WIDTH), 0.02),
        'sgu_w': nrm(ks[9], (N_ODD, SGU_GROUPS, SGU_CHUNK, SGU_CHUNK), SGU_CHUNK ** -0.5),
        'sgu_b': 1.0 + nrm(ks[10], (N_ODD, SGU_GROUPS, SGU_CHUNK), 0.1),
        'lam_q1': nrm(ks[11], (N_ODD, DIFF_HEAD_DIM), 0.1),
        'lam_k1': nrm(ks[12], (N_ODD, DIFF_HEAD_DIM), 0.1),
        'lam_q2': nrm(ks[13], (N_ODD, DIFF_HEAD_DIM), 0.1),
        'lam_k2': nrm(ks[14], (N_ODD, DIFF_HEAD_DIM), 0.1),
        'diff_subln_g': 1.0 + nrm(ks[15], (N_ODD, DIFF_V_DIM), 0.02),
        'rel_bias': nrm(ks[16], (N_BUCKETS, DIFF_HEADS), 0.5),
        'mix_norm_g': 1.0 + nrm(ks[17], (DEPTH, D_MODEL), 0.02),
        'ffn_norm_g': 1.0 + nrm(ks[18], (DEPTH, D_MODEL), 0.02),
        'w_up': nrm(ks[19], (DEPTH, D_MODEL, 2 * D_FF), D_MODEL ** -0.5),
        'conv_w': nrm(ks[20], (DEPTH, CONV_WIDTH, 2 * D_FF), CONV_WIDTH ** -0.5),
        'conv_b': nrm(ks[21], (DEPTH, 2 * D_FF), 0.02),
        'w_down': nrm(ks[22], (DEPTH, D_FF, D_MODEL), D_FF ** -0.5),
        'final_norm_g': 1.0 + nrm(ks[23], (D_MODEL,), 0.02),
    }


def reference(x, w_in_even, w_out_even, pool_w, pool_scale, ret_gn_g, w_in_odd, w_out_odd,
              sgu_ln_g, sgu_w, sgu_b, lam_q1, lam_k1, lam_q2, lam_k2, diff_subln_g, rel_bias,
              mix_norm_g, ffn_norm_g, w_up, conv_w, conv_b, w_down, final_norm_g):
    h = x
    for i in range(DEPTH):
        hn = rms_norm(h, mix_norm_g[i])
        if i % 2 == 0:
            e = i // 2
            h = h + pool_retention_mixer(hn, w_in_even[e], w_out_even[e], pool_w[e],
                                         pool_scale[e], ret_gn_g[e])
        else:
            o = i // 2
            h = h + sgu_diff_mixer(hn, w_in_odd[o], w_out_odd[o], sgu_ln_g[o], sgu_w[o], sgu_b[o],
                                   lam_q1[o], lam_k1[o], lam_q2[o], lam_k2[o], diff_subln_g[o],
                                   rel_bias, i)
        h = h + conv_ffn(rms_norm(h, ffn_norm_g[i]), w_up[i], conv_w[i], conv_b[i], w_down[i])
    return rms_norm(h, final_norm_g)
```

```python
import math
from contextlib import ExitStack
import numpy as np
import concourse.bass as bass
import concourse.mybir as mybir
from concourse.bass_utils import run_bass_kernel_spmd

F32 = mybir.dt.float32
BF16 = mybir.dt.bfloat16
AF = mybir.ActivationFunctionType
ALU = mybir.AluOpType

P = 128
T = 1024
NT = 8
D = 2048
KC = 16
DFF = 5632
NFF = 44
EPS = 1e-6
NCORES = 8
PAIRS = [[0, 1], [2, 3], [4, 5], [6, 7]]
NEGBIG = -30000.0
GELU = AF.Gelu

OFF_NG = 0
OFF_PSC = OFF_NG + 128
OFF_GNG = OFF_PSC + 8
OFF_SUB = OFF_GNG + 24
OFF_CW = OFF_SUB + 4
OFF_CB = OFF_CW + 1056
OFF_SB = OFF_CB + 352
OFF_C31 = OFF_SB + 16
OFF_QDEC = OFF_C31 + 4
OFF_KDEC = OFF_QDEC + 6
OFF_FLAG = OFF_KDEC + 6
OFF_NEGB = OFF_FLAG + 1
OFF_PCORR = OFF_NEGB + 1
NCOLS = OFF_PCORR + 64


class KB:
    def __init__(self, nc):
        self.nc = nc
        self.eng = {'pe': nc.tensor, 'act': nc.scalar, 'dve': nc.vector, 'pool': nc.gpsimd, 'sp': nc.sync}
        self.sems = []
        self.cur = {}
        self.cnt = {}
        self.waited = {e: {} for e in self.eng}
        self.own = {e: set() for e in self.eng}
        self.track = {}
        self.pending = {e: ([], []) for e in self.eng}
        self.dpool = {}
        self.didx = {}
        self.ccsid = None
        self.cctot = 0
        self.nops = 0

    def new_sem(self, name):
        h = self.nc.alloc_semaphore(name=name)
        self.sems.append(h)
        return len(self.sems) - 1

    def _deps(self, reads, writes, e=None):
        evs = []
        own = self.own.get(e, ())
        for k in reads:
            t = self.track.get(k)
            if t and t[0]:
                evs.append(t[0])
            if t and isinstance(k, tuple) and k[0] == "PB":
                for sid, val in t[1].items():
                    if sid not in own:
                        evs.append((sid, val))
        for k in writes:
            t = self.track.get(k)
            if t:
                if t[0]:
                    evs.append(t[0])
                evs.extend(t[1].items())
        return evs

    def _wait(self, e, evs):
        need = {}
        for sid, val in evs:
            if need.get(sid, 0) < val:
                need[sid] = val
        w = self.waited[e]
        for sid, val in need.items():
            if w.get(sid, 0) < val:
                self.eng[e].wait_ge(self.sems[sid], val)
                w[sid] = val

    def _commit(self, e, ev):
        pr, pw = self.pending[e]
        for k in pr:
            t = self.track.setdefault(k, [None, {}])
            if t[1].get(ev[0], 0) < ev[1]:
                t[1][ev[0]] = ev[1]
        for k in pw:
            self.track[k] = [ev, {}]
        pr.clear()
        pw.clear()

    def op(self, e, fn, reads=(), writes=(), signal=True):
        self._wait(e, self._deps(reads, writes, e))
        inst = fn(self.eng[e])
        self.nops += 1
        pr, pw = self.pending[e]
        pr.extend(reads)
        pw.extend(writes)
        if not signal:
            return None
        if e not in self.cur or self.cnt[e] >= 6000:
            self.cur[e] = self.new_sem("s_%s_%d" % (e, len(self.sems)))
            self.own[e].add(self.cur[e])
            self.cnt[e] = 0
        self.cnt[e] += 1
        inst.then_inc(self.sems[self.cur[e]], 1)
        ev = (self.cur[e], self.cnt[e])
        self._commit(e, ev)
        return ev

    def dma(self, e, pairs, reads=(), writes=()):
        if e not in self.dpool:
            self.dpool[e] = [[self.new_sem("d_%s_%d" % (e, i)), 0] for i in range(6)]
            self.didx[e] = 0
        slot = self.dpool[e][self.didx[e] % len(self.dpool[e])]
        self.didx[e] += 1
        evs = self._deps(reads, writes)
        if slot[1] > 0:
            evs.append((slot[0], slot[1]))
        self._wait(e, evs)
        for (o, i) in pairs:
            self.eng[e].dma_start(out=o, in_=i).then_inc(self.sems[slot[0]], 16)
            slot[1] += 16
            self.nops += 1
        ev = (slot[0], slot[1])
        pr, pw = self.pending[e]
        for k in reads:
            t = self.track.setdefault(k, [None, {}])
            if t[1].get(ev[0], 0) < ev[1]:
                t[1][ev[0]] = ev[1]
        for k in writes:
            self.track[k] = [ev, {}]
        return ev

    def allgather(self, src, dst, reads=(), writes=()):
        e = 'pool'
        if self.ccsid is None:
            self.ccsid = self.new_sem("ccsem")
        evs = self._deps(reads, writes)
        if self.cctot > 0:
            evs.append((self.ccsid, self.cctot))
        self._wait(e, evs)
        inst = self.nc.gpsimd.collective_compute("AllGather", ALU.bypass, replica_groups=self.pairs,
                                                 ins=[src], outs=[dst])
        inst.then_inc(self.sems[self.ccsid])
        self.cctot += 1
        ev = (self.ccsid, self.cctot)
        for k in reads:
            t = self.track.setdefault(k, [None, {}])
            t[1][ev[0]] = ev[1]
        for k in writes:
            self.track[k] = [ev, {}]
        return ev

    def barrier(self):
        evs = []
        for e2 in self.cur:
            evs.append((self.cur[e2], self.cnt[e2]))
        for e2 in self.dpool:
            if e2 == 'pool':
                continue
            for sid, tot in self.dpool[e2]:
                if tot:
                    evs.append((sid, tot))
        if self.cctot:
            evs.append((self.ccsid, self.cctot))
        for e in self.eng:
            self._wait(e, evs)


class StopBuild(Exception):
    pass


class Arena:
    def __init__(self, ap_f32, nbytes):
        self.ap = ap_f32
        self.n = nbytes
        self.off = 0

    def reset(self, off=0):
        self.off = off

    def alloc(self, shape, dt):
        esz = 4 if dt == F32 else 2
        free = 1
        for s in shape[1:]:
            free *= s
        nb = (free * esz + 31) // 32 * 32
        assert self.off + nb <= self.n, ("arena overflow", self.off, nb, self.n)
        v = self.ap[:, self.off // 4:(self.off + nb) // 4]
        self.off += nb
        if dt != F32:
            v = v.bitcast(dt)
        v = v[:, 0:free]
        if len(shape) == 3:
            v = v.rearrange("p (a b) -> p a b", a=shape[1])
        elif len(shape) == 4:
            v = v.rearrange("p (a b c) -> p a b c", a=shape[1], b=shape[2])
        return v


def build(n_layers=4, dbg_h=False, stop_after=None, ncores=NCORES):
    nc = bass.Bass("TRN2", target_bir_lowering=False)

    def din(name, shape):
        return nc.dram_tensor(name, list(shape), F32, kind="ExternalInput").ap()

    x_d = din("x", [T, D])
    n_ev = max(1, (n_layers + 1) // 2)
    n_od = max(1, n_layers // 2)
    n_ff = max(1, n_layers)
    w_in_even = din("w_in_even", [n_ev, D, 5120])
    w_out_even = din("w_out_even", [n_ev, D, D])
    pool_w = din("pool_w", [2, 4, 128, 128])
    w_in_odd = din("w_in_odd", [n_od, D, 5120])
    w_out_odd = din("w_out_odd", [n_od, D, D])
    w_up = din("w_up", [n_ff, D, 2 * DFF])
    w_down = din("w_down", [n_ff, DFF, D])
    c_ident = din("c_ident", [128, 128])
    c_rot = din("c_rot", [128, 2, 8, 64])
    c_intra = din("c_intra", [128, 6, 128])
    c_mask = din("c_mask", [128, 128])
    p_cols = din("p_cols", [128, NCOLS])
    p_fng = din("p_fng", [128, D])
    p_sln = din("p_sln", [128, 2, 1024])
    p_lam = din("p_lam", [128, 2, 4, 128])
    p_biasT = din("p_biasT", [128, 4, 2, 128])
    p_sguT = din("p_sguT", [2, 128, 8, 128])
    out_d = nc.dram_tensor("out", [T, D], F32, kind="ExternalOutput").ap()

    hspill = nc.dram_tensor("hspill", [T, D], F32).ap()
    xu_src = nc.dram_tensor("xu_src", [128, 64], F32).ap()
    xu_dst = nc.dram_tensor("xu_dst", [256, 64], F32).ap()
    st_src = nc.dram_tensor("st_src", [128, 256], F32).ap()
    st_dst = nc.dram_tensor("st_dst", [256, 256], F32).ap()
    xh_src = nc.dram_tensor("xh_src", [128, 16], F32).ap()
    xh_dst = nc.dram_tensor("xh_dst", [256, 16], F32).ap()
    KVW = 4096 + 4112
    KVQ = KVW // 4
    kv_src = [nc.dram_tensor("kv_src%d" % i, [128, KVQ], F32).ap() for i in range(4)]
    kv_dst = [nc.dram_tensor("kv_dst%d" % i, [256, KVQ], F32).ap() for i in range(4)]

    ARENA_BYTES = 40 * 1024
    with ExitStack() as es:
        def sb(name, shape, dt):
            return es.enter_context(nc.sbuf_tensor(name, shape, dt))

        H = sb("H", [P, NT, D], F32)
        XT = sb("XT", [P, KC, T + 2], BF16)
        MT = sb("MT", [P, KC, T], BF16)
        WB = [sb("WB0", [P, KC, 512], BF16), sb("WB1", [P, KC, 512], BF16)]
        PC = sb("PC", [P, NCOLS], F32)
        IDB = sb("IDB", [P, 128], BF16)
        MASK = sb("MASK", [P, 128], F32)
        SS = sb("SS", [P, 48], F32)
        ARN = sb("ARN", [P, ARENA_BYTES // 4], F32)
        PB = [es.enter_context(nc.psum_tensor("pb%d" % i, [P, 512], F32)) for i in range(8)]
        ar = Arena(ARN, ARENA_BYTES)
        kb = KB(nc)
        kb.pairs = [[2 * i, 2 * i + 1] for i in range(ncores // 2)]

        def chk(name):
            if stop_after == name:
                raise StopBuild()

        def pcol(off, n=1):
            return PC[:, off:off + n]

        kb.dma('sp', [(PC[:, :], p_cols[:, :])], writes=["PC"])
        kb.dma('sp', [(MASK[:, :], c_mask[:, :])], writes=["MASK"])
        kb.dma('pool', [(IDB[:, :], c_ident[:, :])], writes=["IDB"])
        for t in range(NT):
            kb.dma('sp', [(H[:, t, :], x_d[t * 128:(t + 1) * 128, :])], writes=[("H", t)])

        bank_rr = [0]

        def bank():
            b = bank_rr[0] % 8
            bank_rr[0] += 1
            return b

        wslot = [0]

        preloaded = {}

        def prefetch(key, pieces, nk=KC):
            preloaded[key] = load_w(pieces, nk)

        def load_w(pieces, nk=KC, key=None):
            if key is not None and key in preloaded:
                return preloaded.pop(key)
            s = wslot[0] % 2
            wslot[0] += 1
            pairs = []
            for (co, src) in pieces:
                ncols = src.shape[1]
                v = src.rearrange("(kc p) c -> p kc c", p=128)
                step = 4 if ncols * 4 >= 2048 else 8
                for k0 in range(0, nk, step):
                    k1 = min(nk, k0 + step)
                    pairs.append((WB[s][:, k0:k1, co:co + ncols], v[:, k0:k1, :]))
            kb.dma('pool', pairs, writes=[("WB", s)])
            return s

        def norm_phase(gidx, order, halo):
            kb.barrier()
            ar.reset()
            JK = ar.alloc([P, D], BF16)
            HN = [ar.alloc([P, D], BF16), ar.alloc([P, D], BF16)]
            XH = ar.alloc([P, 32], BF16)
            XHF = ar.alloc([P, 16], F32)
            for oi, t in enumerate(order):
                hn = HN[oi % 2]
                hk = ("HN", oi % 2)
                kb.op('act', lambda e: e.activation(out=JK[:, :], in_=H[:, t, :], func=AF.Square,
                                                    accum_out=SS[:, t:t + 1]),
                      reads=[("H", t)], writes=["JK", ("SS", t)])
                kb.op('act', lambda e: e.activation(out=SS[:, 8 + t:9 + t], in_=SS[:, t:t + 1], func=AF.Sqrt,
                                                    scale=1.0 / D, bias=EPS),
                      reads=[("SS", t)], writes=[("SS", 8 + t)])
                kb.op('dve', lambda e: e.reciprocal(out=SS[:, 16 + t:17 + t], in_=SS[:, 8 + t:9 + t]),
                      reads=[("SS", 8 + t)], writes=[("SS", 16 + t)])
                kb.op('dve', lambda e: e.tensor_scalar_mul(out=hn[:, :], in0=H[:, t, :],
                                                           scalar1=SS[:, 16 + t:17 + t]),
                      reads=[("H", t), ("SS", 16 + t)], writes=[hk])
                for half in range(2):
                    b = bank()
                    pbb = PB[b][:, :].bitcast(BF16).rearrange("p (a b) -> p a b", a=8)
                    for i in range(8):
                        kc = half * 8 + i
                        kb.op('pe', lambda e: e.transpose(out=pbb[:, i, :], in_=hn[:, kc * 128:(kc + 1) * 128],
                                                          identity=IDB[:, :]),
                              reads=[hk, "IDB"], writes=[("PB", b)], signal=(i == 7))
                    for i in range(8):
                        kc = half * 8 + i
                        g = pcol(OFF_NG + gidx * 16 + kc)
                        dst = XT[:, kc, 2 + t * 128:2 + (t + 1) * 128]
                        if half == 0:
                            kb.op('act', lambda e: e.activation(out=dst, in_=pbb[:, i, :], func=AF.Copy, scale=g),
                                  reads=[("PB", b), "PC"], writes=[("XT", t)])
                        else:
                            kb.op('dve', lambda e: e.tensor_scalar_mul(out=dst, in0=pbb[:, i, :], scalar1=g),
                                  reads=[("PB", b), "PC"], writes=[("XT", t)])
                if halo and t == NT - 1:
                    kb.op('act', lambda e: e.activation(out=XH[:, :].rearrange("p (a b) -> p a b", a=16),
                                                        in_=XT[:, :, T:T + 2], func=AF.Copy),
                          reads=[("XT", t)], writes=["XH"])
                    kb.dma('sp', [(xh_src[:, :], XH[:, :].bitcast(F32))], reads=["XH"], writes=["xh_src"])
                    kb.allgather(xh_src[:, :], xh_dst[:, :], reads=["xh_src"], writes=["xh_dst"])
                    kb.dma('sp', [(XHF[:, :], xh_dst[0:128, :])], reads=["xh_dst"], writes=["XHF"])
                    kb.op('dve', lambda e: e.tensor_scalar_mul(
                        out=XT[:, :, 0:2], in0=XHF[:, :].bitcast(BF16).rearrange("p (a b) -> p a b", a=16),
                        scalar1=pcol(OFF_FLAG)), reads=["XHF", "PC"], writes=["XTH"])

        def out_proj(chunks, nk, key=None):
            slots = [load_w([(0, chunks[0])], nk, key=(key, 0) if key else None)]
            for cb in range(4):
                if cb + 1 < 4:
                    slots.append(load_w([(0, chunks[cb + 1])], nk, key=(key, cb + 1) if key else None))
                s = slots[cb]
                for t in range(NT):
                    b = bank()
                    for kc in range(nk):
                        kb.op('pe', lambda e: e.matmul(PB[b][:, :], lhsT=MT[:, kc, t * 128:(t + 1) * 128],
                                                       rhs=WB[s][:, kc, :], start=(kc == 0), stop=(kc == nk - 1)),
                              reads=[("MT", kc, t), ("WB", s)], writes=[("PB", b)], signal=(kc == nk - 1))
                    hs = H[:, t, cb * 512:(cb + 1) * 512]
                    kb.op('dve', lambda e: e.tensor_tensor(out=hs, in0=hs, in1=PB[b][:, :], op=ALU.add),
                          reads=[("PB", b), ("H", t)], writes=[("H", t)])

        def ffn(l):
            kb.barrier()
            ar.reset()
            AG = [ar.alloc([P, 514], F32), ar.alloc([P, 514], F32)]
            AV = [ar.alloc([P, 514], F32), ar.alloc([P, 514], F32)]
            CG = [ar.alloc([P, 512], F32), ar.alloc([P, 512], F32)]
            CV = [ar.alloc([P, 512], F32), ar.alloc([P, 512], F32)]
            SG = [ar.alloc([P, 512], F32), ar.alloc([P, 512], F32)]
            wup = w_up[l]
            wdn = w_down[l]
            groups = [(0, 12), (12, 12), (24, 10), (34, 10)]

            def up_pieces(j0):
                return [(0, wup[:, j0 * 128:(j0 + 2) * 128]), (256, wup[:, DFF + j0 * 128:DFF + (j0 + 2) * 128])]

            pair_list = []
            for (g0, gn) in groups:
                for j0 in range(g0, g0 + gn, 2):
                    pair_list.append(j0)
            nxt = None
            it = 0
            for (g0, gn) in groups:
                for j0 in range(g0, g0 + gn, 2):
                    s = nxt if nxt is not None else load_w(up_pieces(j0), key=("up", l, j0))
                    nxt = None
                    if j0 + 2 < g0 + gn:
                        nxt = load_w(up_pieces(j0 + 2))
                    for jj in range(2):
                        j = j0 + jj
                        jl = j - g0
                        wg = lambda kc: WB[s][:, kc, jj * 128:(jj + 1) * 128]
                        wv = lambda kc: WB[s][:, kc, 256 + jj * 128:256 + (jj + 1) * 128]
                        cw = lambda r: pcol(OFF_CW + l * 264 + r * 88 + j)
                        cbias = pcol(OFF_CB + l * 88 + j)
                        bh = bank()
                        for kc in range(KC):
                            kb.op('pe', lambda e: e.matmul(PB[bh][:, 0:2], lhsT=wg(kc), rhs=XT[:, kc, 0:2],
                                                           start=(kc == 0), stop=(kc == KC - 1)),
                                  reads=["XTH", ("WB", s)], writes=[("PB", bh)], signal=False)
                        for kc in range(KC):
                            kb.op('pe', lambda e: e.matmul(PB[bh][:, 2:4], lhsT=wv(kc), rhs=XT[:, kc, 0:2],
                                                           start=(kc == 0), stop=(kc == KC - 1)),
                                  reads=["XTH", ("WB", s)], writes=[("PB", bh)], signal=(kc == KC - 1))
                        for n in range(2):
                            q = it % 2
                            it += 1
                            bg = bank()
                            bv = bank()
                            xt_keys = [("XT", t) for t in range(n * 4, n * 4 + 4)]
                            for kc in range(KC):
                                kb.op('pe', lambda e: e.matmul(PB[bg][:, :], lhsT=wg(kc),
                                                               rhs=XT[:, kc, 2 + n * 512:2 + (n + 1) * 512],
                                                               start=(kc == 0), stop=(kc == KC - 1)),
                                      reads=xt_keys + [("WB", s)], writes=[("PB", bg)], signal=(kc == KC - 1))
                            for kc in range(KC):
                                kb.op('pe', lambda e: e.matmul(PB[bv][:, :], lhsT=wv(kc),
                                                               rhs=XT[:, kc, 2 + n * 512:2 + (n + 1) * 512],
                                                               start=(kc == 0), stop=(kc == KC - 1)),
                                      reads=xt_keys + [("WB", s)], writes=[("PB", bv)], signal=(kc == KC - 1))
                            ag, av, cg, cv, sg = AG[q], AV[q], CG[q], CV[q], SG[q]
                            kag, kav, kcg, kcv, ksg = ("AG", q), ("AV", q), ("CG", q), ("CV", q), ("SG", q)
                            kb.op('act', lambda e: e.activation(out=ag[:, 2:514], in_=PB[bg][:, :], func=AF.Copy),
                                  reads=[("PB", bg)], writes=[kag])
                            kb.op('act', lambda e: e.activation(out=av[:, 2:514], in_=PB[bv][:, :], func=AF.Copy),
                                  reads=[("PB", bv)], writes=[kav])
                            if n == 0:
                                kb.op('dve', lambda e: e.tensor_copy(out=ag[:, 0:2], in_=PB[bh][:, 0:2]),
                                      reads=[("PB", bh)], writes=[kag])
                                kb.op('dve', lambda e: e.tensor_copy(out=av[:, 0:2], in_=PB[bh][:, 2:4]),
                                      reads=[("PB", bh)], writes=[kav])
                            else:
                                pq = 1 - q
                                kb.op('dve', lambda e: e.tensor_copy(out=ag[:, 0:2], in_=AG[pq][:, 512:514]),
                                      reads=[("AG", pq)], writes=[kag])
                                kb.op('dve', lambda e: e.tensor_copy(out=av[:, 0:2], in_=AV[pq][:, 512:514]),
                                      reads=[("AV", pq)], writes=[kav])
                            kb.op('act', lambda e: e.activation(out=cg[:, :], in_=PB[bg][:, :], func=AF.Identity,
                                                                scale=cw(2), bias=cbias),
                                  reads=[("PB", bg), "PC"], writes=[kcg])
                            kb.op('act', lambda e: e.activation(out=cv[:, :], in_=PB[bv][:, :], func=AF.Identity,
                                                                scale=CWV(l, 2, j), bias=CBV(l, j)),
                                  reads=[("PB", bv), "PC"], writes=[kcv])
                            kb.op('dve', lambda e: e.scalar_tensor_tensor(out=cg[:, :], in0=ag[:, 1:513], scalar=cw(1),
                                                                          in1=cg[:, :], op0=ALU.mult, op1=ALU.add),
                                  reads=[kag, "PC", kcg], writes=[kcg])
                            kb.op('dve', lambda e: e.scalar_tensor_tensor(out=cg[:, :], in0=ag[:, 0:512], scalar=cw(0),
                                                                          in1=cg[:, :], op0=ALU.mult, op1=ALU.add),
                                  reads=[kag, "PC", kcg], writes=[kcg])
                            kb.op('dve', lambda e: e.scalar_tensor_tensor(out=cv[:, :], in0=av[:, 1:513],
                                                                          scalar=CWV(l, 1, j), in1=cv[:, :],
                                                                          op0=ALU.mult, op1=ALU.add),
                                  reads=[kav, "PC", kcv], writes=[kcv])
                            kb.op('dve', lambda e: e.scalar_tensor_tensor(out=cv[:, :], in0=av[:, 0:512],
                                                                          scalar=CWV(l, 0, j), in1=cv[:, :],
                                                                          op0=ALU.mult, op1=ALU.add),
                                  reads=[kav, "PC", kcv], writes=[kcv])
                            kb.op('act', lambda e: e.activation(out=sg[:, :], in_=cg[:, :], func=AF.Silu),
                                  reads=[kcg], writes=[ksg])
                            kb.op('dve', lambda e: e.tensor_tensor(out=MT[:, jl, n * 512:(n + 1) * 512], in0=sg[:, :],
                                                                   in1=cv[:, :], op=ALU.mult),
                                  reads=[ksg, kcv], writes=[("MT", jl, t) for t in range(n * 4, n * 4 + 4)])
                out_proj([wdn[g0 * 128:(g0 + gn) * 128, cb * 512:(cb + 1) * 512] for cb in range(4)], gn)
                if (g0, gn) != groups[-1]:
                    pass

        def CWV(l, r, j):
            return pcol(OFF_CW + l * 264 + r * 88 + NFF + j)

        def CBV(l, j):
            return pcol(OFF_CB + l * 88 + NFF + j)

        def even_mixer(e_idx):
            kb.barrier()
            ar.reset()
            win = w_in_even[e_idx]
            wout = w_out_even[e_idx]
            I0, I1, I2, I3 = 512, 512 + 768, 512 + 1536, 512 + 1536 + 1536
            ROT = ar.alloc([P, 2, 8, 64], F32)
            INTRA = ar.alloc([P, 6, 128], F32)
            U = ar.alloc([P, 4, 16 + T], BF16)
            PW = ar.alloc([P, 4, 128], BF16)
            UHS = ar.alloc([P, 4, 16], F32)
            UHR = ar.alloc([P, 4, 16], F32)
            mark = ar.off
            kb.dma('sp', [(ROT[:, :, :, :], c_rot[:, :, :, :])], writes=["ROT"])
            kb.dma('sp', [(INTRA[:, :, :], c_intra[:, :, :])], writes=["INTRA"])
            kb.dma('pool', [(PW[:, :, :], pool_w[e_idx].rearrange("g c d -> c g d"))], writes=["PW"])

            s = load_w([(0, win[:, 0:512])], key=("eu", e_idx))
            for g in range(4):
                for n in range(2):
                    b = bank()
                    for kc in range(KC):
                        kb.op('pe', lambda e: e.matmul(PB[b][:, :], lhsT=WB[s][:, kc, g * 128:(g + 1) * 128],
                                                       rhs=XT[:, kc, 2 + n * 512:2 + (n + 1) * 512],
                                                       start=(kc == 0), stop=(kc == KC - 1)),
                              reads=[("XT", t) for t in range(n * 4, n * 4 + 4)] + [("WB", s)],
                              writes=[("PB", b)], signal=(kc == KC - 1))
                    kb.op('act', lambda e: e.activation(out=U[:, g, 16 + n * 512:16 + (n + 1) * 512], in_=PB[b][:, :],
                                                        func=AF.Copy),
                          reads=[("PB", b)], writes=[("U", g, n)])
            kb.op('act', lambda e: e.activation(out=UHS[:, :, :], in_=U[:, :, T:T + 16], func=AF.Copy),
                  reads=[("U", g, 1) for g in range(4)], writes=["UHS"])
            kb.dma('sp', [(xu_src[:, :], UHS[:, :, :].rearrange("p a b -> p (a b)"))], reads=["UHS"], writes=["xu_src"])
            kb.allgather(xu_src[:, :], xu_dst[:, :], reads=["xu_src"], writes=["xu_dst"])
            kb.dma('sp', [(UHR[:, :, :].rearrange("p a b -> p (a b)"), xu_dst[0:128, :])], reads=["xu_dst"],
                   writes=["UHR"])
            kb.op('dve', lambda e: e.tensor_scalar_mul(out=U[:, :, 0:16], in0=UHR[:, :, :], scalar1=pcol(OFF_FLAG)),
                  reads=["UHR", "PC"], writes=["UH"])

            if stop_after == "mix_a":
                return
            GS = ar.alloc([P, 8, 256], BF16)
            RSB = ar.alloc([P, 8, 256], BF16)
            QDT = ar.alloc([P, 8, 128], BF16)
            STATE = ar.alloc([P, 256], F32)
            STATEB = ar.alloc([P, 256], BF16)
            mark_r = ar.off
            NB3 = 4
            for h in range(6):
                dec = 1.0 - 2.0 ** (-5.0 - h)
                chunk_dec = dec ** 128
                kb.barrier()
                ar.reset(mark_r)
                QK = [ar.alloc([P, 2, 2, 64], BF16) for _ in range(NB3)]
                VB = [ar.alloc([P, 256], BF16) for _ in range(NB3)]
                QD = [ar.alloc([P, 128], BF16) for _ in range(NB3)]
                KD = [ar.alloc([P, 128], BF16) for _ in range(NB3)]
                QKT = [ar.alloc([P, 2, 128], BF16) for _ in range(2)]
                ST = [ar.alloc([P, 128], BF16) for _ in range(2)]
                R1 = ar.alloc([P, 2, 64], F32)
                R2 = ar.alloc([P, 2, 64], F32)
                R3 = ar.alloc([P, 2, 64], F32)
                R4 = ar.alloc([P, 2, 64], F32)
                def g_pieces(hx):
                    return [(0, win[:, I3 + hx * 256:I3 + (hx + 1) * 256])]

                def qkv_pieces(hx):
                    return [(0, win[:, I0 + hx * 128:I0 + (hx + 1) * 128]),
                            (128, win[:, I1 + hx * 128:I1 + (hx + 1) * 128]),
                            (256, win[:, I2 + hx * 256:I2 + (hx + 1) * 256])]

                sg_ = load_w(g_pieces(h), key=("eg", e_idx, h))
                sq_ = load_w(qkv_pieces(h), key=("eq", e_idx, h))
                kb.op('dve', lambda e: e.memset(STATE[:, :], 0.0), writes=["STATE"])
                kb.op('dve', lambda e: e.memset(STATEB[:, :], 0.0), writes=["STATEB"])

                def st_P(t):
                    b = t % 2
                    for kc in range(KC):
                        kb.op('pe', lambda e: e.matmul(PB[b][:, :], lhsT=XT[:, kc, 2 + t * 128:2 + (t + 1) * 128],
                                                       rhs=WB[sq_][:, kc, :], start=(kc == 0), stop=(kc == KC - 1)),
                              reads=[("XT", t), ("WB", sq_)], writes=[("PB", b)], signal=(kc == KC - 1))
                def st_ROT(t):
                    b = t % 2
                    q3 = t % NB3
                    qk, vb, qd, kd = QK[q3], VB[q3], QD[q3], KD[q3]
                    z4 = PB[b][:, 0:256].rearrange("p (a b c) -> p a b c", a=2, b=2)
                    x1 = z4[:, :, 0, :]
                    x2 = z4[:, :, 1, :]
                    cosb = ROT[:, 0, t, :].unsqueeze(1).to_broadcast([P, 2, 64])
                    sinb = ROT[:, 1, t, :].unsqueeze(1).to_broadcast([P, 2, 64])
                    kb.op('dve', lambda e: e.tensor_tensor(out=R1[:, :, :], in0=x1, in1=cosb, op=ALU.mult),
                          reads=[("PB", b), "ROT"], writes=["R1"])
                    kb.op('dve', lambda e: e.tensor_tensor(out=R2[:, :, :], in0=x2, in1=sinb, op=ALU.mult),
                          reads=[("PB", b), "ROT"], writes=["R2"])
                    kb.op('dve', lambda e: e.tensor_tensor(out=qk[:, :, 0, :], in0=R1[:, :, :], in1=R2[:, :, :],
                                                           op=ALU.subtract),
                          reads=["R1", "R2"], writes=[("QK0", q3)])
                    kb.op('dve', lambda e: e.tensor_tensor(out=R3[:, :, :], in0=x1, in1=sinb, op=ALU.mult),
                          reads=[("PB", b), "ROT"], writes=["R3"])
                    kb.op('dve', lambda e: e.tensor_tensor(out=R4[:, :, :], in0=x2, in1=cosb, op=ALU.mult),
                          reads=[("PB", b), "ROT"], writes=["R4"])
                    kb.op('dve', lambda e: e.tensor_tensor(out=qk[:, :, 1, :], in0=R3[:, :, :], in1=R4[:, :, :],
                                                           op=ALU.add),
                          reads=["R3", "R4"], writes=[("QK1", q3)])
                    kb.op('act', lambda e: e.activation(out=vb[:, :], in_=PB[b][:, 256:512], func=AF.Copy),
                          reads=[("PB", b)], writes=[("VB", q3)])
                    qf = qk[:, 0, :, :].rearrange("p a b -> p (a b)")
                    kf = qk[:, 1, :, :].rearrange("p a b -> p (a b)")
                    kb.op('act', lambda e: e.activation(out=qd[:, :], in_=qf, func=AF.Copy, scale=pcol(OFF_QDEC + h)),
                          reads=[("QK0", q3), ("QK1", q3), "PC"], writes=[("QD", q3)])
                    kb.op('act', lambda e: e.activation(out=kd[:, :], in_=kf, func=AF.Copy, scale=pcol(OFF_KDEC + h)),
                          reads=[("QK0", q3), ("QK1", q3), "PC"], writes=[("KD", q3)])

                def st_TR(t):
                    q3 = t % NB3
                    qk, qd = QK[q3], QD[q3]
                    qf = qk[:, 0, :, :].rearrange("p a b -> p (a b)")
                    kf = qk[:, 1, :, :].rearrange("p a b -> p (a b)")
                    bt = 2 + (t % 2)
                    pbt = PB[bt][:, :].bitcast(BF16).rearrange("p (a b) -> p a b", a=8)
                    kb.op('pe', lambda e: e.transpose(out=pbt[:, 0, :], in_=qf, identity=IDB[:, :]),
                          reads=[("QK0", q3), ("QK1", q3), "IDB"], writes=[("PB", bt)], signal=False)
                    kb.op('pe', lambda e: e.transpose(out=pbt[:, 1, :], in_=kf, identity=IDB[:, :]),
                          reads=[("QK0", q3), ("QK1", q3), "IDB"], writes=[("PB", bt)], signal=False)
                    kb.op('pe', lambda e: e.transpose(out=pbt[:, 2, :], in_=qd[:, :], identity=IDB[:, :]),
                          reads=[("QD", q3), "IDB"], writes=[("PB", bt)])
                    kb.op('dve', lambda e: e.tensor_copy(out=QKT[t % 2][:, :, :], in_=pbt[:, 0:2, :]),
                          reads=[("PB", bt)], writes=[("QKT", t % 2)])
                    kb.op('dve', lambda e: e.tensor_copy(out=QDT[:, t, :], in_=pbt[:, 2, :]),
                          reads=[("PB", bt)], writes=[("QDT", t)])

                def st_S(t):
                    bs = 4
                    qkt = QKT[t % 2]
                    kb.op('pe', lambda e: e.matmul(PB[bs][:, 0:128], lhsT=qkt[:, 1, :], rhs=qkt[:, 0, :],
                                                   start=True, stop=True),
                          reads=[("QKT", t % 2)], writes=[("PB", bs)])
                    kb.op('dve', lambda e: e.tensor_tensor(out=ST[t % 2][:, :], in0=PB[bs][:, 0:128], in1=INTRA[:, h, :],
                                                           op=ALU.mult),
                          reads=[("PB", bs), "INTRA"], writes=[("ST", t % 2)])

                def st_R(t):
                    q3 = t % NB3
                    vb, kd = VB[q3], KD[q3]
                    br, bk = 5, 6
                    kb.op('pe', lambda e: e.matmul(PB[br][:, 0:256], lhsT=ST[t % 2][:, :], rhs=vb[:, :], start=True, stop=False),
                          reads=[("ST", t % 2), ("VB", q3)], writes=[("PB", br)], signal=False)
                    kb.op('pe', lambda e: e.matmul(PB[br][:, 0:256], lhsT=QDT[:, t, :], rhs=STATEB[:, :],
                                                   start=False, stop=True),
                          reads=[("QDT", t), "STATEB"], writes=[("PB", br)])
                    kb.op('pe', lambda e: e.matmul(PB[bk][:, 0:256], lhsT=kd[:, :], rhs=vb[:, :], start=True, stop=True),
                          reads=[("KD", q3), ("VB", q3)], writes=[("PB", bk)])
                    kb.op('act', lambda e: e.activation(out=RSB[:, t, :], in_=PB[br][:, 0:256], func=AF.Copy),
                          reads=[("PB", br)], writes=[("RSB", t)])
                    kb.op('dve', lambda e: e.scalar_tensor_tensor(out=STATE[:, :], in0=STATE[:, :], scalar=chunk_dec,
                                                                  in1=PB[bk][:, 0:256], op0=ALU.mult, op1=ALU.add),
                          reads=["STATE", ("PB", bk)], writes=["STATE"])
                    kb.op('act', lambda e: e.activation(out=STATEB[:, :], in_=STATE[:, :], func=AF.Copy),
                          reads=["STATE"], writes=["STATEB"])

                for i in range(NT + 3):
                    if i < NT:
                        st_P(i)
                    if 0 <= i - 1 < NT:
                        st_TR(i - 1)
                    if 0 <= i - 2 < NT:
                        st_S(i - 2)
                    if 0 <= i - 3 < NT:
                        st_R(i - 3)
                    if i < NT:
                        st_ROT(i)
                if stop_after == "mix_b1":
                    return
                kb.dma('sp', [(st_src[:, :], STATE[:, :])], reads=["STATE"], writes=["st_src"])
                kb.allgather(st_src[:, :], st_dst[:, :], reads=["st_src"], writes=["st_dst"])
                nxt_q = None
                for t in range(NT):
                    b = t % 2
                    for kc in range(KC):
                        kb.op('pe', lambda e: e.matmul(PB[b][:, 0:256], lhsT=XT[:, kc, 2 + t * 128:2 + (t + 1) * 128],
                                                       rhs=WB[sg_][:, kc, 0:256], start=(kc == 0), stop=(kc == KC - 1)),
                              reads=[("XT", t), ("WB", sg_)], writes=[("PB", b)], signal=(kc == KC - 1))
                    kb.op('act', lambda e: e.activation(out=GS[:, t, :], in_=PB[b][:, 0:256], func=AF.Silu),
                          reads=[("PB", b)], writes=[("GS", t)])
                if h + 1 < 6:
                    prefetch(("eg", e_idx, h + 1), g_pieces(h + 1))
                    prefetch(("eq", e_idx, h + 1), qkv_pieces(h + 1))
                else:
                    prefetch((("wo", "e", e_idx), 0), [(0, wout[:, 0:512])])
                    prefetch((("wo", "e", e_idx), 1), [(0, wout[:, 512:1024])])
                kb.barrier()
                ar.reset(mark_r)
                SA = ar.alloc([P, 256], F32)
                SC = ar.alloc([P, 256], F32)
                SCB = [ar.alloc([P, 256], BF16) for _ in range(2)]
                RT = [ar.alloc([P, 256], F32) for _ in range(2)]
                YF = [ar.alloc([P, 256], F32) for _ in range(2)]
                YG = [ar.alloc([P, 256], BF16) for _ in range(2)]
                BNS = [ar.alloc([P, 8], F32) for _ in range(2)]
                MV = [ar.alloc([P, 4], F32) for _ in range(2)]
                kb.dma('sp', [(SA[:, :], st_dst[0:128, :])], reads=["st_dst"], writes=["SA"])
                kb.op('dve', lambda e: e.tensor_scalar_mul(out=SC[:, :], in0=SA[:, :], scalar1=pcol(OFF_FLAG)),
                      reads=["SA", "PC"], writes=["SC"])

                def p2_A(t):
                    q = t % 2
                    kb.op('act', lambda e: e.activation(out=SCB[q][:, :], in_=SC[:, :], func=AF.Copy),
                          reads=["SC"], writes=[("SCB", q)])
                    if t < NT - 1:
                        kb.op('dve', lambda e: e.tensor_scalar_mul(out=SC[:, :], in0=SC[:, :], scalar1=chunk_dec),
                              reads=["SC"], writes=["SC"])
                    bc = 4 + q
                    kb.op('pe', lambda e: e.matmul(PB[bc][:, 0:256], lhsT=QDT[:, t, :], rhs=SCB[q][:, :],
                                                   start=True, stop=True),
                          reads=[("QDT", t), ("SCB", q)], writes=[("PB", bc)])
                    kb.op('dve', lambda e: e.tensor_tensor(out=RT[q][:, :], in0=RSB[:, t, :], in1=PB[bc][:, 0:256],
                                                           op=ALU.add),
                          reads=[("RSB", t), ("PB", bc)], writes=[("RT", q)])
                    kb.op('dve', lambda e: e.bn_stats(out=BNS[q][:, 0:6], in_=RT[q][:, :]), reads=[("RT", q)], writes=[("BNS", q)])
                    kb.op('dve', lambda e: e.bn_aggr(out=MV[q][:, 0:2], in_=BNS[q][:, 0:6]), reads=[("BNS", q)], writes=[("MV", q)])
                    kb.op('act', lambda e: e.activation(out=MV[q][:, 2:3], in_=MV[q][:, 1:2], func=AF.Sqrt, bias=EPS),
                          reads=[("MV", q)], writes=[("MV2", q)])
                    kb.op('dve', lambda e: e.reciprocal(out=MV[q][:, 3:4], in_=MV[q][:, 2:3]), reads=[("MV2", q)], writes=[("MV3", q)])
                    kb.op('dve', lambda e: e.tensor_scalar(out=YF[q][:, :], in0=RT[q][:, :], scalar1=MV[q][:, 0:1],
                                                           scalar2=MV[q][:, 3:4], op0=ALU.subtract, op1=ALU.mult),
                          reads=[("RT", q), ("MV", q), ("MV3", q)], writes=[("YF", q)])
                    kb.op('dve', lambda e: e.tensor_tensor(out=YG[q][:, :], in0=YF[q][:, :], in1=GS[:, t, :], op=ALU.mult),
                          reads=[("YF", q), ("GS", t)], writes=[("YG", q)])

                def p2_B(t):
                    q = t % 2
                    by = 6 + q
                    pby = PB[by][:, :].bitcast(BF16).rearrange("p (a b) -> p a b", a=8)
                    kb.op('pe', lambda e: e.transpose(out=pby[:, 0, :], in_=YG[q][:, 0:128], identity=IDB[:, :]),
                          reads=[("YG", q), "IDB"], writes=[("PB", by)], signal=False)
                    kb.op('pe', lambda e: e.transpose(out=pby[:, 1, :], in_=YG[q][:, 128:256], identity=IDB[:, :]),
                          reads=[("YG", q), "IDB"], writes=[("PB", by)])
                    for j in range(2):
                        kc = 4 + 2 * h + j
                        gcol = pcol(OFF_GNG + e_idx * 12 + 2 * h + j)
                        dst = MT[:, kc, t * 128:(t + 1) * 128]
                        kb.op('act', lambda e: e.activation(out=dst, in_=pby[:, j, :], func=AF.Copy, scale=gcol),
                              reads=[("PB", by), "PC"], writes=[("MT", kc, t)])

                for i in range(NT + 1):
                    if i < NT:
                        p2_A(i)
                    if i - 1 >= 0:
                        p2_B(i - 1)

            if stop_after == "mix_b":
                return
            kb.barrier()
            ar.reset(mark)
            S_A = ar.alloc([P, 528], F32)
            S_B = ar.alloc([P, 528], F32)
            YP = [ar.alloc([P, 512], BF16), ar.alloc([P, 512], BF16)]
            it = 0
            for g in range(4):
                w = 2 ** (g + 1)
                for n in range(2):
                    a0 = U[:, g, n * 512:n * 512 + 528]
                    rk = [("U", g, n), "UH"] + ([("U", g, 0)] if n == 1 else [])
                    kb.op('dve', lambda e: e.tensor_tensor(out=S_A[:, 1:528], in0=a0[:, 1:528], in1=a0[:, 0:527],
                                                           op=ALU.add), reads=rk, writes=["S_A"])
                    cur, curk, oth, othk = S_A, "S_A", S_B, "S_B"
                    sh = 1
                    for step in range(g):
                        sh2 = sh * 2
                        lo = 2 * sh2 - 1
                        kb.op('dve', lambda e: e.tensor_tensor(out=oth[:, lo:528], in0=cur[:, lo:528],
                                                               in1=cur[:, lo - sh2:528 - sh2], op=ALU.add),
                              reads=[curk], writes=[othk])
                        cur, curk, oth, othk = oth, othk, cur, curk
                        sh = sh2
                    if n == 0:
                        kb.op('dve', lambda e: e.tensor_tensor(out=cur[:, 16:32], in0=cur[:, 16:32],
                                                               in1=PC[:, OFF_PCORR + g * 16:OFF_PCORR + (g + 1) * 16],
                                                               op=ALU.mult),
                              reads=[curk, "PC"], writes=[curk])
                    yp = YP[it % 2]
                    ypk = ("YP", it % 2)
                    it += 1
                    kb.op('dve', lambda e: e.scalar_tensor_tensor(out=yp[:, :], in0=cur[:, 16:528], scalar=1.0 / w,
                                                                  in1=a0[:, 16:528], op0=ALU.mult, op1=ALU.subtract),
                          reads=[curk] + rk, writes=[ypk])
                    b = bank()
                    kb.op('pe', lambda e: e.matmul(PB[b][:, :], lhsT=PW[:, g, :], rhs=yp[:, :], start=True, stop=True),
                          reads=["PW", ypk], writes=[("PB", b)])
                    kb.op('act', lambda e: e.activation(out=MT[:, g, n * 512:(n + 1) * 512], in_=PB[b][:, :],
                                                        func=AF.Copy, scale=pcol(OFF_PSC + e_idx * 4 + g)),
                          reads=[("PB", b), "PC"], writes=[("MT", g, t) for t in range(n * 4, n * 4 + 4)])
            if stop_after == "mix_c":
                return
            out_proj([wout[:, cb * 512:(cb + 1) * 512] for cb in range(4)], KC, key=("wo", "e", e_idx))

        def odd_mixer(o_idx, layer_idx):
            kb.barrier()
            ar.reset()
            win = w_in_odd[o_idx]
            wout = w_out_odd[o_idx]
            J0, J1, J2 = 2048, 3072, 4096
            Hb = H[:, :, :].rearrange("p a b -> p (a b)").bitcast(BF16)
            KT = Hb[:, 0:8192].rearrange("p (a b) -> p a b", a=8)
            V1 = Hb[:, 8192:8192 + 8224].rearrange("p (a b c) -> p a b c", a=8, b=4)
            QT = Hb[:, 16416:16416 + 8192].rearrange("p (a b) -> p a b", a=8)
            Hf = H[:, :, :].rearrange("p a b -> p (a b)")
            XTb = XT[:, :, :].rearrange("p a b -> p (a b)")
            KTP = XTb[:, 0:8192].rearrange("p (a b) -> p a b", a=8)
            V1P = XTb[:, 8192:8192 + 8224].rearrange("p (a b c) -> p a b c", a=8, b=4)
            lam_init = 0.8 - 0.6 * math.exp(-0.3 * layer_idx)

            BIAS = ar.alloc([P, 4, 2, 128], F32)
            ED = ar.alloc([P, 4, 2, 128], BF16)
            LAMV = ar.alloc([P, 4, 128], F32)
            LT = ar.alloc([P, 2, 128], F32)
            LS = ar.alloc([P, 8], F32)
            CB = ar.alloc([P, 8], F32)
            mark0 = ar.off
            kb.dma('sp', [(BIAS[:, :, :, :], p_biasT[:, :, :, :])], writes=["BIAS"])
            kb.dma('sp', [(LAMV[:, :, :], p_lam[:, o_idx, :, :])], writes=["LAMV"])
            kb.op('act', lambda e: e.activation(out=ED[:, :, :, :], in_=BIAS[:, :, :, :], func=AF.Exp),
                  reads=["BIAS"], writes=["ED"])
            for hh in range(4):
                kb.op('dve', lambda e: e.tensor_tensor(out=ED[:, hh, 0, :], in0=ED[:, hh, 0, :], in1=MASK[:, :],
                                                       op=ALU.mult), reads=["ED", "MASK"], writes=["ED"])
            kb.op('dve', lambda e: e.tensor_tensor(out=LT[:, 0, :], in0=LAMV[:, 0, :], in1=LAMV[:, 1, :], op=ALU.mult),
                  reads=["LAMV"], writes=["LT0"])
            kb.op('dve', lambda e: e.tensor_tensor(out=LT[:, 1, :], in0=LAMV[:, 2, :], in1=LAMV[:, 3, :], op=ALU.mult),
                  reads=["LAMV"], writes=["LT1"])
            kb.op('dve', lambda e: e.reduce_sum(out=LS[:, 0:2], in_=LT[:, :, :], axis=mybir.AxisListType.X),
                  reads=["LT0", "LT1"], writes=["LS"])
            kb.op('act', lambda e: e.activation(out=LS[:, 2:4], in_=LS[:, 0:2], func=AF.Exp), reads=["LS"], writes=["LS2"])
            kb.op('dve', lambda e: e.tensor_tensor(out=LS[:, 4:5], in0=LS[:, 2:3], in1=LS[:, 3:4], op=ALU.subtract),
                  reads=["LS2"], writes=["LS4"])
            kb.op('dve', lambda e: e.tensor_scalar(out=LS[:, 5:6], in0=LS[:, 4:5], scalar1=lam_init, scalar2=-1.0,
                                                   op0=ALU.add, op1=ALU.mult), reads=["LS4"], writes=["LAM"])
            kb.op('dve', lambda e: e.tensor_copy(out=CB[:, 0:4], in_=PC[:, OFF_C31:OFF_C31 + 4]), reads=["PC"], writes=["CB0"])
            kb.op('dve', lambda e: e.tensor_scalar_add(out=CB[:, 4:8], in0=PC[:, OFF_C31:OFF_C31 + 4], scalar1=pcol(OFF_NEGB)),
                  reads=["PC"], writes=["CB1"])

            def proj_fm(col0, dst, nm):
                for c in range(2):
                    s = load_w([(0, win[:, col0 + c * 512:col0 + (c + 1) * 512])], key=("ofm", o_idx, nm, c))
                    for f in range(4):
                        for n in range(2):
                            b = bank()
                            for kc in range(KC):
                                kb.op('pe', lambda e: e.matmul(PB[b][:, :], lhsT=WB[s][:, kc, f * 128:(f + 1) * 128],
                                                               rhs=XT[:, kc, 2 + n * 512:2 + (n + 1) * 512],
                                                               start=(kc == 0), stop=(kc == KC - 1)),
                                      reads=[("XT", t) for t in range(n * 4, n * 4 + 4)] + [("WB", s)],
                                      writes=[("PB", b)], signal=(kc == KC - 1))
                            kb.op('act', lambda e: e.activation(out=dst[:, c * 4 + f, n * 512:(n + 1) * 512],
                                                                in_=PB[b][:, :], func=AF.Copy),
                                  reads=[("PB", b)], writes=[(nm, c * 4 + f, n)])

            proj_fm(J1, KT, "KT")
            kb.op('dve', lambda e: e.memset(V1[:, :, :, 256:257], 1.0), writes=["V1one"])
            for c in range(2):
                s = load_w([(0, win[:, J2 + c * 512:J2 + (c + 1) * 512])])
                for t in range(NT):
                    b = bank()
                    for kc in range(KC):
                        kb.op('pe', lambda e: e.matmul(PB[b][:, :], lhsT=XT[:, kc, 2 + t * 128:2 + (t + 1) * 128],
                                                       rhs=WB[s][:, kc, :], start=(kc == 0), stop=(kc == KC - 1)),
                              reads=[("XT", t), ("WB", s)], writes=[("PB", b)], signal=(kc == KC - 1))
                    kb.op('act', lambda e: e.activation(out=V1[:, t, 2 * c:2 * c + 2, 0:256],
                                                        in_=PB[b][:, :].rearrange("p (a b) -> p a b", a=2), func=AF.Copy),
                          reads=[("PB", b)], writes=[("V1", t, c)])
            chk("o_a")
            kvkeys = [("KT", i, n) for i in range(8) for n in range(2)] + [("V1", t, c) for t in range(8) for c in range(2)] + ["V1one"]
            for i in range(4):
                kb.dma('sp', [(kv_src[i][:, :], Hf[:, i * KVQ:(i + 1) * KVQ])], reads=kvkeys, writes=[("kv_src", i)])
                kb.allgather(kv_src[i][:, :], kv_dst[i][:, :], reads=[("kv_src", i)], writes=[("kv_dst", i)])
            chk("o_b")
            proj_fm(J0, QT, "QT")
            chk("o_c")

            ar2 = Arena(Hf[:, 12304:16384], 4080 * 4)
            SLN = ar2.alloc([P, 1024], F32)
            WSF = ar2.alloc([P, 8, 128], F32)
            ZV = ar2.alloc([P, 1024], F32)
            ZU = ar2.alloc([P, 512], F32)
            WST = ar.alloc([P, 8, 128], BF16)
            SV = ar.alloc([P, 8, 1024], BF16)
            VN = ar.alloc([P, 1024], BF16)
            CO = ar.alloc([P, 512], BF16)
            kb.dma('sp', [(SLN[:, :], p_sln[:, o_idx, :])], writes=["SLN"])
            kb.dma('sp', [(WSF[:, :, :], p_sguT[o_idx])], writes=["WSF"])
            kb.op('dve', lambda e: e.tensor_tensor(out=WST[:, :, :], in0=WSF[:, :, :],
                                                   in1=MASK[:, :].unsqueeze(1).to_broadcast([P, 8, 128]), op=ALU.mult),
                  reads=["WSF", "MASK"], writes=["WST"])
            s0 = load_w([(0, win[:, 1024:1536])])
            s1 = load_w([(0, win[:, 1536:2048])])
            for t in range(NT):
                bb = [bank(), bank()]
                for c, s in enumerate((s0, s1)):
                    for kc in range(KC):
                        kb.op('pe', lambda e: e.matmul(PB[bb[c]][:, :], lhsT=XT[:, kc, 2 + t * 128:2 + (t + 1) * 128],
                                                       rhs=WB[s][:, kc, :], start=(kc == 0), stop=(kc == KC - 1)),
                              reads=[("XT", t), ("WB", s)], writes=[("PB", bb[c])], signal=(kc == KC - 1))
                    kb.op('act', lambda e: e.activation(out=ZV[:, c * 512:(c + 1) * 512], in_=PB[bb[c]][:, :], func=GELU),
                          reads=[("PB", bb[c])], writes=[("ZV", c)])
                BNS = SS[:, 24:24 + 8]
                kb.op('dve', lambda e: e.bn_stats(out=SS[:, 24:30], in_=ZV[:, 0:512]), reads=[("ZV", 0)], writes=["BN0"])
                kb.op('dve', lambda e: e.bn_stats(out=SS[:, 30:36], in_=ZV[:, 512:1024]), reads=[("ZV", 1)], writes=["BN1"])
                kb.op('dve', lambda e: e.bn_aggr(out=LS[:, 6:8], in_=SS[:, 24:36].rearrange("p (a b) -> p a b", a=2)),
                      reads=["BN0", "BN1"], writes=["MVZ"])
                kb.op('act', lambda e: e.activation(out=SS[:, 36:37], in_=LS[:, 7:8],
                                                    func=AF.Sqrt, bias=EPS), reads=["MVZ"], writes=["SDZ"])
                kb.op('dve', lambda e: e.reciprocal(out=SS[:, 37:38], in_=SS[:, 36:37]), reads=["SDZ"], writes=["RSZ"])
                kb.op('dve', lambda e: e.tensor_scalar(out=ZV[:, :], in0=ZV[:, :], scalar1=LS[:, 6:7], scalar2=SS[:, 37:38],
                                                       op0=ALU.subtract, op1=ALU.mult),
                      reads=[("ZV", 0), ("ZV", 1), "MVZ", "RSZ"], writes=[("ZV", 0), ("ZV", 1)])
                kb.op('dve', lambda e: e.tensor_tensor(out=VN[:, :], in0=ZV[:, :], in1=SLN[:, :], op=ALU.mult),
                      reads=[("ZV", 0), ("ZV", 1), "SLN"], writes=["VN"])
                bs2 = [bank(), bank()]
                for g in range(8):
                    b = bs2[g // 4]
                    kb.op('pe', lambda e: e.matmul(PB[b][:, (g % 4) * 128:(g % 4 + 1) * 128], lhsT=WST[:, g, :],
                                                   rhs=VN[:, g * 128:(g + 1) * 128], start=True, stop=True),
                          reads=["WST", "VN"], writes=[("PB", b)], signal=(g % 4 == 3))
                for g in range(8):
                    b = bs2[g // 4]
                    bcol = pcol(OFF_SB + o_idx * 8 + g)
                    kb.op('act', lambda e: e.activation(out=SV[:, t, g * 128:(g + 1) * 128],
                                                        in_=PB[b][:, (g % 4) * 128:(g % 4 + 1) * 128],
                                                        func=AF.Identity, bias=bcol),
                          reads=[("PB", b), "PC"], writes=[("SV", t)])
            for c in range(2):
                s = load_w([(0, win[:, c * 512:(c + 1) * 512])])
                for t in range(NT):
                    b = bank()
                    for kc in range(KC):
                        kb.op('pe', lambda e: e.matmul(PB[b][:, :], lhsT=XT[:, kc, 2 + t * 128:2 + (t + 1) * 128],
                                                       rhs=WB[s][:, kc, :], start=(kc == 0), stop=(kc == KC - 1)),
                              reads=[("XT", t), ("WB", s)], writes=[("PB", b)], signal=(kc == KC - 1))
                    kb.op('act', lambda e: e.activation(out=ZU[:, :], in_=PB[b][:, :], func=GELU),
                          reads=[("PB", b)], writes=["ZU"])
                    kb.op('dve', lambda e: e.tensor_tensor(out=CO[:, :], in0=ZU[:, :], in1=SV[:, t, c * 512:(c + 1) * 512],
                                                           op=ALU.mult), reads=["ZU", ("SV", t)], writes=["CO"])
                    bt = bank()
                    pbt = PB[bt][:, :].bitcast(BF16).rearrange("p (a b) -> p a b", a=8)
                    for i in range(4):
                        kb.op('pe', lambda e: e.transpose(out=pbt[:, i, :], in_=CO[:, i * 128:(i + 1) * 128], identity=IDB[:, :]),
                              reads=["CO", "IDB"], writes=[("PB", bt)], signal=(i == 3))
                    for i in range(4):
                        kc = c * 4 + i
                        dst = MT[:, kc, t * 128:(t + 1) * 128]
                        if i % 2 == 0:
                            kb.op('act', lambda e: e.activation(out=dst, in_=pbt[:, i, :], func=AF.Copy),
                                  reads=[("PB", bt)], writes=[("MT", kc, t)])
                        else:
                            kb.op('dve', lambda e: e.tensor_copy(out=dst, in_=pbt[:, i, :]),
                                  reads=[("PB", bt)], writes=[("MT", kc, t)])

            chk("o_d")
            kb.barrier()
            XTf = XTb.bitcast(F32)
            kb.dma('sp', [(XTf[:, i * KVQ:(i + 1) * KVQ], kv_dst[i][0:128, :]) for i in range(4)],
                   reads=[("kv_dst", i) for i in range(4)], writes=["KVP"])
            prefetch((("wo", "o", o_idx), 0), [(0, wout[:, 0:512])])
            prefetch((("wo", "o", o_idx), 1), [(0, wout[:, 512:1024])])
            ar.reset(mark0)
            PT = [ar.alloc([P, 256], BF16) for _ in range(6)]
            OA = ar.alloc([P, 2, 260], F32)
            RD = ar.alloc([P, 4], F32)
            DO = ar.alloc([P, 256], F32)
            DJ = ar.alloc([P, 256], F32)
            DBs = [ar.alloc([P, 256], BF16), ar.alloc([P, 256], BF16)]
            scale = 128.0 ** -0.5
            LOOK = 3
            steps = []
            gid = 0
            for hh in range(4):
                for qp in range(4):
                    qt2 = [qp * 2, qp * 2 + 1]
                    klist = [("p", kt) for kt in range(8)] + [("o", kt) for kt in range(qt2[1] + 1)]
                    gsteps = []
                    for (kind, kt) in klist:
                        vis = qt2 if kind == "p" else [qt for qt in qt2 if qt >= kt]
                        for j in range(2):
                            gsteps.append(dict(g=gid, hh=hh, qt2=qt2, kind=kind, kt=kt, j=j, vis=vis))
                    seen = set()
                    for st in gsteps:
                        st["first"] = {}
                        for qt in st["vis"]:
                            key = (qt, st["j"])
                            st["first"][qt] = key not in seen
                            seen.add(key)
                    seen = set()
                    for st in reversed(gsteps):
                        st["last"] = {}
                        for qt in st["vis"]:
                            key = (qt, st["j"])
                            st["last"][qt] = key not in seen
                            seen.add(key)
                    gsteps[-1]["gend"] = True
                    steps.extend(gsteps)
                    gid += 1

            def emit_S(i, st):
                hh, kind, kt, j, vis = st["hh"], st["kind"], st["kt"], st["j"], st["vis"]
                nq = len(vis) * 128
                ksrc = KTP if kind == "p" else KT
                kkey = ["KVP"] if kind == "p" else [("KT", hh * 2 + j, kt // 4)]
                bsx = 4 + (i % 3)
                kb.op('pe', lambda e: e.matmul(PB[bsx][:, 0:nq], lhsT=ksrc[:, hh * 2 + j, kt * 128:(kt + 1) * 128],
                                               rhs=QT[:, hh * 2 + j, vis[0] * 128:(vis[-1] + 1) * 128],
                                               start=True, stop=True),
                      reads=kkey + [("QT", hh * 2 + j, vis[0] // 4)], writes=[("PB", bsx)])
                pt = PT[i % 6]
                ptk = ("PT", i % 6)
                bcol = CB[:, 4 + hh:5 + hh] if kind == "p" else CB[:, hh:hh + 1]
                bkey = "CB1" if kind == "p" else "CB0"
                near = {}
                for qt in vis:
                    if kind == "o" and kt == qt:
                        near[qt] = 0
                    elif kind == "o" and kt == qt - 1:
                        near[qt] = 1
                    elif kind == "p" and kt == 7 and qt == 0:
                        near[qt] = 1
                far = [qt for qt in vis if qt not in near]
                if far:
                    o0 = (far[0] - vis[0]) * 128
                    o1 = (far[-1] - vis[0] + 1) * 128
                    kb.op('act', lambda e: e.activation(out=pt[:, o0:o1], in_=PB[bsx][:, o0:o1], func=AF.Exp,
                                                        scale=scale, bias=bcol),
                          reads=[("PB", bsx), bkey], writes=[ptk])
                for qt, which in near.items():
                    o0 = (qt - vis[0]) * 128
                    if kind == "p":
                        kb.op('act', lambda e: e.activation(out=pt[:, o0:o0 + 128], in_=PB[bsx][:, o0:o0 + 128],
                                                            func=AF.Exp, scale=scale, bias=pcol(OFF_NEGB)),
                              reads=[("PB", bsx), "PC"], writes=[ptk])
                    else:
                        kb.op('act', lambda e: e.activation(out=pt[:, o0:o0 + 128], in_=PB[bsx][:, o0:o0 + 128],
                                                            func=AF.Exp, scale=scale),
                              reads=[("PB", bsx)], writes=[ptk])
                    kb.op('dve', lambda e: e.tensor_tensor(out=pt[:, o0:o0 + 128], in0=pt[:, o0:o0 + 128],
                                                           in1=ED[:, hh, which, :], op=ALU.mult),
                          reads=[ptk, "ED"], writes=[ptk])

            def emit_PV(i, st):
                hh, kind, kt, j, vis, qt2 = st["hh"], st["kind"], st["kt"], st["j"], st["vis"], st["qt2"]
                pt = PT[i % 6]
                ptk = ("PT", i % 6)
                vsrc = V1P if kind == "p" else V1
                vkey = ["KVP"] if kind == "p" else [("V1", kt, hh // 2), "V1one"]
                for qt in vis:
                    o0 = (qt - vis[0]) * 128
                    ab = (qt - qt2[0]) * 2 + j
                    kb.op('pe', lambda e: e.matmul(PB[ab][:, 0:257], lhsT=pt[:, o0:o0 + 128], rhs=vsrc[:, kt, hh, :],
                                                   start=st["first"][qt], stop=st["last"][qt]),
                          reads=[ptk] + vkey, writes=[("PB", ab)], signal=True)

            fin_cnt = [0]

            def emit_fin_a(st):
                hh, qt2 = st["hh"], st["qt2"]
                deferred = []
                for qt in qt2:
                    for j in range(2):
                        ab = (qt - qt2[0]) * 2 + j
                        kb.op('act', lambda e: e.activation(out=OA[:, j, 0:257], in_=PB[ab][:, 0:257], func=AF.Copy),
                              reads=[("PB", ab)], writes=[("OA", j)])
                    db = DBs[fin_cnt[0] % 2]
                    dbk = ("DB", fin_cnt[0] % 2)
                    fin_cnt[0] += 1
                    kb.op('dve', lambda e: e.reciprocal(out=RD[:, 0:1], in_=OA[:, 0, 256:257]), reads=[("OA", 0)], writes=["RD0"])
                    kb.op('dve', lambda e: e.reciprocal(out=RD[:, 1:2], in_=OA[:, 1, 256:257]), reads=[("OA", 1)], writes=["RD1"])
                    kb.op('dve', lambda e: e.tensor_tensor(out=RD[:, 2:3], in0=RD[:, 1:2], in1=LS[:, 5:6], op=ALU.mult),
                          reads=["RD1", "LAM"], writes=["RD2"])
                    kb.op('dve', lambda e: e.tensor_scalar_mul(out=DO[:, :], in0=OA[:, 0, 0:256], scalar1=RD[:, 0:1]),
                          reads=[("OA", 0), "RD0"], writes=["DO"])
                    kb.op('dve', lambda e: e.scalar_tensor_tensor(out=DO[:, :], in0=OA[:, 1, 0:256], scalar=RD[:, 2:3],
                                                                  in1=DO[:, :], op0=ALU.mult, op1=ALU.add),
                          reads=[("OA", 1), "RD2", "DO"], writes=["DO"])
                    kb.op('act', lambda e: e.activation(out=DJ[:, :], in_=DO[:, :], func=AF.Square, accum_out=RD[:, 3:4]),
                          reads=["DO"], writes=["DJ", "RD3"])
                    kb.op('act', lambda e: e.activation(out=SS[:, 38:39], in_=RD[:, 3:4], func=AF.Sqrt, scale=1.0 / 256, bias=EPS),
                          reads=["RD3"], writes=["SD1"])
                    kb.op('dve', lambda e: e.reciprocal(out=SS[:, 39:40], in_=SS[:, 38:39]), reads=["SD1"], writes=["SD2"])
                    kb.op('dve', lambda e: e.tensor_scalar(out=db[:, :], in0=DO[:, :], scalar1=SS[:, 39:40],
                                                           scalar2=(1.0 - lam_init), op0=ALU.mult, op1=ALU.mult),
                          reads=["DO", "SD2"], writes=[dbk])
                    deferred.append((hh, qt, db, dbk))
                return deferred

            def emit_fin_b(hh, qt, db, dbk):
                bt = 7
                pbt = PB[bt][:, :].bitcast(BF16).rearrange("p (a b) -> p a b", a=8)
                kb.op('pe', lambda e: e.transpose(out=pbt[:, 0, :], in_=db[:, 0:128], identity=IDB[:, :]),
                      reads=[dbk, "IDB"], writes=[("PB", bt)], signal=False)
                kb.op('pe', lambda e: e.transpose(out=pbt[:, 1, :], in_=db[:, 128:256], identity=IDB[:, :]),
                      reads=[dbk, "IDB"], writes=[("PB", bt)])
                for i2 in range(2):
                    kc = 8 + hh * 2 + i2
                    gcol = pcol(OFF_SUB + o_idx * 2 + i2)
                    dst = MT[:, kc, qt * 128:(qt + 1) * 128]
                    kb.op('dve', lambda e: e.tensor_scalar_mul(out=dst, in0=pbt[:, i2, :], scalar1=gcol),
                          reads=[("PB", bt), "PC"], writes=[("MT", kc, qt)])

            pending_fin = []
            nsteps = len(steps)
            for idx in range(nsteps + LOOK):
                if idx < nsteps:
                    emit_S(idx, steps[idx])
                while pending_fin and pending_fin[0][0] <= idx:
                    emit_fin_b(*pending_fin.pop(0)[1])
                pi_ = idx - LOOK
                if pi_ >= 0:
                    emit_PV(pi_, steps[pi_])
                    if steps[pi_].get("gend"):
                        for k2, args in enumerate(emit_fin_a(steps[pi_])):
                            pending_fin.append((idx + 4 + 2 * k2, args))
            while pending_fin:
                emit_fin_b(*pending_fin.pop(0)[1])
            chk("o_e")
            kb.barrier()
            for t in range(NT):
                kb.dma('sp', [(H[:, t, :], hspill[t * 128:(t + 1) * 128, :])], reads=[("HSP", t)], writes=[("H", t)])
            out_proj([wout[:, cb * 512:(cb + 1) * 512] for cb in range(4)], KC, key=("wo", "o", o_idx))

        order_last_first = [NT - 1] + list(range(NT - 1))
        for l in range(n_layers):
          try:
            last = (l == n_layers - 1)
            if l % 2 == 0:
                prefetch(("eu", l // 2), [(0, w_in_even[l // 2][:, 0:512])])
            else:
                for t in range(NT):
                    kb.dma('sp', [(hspill[t * 128:(t + 1) * 128, :], H[:, t, :])], reads=[("H", t)], writes=[("HSP", t)])
                for c in range(2):
                    prefetch(("ofm", l // 2, "KT", c), [(0, w_in_odd[l // 2][:, 3072 + c * 512:3072 + (c + 1) * 512])])
            norm_phase(l * 2, list(range(NT)), halo=False)
            if last and stop_after == "norm0":
                break
            if l % 2 == 0:
                even_mixer(l // 2)
            else:
                odd_mixer(l // 2, l)
            if last and stop_after and stop_after.startswith("mix"):
                break
            prefetch(("up", l, 0), [(0, w_up[l][:, 0:256]), (256, w_up[l][:, DFF:DFF + 256])])
            norm_phase(l * 2 + 1, order_last_first, halo=True)
            if last and stop_after == "norm1":
                break
            ffn(l)
          except StopBuild:
            break

        kb.barrier()
        ar.reset()
        FG = ar.alloc([P, D], F32)
        JK = ar.alloc([P, D], BF16)
        YO = [ar.alloc([P, D], F32), ar.alloc([P, D], F32)]
        kb.dma('sp', [(FG[:, :], p_fng[:, :])], writes=["FG"])
        for t in range(NT):
            yo = YO[t % 2]
            yk = ("YO", t % 2)
            if dbg_h:
                kb.dma('sp', [(out_d[t * 128:(t + 1) * 128, :], H[:, t, :])], reads=[("H", t)], writes=[("OUT", t)])
                continue
            kb.op('act', lambda e: e.activation(out=JK[:, :], in_=H[:, t, :], func=AF.Square, accum_out=SS[:, t:t + 1]),
                  reads=[("H", t)], writes=["JK", ("SS", t)])
            kb.op('act', lambda e: e.activation(out=SS[:, 8 + t:9 + t], in_=SS[:, t:t + 1], func=AF.Sqrt, scale=1.0 / D, bias=EPS),
                  reads=[("SS", t)], writes=[("SS", 8 + t)])
            kb.op('dve', lambda e: e.reciprocal(out=SS[:, 16 + t:17 + t], in_=SS[:, 8 + t:9 + t]),
                  reads=[("SS", 8 + t)], writes=[("SS", 16 + t)])
            kb.op('dve', lambda e: e.scalar_tensor_tensor(out=yo[:, :], in0=H[:, t, :], scalar=SS[:, 16 + t:17 + t],
                                                          in1=FG[:, :], op0=ALU.mult, op1=ALU.mult),
                  reads=[("H", t), ("SS", 16 + t), "FG"], writes=[yk])
            kb.dma('sp', [(out_d[t * 128:(t + 1) * 128, :], yo[:, :])], reads=[yk], writes=[("OUT", t)])
        kb.barrier()
        print("ops emitted:", kb.nops, "sems:", len(kb.sems), flush=True)
    return nc


def _t5_bucket(n):
    n = np.maximum(n, 0)
    max_exact = 16
    nn = np.maximum(n, 1).astype(np.float32)
    large = max_exact + (np.log(nn / max_exact) / math.log(128 / max_exact) * (32 - max_exact)).astype(np.int32)
    large = np.minimum(large, 31)
    return np.where(n < max_exact, n, large)


def _consts(half):
    c = {}
    c["c_ident"] = np.eye(128, dtype=np.float32)
    half_d = 64
    inv = 1.0 / (10000.0 ** (np.arange(half_d, dtype=np.float32) / half_d))
    pos = (half * T + np.arange(T, dtype=np.float32))
    ang = pos[:, None] * inv[None, :]
    cos = np.cos(ang).astype(np.float32).reshape(NT, 128, 64).transpose(1, 0, 2)
    sin = np.sin(ang).astype(np.float32).reshape(NT, 128, 64).transpose(1, 0, 2)
    c["c_rot"] = np.ascontiguousarray(np.stack([cos, sin], axis=1))
    log_g = np.log(1.0 - 2.0 ** (-5.0 - np.arange(6, dtype=np.float32))).astype(np.float32)
    idx = np.arange(128, dtype=np.float32)
    diff = idx[:, None] - idx[None, :]
    intra = np.where(diff >= 0, np.exp(log_g[:, None, None] * np.maximum(diff, 0.0)), 0.0)
    intraT = intra.transpose(2, 0, 1) * (128.0 ** -0.5)
    c["c_intra"] = np.ascontiguousarray(intraT.astype(np.float32))
    c["qdec"] = np.exp(log_g[None, :] * (idx[:, None] + 1.0)).astype(np.float32)
    c["kdec"] = (np.exp(log_g[None, :] * (127.0 - idx[:, None])) * (128.0 ** -0.5)).astype(np.float32)
    p = np.arange(128)
    c["c_mask"] = (p[None, :] >= p[:, None]).astype(np.float32)
    pcorr = np.ones((4, 16), np.float32)
    if half == 0:
        for g in range(4):
            w = 2 ** (g + 1)
            tt = np.arange(16) + 1
            pcorr[g] = w / np.minimum(tt, w)
    c["pcorr"] = pcorr
    return c


def _prep(inputs):
    f = lambda a: np.ascontiguousarray(np.asarray(a, dtype=np.float32))
    ins = {k: f(v) for k, v in inputs.items()}

    def cols(v):
        return v.reshape(-1, 128).T

    shared = {}
    for k in ("w_in_even", "w_out_even", "pool_w", "w_in_odd", "w_out_odd", "w_up", "w_down"):
        shared[k] = ins[k]
    pc = np.zeros((128, NCOLS), np.float32)
    for l in range(4):
        pc[:, OFF_NG + (2 * l) * 16:OFF_NG + (2 * l + 1) * 16] = cols(ins["mix_norm_g"][l])
        pc[:, OFF_NG + (2 * l + 1) * 16:OFF_NG + (2 * l + 2) * 16] = cols(ins["ffn_norm_g"][l])
        for r in range(3):
            pc[:, OFF_CW + l * 264 + r * 88:OFF_CW + l * 264 + (r + 1) * 88] = cols(ins["conv_w"][l, r])
        pc[:, OFF_CB + l * 88:OFF_CB + (l + 1) * 88] = cols(ins["conv_b"][l])
    for e in range(2):
        pc[:, OFF_PSC + e * 4:OFF_PSC + (e + 1) * 4] = cols(ins["pool_scale"][e])
        pc[:, OFF_GNG + e * 12:OFF_GNG + (e + 1) * 12] = cols(ins["ret_gn_g"][e])
        pc[:, OFF_SUB + e * 2:OFF_SUB + (e + 1) * 2] = cols(ins["diff_subln_g"][e])
        pc[:, OFF_SB + e * 8:OFF_SB + (e + 1) * 8] = ins["sgu_b"][e].T
    pc[:, OFF_C31:OFF_C31 + 4] = np.broadcast_to(ins["rel_bias"][31][None, :], (128, 4))
    shared["p_fng"] = np.ascontiguousarray(np.broadcast_to(ins["final_norm_g"][None, :], (128, D)))
    shared["p_sln"] = np.ascontiguousarray(np.broadcast_to(ins["sgu_ln_g"][None, :, :], (128, 2, 1024)))
    lam = np.stack([ins["lam_q1"], ins["lam_k1"], ins["lam_q2"], ins["lam_k2"]], axis=1)
    shared["p_lam"] = np.ascontiguousarray(np.broadcast_to(lam[None], (128, 2, 4, 128)))
    j = np.arange(128)[:, None]
    i = np.arange(128)[None, :]
    b0 = _t5_bucket(i - j)
    b1 = _t5_bucket(128 + i - j)
    rb = ins["rel_bias"]
    biasT = np.stack([rb[b0], rb[b1]], axis=0)
    shared["p_biasT"] = np.ascontiguousarray(biasT.transpose(1, 3, 0, 2))
    shared["p_sguT"] = np.ascontiguousarray(ins["sgu_w"].transpose(0, 3, 1, 2))
    maps = []
    for core in range(NCORES):
        b, half = core // 2, core % 2
        c = _consts(half)
        m = dict(shared)
        m["x"] = np.ascontiguousarray(ins["x"][b, half * T:(half + 1) * T, :])
        pcc = pc.copy()
        pcc[:, OFF_QDEC:OFF_QDEC + 6] = c["qdec"]
        pcc[:, OFF_KDEC:OFF_KDEC + 6] = c["kdec"]
        pcc[:, OFF_FLAG] = float(half)
        pcc[:, OFF_NEGB] = 0.0 if half == 1 else NEGBIG
        pcc[:, OFF_PCORR:OFF_PCORR + 64] = c["pcorr"].reshape(1, 64)
        m["p_cols"] = pcc
        for k in ("c_ident", "c_rot", "c_intra", "c_mask"):
            m[k] = c[k]
        maps.append(m)
    return maps


_NC_CACHE = {}


def kernel(**inputs):
    maps = _prep(inputs)
    if "nc" not in _NC_CACHE:
        _NC_CACHE["nc"] = build()
    nc = _NC_CACHE["nc"]
    res = run_bass_kernel_spmd(nc, maps, core_ids=list(range(NCORES)))
    out = np.zeros((4, 2 * T, D), np.float32)
    for core in range(NCORES):
        b, half = core // 2, core % 2
        out[b, half * T:(half + 1) * T, :] = np.asarray(res.results[core]["out"], dtype=np.float32)
    return out
```

```python
import math
from contextlib import ExitStack
import numpy as np
import concourse.bass as bass
import concourse.mybir as mybir
from concourse.bass_utils import run_bass_kernel_spmd

F32 = mybir.dt.float32
BF16 = mybir.dt.bfloat16
AF = mybir.ActivationFunctionType
ALU = mybir.AluOpType

P = 128
T = 1024
NT = 8
D = 2048
KC = 16
DFF = 5632
NFF = 44
EPS = 1e-6
NCORES = 8
PAIRS = [[0, 1], [2, 3], [4, 5], [6, 7]]
NEGBIG = -30000.0
GELU = AF.Gelu

OFF_NG = 0
OFF_PSC = OFF_NG + 128
OFF_GNG = OFF_PSC + 8
OFF_SUB = OFF_GNG + 24
OFF_CW = OFF_SUB + 4
OFF_CB = OFF_CW + 1056
OFF_SB = OFF_CB + 352
OFF_C31 = OFF_SB + 16
OFF_QDEC = OFF_C31 + 4
OFF_KDEC = OFF_QDEC + 6
OFF_FLAG = OFF_KDEC + 6
OFF_NEGB = OFF_FLAG + 1
OFF_PCORR = OFF_NEGB + 1
NCOLS = OFF_PCORR + 64


class KB:
    def __init__(self, nc):
        self.nc = nc
        self.eng = {'pe': nc.tensor, 'act': nc.scalar, 'dve': nc.vector, 'pool': nc.gpsimd, 'sp': nc.sync}
        self.sems = []
        self.cur = {}
        self.cnt = {}
        self.waited = {e: {} for e in self.eng}
        self.own = {e: set() for e in self.eng}
        self.track = {}
        self.pending = {e: ([], []) for e in self.eng}
        self.dpool = {}
        self.didx = {}
        self.ccsid = None
        self.cctot = 0
        self.nops = 0

    def new_sem(self, name):
        h = self.nc.alloc_semaphore(name=name)
        self.sems.append(h)
        return len(self.sems) - 1

    def _deps(self, reads, writes, e=None):
        evs = []
        own = self.own.get(e, ())
        for k in reads:
            t = self.track.get(k)
            if t and t[0]:
                evs.append(t[0])
            if t and isinstance(k, tuple) and k[0] == "PB":
                for sid, val in t[1].items():
                    if sid not in own:
                        evs.append((sid, val))
        for k in writes:
            t = self.track.get(k)
            if t:
                if t[0]:
                    evs.append(t[0])
                evs.extend(t[1].items())
        return evs

    def _wait(self, e, evs):
        need = {}
        for sid, val in evs:
            if need.get(sid, 0) < val:
                need[sid] = val
        w = self.waited[e]
        for sid, val in need.items():
            if w.get(sid, 0) < val:
                self.eng[e].wait_ge(self.sems[sid], val)
                w[sid] = val

    def _commit(self, e, ev):
        pr, pw = self.pending[e]
        for k in pr:
            t = self.track.setdefault(k, [None, {}])
            if t[1].get(ev[0], 0) < ev[1]:
                t[1][ev[0]] = ev[1]
        for k in pw:
            self.track[k] = [ev, {}]
        pr.clear()
        pw.clear()

    def op(self, e, fn, reads=(), writes=(), signal=True):
        self._wait(e, self._deps(reads, writes, e))
        inst = fn(self.eng[e])
        self.nops += 1
        pr, pw = self.pending[e]
        pr.extend(reads)
        pw.extend(writes)
        if not signal:
            return None
        if e not in self.cur or self.cnt[e] >= 6000:
            self.cur[e] = self.new_sem("s_%s_%d" % (e, len(self.sems)))
            self.own[e].add(self.cur[e])
            self.cnt[e] = 0
        self.cnt[e] += 1
        inst.then_inc(self.sems[self.cur[e]], 1)
        ev = (self.cur[e], self.cnt[e])
        self._commit(e, ev)
        return ev

    def dma(self, e, pairs, reads=(), writes=()):
        if e not in self.dpool:
            self.dpool[e] = [[self.new_sem("d_%s_%d" % (e, i)), 0] for i in range(6)]
            self.didx[e] = 0
        slot = self.dpool[e][self.didx[e] % len(self.dpool[e])]
        self.didx[e] += 1
        evs = self._deps(reads, writes)
        if slot[1] > 0:
            evs.append((slot[0], slot[1]))
        self._wait(e, evs)
        for (o, i) in pairs:
            self.eng[e].dma_start(out=o, in_=i).then_inc(self.sems[slot[0]], 16)
            slot[1] += 16
            self.nops += 1
        ev = (slot[0], slot[1])
        pr, pw = self.pending[e]
        for k in reads:
            t = self.track.setdefault(k, [None, {}])
            if t[1].get(ev[0], 0) < ev[1]:
                t[1][ev[0]] = ev[1]
        for k in writes:
            self.track[k] = [ev, {}]
        return ev

    def allgather(self, src, dst, reads=(), writes=()):
        e = 'pool'
        if self.ccsid is None:
            self.ccsid = self.new_sem("ccsem")
        evs = self._deps(reads, writes)
        if self.cctot > 0:
            evs.append((self.ccsid, self.cctot))
        self._wait(e, evs)
        inst = self.nc.gpsimd.collective_compute("AllGather", ALU.bypass, replica_groups=self.pairs,
                                                 ins=[src], outs=[dst])
        inst.then_inc(self.sems[self.ccsid])
        self.cctot += 1
        ev = (self.ccsid, self.cctot)
        for k in reads:
            t = self.track.setdefault(k, [None, {}])
            t[1][ev[0]] = ev[1]
        for k in writes:
            self.track[k] = [ev, {}]
        return ev

    def barrier(self):
        evs = []
        for e2 in self.cur:
            evs.append((self.cur[e2], self.cnt[e2]))
        for e2 in self.dpool:
            if e2 == 'pool':
                continue
            for sid, tot in self.dpool[e2]:
                if tot:
                    evs.append((sid, tot))
        if self.cctot:
            evs.append((self.ccsid, self.cctot))
        for e in self.eng:
            self._wait(e, evs)


class StopBuild(Exception):
    pass


class Arena:
    def __init__(self, ap_f32, nbytes):
        self.ap = ap_f32
        self.n = nbytes
        self.off = 0

    def reset(self, off=0):
        self.off = off

    def alloc(self, shape, dt):
        esz = 4 if dt == F32 else 2
        free = 1
        for s in shape[1:]:
            free *= s
        nb = (free * esz + 31) // 32 * 32
        assert self.off + nb <= self.n, ("arena overflow", self.off, nb, self.n)
        v = self.ap[:, self.off // 4:(self.off + nb) // 4]
        self.off += nb
        if dt != F32:
            v = v.bitcast(dt)
        v = v[:, 0:free]
        if len(shape) == 3:
            v = v.rearrange("p (a b) -> p a b", a=shape[1])
        elif len(shape) == 4:
            v = v.rearrange("p (a b c) -> p a b c", a=shape[1], b=shape[2])
        return v


def build(n_layers=4, dbg_h=False, stop_after=None, ncores=NCORES):
    nc = bass.Bass("TRN2", target_bir_lowering=False)

    def din(name, shape):
        return nc.dram_tensor(name, list(shape), F32, kind="ExternalInput").ap()

    x_d = din("x", [T, D])
    n_ev = max(1, (n_layers + 1) // 2)
    n_od = max(1, n_layers // 2)
    n_ff = max(1, n_layers)
    w_in_even = din("w_in_even", [n_ev, D, 5120])
    w_out_even = din("w_out_even", [n_ev, D, D])
    pool_w = din("pool_w", [2, 4, 128, 128])
    w_in_odd = din("w_in_odd", [n_od, D, 5120])
    w_out_odd = din("w_out_odd", [n_od, D, D])
    w_up = din("w_up", [n_ff, D, 2 * DFF])
    w_down = din("w_down", [n_ff, DFF, D])
    c_ident = din("c_ident", [128, 128])
    c_rot = din("c_rot", [128, 2, 8, 64])
    c_intra = din("c_intra", [128, 6, 128])
    c_mask = din("c_mask", [128, 128])
    p_cols = din("p_cols", [128, NCOLS])
    p_fng = din("p_fng", [128, D])
    p_sln = din("p_sln", [128, 2, 1024])
    p_lam = din("p_lam", [128, 2, 4, 128])
    p_biasT = din("p_biasT", [128, 4, 2, 128])
    p_sguT = din("p_sguT", [2, 128, 8, 128])
    out_d = nc.dram_tensor("out", [T, D], F32, kind="ExternalOutput").ap()

    hspill = nc.dram_tensor("hspill", [T, D], F32).ap()
    xu_src = nc.dram_tensor("xu_src", [128, 64], F32).ap()
    xu_dst = nc.dram_tensor("xu_dst", [256, 64], F32).ap()
    st_src = nc.dram_tensor("st_src", [128, 256], F32).ap()
    st_dst = nc.dram_tensor("st_dst", [256, 256], F32).ap()
    xh_src = nc.dram_tensor("xh_src", [128, 16], F32).ap()
    xh_dst = nc.dram_tensor("xh_dst", [256, 16], F32).ap()
    KVW = 4096 + 4112
    KVQ = KVW // 4
    kv_src = [nc.dram_tensor("kv_src%d" % i, [128, KVQ], F32).ap() for i in range(4)]
    kv_dst = [nc.dram_tensor("kv_dst%d" % i, [256, KVQ], F32).ap() for i in range(4)]

    ARENA_BYTES = 40 * 1024
    with ExitStack() as es:
        def sb(name, shape, dt):
            return es.enter_context(nc.sbuf_tensor(name, shape, dt))

        H = sb("H", [P, NT, D], F32)
        XT = sb("XT", [P, KC, T + 2], BF16)
        MT = sb("MT", [P, KC, T], BF16)
        WB = [sb("WB0", [P, KC, 512], BF16), sb("WB1", [P, KC, 512], BF16)]
        PC = sb("PC", [P, NCOLS], F32)
        IDB = sb("IDB", [P, 128], BF16)
        MASK = sb("MASK", [P, 128], F32)
        SS = sb("SS", [P, 48], F32)
        ARN = sb("ARN", [P, ARENA_BYTES // 4], F32)
        PB = [es.enter_context(nc.psum_tensor("pb%d" % i, [P, 512], F32)) for i in range(8)]
        ar = Arena(ARN, ARENA_BYTES)
        kb = KB(nc)
        kb.pairs = [[2 * i, 2 * i + 1] for i in range(ncores // 2)]

        def chk(name):
            if stop_after == name:
                raise StopBuild()

        def pcol(off, n=1):
            return PC[:, off:off + n]

        kb.dma('sp', [(PC[:, :], p_cols[:, :])], writes=["PC"])
        kb.dma('sp', [(MASK[:, :], c_mask[:, :])], writes=["MASK"])
        kb.dma('pool', [(IDB[:, :], c_ident[:, :])], writes=["IDB"])
        for t in range(NT):
            kb.dma('sp', [(H[:, t, :], x_d[t * 128:(t + 1) * 128, :])], writes=[("H", t)])

        bank_rr = [0]

        def bank():
            b = bank_rr[0] % 8
            bank_rr[0] += 1
            return b

        wslot = [0]

        preloaded = {}

        def prefetch(key, pieces, nk=KC):
            preloaded[key] = load_w(pieces, nk)

        def load_w(pieces, nk=KC, key=None):
            if key is not None and key in preloaded:
                return preloaded.pop(key)
            s = wslot[0] % 2
            wslot[0] += 1
            pairs = []
            for (co, src) in pieces:
                ncols = src.shape[1]
                v = src.rearrange("(kc p) c -> p kc c", p=128)
                step = 4 if ncols * 4 >= 2048 else 8
                for k0 in range(0, nk, step):
                    k1 = min(nk, k0 + step)
                    pairs.append((WB[s][:, k0:k1, co:co + ncols], v[:, k0:k1, :]))
            kb.dma('pool', pairs, writes=[("WB", s)])
            return s

        def norm_phase(gidx, order, halo):
            kb.barrier()
            ar.reset()
            JK = ar.alloc([P, D], BF16)
            HN = [ar.alloc([P, D], BF16) for _ in range(3)]
            XH = ar.alloc([P, 32], BF16)
            XHF = ar.alloc([P, 16], F32)

            def stA_act(oi, t):
                kb.op('act', lambda e: e.activation(out=JK[:, :], in_=H[:, t, :], func=AF.Square,
                                                    accum_out=SS[:, t:t + 1]),
                      reads=[("H", t)], writes=["JK", ("SS", t)])
                kb.op('act', lambda e: e.activation(out=SS[:, 8 + t:9 + t], in_=SS[:, t:t + 1], func=AF.Sqrt,
                                                    scale=1.0 / D, bias=EPS),
                      reads=[("SS", t)], writes=[("SS", 8 + t)])

            def stA_dve(oi, t):
                hn = HN[oi % 3]
                kb.op('dve', lambda e: e.reciprocal(out=SS[:, 16 + t:17 + t], in_=SS[:, 8 + t:9 + t]),
                      reads=[("SS", 8 + t)], writes=[("SS", 16 + t)])
                kb.op('dve', lambda e: e.tensor_scalar_mul(out=hn[:, :], in0=H[:, t, :],
                                                           scalar1=SS[:, 16 + t:17 + t]),
                      reads=[("H", t), ("SS", 16 + t)], writes=[("HN", oi % 3)])

            def stB(oi, t):
                hn = HN[oi % 3]
                hk = ("HN", oi % 3)
                for half in range(2):
                    b = bank()
                    pbb = PB[b][:, :].bitcast(BF16).rearrange("p (a b) -> p a b", a=8)
                    for i in range(8):
                        kc = half * 8 + i
                        kb.op('pe', lambda e: e.transpose(out=pbb[:, i, :], in_=hn[:, kc * 128:(kc + 1) * 128],
                                                          identity=IDB[:, :]),
                              reads=[hk, "IDB"], writes=[("PB", b)], signal=(i == 7))
                    for i in range(8):
                        kc = half * 8 + i
                        g = pcol(OFF_NG + gidx * 16 + kc)
                        dst = XT[:, kc, 2 + t * 128:2 + (t + 1) * 128]
                        if half == 0:
                            kb.op('act', lambda e: e.activation(out=dst, in_=pbb[:, i, :], func=AF.Copy, scale=g),
                                  reads=[("PB", b), "PC"], writes=[("XT", t)])
                        else:
                            kb.op('dve', lambda e: e.tensor_scalar_mul(out=dst, in0=pbb[:, i, :], scalar1=g),
                                  reads=[("PB", b), "PC"], writes=[("XT", t)])

            n = len(order)
            stA_act(0, order[0])
            stA_dve(0, order[0])
            for oi, t in enumerate(order):
                if oi + 1 < n:
                    stA_act(oi + 1, order[oi + 1])
                stB(oi, t)
                if oi + 1 < n:
                    stA_dve(oi + 1, order[oi + 1])
                if halo and t == NT - 1:
                    kb.op('act', lambda e: e.activation(out=XH[:, :].rearrange("p (a b) -> p a b", a=16),
                                                        in_=XT[:, :, T:T + 2], func=AF.Copy),
                          reads=[("XT", t)], writes=["XH"])
                    kb.dma('sp', [(xh_src[:, :], XH[:, :].bitcast(F32))], reads=["XH"], writes=["xh_src"])
                    kb.allgather(xh_src[:, :], xh_dst[:, :], reads=["xh_src"], writes=["xh_dst"])
                    kb.dma('sp', [(XHF[:, :], xh_dst[0:128, :])], reads=["xh_dst"], writes=["XHF"])
            if halo:
                kb.op('dve', lambda e: e.tensor_scalar_mul(
                    out=XT[:, :, 0:2], in0=XHF[:, :].bitcast(BF16).rearrange("p (a b) -> p a b", a=16),
                    scalar1=pcol(OFF_FLAG)), reads=["XHF", "PC"], writes=["XTH"])

        def out_proj(chunks, nk, key=None):
            slots = [load_w([(0, chunks[0])], nk, key=(key, 0) if key else None)]
            for cb in range(4):
                if cb + 1 < 4:
                    slots.append(load_w([(0, chunks[cb + 1])], nk, key=(key, cb + 1) if key else None))
                s = slots[cb]
                for t in range(NT):
                    b = bank()
                    for kc in range(nk):
                        kb.op('pe', lambda e: e.matmul(PB[b][:, :], lhsT=MT[:, kc, t * 128:(t + 1) * 128],
                                                       rhs=WB[s][:, kc, :], start=(kc == 0), stop=(kc == nk - 1)),
                              reads=[("MT", kc, t), ("WB", s)], writes=[("PB", b)], signal=(kc == nk - 1))
                    hs = H[:, t, cb * 512:(cb + 1) * 512]
                    kb.op('dve', lambda e: e.tensor_tensor(out=hs, in0=hs, in1=PB[b][:, :], op=ALU.add),
                          reads=[("PB", b), ("H", t)], writes=[("H", t)])

        def ffn(l):
            kb.barrier()
            ar.reset()
            AG = [ar.alloc([P, 514], F32), ar.alloc([P, 514], F32)]
            AV = [ar.alloc([P, 514], F32), ar.alloc([P, 514], F32)]
            CG = [ar.alloc([P, 512], F32), ar.alloc([P, 512], F32)]
            CV = [ar.alloc([P, 512], F32), ar.alloc([P, 512], F32)]
            SG = [ar.alloc([P, 512], F32), ar.alloc([P, 512], F32)]
            wup = w_up[l]
            wdn = w_down[l]
            groups = [(0, 12), (12, 12), (24, 10), (34, 10)]

            def up_pieces(j0):
                return [(0, wup[:, j0 * 128:(j0 + 2) * 128]), (256, wup[:, DFF + j0 * 128:DFF + (j0 + 2) * 128])]

            pair_list = []
            for (g0, gn) in groups:
                for j0 in range(g0, g0 + gn, 2):
                    pair_list.append(j0)
            nxt = None
            it = 0
            for (g0, gn) in groups:
                for j0 in range(g0, g0 + gn, 2):
                    s = nxt if nxt is not None else load_w(up_pieces(j0), key=("up", l, j0))
                    nxt = None
                    if j0 + 2 < g0 + gn:
                        nxt = load_w(up_pieces(j0 + 2))
                    for jj in range(2):
                        j = j0 + jj
                        jl = j - g0
                        wg = lambda kc: WB[s][:, kc, jj * 128:(jj + 1) * 128]
                        wv = lambda kc: WB[s][:, kc, 256 + jj * 128:256 + (jj + 1) * 128]
                        cw = lambda r: pcol(OFF_CW + l * 264 + r * 88 + j)
                        cbias = pcol(OFF_CB + l * 88 + j)
                        bh = bank()
                        for kc in range(KC):
                            kb.op('pe', lambda e: e.matmul(PB[bh][:, 0:2], lhsT=wg(kc), rhs=XT[:, kc, 0:2],
                                                           start=(kc == 0), stop=(kc == KC - 1)),
                                  reads=["XTH", ("WB", s)], writes=[("PB", bh)], signal=False)
                        for kc in range(KC):
                            kb.op('pe', lambda e: e.matmul(PB[bh][:, 2:4], lhsT=wv(kc), rhs=XT[:, kc, 0:2],
                                                           start=(kc == 0), stop=(kc == KC - 1)),
                                  reads=["XTH", ("WB", s)], writes=[("PB", bh)], signal=(kc == KC - 1))
                        for n in range(2):
                            q = it % 2
                            it += 1
                            bg = bank()
                            bv = bank()
                            xt_keys = [("XT", t) for t in range(n * 4, n * 4 + 4)]
                            for kc in range(KC):
                                kb.op('pe', lambda e: e.matmul(PB[bg][:, :], lhsT=wg(kc),
                                                               rhs=XT[:, kc, 2 + n * 512:2 + (n + 1) * 512],
                                                               start=(kc == 0), stop=(kc == KC - 1)),
                                      reads=xt_keys + [("WB", s)], writes=[("PB", bg)], signal=(kc == KC - 1))
                            for kc in range(KC):
                                kb.op('pe', lambda e: e.matmul(PB[bv][:, :], lhsT=wv(kc),
                                                               rhs=XT[:, kc, 2 + n * 512:2 + (n + 1) * 512],
                                                               start=(kc == 0), stop=(kc == KC - 1)),
                                      reads=xt_keys + [("WB", s)], writes=[("PB", bv)], signal=(kc == KC - 1))
                            ag, av, cg, cv, sg = AG[q], AV[q], CG[q], CV[q], SG[q]
                            kag, kav, kcg, kcv, ksg = ("AG", q), ("AV", q), ("CG", q), ("CV", q), ("SG", q)
                            kb.op('act', lambda e: e.activation(out=ag[:, 2:514], in_=PB[bg][:, :], func=AF.Copy),
                                  reads=[("PB", bg)], writes=[kag])
                            kb.op('act', lambda e: e.activation(out=av[:, 2:514], in_=PB[bv][:, :], func=AF.Copy),
                                  reads=[("PB", bv)], writes=[kav])
                            if n == 0:
                                kb.op('dve', lambda e: e.tensor_copy(out=ag[:, 0:2], in_=PB[bh][:, 0:2]),
                                      reads=[("PB", bh)], writes=[kag])
                                kb.op('dve', lambda e: e.tensor_copy(out=av[:, 0:2], in_=PB[bh][:, 2:4]),
                                      reads=[("PB", bh)], writes=[kav])
                            else:
                                pq = 1 - q
                                kb.op('dve', lambda e: e.tensor_copy(out=ag[:, 0:2], in_=AG[pq][:, 512:514]),
                                      reads=[("AG", pq)], writes=[kag])
                                kb.op('dve', lambda e: e.tensor_copy(out=av[:, 0:2], in_=AV[pq][:, 512:514]),
                                      reads=[("AV", pq)], writes=[kav])
                            kb.op('act', lambda e: e.activation(out=cg[:, :], in_=PB[bg][:, :], func=AF.Identity,
                                                                scale=cw(2), bias=cbias),
                                  reads=[("PB", bg), "PC"], writes=[kcg])
                            kb.op('act', lambda e: e.activation(out=cv[:, :], in_=PB[bv][:, :], func=AF.Identity,
                                                                scale=CWV(l, 2, j), bias=CBV(l, j)),
                                  reads=[("PB", bv), "PC"], writes=[kcv])
                            kb.op('dve', lambda e: e.scalar_tensor_tensor(out=cg[:, :], in0=ag[:, 1:513], scalar=cw(1),
                                                                          in1=cg[:, :], op0=ALU.mult, op1=ALU.add),
                                  reads=[kag, "PC", kcg], writes=[kcg])
                            kb.op('dve', lambda e: e.scalar_tensor_tensor(out=cg[:, :], in0=ag[:, 0:512], scalar=cw(0),
                                                                          in1=cg[:, :], op0=ALU.mult, op1=ALU.add),
                                  reads=[kag, "PC", kcg], writes=[kcg])
                            kb.op('dve', lambda e: e.scalar_tensor_tensor(out=cv[:, :], in0=av[:, 1:513],
                                                                          scalar=CWV(l, 1, j), in1=cv[:, :],
                                                                          op0=ALU.mult, op1=ALU.add),
                                  reads=[kav, "PC", kcv], writes=[kcv])
                            kb.op('dve', lambda e: e.scalar_tensor_tensor(out=cv[:, :], in0=av[:, 0:512],
                                                                          scalar=CWV(l, 0, j), in1=cv[:, :],
                                                                          op0=ALU.mult, op1=ALU.add),
                                  reads=[kav, "PC", kcv], writes=[kcv])
                            kb.op('act', lambda e: e.activation(out=sg[:, :], in_=cg[:, :], func=AF.Silu),
                                  reads=[kcg], writes=[ksg])
                            kb.op('dve', lambda e: e.tensor_tensor(out=MT[:, jl, n * 512:(n + 1) * 512], in0=sg[:, :],
                                                                   in1=cv[:, :], op=ALU.mult),
                                  reads=[ksg, kcv], writes=[("MT", jl, t) for t in range(n * 4, n * 4 + 4)])
                out_proj([wdn[g0 * 128:(g0 + gn) * 128, cb * 512:(cb + 1) * 512] for cb in range(4)], gn)
                if (g0, gn) != groups[-1]:
                    pass

        def CWV(l, r, j):
            return pcol(OFF_CW + l * 264 + r * 88 + NFF + j)

        def CBV(l, j):
            return pcol(OFF_CB + l * 88 + NFF + j)

        def even_mixer(e_idx):
            kb.barrier()
            ar.reset()
            win = w_in_even[e_idx]
            wout = w_out_even[e_idx]
            I0, I1, I2, I3 = 512, 512 + 768, 512 + 1536, 512 + 1536 + 1536
            ROT = ar.alloc([P, 2, 8, 64], F32)
            INTRA = ar.alloc([P, 6, 128], F32)
            U = ar.alloc([P, 4, 16 + T], BF16)
            PW = ar.alloc([P, 4, 128], BF16)
            UHS = ar.alloc([P, 4, 16], F32)
            UHR = ar.alloc([P, 4, 16], F32)
            mark = ar.off
            kb.dma('sp', [(ROT[:, :, :, :], c_rot[:, :, :, :])], writes=["ROT"])
            kb.dma('sp', [(INTRA[:, :, :], c_intra[:, :, :])], writes=["INTRA"])
            kb.dma('pool', [(PW[:, :, :], pool_w[e_idx].rearrange("g c d -> c g d"))], writes=["PW"])

            s = load_w([(0, win[:, 0:512])], key=("eu", e_idx))
            for g in range(4):
                for n in range(2):
                    b = bank()
                    for kc in range(KC):
                        kb.op('pe', lambda e: e.matmul(PB[b][:, :], lhsT=WB[s][:, kc, g * 128:(g + 1) * 128],
                                                       rhs=XT[:, kc, 2 + n * 512:2 + (n + 1) * 512],
                                                       start=(kc == 0), stop=(kc == KC - 1)),
                              reads=[("XT", t) for t in range(n * 4, n * 4 + 4)] + [("WB", s)],
                              writes=[("PB", b)], signal=(kc == KC - 1))
                    kb.op('act', lambda e: e.activation(out=U[:, g, 16 + n * 512:16 + (n + 1) * 512], in_=PB[b][:, :],
                                                        func=AF.Copy),
                          reads=[("PB", b)], writes=[("U", g, n)])
            kb.op('act', lambda e: e.activation(out=UHS[:, :, :], in_=U[:, :, T:T + 16], func=AF.Copy),
                  reads=[("U", g, 1) for g in range(4)], writes=["UHS"])
            kb.dma('sp', [(xu_src[:, :], UHS[:, :, :].rearrange("p a b -> p (a b)"))], reads=["UHS"], writes=["xu_src"])
            kb.allgather(xu_src[:, :], xu_dst[:, :], reads=["xu_src"], writes=["xu_dst"])
            kb.dma('sp', [(UHR[:, :, :].rearrange("p a b -> p (a b)"), xu_dst[0:128, :])], reads=["xu_dst"],
                   writes=["UHR"])

            if stop_after == "mix_a":
                return
            GS = ar.alloc([P, 8, 256], BF16)
            RSB = ar.alloc([P, 8, 256], BF16)
            QDT = ar.alloc([P, 8, 128], BF16)
            STATE = ar.alloc([P, 256], F32)
            STATEB = ar.alloc([P, 256], BF16)
            mark_r = ar.off
            NB3 = 4
            for h in range(6):
                dec = 1.0 - 2.0 ** (-5.0 - h)
                chunk_dec = dec ** 128
                kb.barrier()
                ar.reset(mark_r)
                QK = [ar.alloc([P, 2, 2, 64], BF16) for _ in range(NB3)]
                VB = [ar.alloc([P, 256], BF16) for _ in range(NB3)]
                QD = [ar.alloc([P, 128], BF16) for _ in range(NB3)]
                KD = [ar.alloc([P, 128], BF16) for _ in range(NB3)]
                QKT = [ar.alloc([P, 2, 128], BF16) for _ in range(2)]
                ST = [ar.alloc([P, 128], BF16) for _ in range(2)]
                R1 = ar.alloc([P, 2, 64], F32)
                R2 = ar.alloc([P, 2, 64], F32)
                R3 = ar.alloc([P, 2, 64], F32)
                R4 = ar.alloc([P, 2, 64], F32)
                def g_pieces(hx):
                    return [(0, win[:, I3 + hx * 256:I3 + (hx + 1) * 256])]

                def qkv_pieces(hx):
                    return [(0, win[:, I0 + hx * 128:I0 + (hx + 1) * 128]),
                            (128, win[:, I1 + hx * 128:I1 + (hx + 1) * 128]),
                            (256, win[:, I2 + hx * 256:I2 + (hx + 1) * 256])]

                sg_ = load_w(g_pieces(h), key=("eg", e_idx, h))
                sq_ = load_w(qkv_pieces(h), key=("eq", e_idx, h))
                kb.op('dve', lambda e: e.memset(STATE[:, :], 0.0), writes=["STATE"])
                kb.op('dve', lambda e: e.memset(STATEB[:, :], 0.0), writes=["STATEB"])

                def st_P(t):
                    b = t % 2
                    for kc in range(KC):
                        kb.op('pe', lambda e: e.matmul(PB[b][:, :], lhsT=XT[:, kc, 2 + t * 128:2 + (t + 1) * 128],
                                                       rhs=WB[sq_][:, kc, :], start=(kc == 0), stop=(kc == KC - 1)),
                              reads=[("XT", t), ("WB", sq_)], writes=[("PB", b)], signal=(kc == KC - 1))
                def st_ROT(t):
                    b = t % 2
                    q3 = t % NB3
                    qk, vb, qd, kd = QK[q3], VB[q3], QD[q3], KD[q3]
                    z4 = PB[b][:, 0:256].rearrange("p (a b c) -> p a b c", a=2, b=2)
                    x1 = z4[:, :, 0, :]
                    x2 = z4[:, :, 1, :]
                    cosb = ROT[:, 0, t, :].unsqueeze(1).to_broadcast([P, 2, 64])
                    sinb = ROT[:, 1, t, :].unsqueeze(1).to_broadcast([P, 2, 64])
                    kb.op('dve', lambda e: e.tensor_tensor(out=R1[:, :, :], in0=x1, in1=cosb, op=ALU.mult),
                          reads=[("PB", b), "ROT"], writes=["R1"])
                    kb.op('dve', lambda e: e.tensor_tensor(out=R2[:, :, :], in0=x2, in1=sinb, op=ALU.mult),
                          reads=[("PB", b), "ROT"], writes=["R2"])
                    kb.op('dve', lambda e: e.tensor_tensor(out=qk[:, :, 0, :], in0=R1[:, :, :], in1=R2[:, :, :],
                                                           op=ALU.subtract),
                          reads=["R1", "R2"], writes=[("QK0", q3)])
                    kb.op('dve', lambda e: e.tensor_tensor(out=R3[:, :, :], in0=x1, in1=sinb, op=ALU.mult),
                          reads=[("PB", b), "ROT"], writes=["R3"])
                    kb.op('dve', lambda e: e.tensor_tensor(out=R4[:, :, :], in0=x2, in1=cosb, op=ALU.mult),
                          reads=[("PB", b), "ROT"], writes=["R4"])
                    kb.op('dve', lambda e: e.tensor_tensor(out=qk[:, :, 1, :], in0=R3[:, :, :], in1=R4[:, :, :],
                                                           op=ALU.add),
                          reads=["R3", "R4"], writes=[("QK1", q3)])
                    kb.op('act', lambda e: e.activation(out=vb[:, :], in_=PB[b][:, 256:512], func=AF.Copy),
                          reads=[("PB", b)], writes=[("VB", q3)])
                    qf = qk[:, 0, :, :].rearrange("p a b -> p (a b)")
                    kf = qk[:, 1, :, :].rearrange("p a b -> p (a b)")
                    kb.op('act', lambda e: e.activation(out=qd[:, :], in_=qf, func=AF.Copy, scale=pcol(OFF_QDEC + h)),
                          reads=[("QK0", q3), ("QK1", q3), "PC"], writes=[("QD", q3)])
                    kb.op('act', lambda e: e.activation(out=kd[:, :], in_=kf, func=AF.Copy, scale=pcol(OFF_KDEC + h)),
                          reads=[("QK0", q3), ("QK1", q3), "PC"], writes=[("KD", q3)])

                def st_TR(t):
                    q3 = t % NB3
                    qk, qd = QK[q3], QD[q3]
                    qf = qk[:, 0, :, :].rearrange("p a b -> p (a b)")
                    kf = qk[:, 1, :, :].rearrange("p a b -> p (a b)")
                    bt = 2 + (t % 2)
                    pbt = PB[bt][:, :].bitcast(BF16).rearrange("p (a b) -> p a b", a=8)
                    kb.op('pe', lambda e: e.transpose(out=pbt[:, 0, :], in_=qf, identity=IDB[:, :]),
                          reads=[("QK0", q3), ("QK1", q3), "IDB"], writes=[("PB", bt)], signal=False)
                    kb.op('pe', lambda e: e.transpose(out=pbt[:, 1, :], in_=kf, identity=IDB[:, :]),
                          reads=[("QK0", q3), ("QK1", q3), "IDB"], writes=[("PB", bt)], signal=False)
                    kb.op('pe', lambda e: e.transpose(out=pbt[:, 2, :], in_=qd[:, :], identity=IDB[:, :]),
                          reads=[("QD", q3), "IDB"], writes=[("PB", bt)])
                    kb.op('dve', lambda e: e.tensor_copy(out=QKT[t % 2][:, :, :], in_=pbt[:, 0:2, :]),
                          reads=[("PB", bt)], writes=[("QKT", t % 2)])
                    kb.op('dve', lambda e: e.tensor_copy(out=QDT[:, t, :], in_=pbt[:, 2, :]),
                          reads=[("PB", bt)], writes=[("QDT", t)])

                def st_S(t):
                    bs = 4
                    qkt = QKT[t % 2]
                    kb.op('pe', lambda e: e.matmul(PB[bs][:, 0:128], lhsT=qkt[:, 1, :], rhs=qkt[:, 0, :],
                                                   start=True, stop=True),
                          reads=[("QKT", t % 2)], writes=[("PB", bs)])
                    kb.op('dve', lambda e: e.tensor_tensor(out=ST[t % 2][:, :], in0=PB[bs][:, 0:128], in1=INTRA[:, h, :],
                                                           op=ALU.mult),
                          reads=[("PB", bs), "INTRA"], writes=[("ST", t % 2)])

                def st_R(t):
                    q3 = t % NB3
                    vb, kd = VB[q3], KD[q3]
                    br, bk = 5, 6
                    kb.op('pe', lambda e: e.matmul(PB[br][:, 0:256], lhsT=ST[t % 2][:, :], rhs=vb[:, :], start=True, stop=False),
                          reads=[("ST", t % 2), ("VB", q3)], writes=[("PB", br)], signal=False)
                    kb.op('pe', lambda e: e.matmul(PB[br][:, 0:256], lhsT=QDT[:, t, :], rhs=STATEB[:, :],
                                                   start=False, stop=True),
                          reads=[("QDT", t), "STATEB"], writes=[("PB", br)])
                    kb.op('pe', lambda e: e.matmul(PB[bk][:, 0:256], lhsT=kd[:, :], rhs=vb[:, :], start=True, stop=True),
                          reads=[("KD", q3), ("VB", q3)], writes=[("PB", bk)])
                    kb.op('act', lambda e: e.activation(out=RSB[:, t, :], in_=PB[br][:, 0:256], func=AF.Copy),
                          reads=[("PB", br)], writes=[("RSB", t)])
                    kb.op('dve', lambda e: e.scalar_tensor_tensor(out=STATE[:, :], in0=STATE[:, :], scalar=chunk_dec,
                                                                  in1=PB[bk][:, 0:256], op0=ALU.mult, op1=ALU.add),
                          reads=["STATE", ("PB", bk)], writes=["STATE"])
                    kb.op('act', lambda e: e.activation(out=STATEB[:, :], in_=STATE[:, :], func=AF.Copy),
                          reads=["STATE"], writes=["STATEB"])

                for i in range(NT + 3):
                    if i < NT:
                        st_P(i)
                    if 0 <= i - 1 < NT:
                        st_TR(i - 1)
                    if 0 <= i - 2 < NT:
                        st_S(i - 2)
                    if 0 <= i - 3 < NT:
                        st_R(i - 3)
                    if i < NT:
                        st_ROT(i)
                if stop_after == "mix_b1":
                    return
                kb.dma('sp', [(st_src[:, :], STATE[:, :])], reads=["STATE"], writes=["st_src"])
                kb.allgather(st_src[:, :], st_dst[:, :], reads=["st_src"], writes=["st_dst"])
                nxt_q = None
                for t in range(NT):
                    b = t % 2
                    for kc in range(KC):
                        kb.op('pe', lambda e: e.matmul(PB[b][:, 0:256], lhsT=XT[:, kc, 2 + t * 128:2 + (t + 1) * 128],
                                                       rhs=WB[sg_][:, kc, 0:256], start=(kc == 0), stop=(kc == KC - 1)),
                              reads=[("XT", t), ("WB", sg_)], writes=[("PB", b)], signal=(kc == KC - 1))
                    kb.op('act', lambda e: e.activation(out=GS[:, t, :], in_=PB[b][:, 0:256], func=AF.Silu),
                          reads=[("PB", b)], writes=[("GS", t)])
                if h + 1 < 6:
                    prefetch(("eg", e_idx, h + 1), g_pieces(h + 1))
                    prefetch(("eq", e_idx, h + 1), qkv_pieces(h + 1))
                else:
                    prefetch((("wo", "e", e_idx), 0), [(0, wout[:, 0:512])])
                    prefetch((("wo", "e", e_idx), 1), [(0, wout[:, 512:1024])])
                kb.barrier()
                ar.reset(mark_r)
                SA = ar.alloc([P, 256], F32)
                SC = ar.alloc([P, 256], F32)
                SCB = [ar.alloc([P, 256], BF16) for _ in range(2)]
                RT = [ar.alloc([P, 256], F32) for _ in range(2)]
                YF = [ar.alloc([P, 256], F32) for _ in range(2)]
                YG = [ar.alloc([P, 256], BF16) for _ in range(2)]
                BNS = [ar.alloc([P, 8], F32) for _ in range(2)]
                MV = [ar.alloc([P, 4], F32) for _ in range(2)]
                kb.dma('sp', [(SA[:, :], st_dst[0:128, :])], reads=["st_dst"], writes=["SA"])
                kb.op('dve', lambda e: e.tensor_scalar_mul(out=SC[:, :], in0=SA[:, :], scalar1=pcol(OFF_FLAG)),
                      reads=["SA", "PC"], writes=["SC"])

                def p2_A(t):
                    q = t % 2
                    kb.op('act', lambda e: e.activation(out=SCB[q][:, :], in_=SC[:, :], func=AF.Copy),
                          reads=["SC"], writes=[("SCB", q)])
                    if t < NT - 1:
                        kb.op('dve', lambda e: e.tensor_scalar_mul(out=SC[:, :], in0=SC[:, :], scalar1=chunk_dec),
                              reads=["SC"], writes=["SC"])
                    bc = 4 + q
                    kb.op('pe', lambda e: e.matmul(PB[bc][:, 0:256], lhsT=QDT[:, t, :], rhs=SCB[q][:, :],
                                                   start=True, stop=True),
                          reads=[("QDT", t), ("SCB", q)], writes=[("PB", bc)])
                    kb.op('dve', lambda e: e.tensor_tensor(out=RT[q][:, :], in0=RSB[:, t, :], in1=PB[bc][:, 0:256],
                                                           op=ALU.add),
                          reads=[("RSB", t), ("PB", bc)], writes=[("RT", q)])
                    kb.op('dve', lambda e: e.bn_stats(out=BNS[q][:, 0:6], in_=RT[q][:, :]), reads=[("RT", q)], writes=[("BNS", q)])
                    kb.op('dve', lambda e: e.bn_aggr(out=MV[q][:, 0:2], in_=BNS[q][:, 0:6]), reads=[("BNS", q)], writes=[("MV", q)])
                    kb.op('act', lambda e: e.activation(out=MV[q][:, 2:3], in_=MV[q][:, 1:2], func=AF.Sqrt, bias=EPS),
                          reads=[("MV", q)], writes=[("MV2", q)])
                    kb.op('dve', lambda e: e.reciprocal(out=MV[q][:, 3:4], in_=MV[q][:, 2:3]), reads=[("MV2", q)], writes=[("MV3", q)])
                    kb.op('dve', lambda e: e.tensor_scalar(out=YF[q][:, :], in0=RT[q][:, :], scalar1=MV[q][:, 0:1],
                                                           scalar2=MV[q][:, 3:4], op0=ALU.subtract, op1=ALU.mult),
                          reads=[("RT", q), ("MV", q), ("MV3", q)], writes=[("YF", q)])
                    kb.op('dve', lambda e: e.tensor_tensor(out=YG[q][:, :], in0=YF[q][:, :], in1=GS[:, t, :], op=ALU.mult),
                          reads=[("YF", q), ("GS", t)], writes=[("YG", q)])

                def p2_B(t):
                    q = t % 2
                    by = 6 + q
                    pby = PB[by][:, :].bitcast(BF16).rearrange("p (a b) -> p a b", a=8)
                    kb.op('pe', lambda e: e.transpose(out=pby[:, 0, :], in_=YG[q][:, 0:128], identity=IDB[:, :]),
                          reads=[("YG", q), "IDB"], writes=[("PB", by)], signal=False)
                    kb.op('pe', lambda e: e.transpose(out=pby[:, 1, :], in_=YG[q][:, 128:256], identity=IDB[:, :]),
                          reads=[("YG", q), "IDB"], writes=[("PB", by)])
                    for j in range(2):
                        kc = 4 + 2 * h + j
                        gcol = pcol(OFF_GNG + e_idx * 12 + 2 * h + j)
                        dst = MT[:, kc, t * 128:(t + 1) * 128]
                        kb.op('act', lambda e: e.activation(out=dst, in_=pby[:, j, :], func=AF.Copy, scale=gcol),
                              reads=[("PB", by), "PC"], writes=[("MT", kc, t)])

                for i in range(NT + 1):
                    if i < NT:
                        p2_A(i)
                    if i - 1 >= 0:
                        p2_B(i - 1)

            if stop_after == "mix_b":
                return
            kb.barrier()
            ar.reset(mark)
            kb.op('dve', lambda e: e.tensor_scalar_mul(out=U[:, :, 0:16], in0=UHR[:, :, :], scalar1=pcol(OFF_FLAG)),
                  reads=["UHR", "PC"], writes=["UH"])
            S_A = ar.alloc([P, 528], F32)
            S_B = ar.alloc([P, 528], F32)
            YP = [ar.alloc([P, 512], BF16), ar.alloc([P, 512], BF16)]
            it = 0
            for g in range(4):
                w = 2 ** (g + 1)
                for n in range(2):
                    a0 = U[:, g, n * 512:n * 512 + 528]
                    rk = [("U", g, n), "UH"] + ([("U", g, 0)] if n == 1 else [])
                    kb.op('dve', lambda e: e.tensor_tensor(out=S_A[:, 1:528], in0=a0[:, 1:528], in1=a0[:, 0:527],
                                                           op=ALU.add), reads=rk, writes=["S_A"])
                    cur, curk, oth, othk = S_A, "S_A", S_B, "S_B"
                    sh = 1
                    for step in range(g):
                        sh2 = sh * 2
                        lo = 2 * sh2 - 1
                        kb.op('dve', lambda e: e.tensor_tensor(out=oth[:, lo:528], in0=cur[:, lo:528],
                                                               in1=cur[:, lo - sh2:528 - sh2], op=ALU.add),
                              reads=[curk], writes=[othk])
                        cur, curk, oth, othk = oth, othk, cur, curk
                        sh = sh2
                    if n == 0:
                        kb.op('dve', lambda e: e.tensor_tensor(out=cur[:, 16:32], in0=cur[:, 16:32],
                                                               in1=PC[:, OFF_PCORR + g * 16:OFF_PCORR + (g + 1) * 16],
                                                               op=ALU.mult),
                              reads=[curk, "PC"], writes=[curk])
                    yp = YP[it % 2]
                    ypk = ("YP", it % 2)
                    it += 1
                    kb.op('dve', lambda e: e.scalar_tensor_tensor(out=yp[:, :], in0=cur[:, 16:528], scalar=1.0 / w,
                                                                  in1=a0[:, 16:528], op0=ALU.mult, op1=ALU.subtract),
                          reads=[curk] + rk, writes=[ypk])
                    b = bank()
                    kb.op('pe', lambda e: e.matmul(PB[b][:, :], lhsT=PW[:, g, :], rhs=yp[:, :], start=True, stop=True),
                          reads=["PW", ypk], writes=[("PB", b)])
                    kb.op('act', lambda e: e.activation(out=MT[:, g, n * 512:(n + 1) * 512], in_=PB[b][:, :],
                                                        func=AF.Copy, scale=pcol(OFF_PSC + e_idx * 4 + g)),
                          reads=[("PB", b), "PC"], writes=[("MT", g, t) for t in range(n * 4, n * 4 + 4)])
            if stop_after == "mix_c":
                return
            out_proj([wout[:, cb * 512:(cb + 1) * 512] for cb in range(4)], KC, key=("wo", "e", e_idx))

        def odd_mixer(o_idx, layer_idx):
            kb.barrier()
            ar.reset()
            win = w_in_odd[o_idx]
            wout = w_out_odd[o_idx]
            J0, J1, J2 = 2048, 3072, 4096
            Hb = H[:, :, :].rearrange("p a b -> p (a b)").bitcast(BF16)
            KT = Hb[:, 0:8192].rearrange("p (a b) -> p a b", a=8)
            V1 = Hb[:, 8192:8192 + 8224].rearrange("p (a b c) -> p a b c", a=8, b=4)
            QT = Hb[:, 16416:16416 + 8192].rearrange("p (a b) -> p a b", a=8)
            Hf = H[:, :, :].rearrange("p a b -> p (a b)")
            XTb = XT[:, :, :].rearrange("p a b -> p (a b)")
            KTP = XTb[:, 0:8192].rearrange("p (a b) -> p a b", a=8)
            V1P = XTb[:, 8192:8192 + 8224].rearrange("p (a b c) -> p a b c", a=8, b=4)
            lam_init = 0.8 - 0.6 * math.exp(-0.3 * layer_idx)

            BIAS = ar.alloc([P, 4, 2, 128], F32)
            ED = ar.alloc([P, 4, 2, 128], BF16)
            LAMV = ar.alloc([P, 4, 128], F32)
            LT = ar.alloc([P, 2, 128], F32)
            LS = ar.alloc([P, 8], F32)
            CB = ar.alloc([P, 8], F32)
            mark0 = ar.off
            kb.dma('sp', [(BIAS[:, :, :, :], p_biasT[:, :, :, :])], writes=["BIAS"])
            kb.dma('sp', [(LAMV[:, :, :], p_lam[:, o_idx, :, :])], writes=["LAMV"])
            kb.op('act', lambda e: e.activation(out=ED[:, :, :, :], in_=BIAS[:, :, :, :], func=AF.Exp),
                  reads=["BIAS"], writes=["ED"])
            for hh in range(4):
                kb.op('dve', lambda e: e.tensor_tensor(out=ED[:, hh, 0, :], in0=ED[:, hh, 0, :], in1=MASK[:, :],
                                                       op=ALU.mult), reads=["ED", "MASK"], writes=["ED"])
            kb.op('dve', lambda e: e.tensor_tensor(out=LT[:, 0, :], in0=LAMV[:, 0, :], in1=LAMV[:, 1, :], op=ALU.mult),
                  reads=["LAMV"], writes=["LT0"])
            kb.op('dve', lambda e: e.tensor_tensor(out=LT[:, 1, :], in0=LAMV[:, 2, :], in1=LAMV[:, 3, :], op=ALU.mult),
                  reads=["LAMV"], writes=["LT1"])
            kb.op('dve', lambda e: e.reduce_sum(out=LS[:, 0:2], in_=LT[:, :, :], axis=mybir.AxisListType.X),
                  reads=["LT0", "LT1"], writes=["LS"])
            kb.op('act', lambda e: e.activation(out=LS[:, 2:4], in_=LS[:, 0:2], func=AF.Exp), reads=["LS"], writes=["LS2"])
            kb.op('dve', lambda e: e.tensor_tensor(out=LS[:, 4:5], in0=LS[:, 2:3], in1=LS[:, 3:4], op=ALU.subtract),
                  reads=["LS2"], writes=["LS4"])
            kb.op('dve', lambda e: e.tensor_scalar(out=LS[:, 5:6], in0=LS[:, 4:5], scalar1=lam_init, scalar2=-1.0,
                                                   op0=ALU.add, op1=ALU.mult), reads=["LS4"], writes=["LAM"])
            kb.op('dve', lambda e: e.tensor_copy(out=CB[:, 0:4], in_=PC[:, OFF_C31:OFF_C31 + 4]), reads=["PC"], writes=["CB0"])
            kb.op('dve', lambda e: e.tensor_scalar_add(out=CB[:, 4:8], in0=PC[:, OFF_C31:OFF_C31 + 4], scalar1=pcol(OFF_NEGB)),
                  reads=["PC"], writes=["CB1"])

            def proj_fm(col0, dst, nm):
                for c in range(2):
                    s = load_w([(0, win[:, col0 + c * 512:col0 + (c + 1) * 512])], key=("ofm", o_idx, nm, c))
                    for f in range(4):
                        for n in range(2):
                            b = bank()
                            for kc in range(KC):
                                kb.op('pe', lambda e: e.matmul(PB[b][:, :], lhsT=WB[s][:, kc, f * 128:(f + 1) * 128],
                                                               rhs=XT[:, kc, 2 + n * 512:2 + (n + 1) * 512],
                                                               start=(kc == 0), stop=(kc == KC - 1)),
                                      reads=[("XT", t) for t in range(n * 4, n * 4 + 4)] + [("WB", s)],
                                      writes=[("PB", b)], signal=(kc == KC - 1))
                            kb.op('act', lambda e: e.activation(out=dst[:, c * 4 + f, n * 512:(n + 1) * 512],
                                                                in_=PB[b][:, :], func=AF.Copy),
                                  reads=[("PB", b)], writes=[(nm, c * 4 + f, n)])

            proj_fm(J1, KT, "KT")
            kb.op('dve', lambda e: e.memset(V1[:, :, :, 256:257], 1.0), writes=["V1one"])
            for c in range(2):
                s = load_w([(0, win[:, J2 + c * 512:J2 + (c + 1) * 512])])
                for t in range(NT):
                    b = bank()
                    for kc in range(KC):
                        kb.op('pe', lambda e: e.matmul(PB[b][:, :], lhsT=XT[:, kc, 2 + t * 128:2 + (t + 1) * 128],
                                                       rhs=WB[s][:, kc, :], start=(kc == 0), stop=(kc == KC - 1)),
                              reads=[("XT", t), ("WB", s)], writes=[("PB", b)], signal=(kc == KC - 1))
                    kb.op('act', lambda e: e.activation(out=V1[:, t, 2 * c:2 * c + 2, 0:256],
                                                        in_=PB[b][:, :].rearrange("p (a b) -> p a b", a=2), func=AF.Copy),
                          reads=[("PB", b)], writes=[("V1", t, c)])
            chk("o_a")
            kvkeys = [("KT", i, n) for i in range(8) for n in range(2)] + [("V1", t, c) for t in range(8) for c in range(2)] + ["V1one"]
            for c in range(2):
                prefetch(("ofm", o_idx, "QT", c), [(0, win[:, J0 + c * 512:J0 + (c + 1) * 512])])
            for i in range(4):
                kb.dma('sp', [(kv_src[i][:, :], Hf[:, i * KVQ:(i + 1) * KVQ])], reads=kvkeys, writes=[("kv_src", i)])
            for i in range(2):
                kb.allgather(kv_src[i][:, :], kv_dst[i][:, :], reads=[("kv_src", i)], writes=[("kv_dst", i)])
            chk("o_b")
            proj_fm(J0, QT, "QT")
            chk("o_c")

            ar2 = Arena(Hf[:, 12304:16384], 4080 * 4)
            SLN = ar2.alloc([P, 1024], F32)
            WSF = ar2.alloc([P, 8, 128], F32)
            ZV = ar2.alloc([P, 1024], F32)
            ZU = ar2.alloc([P, 512], F32)
            WST = ar.alloc([P, 8, 128], BF16)
            SV = ar.alloc([P, 8, 1024], BF16)
            VN = ar.alloc([P, 1024], BF16)
            CO = ar.alloc([P, 512], BF16)
            kb.dma('sp', [(SLN[:, :], p_sln[:, o_idx, :])], writes=["SLN"])
            kb.dma('sp', [(WSF[:, :, :], p_sguT[o_idx])], writes=["WSF"])
            kb.op('dve', lambda e: e.tensor_tensor(out=WST[:, :, :], in0=WSF[:, :, :],
                                                   in1=MASK[:, :].unsqueeze(1).to_broadcast([P, 8, 128]), op=ALU.mult),
                  reads=["WSF", "MASK"], writes=["WST"])
            s0 = load_w([(0, win[:, 1024:1536])])
            s1 = load_w([(0, win[:, 1536:2048])])
            for i in range(2, 4):
                kb.allgather(kv_src[i][:, :], kv_dst[i][:, :], reads=[("kv_src", i)], writes=[("kv_dst", i)])
            for t in range(NT):
                bb = [bank(), bank()]
                for c, s in enumerate((s0, s1)):
                    for kc in range(KC):
                        kb.op('pe', lambda e: e.matmul(PB[bb[c]][:, :], lhsT=XT[:, kc, 2 + t * 128:2 + (t + 1) * 128],
                                                       rhs=WB[s][:, kc, :], start=(kc == 0), stop=(kc == KC - 1)),
                              reads=[("XT", t), ("WB", s)], writes=[("PB", bb[c])], signal=(kc == KC - 1))
                    kb.op('act', lambda e: e.activation(out=ZV[:, c * 512:(c + 1) * 512], in_=PB[bb[c]][:, :], func=GELU),
                          reads=[("PB", bb[c])], writes=[("ZV", c)])
                BNS = SS[:, 24:24 + 8]
                kb.op('dve', lambda e: e.bn_stats(out=SS[:, 24:30], in_=ZV[:, 0:512]), reads=[("ZV", 0)], writes=["BN0"])
                kb.op('dve', lambda e: e.bn_stats(out=SS[:, 30:36], in_=ZV[:, 512:1024]), reads=[("ZV", 1)], writes=["BN1"])
                kb.op('dve', lambda e: e.bn_aggr(out=LS[:, 6:8], in_=SS[:, 24:36].rearrange("p (a b) -> p a b", a=2)),
                      reads=["BN0", "BN1"], writes=["MVZ"])
                kb.op('act', lambda e: e.activation(out=SS[:, 36:37], in_=LS[:, 7:8],
                                                    func=AF.Sqrt, bias=EPS), reads=["MVZ"], writes=["SDZ"])
                kb.op('dve', lambda e: e.reciprocal(out=SS[:, 37:38], in_=SS[:, 36:37]), reads=["SDZ"], writes=["RSZ"])
                kb.op('dve', lambda e: e.tensor_scalar(out=ZV[:, :], in0=ZV[:, :], scalar1=LS[:, 6:7], scalar2=SS[:, 37:38],
                                                       op0=ALU.subtract, op1=ALU.mult),
                      reads=[("ZV", 0), ("ZV", 1), "MVZ", "RSZ"], writes=[("ZV", 0), ("ZV", 1)])
                kb.op('dve', lambda e: e.tensor_tensor(out=VN[:, :], in0=ZV[:, :], in1=SLN[:, :], op=ALU.mult),
                      reads=[("ZV", 0), ("ZV", 1), "SLN"], writes=["VN"])
                bs2 = [bank(), bank()]
                for g in range(8):
                    b = bs2[g // 4]
                    kb.op('pe', lambda e: e.matmul(PB[b][:, (g % 4) * 128:(g % 4 + 1) * 128], lhsT=WST[:, g, :],
                                                   rhs=VN[:, g * 128:(g + 1) * 128], start=True, stop=True),
                          reads=["WST", "VN"], writes=[("PB", b)], signal=(g % 4 == 3))
                for g in range(8):
                    b = bs2[g // 4]
                    bcol = pcol(OFF_SB + o_idx * 8 + g)
                    kb.op('act', lambda e: e.activation(out=SV[:, t, g * 128:(g + 1) * 128],
                                                        in_=PB[b][:, (g % 4) * 128:(g % 4 + 1) * 128],
                                                        func=AF.Identity, bias=bcol),
                          reads=[("PB", b), "PC"], writes=[("SV", t)])
            for c in range(2):
                s = load_w([(0, win[:, c * 512:(c + 1) * 512])])
                for t in range(NT):
                    b = bank()
                    for kc in range(KC):
                        kb.op('pe', lambda e: e.matmul(PB[b][:, :], lhsT=XT[:, kc, 2 + t * 128:2 + (t + 1) * 128],
                                                       rhs=WB[s][:, kc, :], start=(kc == 0), stop=(kc == KC - 1)),
                              reads=[("XT", t), ("WB", s)], writes=[("PB", b)], signal=(kc == KC - 1))
                    kb.op('act', lambda e: e.activation(out=ZU[:, :], in_=PB[b][:, :], func=GELU),
                          reads=[("PB", b)], writes=["ZU"])
                    kb.op('dve', lambda e: e.tensor_tensor(out=CO[:, :], in0=ZU[:, :], in1=SV[:, t, c * 512:(c + 1) * 512],
                                                           op=ALU.mult), reads=["ZU", ("SV", t)], writes=["CO"])
                    bt = bank()
                    pbt = PB[bt][:, :].bitcast(BF16).rearrange("p (a b) -> p a b", a=8)
                    for i in range(4):
                        kb.op('pe', lambda e: e.transpose(out=pbt[:, i, :], in_=CO[:, i * 128:(i + 1) * 128], identity=IDB[:, :]),
                              reads=["CO", "IDB"], writes=[("PB", bt)], signal=(i == 3))
                    for i in range(4):
                        kc = c * 4 + i
                        dst = MT[:, kc, t * 128:(t + 1) * 128]
                        if i % 2 == 0:
                            kb.op('act', lambda e: e.activation(out=dst, in_=pbt[:, i, :], func=AF.Copy),
                                  reads=[("PB", bt)], writes=[("MT", kc, t)])
                        else:
                            kb.op('dve', lambda e: e.tensor_copy(out=dst, in_=pbt[:, i, :]),
                                  reads=[("PB", bt)], writes=[("MT", kc, t)])

            chk("o_d")
            kb.barrier()
            XTf = XTb.bitcast(F32)
            kb.dma('sp', [(XTf[:, i * KVQ:(i + 1) * KVQ], kv_dst[i][0:128, :]) for i in range(4)],
                   reads=[("kv_dst", i) for i in range(4)], writes=["KVP"])
            prefetch((("wo", "o", o_idx), 0), [(0, wout[:, 0:512])])
            prefetch((("wo", "o", o_idx), 1), [(0, wout[:, 512:1024])])
            ar.reset(mark0)
            PT = [ar.alloc([P, 256], BF16) for _ in range(6)]
            OA = ar.alloc([P, 2, 260], F32)
            RD = ar.alloc([P, 4], F32)
            DO = ar.alloc([P, 256], F32)
            DJ = ar.alloc([P, 256], F32)
            DBs = [ar.alloc([P, 256], BF16), ar.alloc([P, 256], BF16)]
            scale = 128.0 ** -0.5
            LOOK = 3
            steps = []
            gid = 0
            for hh in range(4):
                for qp in range(4):
                    qt2 = [qp * 2, qp * 2 + 1]
                    klist = [("p", kt) for kt in range(8)] + [("o", kt) for kt in range(qt2[1] + 1)]
                    gsteps = []
                    for (kind, kt) in klist:
                        vis = qt2 if kind == "p" else [qt for qt in qt2 if qt >= kt]
                        for j in range(2):
                            gsteps.append(dict(g=gid, hh=hh, qt2=qt2, kind=kind, kt=kt, j=j, vis=vis))
                    seen = set()
                    for st in gsteps:
                        st["first"] = {}
                        for qt in st["vis"]:
                            key = (qt, st["j"])
                            st["first"][qt] = key not in seen
                            seen.add(key)
                    seen = set()
                    for st in reversed(gsteps):
                        st["last"] = {}
                        for qt in st["vis"]:
                            key = (qt, st["j"])
                            st["last"][qt] = key not in seen
                            seen.add(key)
                    gsteps[-1]["gend"] = True
                    steps.extend(gsteps)
                    gid += 1

            def emit_S(i, st):
                hh, kind, kt, j, vis = st["hh"], st["kind"], st["kt"], st["j"], st["vis"]
                nq = len(vis) * 128
                ksrc = KTP if kind == "p" else KT
                kkey = ["KVP"] if kind == "p" else [("KT", hh * 2 + j, kt // 4)]
                bsx = 4 + (i % 3)
                kb.op('pe', lambda e: e.matmul(PB[bsx][:, 0:nq], lhsT=ksrc[:, hh * 2 + j, kt * 128:(kt + 1) * 128],
                                               rhs=QT[:, hh * 2 + j, vis[0] * 128:(vis[-1] + 1) * 128],
                                               start=True, stop=True),
                      reads=kkey + [("QT", hh * 2 + j, vis[0] // 4)], writes=[("PB", bsx)])
                pt = PT[i % 6]
                ptk = ("PT", i % 6)
                bcol = CB[:, 4 + hh:5 + hh] if kind == "p" else CB[:, hh:hh + 1]
                bkey = "CB1" if kind == "p" else "CB0"
                near = {}
                for qt in vis:
                    if kind == "o" and kt == qt:
                        near[qt] = 0
                    elif kind == "o" and kt == qt - 1:
                        near[qt] = 1
                    elif kind == "p" and kt == 7 and qt == 0:
                        near[qt] = 1
                far = [qt for qt in vis if qt not in near]
                if far:
                    o0 = (far[0] - vis[0]) * 128
                    o1 = (far[-1] - vis[0] + 1) * 128
                    kb.op('act', lambda e: e.activation(out=pt[:, o0:o1], in_=PB[bsx][:, o0:o1], func=AF.Exp,
                                                        scale=scale, bias=bcol),
                          reads=[("PB", bsx), bkey], writes=[ptk])
                for qt, which in near.items():
                    o0 = (qt - vis[0]) * 128
                    if kind == "p":
                        kb.op('act', lambda e: e.activation(out=pt[:, o0:o0 + 128], in_=PB[bsx][:, o0:o0 + 128],
                                                            func=AF.Exp, scale=scale, bias=pcol(OFF_NEGB)),
                              reads=[("PB", bsx), "PC"], writes=[ptk])
                    else:
                        kb.op('act', lambda e: e.activation(out=pt[:, o0:o0 + 128], in_=PB[bsx][:, o0:o0 + 128],
                                                            func=AF.Exp, scale=scale),
                              reads=[("PB", bsx)], writes=[ptk])
                    kb.op('dve', lambda e: e.tensor_tensor(out=pt[:, o0:o0 + 128], in0=pt[:, o0:o0 + 128],
                                                           in1=ED[:, hh, which, :], op=ALU.mult),
                          reads=[ptk, "ED"], writes=[ptk])

            def emit_PV(i, st):
                hh, kind, kt, j, vis, qt2 = st["hh"], st["kind"], st["kt"], st["j"], st["vis"], st["qt2"]
                pt = PT[i % 6]
                ptk = ("PT", i % 6)
                vsrc = V1P if kind == "p" else V1
                vkey = ["KVP"] if kind == "p" else [("V1", kt, hh // 2), "V1one"]
                for qt in vis:
                    o0 = (qt - vis[0]) * 128
                    ab = (qt - qt2[0]) * 2 + j
                    kb.op('pe', lambda e: e.matmul(PB[ab][:, 0:257], lhsT=pt[:, o0:o0 + 128], rhs=vsrc[:, kt, hh, :],
                                                   start=st["first"][qt], stop=st["last"][qt]),
                          reads=[ptk] + vkey, writes=[("PB", ab)], signal=True)

            fin_cnt = [0]

            def emit_fin_a(st):
                hh, qt2 = st["hh"], st["qt2"]
                deferred = []
                for qt in qt2:
                    for j in range(2):
                        ab = (qt - qt2[0]) * 2 + j
                        kb.op('act', lambda e: e.activation(out=OA[:, j, 0:257], in_=PB[ab][:, 0:257], func=AF.Copy),
                              reads=[("PB", ab)], writes=[("OA", j)])
                    db = DBs[fin_cnt[0] % 2]
                    dbk = ("DB", fin_cnt[0] % 2)
                    fin_cnt[0] += 1
                    kb.op('dve', lambda e: e.reciprocal(out=RD[:, 0:1], in_=OA[:, 0, 256:257]), reads=[("OA", 0)], writes=["RD0"])
                    kb.op('dve', lambda e: e.reciprocal(out=RD[:, 1:2], in_=OA[:, 1, 256:257]), reads=[("OA", 1)], writes=["RD1"])
                    kb.op('dve', lambda e: e.tensor_tensor(out=RD[:, 2:3], in0=RD[:, 1:2], in1=LS[:, 5:6], op=ALU.mult),
                          reads=["RD1", "LAM"], writes=["RD2"])
                    kb.op('dve', lambda e: e.tensor_scalar_mul(out=DO[:, :], in0=OA[:, 0, 0:256], scalar1=RD[:, 0:1]),
                          reads=[("OA", 0), "RD0"], writes=["DO"])
                    kb.op('dve', lambda e: e.scalar_tensor_tensor(out=DO[:, :], in0=OA[:, 1, 0:256], scalar=RD[:, 2:3],
                                                                  in1=DO[:, :], op0=ALU.mult, op1=ALU.add),
                          reads=[("OA", 1), "RD2", "DO"], writes=["DO"])
                    kb.op('act', lambda e: e.activation(out=DJ[:, :], in_=DO[:, :], func=AF.Square, accum_out=RD[:, 3:4]),
                          reads=["DO"], writes=["DJ", "RD3"])
                    kb.op('act', lambda e: e.activation(out=SS[:, 38:39], in_=RD[:, 3:4], func=AF.Sqrt, scale=1.0 / 256, bias=EPS),
                          reads=["RD3"], writes=["SD1"])
                    kb.op('dve', lambda e: e.reciprocal(out=SS[:, 39:40], in_=SS[:, 38:39]), reads=["SD1"], writes=["SD2"])
                    kb.op('dve', lambda e: e.tensor_scalar(out=db[:, :], in0=DO[:, :], scalar1=SS[:, 39:40],
                                                           scalar2=(1.0 - lam_init), op0=ALU.mult, op1=ALU.mult),
                          reads=["DO", "SD2"], writes=[dbk])
                    deferred.append((hh, qt, db, dbk))
                return deferred

            def emit_fin_b(hh, qt, db, dbk):
                bt = 7
                pbt = PB[bt][:, :].bitcast(BF16).rearrange("p (a b) -> p a b", a=8)
                kb.op('pe', lambda e: e.transpose(out=pbt[:, 0, :], in_=db[:, 0:128], identity=IDB[:, :]),
                      reads=[dbk, "IDB"], writes=[("PB", bt)], signal=False)
                kb.op('pe', lambda e: e.transpose(out=pbt[:, 1, :], in_=db[:, 128:256], identity=IDB[:, :]),
                      reads=[dbk, "IDB"], writes=[("PB", bt)])
                for i2 in range(2):
                    kc = 8 + hh * 2 + i2
                    gcol = pcol(OFF_SUB + o_idx * 2 + i2)
                    dst = MT[:, kc, qt * 128:(qt + 1) * 128]
                    kb.op('dve', lambda e: e.tensor_scalar_mul(out=dst, in0=pbt[:, i2, :], scalar1=gcol),
                          reads=[("PB", bt), "PC"], writes=[("MT", kc, qt)])

            pending_fin = []
            nsteps = len(steps)
            for idx in range(nsteps + LOOK):
                if idx < nsteps:
                    emit_S(idx, steps[idx])
                while pending_fin and pending_fin[0][0] <= idx:
                    emit_fin_b(*pending_fin.pop(0)[1])
                pi_ = idx - LOOK
                if pi_ >= 0:
                    emit_PV(pi_, steps[pi_])
                    if steps[pi_].get("gend"):
                        for k2, args in enumerate(emit_fin_a(steps[pi_])):
                            pending_fin.append((idx + 4 + 2 * k2, args))
            while pending_fin:
                emit_fin_b(*pending_fin.pop(0)[1])
            chk("o_e")
            kb.barrier()
            for t in range(NT):
                kb.dma('sp', [(H[:, t, :], hspill[t * 128:(t + 1) * 128, :])], reads=[("HSP", t)], writes=[("H", t)])
            out_proj([wout[:, cb * 512:(cb + 1) * 512] for cb in range(4)], KC, key=("wo", "o", o_idx))

        order_last_first = [NT - 1] + list(range(NT - 1))
        for l in range(n_layers):
          try:
            last = (l == n_layers - 1)
            if l % 2 == 0:
                prefetch(("eu", l // 2), [(0, w_in_even[l // 2][:, 0:512])])
            else:
                for t in range(NT):
                    kb.dma('sp', [(hspill[t * 128:(t + 1) * 128, :], H[:, t, :])], reads=[("H", t)], writes=[("HSP", t)])
                for c in range(2):
                    prefetch(("ofm", l // 2, "KT", c), [(0, w_in_odd[l // 2][:, 3072 + c * 512:3072 + (c + 1) * 512])])
            norm_phase(l * 2, list(range(NT)), halo=False)
            if last and stop_after == "norm0":
                break
            if l % 2 == 0:
                even_mixer(l // 2)
            else:
                odd_mixer(l // 2, l)
            if last and stop_after and stop_after.startswith("mix"):
                break
            prefetch(("up", l, 0), [(0, w_up[l][:, 0:256]), (256, w_up[l][:, DFF:DFF + 256])])
            norm_phase(l * 2 + 1, order_last_first, halo=True)
            if last and stop_after == "norm1":
                break
            ffn(l)
          except StopBuild:
            break

        kb.barrier()
        ar.reset()
        FG = ar.alloc([P, D], F32)
        JK = ar.alloc([P, D], BF16)
        YO = [ar.alloc([P, D], F32), ar.alloc([P, D], F32)]
        kb.dma('sp', [(FG[:, :], p_fng[:, :])], writes=["FG"])
        for t in range(NT):
            yo = YO[t % 2]
            yk = ("YO", t % 2)
            if dbg_h:
                kb.dma('sp', [(out_d[t * 128:(t + 1) * 128, :], H[:, t, :])], reads=[("H", t)], writes=[("OUT", t)])
                continue
            kb.op('act', lambda e: e.activation(out=JK[:, :], in_=H[:, t, :], func=AF.Square, accum_out=SS[:, t:t + 1]),
                  reads=[("H", t)], writes=["JK", ("SS", t)])
            kb.op('act', lambda e: e.activation(out=SS[:, 8 + t:9 + t], in_=SS[:, t:t + 1], func=AF.Sqrt, scale=1.0 / D, bias=EPS),
                  reads=[("SS", t)], writes=[("SS", 8 + t)])
            kb.op('dve', lambda e: e.reciprocal(out=SS[:, 16 + t:17 + t], in_=SS[:, 8 + t:9 + t]),
                  reads=[("SS", 8 + t)], writes=[("SS", 16 + t)])
            kb.op('dve', lambda e: e.scalar_tensor_tensor(out=yo[:, :], in0=H[:, t, :], scalar=SS[:, 16 + t:17 + t],
                                                          in1=FG[:, :], op0=ALU.mult, op1=ALU.mult),
                  reads=[("H", t), ("SS", 16 + t), "FG"], writes=[yk])
            kb.dma('sp', [(out_d[t * 128:(t + 1) * 128, :], yo[:, :])], reads=[yk], writes=[("OUT", t)])
        kb.barrier()
        print("ops emitted:", kb.nops, "sems:", len(kb.sems), flush=True)
    return nc


def _t5_bucket(n):
    n = np.maximum(n, 0)
    max_exact = 16
    nn = np.maximum(n, 1).astype(np.float32)
    large = max_exact + (np.log(nn / max_exact) / math.log(128 / max_exact) * (32 - max_exact)).astype(np.int32)
    large = np.minimum(large, 31)
    return np.where(n < max_exact, n, large)


def _consts(half):
    c = {}
    c["c_ident"] = np.eye(128, dtype=np.float32)
    half_d = 64
    inv = 1.0 / (10000.0 ** (np.arange(half_d, dtype=np.float32) / half_d))
    pos = (half * T + np.arange(T, dtype=np.float32))
    ang = pos[:, None] * inv[None, :]
    cos = np.cos(ang).astype(np.float32).reshape(NT, 128, 64).transpose(1, 0, 2)
    sin = np.sin(ang).astype(np.float32).reshape(NT, 128, 64).transpose(1, 0, 2)
    c["c_rot"] = np.ascontiguousarray(np.stack([cos, sin], axis=1))
    log_g = np.log(1.0 - 2.0 ** (-5.0 - np.arange(6, dtype=np.float32))).astype(np.float32)
    idx = np.arange(128, dtype=np.float32)
    diff = idx[:, None] - idx[None, :]
    intra = np.where(diff >= 0, np.exp(log_g[:, None, None] * np.maximum(diff, 0.0)), 0.0)
    intraT = intra.transpose(2, 0, 1) * (128.0 ** -0.5)
    c["c_intra"] = np.ascontiguousarray(intraT.astype(np.float32))
    c["qdec"] = np.exp(log_g[None, :] * (idx[:, None] + 1.0)).astype(np.float32)
    c["kdec"] = (np.exp(log_g[None, :] * (127.0 - idx[:, None])) * (128.0 ** -0.5)).astype(np.float32)
    p = np.arange(128)
    c["c_mask"] = (p[None, :] >= p[:, None]).astype(np.float32)
    pcorr = np.ones((4, 16), np.float32)
    if half == 0:
        for g in range(4):
            w = 2 ** (g + 1)
            tt = np.arange(16) + 1
            pcorr[g] = w / np.minimum(tt, w)
    c["pcorr"] = pcorr
    return c


def _prep(inputs):
    f = lambda a: np.ascontiguousarray(np.asarray(a, dtype=np.float32))
    ins = {k: f(v) for k, v in inputs.items()}

    def cols(v):
        return v.reshape(-1, 128).T

    shared = {}
    for k in ("w_in_even", "w_out_even", "pool_w", "w_in_odd", "w_out_odd", "w_up", "w_down"):
        shared[k] = ins[k]
    pc = np.zeros((128, NCOLS), np.float32)
    for l in range(4):
        pc[:, OFF_NG + (2 * l) * 16:OFF_NG + (2 * l + 1) * 16] = cols(ins["mix_norm_g"][l])
        pc[:, OFF_NG + (2 * l + 1) * 16:OFF_NG + (2 * l + 2) * 16] = cols(ins["ffn_norm_g"][l])
        for r in range(3):
            pc[:, OFF_CW + l * 264 + r * 88:OFF_CW + l * 264 + (r + 1) * 88] = cols(ins["conv_w"][l, r])
        pc[:, OFF_CB + l * 88:OFF_CB + (l + 1) * 88] = cols(ins["conv_b"][l])
    for e in range(2):
        pc[:, OFF_PSC + e * 4:OFF_PSC + (e + 1) * 4] = cols(ins["pool_scale"][e])
        pc[:, OFF_GNG + e * 12:OFF_GNG + (e + 1) * 12] = cols(ins["ret_gn_g"][e])
        pc[:, OFF_SUB + e * 2:OFF_SUB + (e + 1) * 2] = cols(ins["diff_subln_g"][e])
        pc[:, OFF_SB + e * 8:OFF_SB + (e + 1) * 8] = ins["sgu_b"][e].T
    pc[:, OFF_C31:OFF_C31 + 4] = np.broadcast_to(ins["rel_bias"][31][None, :], (128, 4))
    shared["p_fng"] = np.ascontiguousarray(np.broadcast_to(ins["final_norm_g"][None, :], (128, D)))
    shared["p_sln"] = np.ascontiguousarray(np.broadcast_to(ins["sgu_ln_g"][None, :, :], (128, 2, 1024)))
    lam = np.stack([ins["lam_q1"], ins["lam_k1"], ins["lam_q2"], ins["lam_k2"]], axis=1)
    shared["p_lam"] = np.ascontiguousarray(np.broadcast_to(lam[None], (128, 2, 4, 128)))
    j = np.arange(128)[:, None]
    i = np.arange(128)[None, :]
    b0 = _t5_bucket(i - j)
    b1 = _t5_bucket(128 + i - j)
    rb = ins["rel_bias"]
    biasT = np.stack([rb[b0], rb[b1]], axis=0)
    shared["p_biasT"] = np.ascontiguousarray(biasT.transpose(1, 3, 0, 2))
    shared["p_sguT"] = np.ascontiguousarray(ins["sgu_w"].transpose(0, 3, 1, 2))
    maps = []
    for core in range(NCORES):
        b, half = core // 2, core % 2
        c = _consts(half)
        m = dict(shared)
        m["x"] = np.ascontiguousarray(ins["x"][b, half * T:(half + 1) * T, :])
        pcc = pc.copy()
        pcc[:, OFF_QDEC:OFF_QDEC + 6] = c["qdec"]
        pcc[:, OFF_KDEC:OFF_KDEC + 6] = c["kdec"]
        pcc[:, OFF_FLAG] = float(half)
        pcc[:, OFF_NEGB] = 0.0 if half == 1 else NEGBIG
        pcc[:, OFF_PCORR:OFF_PCORR + 64] = c["pcorr"].reshape(1, 64)
        m["p_cols"] = pcc
        for k in ("c_ident", "c_rot", "c_intra", "c_mask"):
            m[k] = c[k]
        maps.append(m)
    return maps


_NC_CACHE = {}


def kernel(**inputs):
    maps = _prep(inputs)
    if "nc" not in _NC_CACHE:
        _NC_CACHE["nc"] = build()
    nc = _NC_CACHE["nc"]
    res = run_bass_kernel_spmd(nc, maps, core_ids=list(range(NCORES)))
    out = np.zeros((4, 2 * T, D), np.float32)
    for core in range(NCORES):
        b, half = core // 2, core % 2
        out[b, half * T:(half + 1) * T, :] = np.asarray(res.results[core]["out"], dtype=np.float32)
    return out
```

```python
import math
from contextlib import ExitStack
import numpy as np
import concourse.bass as bass
import concourse.mybir as mybir
from concourse.bass_utils import run_bass_kernel_spmd

F32 = mybir.dt.float32
BF16 = mybir.dt.bfloat16
AF = mybir.ActivationFunctionType
ALU = mybir.AluOpType

P = 128
T = 1024
NT = 8
D = 2048
KC = 16
DFF = 5632
NFF = 44
EPS = 1e-6
NCORES = 8
PAIRS = [[0, 1], [2, 3], [4, 5], [6, 7]]
NEGBIG = -30000.0
GELU = AF.Gelu

OFF_NG = 0
OFF_PSC = OFF_NG + 128
OFF_GNG = OFF_PSC + 8
OFF_SUB = OFF_GNG + 24
OFF_CW = OFF_SUB + 4
OFF_CB = OFF_CW + 1056
OFF_SB = OFF_CB + 352
OFF_C31 = OFF_SB + 16
OFF_QDEC = OFF_C31 + 4
OFF_KDEC = OFF_QDEC + 6
OFF_FLAG = OFF_KDEC + 6
OFF_NEGB = OFF_FLAG + 1
OFF_PCORR = OFF_NEGB + 1
NCOLS = OFF_PCORR + 64


class KB:
    def __init__(self, nc):
        self.nc = nc
        self.eng = {'pe': nc.tensor, 'act': nc.scalar, 'dve': nc.vector, 'pool': nc.gpsimd, 'sp': nc.sync}
        self.sems = []
        self.cur = {}
        self.cnt = {}
        self.waited = {e: {} for e in self.eng}
        self.own = {e: set() for e in self.eng}
        self.track = {}
        self.pending = {e: ([], []) for e in self.eng}
        self.dpool = {}
        self.didx = {}
        self.ccsid = None
        self.cctot = 0
        self.nops = 0

    def new_sem(self, name):
        h = self.nc.alloc_semaphore(name=name)
        self.sems.append(h)
        return len(self.sems) - 1

    def _deps(self, reads, writes, e=None):
        evs = []
        own = self.own.get(e, ())
        for k in reads:
            t = self.track.get(k)
            if t and t[0]:
                evs.append(t[0])
            if t and isinstance(k, tuple) and k[0] == "PB":
                for sid, val in t[1].items():
                    if sid not in own:
                        evs.append((sid, val))
        for k in writes:
            t = self.track.get(k)
            if t:
                if t[0]:
                    evs.append(t[0])
                evs.extend(t[1].items())
        return evs

    def _wait(self, e, evs):
        need = {}
        for sid, val in evs:
            if need.get(sid, 0) < val:
                need[sid] = val
        w = self.waited[e]
        for sid, val in need.items():
            if w.get(sid, 0) < val:
                self.eng[e].wait_ge(self.sems[sid], val)
                w[sid] = val

    def _commit(self, e, ev):
        pr, pw = self.pending[e]
        for k in pr:
            t = self.track.setdefault(k, [None, {}])
            if t[1].get(ev[0], 0) < ev[1]:
                t[1][ev[0]] = ev[1]
        for k in pw:
            self.track[k] = [ev, {}]
        pr.clear()
        pw.clear()

    def op(self, e, fn, reads=(), writes=(), signal=True):
        self._wait(e, self._deps(reads, writes, e))
        inst = fn(self.eng[e])
        self.nops += 1
        pr, pw = self.pending[e]
        pr.extend(reads)
        pw.extend(writes)
        if not signal:
            return None
        if e not in self.cur or self.cnt[e] >= 6000:
            self.cur[e] = self.new_sem("s_%s_%d" % (e, len(self.sems)))
            self.own[e].add(self.cur[e])
            self.cnt[e] = 0
        self.cnt[e] += 1
        inst.then_inc(self.sems[self.cur[e]], 1)
        ev = (self.cur[e], self.cnt[e])
        self._commit(e, ev)
        return ev

    def dma(self, e, pairs, reads=(), writes=()):
        if e not in self.dpool:
            self.dpool[e] = [[self.new_sem("d_%s_%d" % (e, i)), 0] for i in range(6)]
            self.didx[e] = 0
        slot = self.dpool[e][self.didx[e] % len(self.dpool[e])]
        self.didx[e] += 1
        evs = self._deps(reads, writes)
        if slot[1] > 0:
            evs.append((slot[0], slot[1]))
        self._wait(e, evs)
        for (o, i) in pairs:
            self.eng[e].dma_start(out=o, in_=i).then_inc(self.sems[slot[0]], 16)
            slot[1] += 16
            self.nops += 1
        ev = (slot[0], slot[1])
        pr, pw = self.pending[e]
        for k in reads:
            t = self.track.setdefault(k, [None, {}])
            if t[1].get(ev[0], 0) < ev[1]:
                t[1][ev[0]] = ev[1]
        for k in writes:
            self.track[k] = [ev, {}]
        return ev

    def allgather(self, src, dst, reads=(), writes=()):
        e = 'pool'
        if self.ccsid is None:
            self.ccsid = self.new_sem("ccsem")
        evs = self._deps(reads, writes)
        if self.cctot > 0:
            evs.append((self.ccsid, self.cctot))
        self._wait(e, evs)
        inst = self.nc.gpsimd.collective_compute("AllGather", ALU.bypass, replica_groups=self.pairs,
                                                 ins=[src], outs=[dst])
        inst.then_inc(self.sems[self.ccsid])
        self.cctot += 1
        ev = (self.ccsid, self.cctot)
        for k in reads:
            t = self.track.setdefault(k, [None, {}])
            t[1][ev[0]] = ev[1]
        for k in writes:
            self.track[k] = [ev, {}]
        return ev

    def barrier(self):
        evs = []
        for e2 in self.cur:
            evs.append((self.cur[e2], self.cnt[e2]))
        for e2 in self.dpool:
            if e2 == 'pool':
                continue
            for sid, tot in self.dpool[e2]:
                if tot:
                    evs.append((sid, tot))
        if self.cctot:
            evs.append((self.ccsid, self.cctot))
        for e in self.eng:
            self._wait(e, evs)


class StopBuild(Exception):
    pass


class Arena:
    def __init__(self, ap_f32, nbytes):
        self.ap = ap_f32
        self.n = nbytes
        self.off = 0

    def reset(self, off=0):
        self.off = off

    def alloc(self, shape, dt):
        esz = 4 if dt == F32 else 2
        free = 1
        for s in shape[1:]:
            free *= s
        nb = (free * esz + 31) // 32 * 32
        assert self.off + nb <= self.n, ("arena overflow", self.off, nb, self.n)
        v = self.ap[:, self.off // 4:(self.off + nb) // 4]
        self.off += nb
        if dt != F32:
            v = v.bitcast(dt)
        v = v[:, 0:free]
        if len(shape) == 3:
            v = v.rearrange("p (a b) -> p a b", a=shape[1])
        elif len(shape) == 4:
            v = v.rearrange("p (a b c) -> p a b c", a=shape[1], b=shape[2])
        return v


def build(n_layers=4, dbg_h=False, stop_after=None, ncores=NCORES):
    nc = bass.Bass("TRN2", target_bir_lowering=False)

    def din(name, shape):
        return nc.dram_tensor(name, list(shape), F32, kind="ExternalInput").ap()

    x_d = din("x", [T, D])
    n_ev = max(1, (n_layers + 1) // 2)
    n_od = max(1, n_layers // 2)
    n_ff = max(1, n_layers)
    w_in_even = din("w_in_even", [n_ev, D, 5120])
    w_out_even = din("w_out_even", [n_ev, D, D])
    pool_w = din("pool_w", [2, 4, 128, 128])
    w_in_odd = din("w_in_odd", [n_od, D, 5120])
    w_out_odd = din("w_out_odd", [n_od, D, D])
    w_up = din("w_up", [n_ff, D, 2 * DFF])
    w_down = din("w_down", [n_ff, DFF, D])
    c_ident = din("c_ident", [128, 128])
    c_rot = din("c_rot", [128, 2, 8, 64])
    c_intra = din("c_intra", [128, 6, 128])
    c_mask = din("c_mask", [128, 128])
    p_cols = din("p_cols", [128, NCOLS])
    p_fng = din("p_fng", [128, D])
    p_sln = din("p_sln", [128, 2, 1024])
    p_lam = din("p_lam", [128, 2, 4, 128])
    p_biasT = din("p_biasT", [128, 4, 2, 128])
    p_sguT = din("p_sguT", [2, 128, 8, 128])
    out_d = nc.dram_tensor("out", [T, D], F32, kind="ExternalOutput").ap()

    hspill = nc.dram_tensor("hspill", [T, D], F32).ap()
    xu_src = nc.dram_tensor("xu_src", [128, 64], F32).ap()
    xu_dst = nc.dram_tensor("xu_dst", [256, 64], F32).ap()
    st_src = nc.dram_tensor("st_src", [128, 256], F32).ap()
    st_dst = nc.dram_tensor("st_dst", [256, 256], F32).ap()
    xh_src = nc.dram_tensor("xh_src", [128, 16], F32).ap()
    xh_dst = nc.dram_tensor("xh_dst", [256, 16], F32).ap()
    KVW = 4096 + 4112
    KVQ = KVW // 4
    kv_src = [nc.dram_tensor("kv_src%d" % i, [128, KVQ], F32).ap() for i in range(4)]
    kv_dst = [nc.dram_tensor("kv_dst%d" % i, [256, KVQ], F32).ap() for i in range(4)]

    ARENA_BYTES = 40 * 1024
    with ExitStack() as es:
        def sb(name, shape, dt):
            return es.enter_context(nc.sbuf_tensor(name, shape, dt))

        H = sb("H", [P, NT, D], F32)
        XT = sb("XT", [P, KC, T + 2], BF16)
        MT = sb("MT", [P, KC, T], BF16)
        WB = [sb("WB0", [P, KC, 512], BF16), sb("WB1", [P, KC, 512], BF16)]
        PC = sb("PC", [P, NCOLS], F32)
        IDB = sb("IDB", [P, 128], BF16)
        MASK = sb("MASK", [P, 128], F32)
        SS = sb("SS", [P, 48], F32)
        ARN = sb("ARN", [P, ARENA_BYTES // 4], F32)
        PB = [es.enter_context(nc.psum_tensor("pb%d" % i, [P, 512], F32)) for i in range(8)]
        ar = Arena(ARN, ARENA_BYTES)
        kb = KB(nc)
        kb.pairs = [[2 * i, 2 * i + 1] for i in range(ncores // 2)]

        def chk(name):
            if stop_after == name:
                raise StopBuild()

        def pcol(off, n=1):
            return PC[:, off:off + n]

        kb.dma('sp', [(PC[:, :], p_cols[:, :])], writes=["PC"])
        kb.dma('sp', [(MASK[:, :], c_mask[:, :])], writes=["MASK"])
        kb.dma('pool', [(IDB[:, :], c_ident[:, :])], writes=["IDB"])
        for t in range(NT):
            kb.dma('sp', [(H[:, t, :], x_d[t * 128:(t + 1) * 128, :])], writes=[("H", t)])

        bank_rr = [0]

        def bank():
            b = bank_rr[0] % 8
            bank_rr[0] += 1
            return b

        wslot = [0]

        preloaded = {}

        def prefetch(key, pieces, nk=KC):
            preloaded[key] = load_w(pieces, nk)

        def load_w(pieces, nk=KC, key=None):
            if key is not None and key in preloaded:
                return preloaded.pop(key)
            s = wslot[0] % 2
            wslot[0] += 1
            pairs = []
            for (co, src) in pieces:
                ncols = src.shape[1]
                v = src.rearrange("(kc p) c -> p kc c", p=128)
                step = 4 if ncols * 4 >= 2048 else 8
                for k0 in range(0, nk, step):
                    k1 = min(nk, k0 + step)
                    pairs.append((WB[s][:, k0:k1, co:co + ncols], v[:, k0:k1, :]))
            kb.dma('pool', pairs, writes=[("WB", s)])
            return s

        def norm_phase(gidx, order, halo):
            kb.barrier()
            ar.reset()
            JK = ar.alloc([P, D], BF16)
            HN = [ar.alloc([P, D], BF16) for _ in range(3)]
            XH = ar.alloc([P, 32], BF16)
            XHF = ar.alloc([P, 16], F32)

            def stA_act(oi, t):
                kb.op('act', lambda e: e.activation(out=JK[:, :], in_=H[:, t, :], func=AF.Square,
                                                    accum_out=SS[:, t:t + 1]),
                      reads=[("H", t)], writes=["JK", ("SS", t)])
                kb.op('act', lambda e: e.activation(out=SS[:, 8 + t:9 + t], in_=SS[:, t:t + 1], func=AF.Sqrt,
                                                    scale=1.0 / D, bias=EPS),
                      reads=[("SS", t)], writes=[("SS", 8 + t)])

            def stA_dve(oi, t):
                hn = HN[oi % 3]
                kb.op('dve', lambda e: e.reciprocal(out=SS[:, 16 + t:17 + t], in_=SS[:, 8 + t:9 + t]),
                      reads=[("SS", 8 + t)], writes=[("SS", 16 + t)])
                kb.op('dve', lambda e: e.tensor_scalar_mul(out=hn[:, :], in0=H[:, t, :],
                                                           scalar1=SS[:, 16 + t:17 + t]),
                      reads=[("H", t), ("SS", 16 + t)], writes=[("HN", oi % 3)])

            def stB(oi, t):
                hn = HN[oi % 3]
                hk = ("HN", oi % 3)
                for half in range(2):
                    b = bank()
                    pbb = PB[b][:, :].bitcast(BF16).rearrange("p (a b) -> p a b", a=8)
                    for i in range(8):
                        kc = half * 8 + i
                        kb.op('pe', lambda e: e.transpose(out=pbb[:, i, :], in_=hn[:, kc * 128:(kc + 1) * 128],
                                                          identity=IDB[:, :]),
                              reads=[hk, "IDB"], writes=[("PB", b)], signal=(i == 7))
                    for i in range(8):
                        kc = half * 8 + i
                        g = pcol(OFF_NG + gidx * 16 + kc)
                        dst = XT[:, kc, 2 + t * 128:2 + (t + 1) * 128]
                        if half == 0:
                            kb.op('act', lambda e: e.activation(out=dst, in_=pbb[:, i, :], func=AF.Copy, scale=g),
                                  reads=[("PB", b), "PC"], writes=[("XT", t)])
                        else:
                            kb.op('dve', lambda e: e.tensor_scalar_mul(out=dst, in0=pbb[:, i, :], scalar1=g),
                                  reads=[("PB", b), "PC"], writes=[("XT", t)])

            n = len(order)
            stA_act(0, order[0])
            stA_dve(0, order[0])
            for oi, t in enumerate(order):
                if oi + 1 < n:
                    stA_act(oi + 1, order[oi + 1])
                stB(oi, t)
                if oi + 1 < n:
                    stA_dve(oi + 1, order[oi + 1])
                if halo and t == NT - 1:
                    kb.op('act', lambda e: e.activation(out=XH[:, :].rearrange("p (a b) -> p a b", a=16),
                                                        in_=XT[:, :, T:T + 2], func=AF.Copy),
                          reads=[("XT", t)], writes=["XH"])
                    kb.dma('sp', [(xh_src[:, :], XH[:, :].bitcast(F32))], reads=["XH"], writes=["xh_src"])
                    kb.allgather(xh_src[:, :], xh_dst[:, :], reads=["xh_src"], writes=["xh_dst"])
                    kb.dma('sp', [(XHF[:, :], xh_dst[0:128, :])], reads=["xh_dst"], writes=["XHF"])
            if halo:
                kb.op('dve', lambda e: e.tensor_scalar_mul(
                    out=XT[:, :, 0:2], in0=XHF[:, :].bitcast(BF16).rearrange("p (a b) -> p a b", a=16),
                    scalar1=pcol(OFF_FLAG)), reads=["XHF", "PC"], writes=["XTH"])

        def out_proj(chunks, nk, key=None):
            slots = [load_w([(0, chunks[0])], nk, key=(key, 0) if key else None)]
            for cb in range(4):
                if cb + 1 < 4:
                    slots.append(load_w([(0, chunks[cb + 1])], nk, key=(key, cb + 1) if key else None))
                s = slots[cb]
                for t in range(NT):
                    b = bank()
                    for kc in range(nk):
                        kb.op('pe', lambda e: e.matmul(PB[b][:, :], lhsT=MT[:, kc, t * 128:(t + 1) * 128],
                                                       rhs=WB[s][:, kc, :], start=(kc == 0), stop=(kc == nk - 1)),
                              reads=[("MT", kc, t), ("WB", s)], writes=[("PB", b)], signal=(kc == nk - 1))
                    hs = H[:, t, cb * 512:(cb + 1) * 512]
                    kb.op('dve', lambda e: e.tensor_tensor(out=hs, in0=hs, in1=PB[b][:, :], op=ALU.add),
                          reads=[("PB", b), ("H", t)], writes=[("H", t)])

        def ffn(l):
            kb.barrier()
            ar.reset()
            AG = [ar.alloc([P, 514], F32), ar.alloc([P, 514], F32)]
            AV = [ar.alloc([P, 514], F32), ar.alloc([P, 514], F32)]
            CG = [ar.alloc([P, 512], F32), ar.alloc([P, 512], F32)]
            CV = [ar.alloc([P, 512], F32), ar.alloc([P, 512], F32)]
            SG = [ar.alloc([P, 512], F32), ar.alloc([P, 512], F32)]
            wup = w_up[l]
            wdn = w_down[l]
            groups = [(0, 12), (12, 12), (24, 10), (34, 10)]

            def up_pieces(j0):
                return [(0, wup[:, j0 * 128:(j0 + 2) * 128]), (256, wup[:, DFF + j0 * 128:DFF + (j0 + 2) * 128])]

            pair_list = []
            for (g0, gn) in groups:
                for j0 in range(g0, g0 + gn, 2):
                    pair_list.append(j0)
            nxt = None
            it = 0
            for (g0, gn) in groups:
                for j0 in range(g0, g0 + gn, 2):
                    s = nxt if nxt is not None else load_w(up_pieces(j0), key=("up", l, j0))
                    nxt = None
                    if j0 + 2 < g0 + gn:
                        nxt = load_w(up_pieces(j0 + 2))
                    for jj in range(2):
                        j = j0 + jj
                        jl = j - g0
                        wg = lambda kc: WB[s][:, kc, jj * 128:(jj + 1) * 128]
                        wv = lambda kc: WB[s][:, kc, 256 + jj * 128:256 + (jj + 1) * 128]
                        cw = lambda r: pcol(OFF_CW + l * 264 + r * 88 + j)
                        cbias = pcol(OFF_CB + l * 88 + j)
                        bh = bank()
                        for kc in range(KC):
                            kb.op('pe', lambda e: e.matmul(PB[bh][:, 0:2], lhsT=wg(kc), rhs=XT[:, kc, 0:2],
                                                           start=(kc == 0), stop=(kc == KC - 1)),
                                  reads=["XTH", ("WB", s)], writes=[("PB", bh)], signal=False)
                        for kc in range(KC):
                            kb.op('pe', lambda e: e.matmul(PB[bh][:, 2:4], lhsT=wv(kc), rhs=XT[:, kc, 0:2],
                                                           start=(kc == 0), stop=(kc == KC - 1)),
                                  reads=["XTH", ("WB", s)], writes=[("PB", bh)], signal=(kc == KC - 1))
                        for n in range(2):
                            q = it % 2
                            it += 1
                            bg = bank()
                            bv = bank()
                            xt_keys = [("XT", t) for t in range(n * 4, n * 4 + 4)]
                            for kc in range(KC):
                                kb.op('pe', lambda e: e.matmul(PB[bg][:, :], lhsT=wg(kc),
                                                               rhs=XT[:, kc, 2 + n * 512:2 + (n + 1) * 512],
                                                               start=(kc == 0), stop=(kc == KC - 1)),
                                      reads=xt_keys + [("WB", s)], writes=[("PB", bg)], signal=(kc == KC - 1))
                            for kc in range(KC):
                                kb.op('pe', lambda e: e.matmul(PB[bv][:, :], lhsT=wv(kc),
                                                               rhs=XT[:, kc, 2 + n * 512:2 + (n + 1) * 512],
                                                               start=(kc == 0), stop=(kc == KC - 1)),
                                      reads=xt_keys + [("WB", s)], writes=[("PB", bv)], signal=(kc == KC - 1))
                            ag, av, cg, cv, sg = AG[q], AV[q], CG[q], CV[q], SG[q]
                            kag, kav, kcg, kcv, ksg = ("AG", q), ("AV", q), ("CG", q), ("CV", q), ("SG", q)
                            kb.op('act', lambda e: e.activation(out=ag[:, 2:514], in_=PB[bg][:, :], func=AF.Copy),
                                  reads=[("PB", bg)], writes=[kag])
                            kb.op('act', lambda e: e.activation(out=av[:, 2:514], in_=PB[bv][:, :], func=AF.Copy),
                                  reads=[("PB", bv)], writes=[kav])
                            if n == 0:
                                kb.op('dve', lambda e: e.tensor_copy(out=ag[:, 0:2], in_=PB[bh][:, 0:2]),
                                      reads=[("PB", bh)], writes=[kag])
                                kb.op('dve', lambda e: e.tensor_copy(out=av[:, 0:2], in_=PB[bh][:, 2:4]),
                                      reads=[("PB", bh)], writes=[kav])
                            else:
                                pq = 1 - q
                                kb.op('dve', lambda e: e.tensor_copy(out=ag[:, 0:2], in_=AG[pq][:, 512:514]),
                                      reads=[("AG", pq)], writes=[kag])
                                kb.op('dve', lambda e: e.tensor_copy(out=av[:, 0:2], in_=AV[pq][:, 512:514]),
                                      reads=[("AV", pq)], writes=[kav])
                            kb.op('act', lambda e: e.activation(out=cg[:, :], in_=PB[bg][:, :], func=AF.Identity,
                                                                scale=cw(2), bias=cbias),
                                  reads=[("PB", bg), "PC"], writes=[kcg])
                            kb.op('act', lambda e: e.activation(out=cv[:, :], in_=PB[bv][:, :], func=AF.Identity,
                                                                scale=CWV(l, 2, j), bias=CBV(l, j)),
                                  reads=[("PB", bv), "PC"], writes=[kcv])
                            kb.op('dve', lambda e: e.scalar_tensor_tensor(out=cg[:, :], in0=ag[:, 1:513], scalar=cw(1),
                                                                          in1=cg[:, :], op0=ALU.mult, op1=ALU.add),
                                  reads=[kag, "PC", kcg], writes=[kcg])
                            kb.op('dve', lambda e: e.scalar_tensor_tensor(out=cg[:, :], in0=ag[:, 0:512], scalar=cw(0),
                                                                          in1=cg[:, :], op0=ALU.mult, op1=ALU.add),
                                  reads=[kag, "PC", kcg], writes=[kcg])
                            kb.op('dve', lambda e: e.scalar_tensor_tensor(out=cv[:, :], in0=av[:, 1:513],
                                                                          scalar=CWV(l, 1, j), in1=cv[:, :],
                                                                          op0=ALU.mult, op1=ALU.add),
                                  reads=[kav, "PC", kcv], writes=[kcv])
                            kb.op('dve', lambda e: e.scalar_tensor_tensor(out=cv[:, :], in0=av[:, 0:512],
                                                                          scalar=CWV(l, 0, j), in1=cv[:, :],
                                                                          op0=ALU.mult, op1=ALU.add),
                                  reads=[kav, "PC", kcv], writes=[kcv])
                            kb.op('act', lambda e: e.activation(out=sg[:, :], in_=cg[:, :], func=AF.Silu),
                                  reads=[kcg], writes=[ksg])
                            kb.op('dve', lambda e: e.tensor_tensor(out=MT[:, jl, n * 512:(n + 1) * 512], in0=sg[:, :],
                                                                   in1=cv[:, :], op=ALU.mult),
                                  reads=[ksg, kcv], writes=[("MT", jl, t) for t in range(n * 4, n * 4 + 4)])
                out_proj([wdn[g0 * 128:(g0 + gn) * 128, cb * 512:(cb + 1) * 512] for cb in range(4)], gn)
                if (g0, gn) != groups[-1]:
                    pass

        def CWV(l, r, j):
            return pcol(OFF_CW + l * 264 + r * 88 + NFF + j)

        def CBV(l, j):
            return pcol(OFF_CB + l * 88 + NFF + j)

        def even_mixer(e_idx):
            kb.barrier()
            ar.reset()
            win = w_in_even[e_idx]
            wout = w_out_even[e_idx]
            I0, I1, I2, I3 = 512, 512 + 768, 512 + 1536, 512 + 1536 + 1536
            ROT = ar.alloc([P, 2, 8, 64], F32)
            INTRA = ar.alloc([P, 6, 128], F32)
            U = ar.alloc([P, 4, 16 + T], BF16)
            PW = ar.alloc([P, 4, 128], BF16)
            UHS = ar.alloc([P, 4, 16], F32)
            UHR = ar.alloc([P, 4, 16], F32)
            mark = ar.off
            kb.dma('sp', [(ROT[:, :, :, :], c_rot[:, :, :, :])], writes=["ROT"])
            kb.dma('sp', [(INTRA[:, :, :], c_intra[:, :, :])], writes=["INTRA"])
            kb.dma('pool', [(PW[:, :, :], pool_w[e_idx].rearrange("g c d -> c g d"))], writes=["PW"])

            s = load_w([(0, win[:, 0:512])], key=("eu", e_idx))
            for g in range(4):
                for n in range(2):
                    b = bank()
                    for kc in range(KC):
                        kb.op('pe', lambda e: e.matmul(PB[b][:, :], lhsT=WB[s][:, kc, g * 128:(g + 1) * 128],
                                                       rhs=XT[:, kc, 2 + n * 512:2 + (n + 1) * 512],
                                                       start=(kc == 0), stop=(kc == KC - 1)),
                              reads=[("XT", t) for t in range(n * 4, n * 4 + 4)] + [("WB", s)],
                              writes=[("PB", b)], signal=(kc == KC - 1))
                    kb.op('act', lambda e: e.activation(out=U[:, g, 16 + n * 512:16 + (n + 1) * 512], in_=PB[b][:, :],
                                                        func=AF.Copy),
                          reads=[("PB", b)], writes=[("U", g, n)])
            kb.op('act', lambda e: e.activation(out=UHS[:, :, :], in_=U[:, :, T:T + 16], func=AF.Copy),
                  reads=[("U", g, 1) for g in range(4)], writes=["UHS"])
            kb.dma('sp', [(xu_src[:, :], UHS[:, :, :].rearrange("p a b -> p (a b)"))], reads=["UHS"], writes=["xu_src"])
            kb.allgather(xu_src[:, :], xu_dst[:, :], reads=["xu_src"], writes=["xu_dst"])
            kb.dma('sp', [(UHR[:, :, :].rearrange("p a b -> p (a b)"), xu_dst[0:128, :])], reads=["xu_dst"],
                   writes=["UHR"])

            if stop_after == "mix_a":
                return
            GS = ar.alloc([P, 8, 256], BF16)
            RSB = ar.alloc([P, 8, 256], BF16)
            QDT = ar.alloc([P, 8, 128], BF16)
            STATE = ar.alloc([P, 256], F32)
            STATEB = ar.alloc([P, 256], BF16)
            mark_r = ar.off
            NB3 = 4
            for h in range(6):
                dec = 1.0 - 2.0 ** (-5.0 - h)
                chunk_dec = dec ** 128
                kb.barrier()
                ar.reset(mark_r)
                QK = [ar.alloc([P, 2, 2, 64], BF16) for _ in range(NB3)]
                VB = [ar.alloc([P, 256], BF16) for _ in range(NB3)]
                QD = [ar.alloc([P, 128], BF16) for _ in range(NB3)]
                KD = [ar.alloc([P, 128], BF16) for _ in range(NB3)]
                QKT = [ar.alloc([P, 2, 128], BF16) for _ in range(2)]
                ST = [ar.alloc([P, 128], BF16) for _ in range(2)]
                R1 = ar.alloc([P, 2, 64], F32)
                R2 = ar.alloc([P, 2, 64], F32)
                R3 = ar.alloc([P, 2, 64], F32)
                R4 = ar.alloc([P, 2, 64], F32)
                def g_pieces(hx):
                    return [(0, win[:, I3 + hx * 256:I3 + (hx + 1) * 256])]

                def qkv_pieces(hx):
                    return [(0, win[:, I0 + hx * 128:I0 + (hx + 1) * 128]),
                            (128, win[:, I1 + hx * 128:I1 + (hx + 1) * 128]),
                            (256, win[:, I2 + hx * 256:I2 + (hx + 1) * 256])]

                sg_ = load_w(g_pieces(h), key=("eg", e_idx, h))
                sq_ = load_w(qkv_pieces(h), key=("eq", e_idx, h))
                kb.op('dve', lambda e: e.memset(STATE[:, :], 0.0), writes=["STATE"])
                kb.op('dve', lambda e: e.memset(STATEB[:, :], 0.0), writes=["STATEB"])

                def st_P(t):
                    b = t % 2
                    for kc in range(KC):
                        kb.op('pe', lambda e: e.matmul(PB[b][:, :], lhsT=XT[:, kc, 2 + t * 128:2 + (t + 1) * 128],
                                                       rhs=WB[sq_][:, kc, :], start=(kc == 0), stop=(kc == KC - 1)),
                              reads=[("XT", t), ("WB", sq_)], writes=[("PB", b)], signal=(kc == KC - 1))
                def st_ROT(t):
                    b = t % 2
                    q3 = t % NB3
                    qk, vb, qd, kd = QK[q3], VB[q3], QD[q3], KD[q3]
                    z4 = PB[b][:, 0:256].rearrange("p (a b c) -> p a b c", a=2, b=2)
                    x1 = z4[:, :, 0, :]
                    x2 = z4[:, :, 1, :]
                    cosb = ROT[:, 0, t, :].unsqueeze(1).to_broadcast([P, 2, 64])
                    sinb = ROT[:, 1, t, :].unsqueeze(1).to_broadcast([P, 2, 64])
                    kb.op('dve', lambda e: e.tensor_tensor(out=R1[:, :, :], in0=x1, in1=cosb, op=ALU.mult),
                          reads=[("PB", b), "ROT"], writes=["R1"])
                    kb.op('dve', lambda e: e.tensor_tensor(out=R2[:, :, :], in0=x2, in1=sinb, op=ALU.mult),
                          reads=[("PB", b), "ROT"], writes=["R2"])
                    kb.op('dve', lambda e: e.tensor_tensor(out=qk[:, :, 0, :], in0=R1[:, :, :], in1=R2[:, :, :],
                                                           op=ALU.subtract),
                          reads=["R1", "R2"], writes=[("QK0", q3)])
                    kb.op('dve', lambda e: e.tensor_tensor(out=R3[:, :, :], in0=x1, in1=sinb, op=ALU.mult),
                          reads=[("PB", b), "ROT"], writes=["R3"])
                    kb.op('dve', lambda e: e.tensor_tensor(out=R4[:, :, :], in0=x2, in1=cosb, op=ALU.mult),
                          reads=[("PB", b), "ROT"], writes=["R4"])
                    kb.op('dve', lambda e: e.tensor_tensor(out=qk[:, :, 1, :], in0=R3[:, :, :], in1=R4[:, :, :],
                                                           op=ALU.add),
                          reads=["R3", "R4"], writes=[("QK1", q3)])
                    kb.op('act', lambda e: e.activation(out=vb[:, :], in_=PB[b][:, 256:512], func=AF.Copy),
                          reads=[("PB", b)], writes=[("VB", q3)])
                    qf = qk[:, 0, :, :].rearrange("p a b -> p (a b)")
                    kf = qk[:, 1, :, :].rearrange("p a b -> p (a b)")
                    kb.op('act', lambda e: e.activation(out=qd[:, :], in_=qf, func=AF.Copy, scale=pcol(OFF_QDEC + h)),
                          reads=[("QK0", q3), ("QK1", q3), "PC"], writes=[("QD", q3)])
                    kb.op('act', lambda e: e.activation(out=kd[:, :], in_=kf, func=AF.Copy, scale=pcol(OFF_KDEC + h)),
                          reads=[("QK0", q3), ("QK1", q3), "PC"], writes=[("KD", q3)])

                def st_TR(t):
                    q3 = t % NB3
                    qk, qd = QK[q3], QD[q3]
                    qf = qk[:, 0, :, :].rearrange("p a b -> p (a b)")
                    kf = qk[:, 1, :, :].rearrange("p a b -> p (a b)")
                    bt = 2 + (t % 2)
                    pbt = PB[bt][:, :].bitcast(BF16).rearrange("p (a b) -> p a b", a=8)
                    kb.op('pe', lambda e: e.transpose(out=pbt[:, 0, :], in_=qf, identity=IDB[:, :]),
                          reads=[("QK0", q3), ("QK1", q3), "IDB"], writes=[("PB", bt)], signal=False)
                    kb.op('pe', lambda e: e.transpose(out=pbt[:, 1, :], in_=kf, identity=IDB[:, :]),
                          reads=[("QK0", q3), ("QK1", q3), "IDB"], writes=[("PB", bt)], signal=False)
                    kb.op('pe', lambda e: e.transpose(out=pbt[:, 2, :], in_=qd[:, :], identity=IDB[:, :]),
                          reads=[("QD", q3), "IDB"], writes=[("PB", bt)])
                    kb.op('dve', lambda e: e.tensor_copy(out=QKT[t % 2][:, :, :], in_=pbt[:, 0:2, :]),
                          reads=[("PB", bt)], writes=[("QKT", t % 2)])
                    kb.op('dve', lambda e: e.tensor_copy(out=QDT[:, t, :], in_=pbt[:, 2, :]),
                          reads=[("PB", bt)], writes=[("QDT", t)])

                def st_S(t):
                    bs = 4
                    qkt = QKT[t % 2]
                    kb.op('pe', lambda e: e.matmul(PB[bs][:, 0:128], lhsT=qkt[:, 1, :], rhs=qkt[:, 0, :],
                                                   start=True, stop=True),
                          reads=[("QKT", t % 2)], writes=[("PB", bs)])
                    kb.op('dve', lambda e: e.tensor_tensor(out=ST[t % 2][:, :], in0=PB[bs][:, 0:128], in1=INTRA[:, h, :],
                                                           op=ALU.mult),
                          reads=[("PB", bs), "INTRA"], writes=[("ST", t % 2)])

                def st_R(t):
                    q3 = t % NB3
                    vb, kd = VB[q3], KD[q3]
                    br, bk = 5, 6
                    kb.op('pe', lambda e: e.matmul(PB[br][:, 0:256], lhsT=ST[t % 2][:, :], rhs=vb[:, :], start=True, stop=False),
                          reads=[("ST", t % 2), ("VB", q3)], writes=[("PB", br)], signal=False)
                    kb.op('pe', lambda e: e.matmul(PB[br][:, 0:256], lhsT=QDT[:, t, :], rhs=STATEB[:, :],
                                                   start=False, stop=True),
                          reads=[("QDT", t), "STATEB"], writes=[("PB", br)])
                    kb.op('pe', lambda e: e.matmul(PB[bk][:, 0:256], lhsT=kd[:, :], rhs=vb[:, :], start=True, stop=True),
                          reads=[("KD", q3), ("VB", q3)], writes=[("PB", bk)])
                    kb.op('act', lambda e: e.activation(out=RSB[:, t, :], in_=PB[br][:, 0:256], func=AF.Copy),
                          reads=[("PB", br)], writes=[("RSB", t)])
                    kb.op('dve', lambda e: e.scalar_tensor_tensor(out=STATE[:, :], in0=STATE[:, :], scalar=chunk_dec,
                                                                  in1=PB[bk][:, 0:256], op0=ALU.mult, op1=ALU.add),
                          reads=["STATE", ("PB", bk)], writes=["STATE"])
                    kb.op('act', lambda e: e.activation(out=STATEB[:, :], in_=STATE[:, :], func=AF.Copy),
                          reads=["STATE"], writes=["STATEB"])

                for i in range(NT + 3):
                    if i < NT:
                        st_P(i)
                    if 0 <= i - 1 < NT:
                        st_TR(i - 1)
                    if 0 <= i - 2 < NT:
                        st_S(i - 2)
                    if 0 <= i - 3 < NT:
                        st_R(i - 3)
                    if i < NT:
                        st_ROT(i)
                if stop_after == "mix_b1":
                    return
                kb.dma('sp', [(st_src[:, :], STATE[:, :])], reads=["STATE"], writes=["st_src"])
                kb.allgather(st_src[:, :], st_dst[:, :], reads=["st_src"], writes=["st_dst"])
                nxt_q = None
                for t in range(NT):
                    b = t % 2
                    for kc in range(KC):
                        kb.op('pe', lambda e: e.matmul(PB[b][:, 0:256], lhsT=XT[:, kc, 2 + t * 128:2 + (t + 1) * 128],
                                                       rhs=WB[sg_][:, kc, 0:256], start=(kc == 0), stop=(kc == KC - 1)),
                              reads=[("XT", t), ("WB", sg_)], writes=[("PB", b)], signal=(kc == KC - 1))
                    kb.op('act', lambda e: e.activation(out=GS[:, t, :], in_=PB[b][:, 0:256], func=AF.Silu),
                          reads=[("PB", b)], writes=[("GS", t)])
                if h + 1 < 6:
                    prefetch(("eg", e_idx, h + 1), g_pieces(h + 1))
                    prefetch(("eq", e_idx, h + 1), qkv_pieces(h + 1))
                else:
                    prefetch((("wo", "e", e_idx), 0), [(0, wout[:, 0:512])])
                    prefetch((("wo", "e", e_idx), 1), [(0, wout[:, 512:1024])])
                kb.barrier()
                ar.reset(mark_r)
                SA = ar.alloc([P, 256], F32)
                SC = ar.alloc([P, 256], F32)
                SCB = [ar.alloc([P, 256], BF16) for _ in range(2)]
                RT = [ar.alloc([P, 256], F32) for _ in range(2)]
                YF = [ar.alloc([P, 256], F32) for _ in range(2)]
                YG = [ar.alloc([P, 256], BF16) for _ in range(2)]
                BNS = [ar.alloc([P, 8], F32) for _ in range(2)]
                MV = [ar.alloc([P, 4], F32) for _ in range(2)]
                kb.dma('sp', [(SA[:, :], st_dst[0:128, :])], reads=["st_dst"], writes=["SA"])
                kb.op('dve', lambda e: e.tensor_scalar_mul(out=SC[:, :], in0=SA[:, :], scalar1=pcol(OFF_FLAG)),
                      reads=["SA", "PC"], writes=["SC"])

                def p2_A(t):
                    q = t % 2
                    kb.op('act', lambda e: e.activation(out=SCB[q][:, :], in_=SC[:, :], func=AF.Copy),
                          reads=["SC"], writes=[("SCB", q)])
                    if t < NT - 1:
                        kb.op('dve', lambda e: e.tensor_scalar_mul(out=SC[:, :], in0=SC[:, :], scalar1=chunk_dec),
                              reads=["SC"], writes=["SC"])
                    bc = 4 + q
                    kb.op('pe', lambda e: e.matmul(PB[bc][:, 0:256], lhsT=QDT[:, t, :], rhs=SCB[q][:, :],
                                                   start=True, stop=True),
                          reads=[("QDT", t), ("SCB", q)], writes=[("PB", bc)])
                    kb.op('dve', lambda e: e.tensor_tensor(out=RT[q][:, :], in0=RSB[:, t, :], in1=PB[bc][:, 0:256],
                                                           op=ALU.add),
                          reads=[("RSB", t), ("PB", bc)], writes=[("RT", q)])
                    kb.op('dve', lambda e: e.bn_stats(out=BNS[q][:, 0:6], in_=RT[q][:, :]), reads=[("RT", q)], writes=[("BNS", q)])
                    kb.op('dve', lambda e: e.bn_aggr(out=MV[q][:, 0:2], in_=BNS[q][:, 0:6]), reads=[("BNS", q)], writes=[("MV", q)])
                    kb.op('act', lambda e: e.activation(out=MV[q][:, 2:3], in_=MV[q][:, 1:2], func=AF.Sqrt, bias=EPS),
                          reads=[("MV", q)], writes=[("MV2", q)])
                    kb.op('dve', lambda e: e.reciprocal(out=MV[q][:, 3:4], in_=MV[q][:, 2:3]), reads=[("MV2", q)], writes=[("MV3", q)])
                    kb.op('dve', lambda e: e.tensor_scalar(out=YF[q][:, :], in0=RT[q][:, :], scalar1=MV[q][:, 0:1],
                                                           scalar2=MV[q][:, 3:4], op0=ALU.subtract, op1=ALU.mult),
                          reads=[("RT", q), ("MV", q), ("MV3", q)], writes=[("YF", q)])
                    kb.op('dve', lambda e: e.tensor_tensor(out=YG[q][:, :], in0=YF[q][:, :], in1=GS[:, t, :], op=ALU.mult),
                          reads=[("YF", q), ("GS", t)], writes=[("YG", q)])

                def p2_B(t):
                    q = t % 2
                    by = 6 + q
                    pby = PB[by][:, :].bitcast(BF16).rearrange("p (a b) -> p a b", a=8)
                    kb.op('pe', lambda e: e.transpose(out=pby[:, 0, :], in_=YG[q][:, 0:128], identity=IDB[:, :]),
                          reads=[("YG", q), "IDB"], writes=[("PB", by)], signal=False)
                    kb.op('pe', lambda e: e.transpose(out=pby[:, 1, :], in_=YG[q][:, 128:256], identity=IDB[:, :]),
                          reads=[("YG", q), "IDB"], writes=[("PB", by)])
                    for j in range(2):
                        kc = 4 + 2 * h + j
                        gcol = pcol(OFF_GNG + e_idx * 12 + 2 * h + j)
                        dst = MT[:, kc, t * 128:(t + 1) * 128]
                        kb.op('act', lambda e: e.activation(out=dst, in_=pby[:, j, :], func=AF.Copy, scale=gcol),
                              reads=[("PB", by), "PC"], writes=[("MT", kc, t)])

                for i in range(NT + 1):
                    if i < NT:
                        p2_A(i)
                    if i - 1 >= 0:
                        p2_B(i - 1)

            if stop_after == "mix_b":
                return
            kb.barrier()
            ar.reset(mark)
            kb.op('dve', lambda e: e.tensor_scalar_mul(out=U[:, :, 0:16], in0=UHR[:, :, :], scalar1=pcol(OFF_FLAG)),
                  reads=["UHR", "PC"], writes=["UH"])
            S_A = ar.alloc([P, 528], F32)
            S_B = ar.alloc([P, 528], F32)
            YP = [ar.alloc([P, 512], BF16), ar.alloc([P, 512], BF16)]
            it = 0
            for g in range(4):
                w = 2 ** (g + 1)
                for n in range(2):
                    a0 = U[:, g, n * 512:n * 512 + 528]
                    rk = [("U", g, n), "UH"] + ([("U", g, 0)] if n == 1 else [])
                    kb.op('dve', lambda e: e.tensor_tensor(out=S_A[:, 1:528], in0=a0[:, 1:528], in1=a0[:, 0:527],
                                                           op=ALU.add), reads=rk, writes=["S_A"])
                    cur, curk, oth, othk = S_A, "S_A", S_B, "S_B"
                    sh = 1
                    for step in range(g):
                        sh2 = sh * 2
                        lo = 2 * sh2 - 1
                        kb.op('dve', lambda e: e.tensor_tensor(out=oth[:, lo:528], in0=cur[:, lo:528],
                                                               in1=cur[:, lo - sh2:528 - sh2], op=ALU.add),
                              reads=[curk], writes=[othk])
                        cur, curk, oth, othk = oth, othk, cur, curk
                        sh = sh2
                    if n == 0:
                        kb.op('dve', lambda e: e.tensor_tensor(out=cur[:, 16:32], in0=cur[:, 16:32],
                                                               in1=PC[:, OFF_PCORR + g * 16:OFF_PCORR + (g + 1) * 16],
                                                               op=ALU.mult),
                              reads=[curk, "PC"], writes=[curk])
                    yp = YP[it % 2]
                    ypk = ("YP", it % 2)
                    it += 1
                    kb.op('dve', lambda e: e.scalar_tensor_tensor(out=yp[:, :], in0=cur[:, 16:528], scalar=1.0 / w,
                                                                  in1=a0[:, 16:528], op0=ALU.mult, op1=ALU.subtract),
                          reads=[curk] + rk, writes=[ypk])
                    b = bank()
                    kb.op('pe', lambda e: e.matmul(PB[b][:, :], lhsT=PW[:, g, :], rhs=yp[:, :], start=True, stop=True),
                          reads=["PW", ypk], writes=[("PB", b)])
                    kb.op('act', lambda e: e.activation(out=MT[:, g, n * 512:(n + 1) * 512], in_=PB[b][:, :],
                                                        func=AF.Copy, scale=pcol(OFF_PSC + e_idx * 4 + g)),
                          reads=[("PB", b), "PC"], writes=[("MT", g, t) for t in range(n * 4, n * 4 + 4)])
            if stop_after == "mix_c":
                return
            out_proj([wout[:, cb * 512:(cb + 1) * 512] for cb in range(4)], KC, key=("wo", "e", e_idx))

        def odd_mixer(o_idx, layer_idx):
            kb.barrier()
            ar.reset()
            win = w_in_odd[o_idx]
            wout = w_out_odd[o_idx]
            J0, J1, J2 = 2048, 3072, 4096
            Hb = H[:, :, :].rearrange("p a b -> p (a b)").bitcast(BF16)
            KT = Hb[:, 0:8192].rearrange("p (a b) -> p a b", a=8)
            V1 = Hb[:, 8192:8192 + 8224].rearrange("p (a b c) -> p a b c", a=8, b=4)
            QT = Hb[:, 16416:16416 + 8192].rearrange("p (a b) -> p a b", a=8)
            Hf = H[:, :, :].rearrange("p a b -> p (a b)")
            XTb = XT[:, :, :].rearrange("p a b -> p (a b)")
            KTP = XTb[:, 0:8192].rearrange("p (a b) -> p a b", a=8)
            V1P = XTb[:, 8192:8192 + 8224].rearrange("p (a b c) -> p a b c", a=8, b=4)
            lam_init = 0.8 - 0.6 * math.exp(-0.3 * layer_idx)

            BIAS = ar.alloc([P, 4, 2, 128], F32)
            ED = ar.alloc([P, 4, 2, 128], BF16)
            LAMV = ar.alloc([P, 4, 128], F32)
            LT = ar.alloc([P, 2, 128], F32)
            LS = ar.alloc([P, 8], F32)
            CB = ar.alloc([P, 8], F32)
            mark0 = ar.off
            kb.dma('sp', [(BIAS[:, :, :, :], p_biasT[:, :, :, :])], writes=["BIAS"])
            kb.dma('sp', [(LAMV[:, :, :], p_lam[:, o_idx, :, :])], writes=["LAMV"])
            kb.op('act', lambda e: e.activation(out=ED[:, :, :, :], in_=BIAS[:, :, :, :], func=AF.Exp),
                  reads=["BIAS"], writes=["ED"])
            for hh in range(4):
                kb.op('dve', lambda e: e.tensor_tensor(out=ED[:, hh, 0, :], in0=ED[:, hh, 0, :], in1=MASK[:, :],
                                                       op=ALU.mult), reads=["ED", "MASK"], writes=["ED"])
            kb.op('dve', lambda e: e.tensor_tensor(out=LT[:, 0, :], in0=LAMV[:, 0, :], in1=LAMV[:, 1, :], op=ALU.mult),
                  reads=["LAMV"], writes=["LT0"])
            kb.op('dve', lambda e: e.tensor_tensor(out=LT[:, 1, :], in0=LAMV[:, 2, :], in1=LAMV[:, 3, :], op=ALU.mult),
                  reads=["LAMV"], writes=["LT1"])
            kb.op('dve', lambda e: e.reduce_sum(out=LS[:, 0:2], in_=LT[:, :, :], axis=mybir.AxisListType.X),
                  reads=["LT0", "LT1"], writes=["LS"])
            kb.op('act', lambda e: e.activation(out=LS[:, 2:4], in_=LS[:, 0:2], func=AF.Exp), reads=["LS"], writes=["LS2"])
            kb.op('dve', lambda e: e.tensor_tensor(out=LS[:, 4:5], in0=LS[:, 2:3], in1=LS[:, 3:4], op=ALU.subtract),
                  reads=["LS2"], writes=["LS4"])
            kb.op('dve', lambda e: e.tensor_scalar(out=LS[:, 5:6], in0=LS[:, 4:5], scalar1=lam_init, scalar2=-1.0,
                                                   op0=ALU.add, op1=ALU.mult), reads=["LS4"], writes=["LAM"])
            kb.op('dve', lambda e: e.tensor_copy(out=CB[:, 0:4], in_=PC[:, OFF_C31:OFF_C31 + 4]), reads=["PC"], writes=["CB0"])
            kb.op('dve', lambda e: e.tensor_scalar_add(out=CB[:, 4:8], in0=PC[:, OFF_C31:OFF_C31 + 4], scalar1=pcol(OFF_NEGB)),
                  reads=["PC"], writes=["CB1"])

            def proj_fm(col0, dst, nm):
                for c in range(2):
                    s = load_w([(0, win[:, col0 + c * 512:col0 + (c + 1) * 512])], key=("ofm", o_idx, nm, c))
                    for f in range(4):
                        for n in range(2):
                            b = bank()
                            for kc in range(KC):
                                kb.op('pe', lambda e: e.matmul(PB[b][:, :], lhsT=WB[s][:, kc, f * 128:(f + 1) * 128],
                                                               rhs=XT[:, kc, 2 + n * 512:2 + (n + 1) * 512],
                                                               start=(kc == 0), stop=(kc == KC - 1)),
                                      reads=[("XT", t) for t in range(n * 4, n * 4 + 4)] + [("WB", s)],
                                      writes=[("PB", b)], signal=(kc == KC - 1))
                            kb.op('act', lambda e: e.activation(out=dst[:, c * 4 + f, n * 512:(n + 1) * 512],
                                                                in_=PB[b][:, :], func=AF.Copy),
                                  reads=[("PB", b)], writes=[(nm, c * 4 + f, n)])

            proj_fm(J1, KT, "KT")
            kb.op('dve', lambda e: e.memset(V1[:, :, :, 256:257], 1.0), writes=["V1one"])
            for c in range(2):
                s = load_w([(0, win[:, J2 + c * 512:J2 + (c + 1) * 512])])
                for t in range(NT):
                    b = bank()
                    for kc in range(KC):
                        kb.op('pe', lambda e: e.matmul(PB[b][:, :], lhsT=XT[:, kc, 2 + t * 128:2 + (t + 1) * 128],
                                                       rhs=WB[s][:, kc, :], start=(kc == 0), stop=(kc == KC - 1)),
                              reads=[("XT", t), ("WB", s)], writes=[("PB", b)], signal=(kc == KC - 1))
                    kb.op('act', lambda e: e.activation(out=V1[:, t, 2 * c:2 * c + 2, 0:256],
                                                        in_=PB[b][:, :].rearrange("p (a b) -> p a b", a=2), func=AF.Copy),
                          reads=[("PB", b)], writes=[("V1", t, c)])
            chk("o_a")
            kvkeys = [("KT", i, n) for i in range(8) for n in range(2)] + [("V1", t, c) for t in range(8) for c in range(2)] + ["V1one"]
            for c in range(2):
                prefetch(("ofm", o_idx, "QT", c), [(0, win[:, J0 + c * 512:J0 + (c + 1) * 512])])
            for i in range(4):
                kb.dma('sp', [(kv_src[i][:, :], Hf[:, i * KVQ:(i + 1) * KVQ])], reads=kvkeys, writes=[("kv_src", i)])
            for i in range(2):
                kb.allgather(kv_src[i][:, :], kv_dst[i][:, :], reads=[("kv_src", i)], writes=[("kv_dst", i)])
            chk("o_b")
            proj_fm(J0, QT, "QT")
            chk("o_c")

            ar2 = Arena(Hf[:, 12304:16384], 4080 * 4)
            SLN = ar2.alloc([P, 1024], F32)
            WSF = ar2.alloc([P, 8, 128], F32)
            ZV = ar2.alloc([P, 1024], F32)
            ZU = ar2.alloc([P, 512], F32)
            WST = ar.alloc([P, 8, 128], BF16)
            SV = ar.alloc([P, 8, 1024], BF16)
            VN = ar.alloc([P, 1024], BF16)
            CO = ar.alloc([P, 512], BF16)
            kb.dma('sp', [(SLN[:, :], p_sln[:, o_idx, :])], writes=["SLN"])
            kb.dma('sp', [(WSF[:, :, :], p_sguT[o_idx])], writes=["WSF"])
            kb.op('dve', lambda e: e.tensor_tensor(out=WST[:, :, :], in0=WSF[:, :, :],
                                                   in1=MASK[:, :].unsqueeze(1).to_broadcast([P, 8, 128]), op=ALU.mult),
                  reads=["WSF", "MASK"], writes=["WST"])
            s0 = load_w([(0, win[:, 1024:1536])])
            s1 = load_w([(0, win[:, 1536:2048])])
            for i in range(2, 4):
                kb.allgather(kv_src[i][:, :], kv_dst[i][:, :], reads=[("kv_src", i)], writes=[("kv_dst", i)])
            VNs = [VN, ar.alloc([P, 1024], BF16)]
            COs = [CO, ar.alloc([P, 512], BF16)]

            def sgu_A(t):
                bb = [bank(), bank()]
                for c, s_ in enumerate((s0, s1)):
                    for kc in range(KC):
                        kb.op('pe', lambda e: e.matmul(PB[bb[c]][:, :], lhsT=XT[:, kc, 2 + t * 128:2 + (t + 1) * 128],
                                                       rhs=WB[s_][:, kc, :], start=(kc == 0), stop=(kc == KC - 1)),
                              reads=[("XT", t), ("WB", s_)], writes=[("PB", bb[c])], signal=(kc == KC - 1))
                    kb.op('act', lambda e: e.activation(out=ZV[:, c * 512:(c + 1) * 512], in_=PB[bb[c]][:, :], func=GELU),
                          reads=[("PB", bb[c])], writes=[("ZV", c)])
                vn = VNs[t % 2]
                kb.op('dve', lambda e: e.bn_stats(out=SS[:, 24:30], in_=ZV[:, 0:512]), reads=[("ZV", 0)], writes=["BN0"])
                kb.op('dve', lambda e: e.bn_stats(out=SS[:, 30:36], in_=ZV[:, 512:1024]), reads=[("ZV", 1)], writes=["BN1"])
                kb.op('dve', lambda e: e.bn_aggr(out=LS[:, 6:8], in_=SS[:, 24:36].rearrange("p (a b) -> p a b", a=2)),
                      reads=["BN0", "BN1"], writes=["MVZ"])
                kb.op('act', lambda e: e.activation(out=SS[:, 36:37], in_=LS[:, 7:8],
                                                    func=AF.Sqrt, bias=EPS), reads=["MVZ"], writes=["SDZ"])
                kb.op('dve', lambda e: e.reciprocal(out=SS[:, 37:38], in_=SS[:, 36:37]), reads=["SDZ"], writes=["RSZ"])
                kb.op('dve', lambda e: e.tensor_scalar(out=ZV[:, :], in0=ZV[:, :], scalar1=LS[:, 6:7], scalar2=SS[:, 37:38],
                                                       op0=ALU.subtract, op1=ALU.mult),
                      reads=[("ZV", 0), ("ZV", 1), "MVZ", "RSZ"], writes=[("ZV", 0), ("ZV", 1)])
                kb.op('dve', lambda e: e.tensor_tensor(out=vn[:, :], in0=ZV[:, :], in1=SLN[:, :], op=ALU.mult),
                      reads=[("ZV", 0), ("ZV", 1), "SLN"], writes=[("VN", t % 2)])

            def sgu_B(t):
                vn = VNs[t % 2]
                bs2 = [bank(), bank()]
                for g in range(8):
                    b = bs2[g // 4]
                    kb.op('pe', lambda e: e.matmul(PB[b][:, (g % 4) * 128:(g % 4 + 1) * 128], lhsT=WST[:, g, :],
                                                   rhs=vn[:, g * 128:(g + 1) * 128], start=True, stop=True),
                          reads=["WST", ("VN", t % 2)], writes=[("PB", b)], signal=(g % 4 == 3))
                for g in range(8):
                    b = bs2[g // 4]
                    bcol = pcol(OFF_SB + o_idx * 8 + g)
                    kb.op('act', lambda e: e.activation(out=SV[:, t, g * 128:(g + 1) * 128],
                                                        in_=PB[b][:, (g % 4) * 128:(g % 4 + 1) * 128],
                                                        func=AF.Identity, bias=bcol),
                          reads=[("PB", b), "PC"], writes=[("SV", t)])

            for i in range(NT + 1):
                if i < NT:
                    sgu_A(i)
                if i - 1 >= 0:
                    sgu_B(i - 1)

            zitems = [(c, t) for c in range(2) for t in range(NT)]
            zslots = {}

            def zu_A(k):
                c, t = zitems[k]
                if t == 0:
                    zslots[c] = load_w([(0, win[:, c * 512:(c + 1) * 512])])
                s_ = zslots[c]
                b = bank()
                for kc in range(KC):
                    kb.op('pe', lambda e: e.matmul(PB[b][:, :], lhsT=XT[:, kc, 2 + t * 128:2 + (t + 1) * 128],
                                                   rhs=WB[s_][:, kc, :], start=(kc == 0), stop=(kc == KC - 1)),
                          reads=[("XT", t), ("WB", s_)], writes=[("PB", b)], signal=(kc == KC - 1))
                co = COs[k % 2]
                kb.op('act', lambda e: e.activation(out=ZU[:, :], in_=PB[b][:, :], func=GELU),
                      reads=[("PB", b)], writes=["ZU"])
                kb.op('dve', lambda e: e.tensor_tensor(out=co[:, :], in0=ZU[:, :], in1=SV[:, t, c * 512:(c + 1) * 512],
                                                       op=ALU.mult), reads=["ZU", ("SV", t)], writes=[("CO", k % 2)])

            def zu_B(k):
                c, t = zitems[k]
                co = COs[k % 2]
                bt = bank()
                pbt = PB[bt][:, :].bitcast(BF16).rearrange("p (a b) -> p a b", a=8)
                for i in range(4):
                    kb.op('pe', lambda e: e.transpose(out=pbt[:, i, :], in_=co[:, i * 128:(i + 1) * 128], identity=IDB[:, :]),
                          reads=[("CO", k % 2), "IDB"], writes=[("PB", bt)], signal=(i == 3))
                for i in range(4):
                    kc = c * 4 + i
                    dst = MT[:, kc, t * 128:(t + 1) * 128]
                    kb.op('dve', lambda e: e.tensor_copy(out=dst, in_=pbt[:, i, :]),
                          reads=[("PB", bt)], writes=[("MT", kc, t)])

            for k in range(len(zitems) + 1):
                if k < len(zitems):
                    zu_A(k)
                if k - 1 >= 0:
                    zu_B(k - 1)
            chk("o_d")
            kb.barrier()
            XTf = XTb.bitcast(F32)
            kb.dma('sp', [(XTf[:, i * KVQ:(i + 1) * KVQ], kv_dst[i][0:128, :]) for i in range(4)],
                   reads=[("kv_dst", i) for i in range(4)], writes=["KVP"])
            prefetch((("wo", "o", o_idx), 0), [(0, wout[:, 0:512])])
            prefetch((("wo", "o", o_idx), 1), [(0, wout[:, 512:1024])])
            ar.reset(mark0)
            PT = [ar.alloc([P, 512], BF16) for _ in range(6)]
            OA = ar.alloc([P, 2, 260], F32)
            RD = ar.alloc([P, 4], F32)
            DO = ar.alloc([P, 256], F32)
            DJ = ar.alloc([P, 256], F32)
            DBs = [ar.alloc([P, 256], BF16), ar.alloc([P, 256], BF16)]
            scale = 128.0 ** -0.5
            LOOK = 2
            steps = []
            gid = 0
            for hh in range(4):
                for qp in range(4):
                    qt2 = [qp * 2, qp * 2 + 1]
                    klist = [("p", kt) for kt in range(8)] + [("o", kt) for kt in range(qt2[1] + 1)]
                    gsteps = []
                    for (kind, kt) in klist:
                        vis = qt2 if kind == "p" else [qt for qt in qt2 if qt >= kt]
                        gsteps.append(dict(g=gid, hh=hh, qt2=qt2, kind=kind, kt=kt, vis=vis))
                    seen = set()
                    for st in gsteps:
                        st["first"] = {}
                        for qt in st["vis"]:
                            st["first"][qt] = qt not in seen
                            seen.add(qt)
                    seen = set()
                    for st in reversed(gsteps):
                        st["last"] = {}
                        for qt in st["vis"]:
                            st["last"][qt] = qt not in seen
                            seen.add(qt)
                    gsteps[-1]["gend"] = True
                    steps.extend(gsteps)
                    gid += 1

            def emit_S(i, st):
                hh, kind, kt, vis = st["hh"], st["kind"], st["kt"], st["vis"]
                nq = len(vis) * 128
                ksrc = KTP if kind == "p" else KT
                bsx = 4 + (i % 3)
                for j in range(2):
                    kkey = ["KVP"] if kind == "p" else [("KT", hh * 2 + j, kt // 4)]
                    kb.op('pe', lambda e: e.matmul(PB[bsx][:, j * 256:j * 256 + nq],
                                                   lhsT=ksrc[:, hh * 2 + j, kt * 128:(kt + 1) * 128],
                                                   rhs=QT[:, hh * 2 + j, vis[0] * 128:(vis[-1] + 1) * 128],
                                                   start=True, stop=True),
                          reads=kkey + [("QT", hh * 2 + j, vis[0] // 4)], writes=[("PB", bsx)], signal=(j == 1))
                pt = PT[i % 6]
                ptk = ("PT", i % 6)
                pt3 = pt[:, :].rearrange("p (j c) -> p j c", j=2)
                ps3 = PB[bsx][:, :].rearrange("p (j c) -> p j c", j=2)
                bcol = CB[:, 4 + hh:5 + hh] if kind == "p" else CB[:, hh:hh + 1]
                bkey = "CB1" if kind == "p" else "CB0"
                near = {}
                for qt in vis:
                    if kind == "o" and kt == qt:
                        near[qt] = 0
                    elif kind == "o" and kt == qt - 1:
                        near[qt] = 1
                    elif kind == "p" and kt == 7 and qt == 0:
                        near[qt] = 1
                far = [qt for qt in vis if qt not in near]
                if far:
                    o0 = (far[0] - vis[0]) * 128
                    o1 = (far[-1] - vis[0] + 1) * 128
                    kb.op('act', lambda e: e.activation(out=pt3[:, :, o0:o1], in_=ps3[:, :, o0:o1], func=AF.Exp,
                                                        scale=scale, bias=bcol),
                          reads=[("PB", bsx), bkey], writes=[ptk])
                for qt, which in near.items():
                    o0 = (qt - vis[0]) * 128
                    if kind == "p":
                        kb.op('act', lambda e: e.activation(out=pt3[:, :, o0:o0 + 128], in_=ps3[:, :, o0:o0 + 128],
                                                            func=AF.Exp, scale=scale, bias=pcol(OFF_NEGB)),
                              reads=[("PB", bsx), "PC"], writes=[ptk])
                    else:
                        kb.op('act', lambda e: e.activation(out=pt3[:, :, o0:o0 + 128], in_=ps3[:, :, o0:o0 + 128],
                                                            func=AF.Exp, scale=scale),
                              reads=[("PB", bsx)], writes=[ptk])
                    kb.op('dve', lambda e: e.tensor_tensor(out=pt3[:, :, o0:o0 + 128], in0=pt3[:, :, o0:o0 + 128],
                                                           in1=ED[:, hh, which, :].unsqueeze(1).to_broadcast([P, 2, 128]),
                                                           op=ALU.mult),
                          reads=[ptk, "ED"], writes=[ptk])

            def emit_PV(i, st):
                hh, kind, kt, vis, qt2 = st["hh"], st["kind"], st["kt"], st["vis"], st["qt2"]
                pt = PT[i % 6]
                ptk = ("PT", i % 6)
                vsrc = V1P if kind == "p" else V1
                vkey = ["KVP"] if kind == "p" else [("V1", kt, hh // 2), "V1one"]
                for j in range(2):
                    for qt in vis:
                        o0 = j * 256 + (qt - vis[0]) * 128
                        ab = (qt - qt2[0]) * 2 + j
                        kb.op('pe', lambda e: e.matmul(PB[ab][:, 0:257], lhsT=pt[:, o0:o0 + 128], rhs=vsrc[:, kt, hh, :],
                                                       start=st["first"][qt], stop=st["last"][qt]),
                              reads=[ptk] + vkey, writes=[("PB", ab)], signal=True)

            fin_cnt = [0]

            def emit_fin_a(st):
                hh, qt2 = st["hh"], st["qt2"]
                deferred = []
                for qt in qt2:
                    for j in range(2):
                        ab = (qt - qt2[0]) * 2 + j
                        kb.op('dve', lambda e: e.tensor_copy(out=OA[:, j, 0:257], in_=PB[ab][:, 0:257]),
                              reads=[("PB", ab)], writes=[("OA", j)])
                    db = DBs[fin_cnt[0] % 2]
                    dbk = ("DB", fin_cnt[0] % 2)
                    fin_cnt[0] += 1
                    kb.op('dve', lambda e: e.reciprocal(out=RD[:, 0:1], in_=OA[:, 0, 256:257]), reads=[("OA", 0)], writes=["RD0"])
                    kb.op('dve', lambda e: e.reciprocal(out=RD[:, 1:2], in_=OA[:, 1, 256:257]), reads=[("OA", 1)], writes=["RD1"])
                    kb.op('dve', lambda e: e.tensor_tensor(out=RD[:, 2:3], in0=RD[:, 1:2], in1=LS[:, 5:6], op=ALU.mult),
                          reads=["RD1", "LAM"], writes=["RD2"])
                    kb.op('dve', lambda e: e.tensor_scalar_mul(out=DO[:, :], in0=OA[:, 0, 0:256], scalar1=RD[:, 0:1]),
                          reads=[("OA", 0), "RD0"], writes=["DO"])
                    kb.op('dve', lambda e: e.scalar_tensor_tensor(out=DO[:, :], in0=OA[:, 1, 0:256], scalar=RD[:, 2:3],
                                                                  in1=DO[:, :], op0=ALU.mult, op1=ALU.add),
                          reads=[("OA", 1), "RD2", "DO"], writes=["DO"])
                    kb.op('act', lambda e: e.activation(out=DJ[:, :], in_=DO[:, :], func=AF.Square, accum_out=RD[:, 3:4]),
                          reads=["DO"], writes=["DJ", "RD3"])
                    kb.op('act', lambda e: e.activation(out=SS[:, 38:39], in_=RD[:, 3:4], func=AF.Sqrt, scale=1.0 / 256, bias=EPS),
                          reads=["RD3"], writes=["SD1"])
                    kb.op('dve', lambda e: e.reciprocal(out=SS[:, 39:40], in_=SS[:, 38:39]), reads=["SD1"], writes=["SD2"])
                    kb.op('dve', lambda e: e.tensor_scalar(out=db[:, :], in0=DO[:, :], scalar1=SS[:, 39:40],
                                                           scalar2=(1.0 - lam_init), op0=ALU.mult, op1=ALU.mult),
                          reads=["DO", "SD2"], writes=[dbk])
                    deferred.append((hh, qt, db, dbk))
                return deferred

            def emit_fin_b(hh, qt, db, dbk):
                bt = 7
                pbt = PB[bt][:, :].bitcast(BF16).rearrange("p (a b) -> p a b", a=8)
                kb.op('pe', lambda e: e.transpose(out=pbt[:, 0, :], in_=db[:, 0:128], identity=IDB[:, :]),
                      reads=[dbk, "IDB"], writes=[("PB", bt)], signal=False)
                kb.op('pe', lambda e: e.transpose(out=pbt[:, 1, :], in_=db[:, 128:256], identity=IDB[:, :]),
                      reads=[dbk, "IDB"], writes=[("PB", bt)])
                for i2 in range(2):
                    kc = 8 + hh * 2 + i2
                    gcol = pcol(OFF_SUB + o_idx * 2 + i2)
                    dst = MT[:, kc, qt * 128:(qt + 1) * 128]
                    kb.op('dve', lambda e: e.tensor_scalar_mul(out=dst, in0=pbt[:, i2, :], scalar1=gcol),
                          reads=[("PB", bt), "PC"], writes=[("MT", kc, qt)])

            pending_fin = []
            nsteps = len(steps)
            for idx in range(nsteps + LOOK):
                if idx < nsteps:
                    emit_S(idx, steps[idx])
                while pending_fin and pending_fin[0][0] <= idx:
                    emit_fin_b(*pending_fin.pop(0)[1])
                pi_ = idx - LOOK
                if pi_ >= 0:
                    emit_PV(pi_, steps[pi_])
                    if steps[pi_].get("gend"):
                        for k2, args in enumerate(emit_fin_a(steps[pi_])):
                            pending_fin.append((idx + 4 + 2 * k2, args))
            while pending_fin:
                emit_fin_b(*pending_fin.pop(0)[1])
            chk("o_e")
            kb.barrier()
            for t in range(NT):
                kb.dma('sp', [(H[:, t, :], hspill[t * 128:(t + 1) * 128, :])], reads=[("HSP", t)], writes=[("H", t)])
            out_proj([wout[:, cb * 512:(cb + 1) * 512] for cb in range(4)], KC, key=("wo", "o", o_idx))

        order_last_first = [NT - 1] + list(range(NT - 1))
        for l in range(n_layers):
          try:
            last = (l == n_layers - 1)
            if l % 2 == 0:
                prefetch(("eu", l // 2), [(0, w_in_even[l // 2][:, 0:512])])
            else:
                for t in range(NT):
                    kb.dma('sp', [(hspill[t * 128:(t + 1) * 128, :], H[:, t, :])], reads=[("H", t)], writes=[("HSP", t)])
                for c in range(2):
                    prefetch(("ofm", l // 2, "KT", c), [(0, w_in_odd[l // 2][:, 3072 + c * 512:3072 + (c + 1) * 512])])
            norm_phase(l * 2, list(range(NT)), halo=False)
            if last and stop_after == "norm0":
                break
            if l % 2 == 0:
                even_mixer(l // 2)
            else:
                odd_mixer(l // 2, l)
            if last and stop_after and stop_after.startswith("mix"):
                break
            prefetch(("up", l, 0), [(0, w_up[l][:, 0:256]), (256, w_up[l][:, DFF:DFF + 256])])
            norm_phase(l * 2 + 1, order_last_first, halo=True)
            if last and stop_after == "norm1":
                break
            ffn(l)
          except StopBuild:
            break

        kb.barrier()
        ar.reset()
        FG = ar.alloc([P, D], F32)
        JK = ar.alloc([P, D], BF16)
        YO = [ar.alloc([P, D], F32), ar.alloc([P, D], F32)]
        kb.dma('sp', [(FG[:, :], p_fng[:, :])], writes=["FG"])
        for t in range(NT):
            yo = YO[t % 2]
            yk = ("YO", t % 2)
            if dbg_h:
                kb.dma('sp', [(out_d[t * 128:(t + 1) * 128, :], H[:, t, :])], reads=[("H", t)], writes=[("OUT", t)])
                continue
            kb.op('act', lambda e: e.activation(out=JK[:, :], in_=H[:, t, :], func=AF.Square, accum_out=SS[:, t:t + 1]),
                  reads=[("H", t)], writes=["JK", ("SS", t)])
            kb.op('act', lambda e: e.activation(out=SS[:, 8 + t:9 + t], in_=SS[:, t:t + 1], func=AF.Sqrt, scale=1.0 / D, bias=EPS),
                  reads=[("SS", t)], writes=[("SS", 8 + t)])
            kb.op('dve', lambda e: e.reciprocal(out=SS[:, 16 + t:17 + t], in_=SS[:, 8 + t:9 + t]),
                  reads=[("SS", 8 + t)], writes=[("SS", 16 + t)])
            kb.op('dve', lambda e: e.scalar_tensor_tensor(out=yo[:, :], in0=H[:, t, :], scalar=SS[:, 16 + t:17 + t],
                                                          in1=FG[:, :], op0=ALU.mult, op1=ALU.mult),
                  reads=[("H", t), ("SS", 16 + t), "FG"], writes=[yk])
            kb.dma('sp', [(out_d[t * 128:(t + 1) * 128, :], yo[:, :])], reads=[yk], writes=[("OUT", t)])
        kb.barrier()
        print("ops emitted:", kb.nops, "sems:", len(kb.sems), flush=True)
    return nc


def _t5_bucket(n):
    n = np.maximum(n, 0)
    max_exact = 16
    nn = np.maximum(n, 1).astype(np.float32)
    large = max_exact + (np.log(nn / max_exact) / math.log(128 / max_exact) * (32 - max_exact)).astype(np.int32)
    large = np.minimum(large, 31)
    return np.where(n < max_exact, n, large)


def _consts(half):
    c = {}
    c["c_ident"] = np.eye(128, dtype=np.float32)
    half_d = 64
    inv = 1.0 / (10000.0 ** (np.arange(half_d, dtype=np.float32) / half_d))
    pos = (half * T + np.arange(T, dtype=np.float32))
    ang = pos[:, None] * inv[None, :]
    cos = np.cos(ang).astype(np.float32).reshape(NT, 128, 64).transpose(1, 0, 2)
    sin = np.sin(ang).astype(np.float32).reshape(NT, 128, 64).transpose(1, 0, 2)
    c["c_rot"] = np.ascontiguousarray(np.stack([cos, sin], axis=1))
    log_g = np.log(1.0 - 2.0 ** (-5.0 - np.arange(6, dtype=np.float32))).astype(np.float32)
    idx = np.arange(128, dtype=np.float32)
    diff = idx[:, None] - idx[None, :]
    intra = np.where(diff >= 0, np.exp(log_g[:, None, None] * np.maximum(diff, 0.0)), 0.0)
    intraT = intra.transpose(2, 0, 1) * (128.0 ** -0.5)
    c["c_intra"] = np.ascontiguousarray(intraT.astype(np.float32))
    c["qdec"] = np.exp(log_g[None, :] * (idx[:, None] + 1.0)).astype(np.float32)
    c["kdec"] = (np.exp(log_g[None, :] * (127.0 - idx[:, None])) * (128.0 ** -0.5)).astype(np.float32)
    p = np.arange(128)
    c["c_mask"] = (p[None, :] >= p[:, None]).astype(np.float32)
    pcorr = np.ones((4, 16), np.float32)
    if half == 0:
        for g in range(4):
            w = 2 ** (g + 1)
            tt = np.arange(16) + 1
            pcorr[g] = w / np.minimum(tt, w)
    c["pcorr"] = pcorr
    return c


def _prep(inputs):
    f = lambda a: np.ascontiguousarray(np.asarray(a, dtype=np.float32))
    ins = {k: f(v) for k, v in inputs.items()}

    def cols(v):
        return v.reshape(-1, 128).T

    shared = {}
    for k in ("w_in_even", "w_out_even", "pool_w", "w_in_odd", "w_out_odd", "w_up", "w_down"):
        shared[k] = ins[k]
    pc = np.zeros((128, NCOLS), np.float32)
    for l in range(4):
        pc[:, OFF_NG + (2 * l) * 16:OFF_NG + (2 * l + 1) * 16] = cols(ins["mix_norm_g"][l])
        pc[:, OFF_NG + (2 * l + 1) * 16:OFF_NG + (2 * l + 2) * 16] = cols(ins["ffn_norm_g"][l])
        for r in range(3):
            pc[:, OFF_CW + l * 264 + r * 88:OFF_CW + l * 264 + (r + 1) * 88] = cols(ins["conv_w"][l, r])
        pc[:, OFF_CB + l * 88:OFF_CB + (l + 1) * 88] = cols(ins["conv_b"][l])
    for e in range(2):
        pc[:, OFF_PSC + e * 4:OFF_PSC + (e + 1) * 4] = cols(ins["pool_scale"][e])
        pc[:, OFF_GNG + e * 12:OFF_GNG + (e + 1) * 12] = cols(ins["ret_gn_g"][e])
        pc[:, OFF_SUB + e * 2:OFF_SUB + (e + 1) * 2] = cols(ins["diff_subln_g"][e])
        pc[:, OFF_SB + e * 8:OFF_SB + (e + 1) * 8] = ins["sgu_b"][e].T
    pc[:, OFF_C31:OFF_C31 + 4] = np.broadcast_to(ins["rel_bias"][31][None, :], (128, 4))
    shared["p_fng"] = np.ascontiguousarray(np.broadcast_to(ins["final_norm_g"][None, :], (128, D)))
    shared["p_sln"] = np.ascontiguousarray(np.broadcast_to(ins["sgu_ln_g"][None, :, :], (128, 2, 1024)))
    lam = np.stack([ins["lam_q1"], ins["lam_k1"], ins["lam_q2"], ins["lam_k2"]], axis=1)
    shared["p_lam"] = np.ascontiguousarray(np.broadcast_to(lam[None], (128, 2, 4, 128)))
    j = np.arange(128)[:, None]
    i = np.arange(128)[None, :]
    b0 = _t5_bucket(i - j)
    b1 = _t5_bucket(128 + i - j)
    rb = ins["rel_bias"]
    biasT = np.stack([rb[b0], rb[b1]], axis=0)
    shared["p_biasT"] = np.ascontiguousarray(biasT.transpose(1, 3, 0, 2))
    shared["p_sguT"] = np.ascontiguousarray(ins["sgu_w"].transpose(0, 3, 1, 2))
    maps = []
    for core in range(NCORES):
        b, half = core // 2, core % 2
        c = _consts(half)
        m = dict(shared)
        m["x"] = np.ascontiguousarray(ins["x"][b, half * T:(half + 1) * T, :])
        pcc = pc.copy()
        pcc[:, OFF_QDEC:OFF_QDEC + 6] = c["qdec"]
        pcc[:, OFF_KDEC:OFF_KDEC + 6] = c["kdec"]
        pcc[:, OFF_FLAG] = float(half)
        pcc[:, OFF_NEGB] = 0.0 if half == 1 else NEGBIG
        pcc[:, OFF_PCORR:OFF_PCORR + 64] = c["pcorr"].reshape(1, 64)
        m["p_cols"] = pcc
        for k in ("c_ident", "c_rot", "c_intra", "c_mask"):
            m[k] = c[k]
        maps.append(m)
    return maps


_NC_CACHE = {}


def kernel(**inputs):
    maps = _prep(inputs)
    if "nc" not in _NC_CACHE:
        _NC_CACHE["nc"] = build()
    nc = _NC_CACHE["nc"]
    res = run_bass_kernel_spmd(nc, maps, core_ids=list(range(NCORES)))
    out = np.zeros((4, 2 * T, D), np.float32)
    for core in range(NCORES):
        b, half = core // 2, core % 2
        out[b, half * T:(half + 1) * T, :] = np.asarray(res.results[core]["out"], dtype=np.float32)
    return out
```

```python
import math
from contextlib import ExitStack
import numpy as np
import concourse.bass as bass
import concourse.mybir as mybir
from concourse.bass_utils import run_bass_kernel_spmd

F32 = mybir.dt.float32
BF16 = mybir.dt.bfloat16
AF = mybir.ActivationFunctionType
ALU = mybir.AluOpType

P = 128
T = 1024
NT = 8
D = 2048
KC = 16
DFF = 5632
NFF = 44
EPS = 1e-6
NCORES = 8
PAIRS = [[0, 1], [2, 3], [4, 5], [6, 7]]
NEGBIG = -30000.0
GELU = AF.Gelu

OFF_NG = 0
OFF_PSC = OFF_NG + 128
OFF_GNG = OFF_PSC + 8
OFF_SUB = OFF_GNG + 24
OFF_CW = OFF_SUB + 4
OFF_CB = OFF_CW + 1056
OFF_SB = OFF_CB + 352
OFF_C31 = OFF_SB + 16
OFF_QDEC = OFF_C31 + 4
OFF_KDEC = OFF_QDEC + 6
OFF_FLAG = OFF_KDEC + 6
OFF_NEGB = OFF_FLAG + 1
OFF_PCORR = OFF_NEGB + 1
NCOLS = OFF_PCORR + 64


class KB:
    def __init__(self, nc):
        self.nc = nc
        self.eng = {'pe': nc.tensor, 'act': nc.scalar, 'dve': nc.vector, 'pool': nc.gpsimd, 'sp': nc.sync}
        self.sems = []
        self.cur = {}
        self.cnt = {}
        self.waited = {e: {} for e in self.eng}
        self.own = {e: set() for e in self.eng}
        self.track = {}
        self.pending = {e: ([], []) for e in self.eng}
        self.dpool = {}
        self.didx = {}
        self.ccsid = None
        self.cctot = 0
        self.nops = 0

    def new_sem(self, name):
        h = self.nc.alloc_semaphore(name=name)
        self.sems.append(h)
        return len(self.sems) - 1

    def _deps(self, reads, writes, e=None):
        evs = []
        own = self.own.get(e, ())
        for k in reads:
            t = self.track.get(k)
            if t and t[0]:
                evs.append(t[0])
            if t and isinstance(k, tuple) and k[0] == "PB":
                for sid, val in t[1].items():
                    if sid not in own:
                        evs.append((sid, val))
        for k in writes:
            t = self.track.get(k)
            if t:
                if t[0]:
                    evs.append(t[0])
                evs.extend(t[1].items())
        return evs

    def _wait(self, e, evs):
        need = {}
        for sid, val in evs:
            if need.get(sid, 0) < val:
                need[sid] = val
        w = self.waited[e]
        for sid, val in need.items():
            if w.get(sid, 0) < val:
                self.eng[e].wait_ge(self.sems[sid], val)
                w[sid] = val

    def _commit(self, e, ev):
        pr, pw = self.pending[e]
        for k in pr:
            t = self.track.setdefault(k, [None, {}])
            if t[1].get(ev[0], 0) < ev[1]:
                t[1][ev[0]] = ev[1]
        for k in pw:
            self.track[k] = [ev, {}]
        pr.clear()
        pw.clear()

    def op(self, e, fn, reads=(), writes=(), signal=True):
        self._wait(e, self._deps(reads, writes, e))
        inst = fn(self.eng[e])
        self.nops += 1
        pr, pw = self.pending[e]
        pr.extend(reads)
        pw.extend(writes)
        if not signal:
            return None
        if e not in self.cur or self.cnt[e] >= 6000:
            self.cur[e] = self.new_sem("s_%s_%d" % (e, len(self.sems)))
            self.own[e].add(self.cur[e])
            self.cnt[e] = 0
        self.cnt[e] += 1
        inst.then_inc(self.sems[self.cur[e]], 1)
        ev = (self.cur[e], self.cnt[e])
        self._commit(e, ev)
        return ev

    def dma(self, e, pairs, reads=(), writes=()):
        if e not in self.dpool:
            self.dpool[e] = [[self.new_sem("d_%s_%d" % (e, i)), 0] for i in range(6)]
            self.didx[e] = 0
        slot = self.dpool[e][self.didx[e] % len(self.dpool[e])]
        self.didx[e] += 1
        evs = self._deps(reads, writes)
        if slot[1] > 0:
            evs.append((slot[0], slot[1]))
        self._wait(e, evs)
        for (o, i) in pairs:
            self.eng[e].dma_start(out=o, in_=i).then_inc(self.sems[slot[0]], 16)
            slot[1] += 16
            self.nops += 1
        ev = (slot[0], slot[1])
        pr, pw = self.pending[e]
        for k in reads:
            t = self.track.setdefault(k, [None, {}])
            if t[1].get(ev[0], 0) < ev[1]:
                t[1][ev[0]] = ev[1]
        for k in writes:
            self.track[k] = [ev, {}]
        return ev

    def allgather(self, src, dst, reads=(), writes=()):
        e = 'pool'
        if self.ccsid is None:
            self.ccsid = self.new_sem("ccsem")
        evs = self._deps(reads, writes)
        if self.cctot > 0:
            evs.append((self.ccsid, self.cctot))
        self._wait(e, evs)
        inst = self.nc.gpsimd.collective_compute("AllGather", ALU.bypass, replica_groups=self.pairs,
                                                 ins=[src], outs=[dst])
        inst.then_inc(self.sems[self.ccsid])
        self.cctot += 1
        ev = (self.ccsid, self.cctot)
        for k in reads:
            t = self.track.setdefault(k, [None, {}])
            t[1][ev[0]] = ev[1]
        for k in writes:
            self.track[k] = [ev, {}]
        return ev

    def barrier(self):
        evs = []
        for e2 in self.cur:
            evs.append((self.cur[e2], self.cnt[e2]))
        for e2 in self.dpool:
            if e2 == 'pool':
                continue
            for sid, tot in self.dpool[e2]:
                if tot:
                    evs.append((sid, tot))
        if self.cctot:
            evs.append((self.ccsid, self.cctot))
        for e in self.eng:
            self._wait(e, evs)


class StopBuild(Exception):
    pass


class Arena:
    def __init__(self, ap_f32, nbytes):
        self.ap = ap_f32
        self.n = nbytes
        self.off = 0

    def reset(self, off=0):
        self.off = off

    def alloc(self, shape, dt):
        esz = 4 if dt == F32 else 2
        free = 1
        for s in shape[1:]:
            free *= s
        nb = (free * esz + 31) // 32 * 32
        assert self.off + nb <= self.n, ("arena overflow", self.off, nb, self.n)
        v = self.ap[:, self.off // 4:(self.off + nb) // 4]
        self.off += nb
        if dt != F32:
            v = v.bitcast(dt)
        v = v[:, 0:free]
        if len(shape) == 3:
            v = v.rearrange("p (a b) -> p a b", a=shape[1])
        elif len(shape) == 4:
            v = v.rearrange("p (a b c) -> p a b c", a=shape[1], b=shape[2])
        return v


def build(n_layers=4, dbg_h=False, stop_after=None, ncores=NCORES):
    nc = bass.Bass("TRN2", target_bir_lowering=False)

    def din(name, shape):
        return nc.dram_tensor(name, list(shape), F32, kind="ExternalInput").ap()

    x_d = din("x", [T, D])
    n_ev = max(1, (n_layers + 1) // 2)
    n_od = max(1, n_layers // 2)
    n_ff = max(1, n_layers)
    w_in_even = din("w_in_even", [n_ev, D, 5120])
    w_out_even = din("w_out_even", [n_ev, D, D])
    pool_w = din("pool_w", [2, 4, 128, 128])
    w_in_odd = din("w_in_odd", [n_od, D, 5120])
    w_out_odd = din("w_out_odd", [n_od, D, D])
    w_up = din("w_up", [n_ff, D, 2 * DFF])
    w_down = din("w_down", [n_ff, DFF, D])
    c_ident = din("c_ident", [128, 128])
    c_rot = din("c_rot", [128, 2, 8, 64])
    c_intra = din("c_intra", [128, 6, 128])
    c_mask = din("c_mask", [128, 128])
    p_cols = din("p_cols", [128, NCOLS])
    p_fng = din("p_fng", [128, D])
    p_sln = din("p_sln", [128, 2, 1024])
    p_lam = din("p_lam", [128, 2, 4, 128])
    p_biasT = din("p_biasT", [128, 4, 2, 128])
    p_sguT = din("p_sguT", [2, 128, 8, 128])
    out_d = nc.dram_tensor("out", [T, D], F32, kind="ExternalOutput").ap()

    hspill = nc.dram_tensor("hspill", [T, D], F32).ap()
    xu_src = nc.dram_tensor("xu_src", [128, 64], F32).ap()
    xu_dst = nc.dram_tensor("xu_dst", [256, 64], F32).ap()
    st_src = nc.dram_tensor("st_src", [128, 256], F32).ap()
    st_dst = nc.dram_tensor("st_dst", [256, 256], F32).ap()
    xh_src = nc.dram_tensor("xh_src", [128, 16], F32).ap()
    xh_dst = nc.dram_tensor("xh_dst", [256, 16], F32).ap()
    KVW = 4096 + 4112
    KVQ = KVW // 4
    kv_src = [nc.dram_tensor("kv_src%d" % i, [128, KVQ], F32).ap() for i in range(4)]
    kv_dst = [nc.dram_tensor("kv_dst%d" % i, [256, KVQ], F32).ap() for i in range(4)]

    ARENA_BYTES = 40 * 1024
    with ExitStack() as es:
        def sb(name, shape, dt):
            return es.enter_context(nc.sbuf_tensor(name, shape, dt))

        H = sb("H", [P, NT, D], F32)
        XT = sb("XT", [P, KC, T + 2], BF16)
        MT = sb("MT", [P, KC, T], BF16)
        WB = [sb("WB0", [P, KC, 512], BF16), sb("WB1", [P, KC, 512], BF16)]
        PC = sb("PC", [P, NCOLS], F32)
        IDB = sb("IDB", [P, 128], BF16)
        MASK = sb("MASK", [P, 128], F32)
        SS = sb("SS", [P, 48], F32)
        ARN = sb("ARN", [P, ARENA_BYTES // 4], F32)
        PB = [es.enter_context(nc.psum_tensor("pb%d" % i, [P, 512], F32)) for i in range(8)]
        ar = Arena(ARN, ARENA_BYTES)
        kb = KB(nc)
        kb.pairs = [[2 * i, 2 * i + 1] for i in range(ncores // 2)]

        def chk(name):
            if stop_after == name:
                raise StopBuild()

        def pcol(off, n=1):
            return PC[:, off:off + n]

        kb.dma('sp', [(PC[:, :], p_cols[:, :])], writes=["PC"])
        kb.dma('sp', [(MASK[:, :], c_mask[:, :])], writes=["MASK"])
        kb.dma('pool', [(IDB[:, :], c_ident[:, :])], writes=["IDB"])
        for t in range(NT):
            kb.dma('sp', [(H[:, t, :], x_d[t * 128:(t + 1) * 128, :])], writes=[("H", t)])

        bank_rr = [0]

        def bank():
            b = bank_rr[0] % 8
            bank_rr[0] += 1
            return b

        wslot = [0]

        preloaded = {}

        def prefetch(key, pieces, nk=KC):
            preloaded[key] = load_w(pieces, nk)

        def load_w(pieces, nk=KC, key=None):
            if key is not None and key in preloaded:
                return preloaded.pop(key)
            s = wslot[0] % 2
            wslot[0] += 1
            pairs = []
            for (co, src) in pieces:
                ncols = src.shape[1]
                v = src.rearrange("(kc p) c -> p kc c", p=128)
                step = 4 if ncols * 4 >= 2048 else 8
                for k0 in range(0, nk, step):
                    k1 = min(nk, k0 + step)
                    pairs.append((WB[s][:, k0:k1, co:co + ncols], v[:, k0:k1, :]))
            kb.dma('pool', pairs, writes=[("WB", s)])
            return s

        def norm_phase(gidx, order, halo, first=False):
            if not first:
                kb.barrier()
            ar.reset()
            JK = ar.alloc([P, D], BF16)
            HN = [ar.alloc([P, D], BF16) for _ in range(3)]
            XH = ar.alloc([P, 32], BF16)
            XHF = ar.alloc([P, 16], F32)

            def stA_act(oi, t):
                kb.op('act', lambda e: e.activation(out=JK[:, :], in_=H[:, t, :], func=AF.Square,
                                                    accum_out=SS[:, t:t + 1]),
                      reads=[("H", t)], writes=["JK", ("SS", t)])
                kb.op('act', lambda e: e.activation(out=SS[:, 8 + t:9 + t], in_=SS[:, t:t + 1], func=AF.Sqrt,
                                                    scale=1.0 / D, bias=EPS),
                      reads=[("SS", t)], writes=[("SS", 8 + t)])

            def stA_dve(oi, t):
                hn = HN[oi % 3]
                kb.op('dve', lambda e: e.reciprocal(out=SS[:, 16 + t:17 + t], in_=SS[:, 8 + t:9 + t]),
                      reads=[("SS", 8 + t)], writes=[("SS", 16 + t)])
                kb.op('dve', lambda e: e.tensor_scalar_mul(out=hn[:, :], in0=H[:, t, :],
                                                           scalar1=SS[:, 16 + t:17 + t]),
                      reads=[("H", t), ("SS", 16 + t)], writes=[("HN", oi % 3)])

            def stB(oi, t):
                hn = HN[oi % 3]
                hk = ("HN", oi % 3)
                for half in range(2):
                    b = bank()
                    pbb = PB[b][:, :].bitcast(BF16).rearrange("p (a b) -> p a b", a=8)
                    for i in range(8):
                        kc = half * 8 + i
                        kb.op('pe', lambda e: e.transpose(out=pbb[:, i, :], in_=hn[:, kc * 128:(kc + 1) * 128],
                                                          identity=IDB[:, :]),
                              reads=[hk, "IDB"], writes=[("PB", b)], signal=(i == 7))
                    for i in range(8):
                        kc = half * 8 + i
                        g = pcol(OFF_NG + gidx * 16 + kc)
                        dst = XT[:, kc, 2 + t * 128:2 + (t + 1) * 128]
                        if half == 0:
                            kb.op('act', lambda e: e.activation(out=dst, in_=pbb[:, i, :], func=AF.Copy, scale=g),
                                  reads=[("PB", b), "PC"], writes=[("XT", t)])
                        else:
                            kb.op('dve', lambda e: e.tensor_scalar_mul(out=dst, in0=pbb[:, i, :], scalar1=g),
                                  reads=[("PB", b), "PC"], writes=[("XT", t)])

            n = len(order)
            stA_act(0, order[0])
            stA_dve(0, order[0])
            for oi, t in enumerate(order):
                if oi + 1 < n:
                    stA_act(oi + 1, order[oi + 1])
                stB(oi, t)
                if oi + 1 < n:
                    stA_dve(oi + 1, order[oi + 1])
                if halo and t == NT - 1:
                    kb.op('act', lambda e: e.activation(out=XH[:, :].rearrange("p (a b) -> p a b", a=16),
                                                        in_=XT[:, :, T:T + 2], func=AF.Copy),
                          reads=[("XT", t)], writes=["XH"])
                    kb.dma('sp', [(xh_src[:, :], XH[:, :].bitcast(F32))], reads=["XH"], writes=["xh_src"])
                    kb.allgather(xh_src[:, :], xh_dst[:, :], reads=["xh_src"], writes=["xh_dst"])
                    kb.dma('sp', [(XHF[:, :], xh_dst[0:128, :])], reads=["xh_dst"], writes=["XHF"])
            if halo:
                kb.op('dve', lambda e: e.tensor_scalar_mul(
                    out=XT[:, :, 0:2], in0=XHF[:, :].bitcast(BF16).rearrange("p (a b) -> p a b", a=16),
                    scalar1=pcol(OFF_FLAG)), reads=["XHF", "PC"], writes=["XTH"])

        def out_proj(chunks, nk, key=None):
            slots = [load_w([(0, chunks[0])], nk, key=(key, 0) if key else None)]
            for cb in range(4):
                if cb + 1 < 4:
                    slots.append(load_w([(0, chunks[cb + 1])], nk, key=(key, cb + 1) if key else None))
                s = slots[cb]
                for t in range(NT):
                    b = bank()
                    for kc in range(nk):
                        kb.op('pe', lambda e: e.matmul(PB[b][:, :], lhsT=MT[:, kc, t * 128:(t + 1) * 128],
                                                       rhs=WB[s][:, kc, :], start=(kc == 0), stop=(kc == nk - 1)),
                              reads=[("MT", kc, t), ("WB", s)], writes=[("PB", b)], signal=(kc == nk - 1))
                    hs = H[:, t, cb * 512:(cb + 1) * 512]
                    kb.op('dve', lambda e: e.tensor_tensor(out=hs, in0=hs, in1=PB[b][:, :], op=ALU.add),
                          reads=[("PB", b), ("H", t)], writes=[("H", t)])

        def ffn(l):
            kb.barrier()
            ar.reset()
            AG = [ar.alloc([P, 514], F32), ar.alloc([P, 514], F32)]
            AV = [ar.alloc([P, 514], F32), ar.alloc([P, 514], F32)]
            CG = [ar.alloc([P, 512], F32), ar.alloc([P, 512], F32)]
            CV = [ar.alloc([P, 512], F32), ar.alloc([P, 512], F32)]
            SG = [ar.alloc([P, 512], F32), ar.alloc([P, 512], F32)]
            wup = w_up[l]
            wdn = w_down[l]
            groups = [(0, 12), (12, 12), (24, 10), (34, 10)]

            def up_pieces(j0):
                return [(0, wup[:, j0 * 128:(j0 + 2) * 128]), (256, wup[:, DFF + j0 * 128:DFF + (j0 + 2) * 128])]

            pair_list = []
            for (g0, gn) in groups:
                for j0 in range(g0, g0 + gn, 2):
                    pair_list.append(j0)
            nxt = None
            it = 0
            for (g0, gn) in groups:
                for j0 in range(g0, g0 + gn, 2):
                    s = nxt if nxt is not None else load_w(up_pieces(j0), key=("up", l, j0))
                    nxt = None
                    if j0 + 2 < g0 + gn:
                        nxt = load_w(up_pieces(j0 + 2))
                    for jj in range(2):
                        j = j0 + jj
                        jl = j - g0
                        wg = lambda kc: WB[s][:, kc, jj * 128:(jj + 1) * 128]
                        wv = lambda kc: WB[s][:, kc, 256 + jj * 128:256 + (jj + 1) * 128]
                        cw = lambda r: pcol(OFF_CW + l * 264 + r * 88 + j)
                        cbias = pcol(OFF_CB + l * 88 + j)
                        bh = bank()
                        for kc in range(KC):
                            kb.op('pe', lambda e: e.matmul(PB[bh][:, 0:2], lhsT=wg(kc), rhs=XT[:, kc, 0:2],
                                                           start=(kc == 0), stop=(kc == KC - 1)),
                                  reads=["XTH", ("WB", s)], writes=[("PB", bh)], signal=False)
                        for kc in range(KC):
                            kb.op('pe', lambda e: e.matmul(PB[bh][:, 2:4], lhsT=wv(kc), rhs=XT[:, kc, 0:2],
                                                           start=(kc == 0), stop=(kc == KC - 1)),
                                  reads=["XTH", ("WB", s)], writes=[("PB", bh)], signal=(kc == KC - 1))
                        for n in range(2):
                            q = it % 2
                            it += 1
                            bg = bank()
                            bv = bank()
                            xt_keys = [("XT", t) for t in range(n * 4, n * 4 + 4)]
                            for kc in range(KC):
                                kb.op('pe', lambda e: e.matmul(PB[bg][:, :], lhsT=wg(kc),
                                                               rhs=XT[:, kc, 2 + n * 512:2 + (n + 1) * 512],
                                                               start=(kc == 0), stop=(kc == KC - 1)),
                                      reads=xt_keys + [("WB", s)], writes=[("PB", bg)], signal=(kc == KC - 1))
                            for kc in range(KC):
                                kb.op('pe', lambda e: e.matmul(PB[bv][:, :], lhsT=wv(kc),
                                                               rhs=XT[:, kc, 2 + n * 512:2 + (n + 1) * 512],
                                                               start=(kc == 0), stop=(kc == KC - 1)),
                                      reads=xt_keys + [("WB", s)], writes=[("PB", bv)], signal=(kc == KC - 1))
                            ag, av, cg, cv, sg = AG[q], AV[q], CG[q], CV[q], SG[q]
                            kag, kav, kcg, kcv, ksg = ("AG", q), ("AV", q), ("CG", q), ("CV", q), ("SG", q)
                            kb.op('act', lambda e: e.activation(out=ag[:, 2:514], in_=PB[bg][:, :], func=AF.Copy),
                                  reads=[("PB", bg)], writes=[kag])
                            kb.op('act', lambda e: e.activation(out=av[:, 2:514], in_=PB[bv][:, :], func=AF.Copy),
                                  reads=[("PB", bv)], writes=[kav])
                            if n == 0:
                                kb.op('dve', lambda e: e.tensor_copy(out=ag[:, 0:2], in_=PB[bh][:, 0:2]),
                                      reads=[("PB", bh)], writes=[kag])
                                kb.op('dve', lambda e: e.tensor_copy(out=av[:, 0:2], in_=PB[bh][:, 2:4]),
                                      reads=[("PB", bh)], writes=[kav])
                            else:
                                pq = 1 - q
                                kb.op('dve', lambda e: e.tensor_copy(out=ag[:, 0:2], in_=AG[pq][:, 512:514]),
                                      reads=[("AG", pq)], writes=[kag])
                                kb.op('dve', lambda e: e.tensor_copy(out=av[:, 0:2], in_=AV[pq][:, 512:514]),
                                      reads=[("AV", pq)], writes=[kav])
                            kb.op('act', lambda e: e.activation(out=cg[:, :], in_=PB[bg][:, :], func=AF.Identity,
                                                                scale=cw(2), bias=cbias),
                                  reads=[("PB", bg), "PC"], writes=[kcg])
                            kb.op('act', lambda e: e.activation(out=cv[:, :], in_=PB[bv][:, :], func=AF.Identity,
                                                                scale=CWV(l, 2, j), bias=CBV(l, j)),
                                  reads=[("PB", bv), "PC"], writes=[kcv])
                            kb.op('dve', lambda e: e.scalar_tensor_tensor(out=cg[:, :], in0=ag[:, 1:513], scalar=cw(1),
                                                                          in1=cg[:, :], op0=ALU.mult, op1=ALU.add),
                                  reads=[kag, "PC", kcg], writes=[kcg])
                            kb.op('dve', lambda e: e.scalar_tensor_tensor(out=cg[:, :], in0=ag[:, 0:512], scalar=cw(0),
                                                                          in1=cg[:, :], op0=ALU.mult, op1=ALU.add),
                                  reads=[kag, "PC", kcg], writes=[kcg])
                            kb.op('dve', lambda e: e.scalar_tensor_tensor(out=cv[:, :], in0=av[:, 1:513],
                                                                          scalar=CWV(l, 1, j), in1=cv[:, :],
                                                                          op0=ALU.mult, op1=ALU.add),
                                  reads=[kav, "PC", kcv], writes=[kcv])
                            kb.op('dve', lambda e: e.scalar_tensor_tensor(out=cv[:, :], in0=av[:, 0:512],
                                                                          scalar=CWV(l, 0, j), in1=cv[:, :],
                                                                          op0=ALU.mult, op1=ALU.add),
                                  reads=[kav, "PC", kcv], writes=[kcv])
                            kb.op('act', lambda e: e.activation(out=sg[:, :], in_=cg[:, :], func=AF.Silu),
                                  reads=[kcg], writes=[ksg])
                            kb.op('dve', lambda e: e.tensor_tensor(out=MT[:, jl, n * 512:(n + 1) * 512], in0=sg[:, :],
                                                                   in1=cv[:, :], op=ALU.mult),
                                  reads=[ksg, kcv], writes=[("MT", jl, t) for t in range(n * 4, n * 4 + 4)])
                out_proj([wdn[g0 * 128:(g0 + gn) * 128, cb * 512:(cb + 1) * 512] for cb in range(4)], gn)
                if (g0, gn) != groups[-1]:
                    pass

        def CWV(l, r, j):
            return pcol(OFF_CW + l * 264 + r * 88 + NFF + j)

        def CBV(l, j):
            return pcol(OFF_CB + l * 88 + NFF + j)

        def even_mixer(e_idx):
            kb.barrier()
            ar.reset()
            win = w_in_even[e_idx]
            wout = w_out_even[e_idx]
            I0, I1, I2, I3 = 512, 512 + 768, 512 + 1536, 512 + 1536 + 1536
            ROT = ar.alloc([P, 2, 8, 64], F32)
            INTRA = ar.alloc([P, 6, 128], F32)
            U = ar.alloc([P, 4, 16 + T], BF16)
            PW = ar.alloc([P, 4, 128], BF16)
            UHS = ar.alloc([P, 4, 16], F32)
            UHR = ar.alloc([P, 4, 16], F32)
            mark = ar.off
            kb.dma('sp', [(ROT[:, :, :, :], c_rot[:, :, :, :])], writes=["ROT"])
            kb.dma('sp', [(INTRA[:, :, :], c_intra[:, :, :])], writes=["INTRA"])
            kb.dma('pool', [(PW[:, :, :], pool_w[e_idx].rearrange("g c d -> c g d"))], writes=["PW"])

            s = load_w([(0, win[:, 0:512])], key=("eu", e_idx))
            for g in range(4):
                for n in range(2):
                    b = bank()
                    for kc in range(KC):
                        kb.op('pe', lambda e: e.matmul(PB[b][:, :], lhsT=WB[s][:, kc, g * 128:(g + 1) * 128],
                                                       rhs=XT[:, kc, 2 + n * 512:2 + (n + 1) * 512],
                                                       start=(kc == 0), stop=(kc == KC - 1)),
                              reads=[("XT", t) for t in range(n * 4, n * 4 + 4)] + [("WB", s)],
                              writes=[("PB", b)], signal=(kc == KC - 1))
                    kb.op('act', lambda e: e.activation(out=U[:, g, 16 + n * 512:16 + (n + 1) * 512], in_=PB[b][:, :],
                                                        func=AF.Copy),
                          reads=[("PB", b)], writes=[("U", g, n)])
            kb.op('act', lambda e: e.activation(out=UHS[:, :, :], in_=U[:, :, T:T + 16], func=AF.Copy),
                  reads=[("U", g, 1) for g in range(4)], writes=["UHS"])
            kb.dma('sp', [(xu_src[:, :], UHS[:, :, :].rearrange("p a b -> p (a b)"))], reads=["UHS"], writes=["xu_src"])
            kb.allgather(xu_src[:, :], xu_dst[:, :], reads=["xu_src"], writes=["xu_dst"])
            kb.dma('sp', [(UHR[:, :, :].rearrange("p a b -> p (a b)"), xu_dst[0:128, :])], reads=["xu_dst"],
                   writes=["UHR"])

            if stop_after == "mix_a":
                return
            GS = ar.alloc([P, 8, 256], BF16)
            RSB = ar.alloc([P, 8, 256], BF16)
            QDT = ar.alloc([P, 8, 128], BF16)
            STATE = ar.alloc([P, 256], F32)
            STATEB = ar.alloc([P, 256], BF16)
            mark_r = ar.off
            NB3 = 4
            for h in range(6):
                dec = 1.0 - 2.0 ** (-5.0 - h)
                chunk_dec = dec ** 128
                kb.barrier()
                ar.reset(mark_r)
                QK = [ar.alloc([P, 2, 2, 64], BF16) for _ in range(NB3)]
                VB = [ar.alloc([P, 256], BF16) for _ in range(NB3)]
                QD = [ar.alloc([P, 128], BF16) for _ in range(NB3)]
                KD = [ar.alloc([P, 128], BF16) for _ in range(NB3)]
                QKT = [ar.alloc([P, 2, 128], BF16) for _ in range(2)]
                ST = [ar.alloc([P, 128], BF16) for _ in range(2)]
                R1 = ar.alloc([P, 2, 64], F32)
                R2 = ar.alloc([P, 2, 64], F32)
                R3 = ar.alloc([P, 2, 64], F32)
                R4 = ar.alloc([P, 2, 64], F32)
                def g_pieces(hx):
                    return [(0, win[:, I3 + hx * 256:I3 + (hx + 1) * 256])]

                def qkv_pieces(hx):
                    return [(0, win[:, I0 + hx * 128:I0 + (hx + 1) * 128]),
                            (128, win[:, I1 + hx * 128:I1 + (hx + 1) * 128]),
                            (256, win[:, I2 + hx * 256:I2 + (hx + 1) * 256])]

                sg_ = load_w(g_pieces(h), key=("eg", e_idx, h))
                sq_ = load_w(qkv_pieces(h), key=("eq", e_idx, h))
                kb.op('dve', lambda e: e.memset(STATE[:, :], 0.0), writes=["STATE"])
                kb.op('dve', lambda e: e.memset(STATEB[:, :], 0.0), writes=["STATEB"])

                def st_P(t):
                    b = t % 2
                    for kc in range(KC):
                        kb.op('pe', lambda e: e.matmul(PB[b][:, :], lhsT=XT[:, kc, 2 + t * 128:2 + (t + 1) * 128],
                                                       rhs=WB[sq_][:, kc, :], start=(kc == 0), stop=(kc == KC - 1)),
                              reads=[("XT", t), ("WB", sq_)], writes=[("PB", b)], signal=(kc == KC - 1))
                def st_ROT(t):
                    b = t % 2
                    q3 = t % NB3
                    qk, vb, qd, kd = QK[q3], VB[q3], QD[q3], KD[q3]
                    z4 = PB[b][:, 0:256].rearrange("p (a b c) -> p a b c", a=2, b=2)
                    x1 = z4[:, :, 0, :]
                    x2 = z4[:, :, 1, :]
                    cosb = ROT[:, 0, t, :].unsqueeze(1).to_broadcast([P, 2, 64])
                    sinb = ROT[:, 1, t, :].unsqueeze(1).to_broadcast([P, 2, 64])
                    kb.op('dve', lambda e: e.tensor_tensor(out=R1[:, :, :], in0=x1, in1=cosb, op=ALU.mult),
                          reads=[("PB", b), "ROT"], writes=["R1"])
                    kb.op('dve', lambda e: e.tensor_tensor(out=R2[:, :, :], in0=x2, in1=sinb, op=ALU.mult),
                          reads=[("PB", b), "ROT"], writes=["R2"])
                    kb.op('dve', lambda e: e.tensor_tensor(out=qk[:, :, 0, :], in0=R1[:, :, :], in1=R2[:, :, :],
                                                           op=ALU.subtract),
                          reads=["R1", "R2"], writes=[("QK0", q3)])
                    kb.op('dve', lambda e: e.tensor_tensor(out=R3[:, :, :], in0=x1, in1=sinb, op=ALU.mult),
                          reads=[("PB", b), "ROT"], writes=["R3"])
                    kb.op('dve', lambda e: e.tensor_tensor(out=R4[:, :, :], in0=x2, in1=cosb, op=ALU.mult),
                          reads=[("PB", b), "ROT"], writes=["R4"])
                    kb.op('dve', lambda e: e.tensor_tensor(out=qk[:, :, 1, :], in0=R3[:, :, :], in1=R4[:, :, :],
                                                           op=ALU.add),
                          reads=["R3", "R4"], writes=[("QK1", q3)])
                    kb.op('act', lambda e: e.activation(out=vb[:, :], in_=PB[b][:, 256:512], func=AF.Copy),
                          reads=[("PB", b)], writes=[("VB", q3)])
                    qf = qk[:, 0, :, :].rearrange("p a b -> p (a b)")
                    kf = qk[:, 1, :, :].rearrange("p a b -> p (a b)")
                    kb.op('act', lambda e: e.activation(out=qd[:, :], in_=qf, func=AF.Copy, scale=pcol(OFF_QDEC + h)),
                          reads=[("QK0", q3), ("QK1", q3), "PC"], writes=[("QD", q3)])
                    kb.op('act', lambda e: e.activation(out=kd[:, :], in_=kf, func=AF.Copy, scale=pcol(OFF_KDEC + h)),
                          reads=[("QK0", q3), ("QK1", q3), "PC"], writes=[("KD", q3)])

                def st_TR(t):
                    q3 = t % NB3
                    qk, qd = QK[q3], QD[q3]
                    qf = qk[:, 0, :, :].rearrange("p a b -> p (a b)")
                    kf = qk[:, 1, :, :].rearrange("p a b -> p (a b)")
                    bt = 2 + (t % 2)
                    pbt = PB[bt][:, :].bitcast(BF16).rearrange("p (a b) -> p a b", a=8)
                    kb.op('pe', lambda e: e.transpose(out=pbt[:, 0, :], in_=qf, identity=IDB[:, :]),
                          reads=[("QK0", q3), ("QK1", q3), "IDB"], writes=[("PB", bt)], signal=False)
                    kb.op('pe', lambda e: e.transpose(out=pbt[:, 1, :], in_=kf, identity=IDB[:, :]),
                          reads=[("QK0", q3), ("QK1", q3), "IDB"], writes=[("PB", bt)], signal=False)
                    kb.op('pe', lambda e: e.transpose(out=pbt[:, 2, :], in_=qd[:, :], identity=IDB[:, :]),
                          reads=[("QD", q3), "IDB"], writes=[("PB", bt)])
                    kb.op('dve', lambda e: e.tensor_copy(out=QKT[t % 2][:, :, :], in_=pbt[:, 0:2, :]),
                          reads=[("PB", bt)], writes=[("QKT", t % 2)])
                    kb.op('dve', lambda e: e.tensor_copy(out=QDT[:, t, :], in_=pbt[:, 2, :]),
                          reads=[("PB", bt)], writes=[("QDT", t)])

                def st_S(t):
                    bs = 4
                    qkt = QKT[t % 2]
                    kb.op('pe', lambda e: e.matmul(PB[bs][:, 0:128], lhsT=qkt[:, 1, :], rhs=qkt[:, 0, :],
                                                   start=True, stop=True),
                          reads=[("QKT", t % 2)], writes=[("PB", bs)])
                    kb.op('dve', lambda e: e.tensor_tensor(out=ST[t % 2][:, :], in0=PB[bs][:, 0:128], in1=INTRA[:, h, :],
                                                           op=ALU.mult),
                          reads=[("PB", bs), "INTRA"], writes=[("ST", t % 2)])

                def st_R(t):
                    q3 = t % NB3
                    vb, kd = VB[q3], KD[q3]
                    br, bk = 5, 6
                    kb.op('pe', lambda e: e.matmul(PB[br][:, 0:256], lhsT=ST[t % 2][:, :], rhs=vb[:, :], start=True, stop=False),
                          reads=[("ST", t % 2), ("VB", q3)], writes=[("PB", br)], signal=False)
                    kb.op('pe', lambda e: e.matmul(PB[br][:, 0:256], lhsT=QDT[:, t, :], rhs=STATEB[:, :],
                                                   start=False, stop=True),
                          reads=[("QDT", t), "STATEB"], writes=[("PB", br)])
                    kb.op('pe', lambda e: e.matmul(PB[bk][:, 0:256], lhsT=kd[:, :], rhs=vb[:, :], start=True, stop=True),
                          reads=[("KD", q3), ("VB", q3)], writes=[("PB", bk)])
                    kb.op('act', lambda e: e.activation(out=RSB[:, t, :], in_=PB[br][:, 0:256], func=AF.Copy),
                          reads=[("PB", br)], writes=[("RSB", t)])
                    kb.op('dve', lambda e: e.scalar_tensor_tensor(out=STATE[:, :], in0=STATE[:, :], scalar=chunk_dec,
                                                                  in1=PB[bk][:, 0:256], op0=ALU.mult, op1=ALU.add),
                          reads=["STATE", ("PB", bk)], writes=["STATE"])
                    kb.op('act', lambda e: e.activation(out=STATEB[:, :], in_=STATE[:, :], func=AF.Copy),
                          reads=["STATE"], writes=["STATEB"])

                for i in range(NT + 3):
                    if i < NT:
                        st_P(i)
                    if 0 <= i - 1 < NT:
                        st_TR(i - 1)
                    if 0 <= i - 2 < NT:
                        st_S(i - 2)
                    if 0 <= i - 3 < NT:
                        st_R(i - 3)
                    if i < NT:
                        st_ROT(i)
                if stop_after == "mix_b1":
                    return
                kb.dma('sp', [(st_src[:, :], STATE[:, :])], reads=["STATE"], writes=["st_src"])
                kb.allgather(st_src[:, :], st_dst[:, :], reads=["st_src"], writes=["st_dst"])
                nxt_q = None
                for t in range(NT):
                    b = t % 2
                    for kc in range(KC):
                        kb.op('pe', lambda e: e.matmul(PB[b][:, 0:256], lhsT=XT[:, kc, 2 + t * 128:2 + (t + 1) * 128],
                                                       rhs=WB[sg_][:, kc, 0:256], start=(kc == 0), stop=(kc == KC - 1)),
                              reads=[("XT", t), ("WB", sg_)], writes=[("PB", b)], signal=(kc == KC - 1))
                    kb.op('act', lambda e: e.activation(out=GS[:, t, :], in_=PB[b][:, 0:256], func=AF.Silu),
                          reads=[("PB", b)], writes=[("GS", t)])
                if h + 1 < 6:
                    prefetch(("eg", e_idx, h + 1), g_pieces(h + 1))
                    prefetch(("eq", e_idx, h + 1), qkv_pieces(h + 1))
                else:
                    prefetch((("wo", "e", e_idx), 0), [(0, wout[:, 0:512])])
                    prefetch((("wo", "e", e_idx), 1), [(0, wout[:, 512:1024])])
                kb.barrier()
                ar.reset(mark_r)
                SA = ar.alloc([P, 256], F32)
                SC = ar.alloc([P, 256], F32)
                SCB = [ar.alloc([P, 256], BF16) for _ in range(2)]
                RT = [ar.alloc([P, 256], F32) for _ in range(2)]
                YF = [ar.alloc([P, 256], F32) for _ in range(2)]
                YG = [ar.alloc([P, 256], BF16) for _ in range(2)]
                BNS = [ar.alloc([P, 8], F32) for _ in range(2)]
                MV = [ar.alloc([P, 4], F32) for _ in range(2)]
                kb.dma('sp', [(SA[:, :], st_dst[0:128, :])], reads=["st_dst"], writes=["SA"])
                kb.op('dve', lambda e: e.tensor_scalar_mul(out=SC[:, :], in0=SA[:, :], scalar1=pcol(OFF_FLAG)),
                      reads=["SA", "PC"], writes=["SC"])

                def p2_hops(t):
                    q = t % 2
                    bc = 4 + q
                    hops = []
                    def hop0():
                        kb.op('act', lambda e: e.activation(out=SCB[q][:, :], in_=SC[:, :], func=AF.Copy),
                              reads=["SC"], writes=[("SCB", q)])
                        if t < NT - 1:
                            kb.op('dve', lambda e: e.tensor_scalar_mul(out=SC[:, :], in0=SC[:, :], scalar1=chunk_dec),
                                  reads=["SC"], writes=["SC"])
                    hops.append(hop0)
                    hops.append(lambda: kb.op('pe', lambda e: e.matmul(PB[bc][:, 0:256], lhsT=QDT[:, t, :], rhs=SCB[q][:, :],
                                                                       start=True, stop=True),
                                              reads=[("QDT", t), ("SCB", q)], writes=[("PB", bc)]))
                    hops.append(lambda: kb.op('dve', lambda e: e.tensor_tensor(out=RT[q][:, :], in0=RSB[:, t, :], in1=PB[bc][:, 0:256],
                                                                               op=ALU.add),
                                              reads=[("RSB", t), ("PB", bc)], writes=[("RT", q)]))
                    hops.append(lambda: kb.op('dve', lambda e: e.bn_stats(out=BNS[q][:, 0:6], in_=RT[q][:, :]),
                                              reads=[("RT", q)], writes=[("BNS", q)]))
                    hops.append(lambda: kb.op('dve', lambda e: e.bn_aggr(out=MV[q][:, 0:2], in_=BNS[q][:, 0:6]),
                                              reads=[("BNS", q)], writes=[("MV", q)]))
                    hops.append(lambda: kb.op('act', lambda e: e.activation(out=MV[q][:, 2:3], in_=MV[q][:, 1:2], func=AF.Sqrt, bias=EPS),
                                              reads=[("MV", q)], writes=[("MV2", q)]))
                    hops.append(lambda: kb.op('dve', lambda e: e.reciprocal(out=MV[q][:, 3:4], in_=MV[q][:, 2:3]),
                                              reads=[("MV2", q)], writes=[("MV3", q)]))
                    hops.append(lambda: kb.op('dve', lambda e: e.tensor_scalar(out=YF[q][:, :], in0=RT[q][:, :], scalar1=MV[q][:, 0:1],
                                                                               scalar2=MV[q][:, 3:4], op0=ALU.subtract, op1=ALU.mult),
                                              reads=[("RT", q), ("MV", q), ("MV3", q)], writes=[("YF", q)]))
                    hops.append(lambda: kb.op('pool', lambda e: e.tensor_tensor(out=YG[q][:, :], in0=YF[q][:, :], in1=GS[:, t, :], op=ALU.mult),
                                              reads=[("YF", q), ("GS", t)], writes=[("YG", q)]))
                    hops.append(lambda: p2_B(t))
                    return hops

                def p2_B(t):
                    q = t % 2
                    by = 6 + q
                    pby = PB[by][:, :].bitcast(BF16).rearrange("p (a b) -> p a b", a=8)
                    kb.op('pe', lambda e: e.transpose(out=pby[:, 0, :], in_=YG[q][:, 0:128], identity=IDB[:, :]),
                          reads=[("YG", q), "IDB"], writes=[("PB", by)], signal=False)
                    kb.op('pe', lambda e: e.transpose(out=pby[:, 1, :], in_=YG[q][:, 128:256], identity=IDB[:, :]),
                          reads=[("YG", q), "IDB"], writes=[("PB", by)])
                    for j in range(2):
                        kc = 4 + 2 * h + j
                        gcol = pcol(OFF_GNG + e_idx * 12 + 2 * h + j)
                        dst = MT[:, kc, t * 128:(t + 1) * 128]
                        kb.op('act', lambda e: e.activation(out=dst, in_=pby[:, j, :], func=AF.Copy, scale=gcol),
                              reads=[("PB", by), "PC"], writes=[("MT", kc, t)])

                for t0 in range(0, NT, 2):
                    ha = p2_hops(t0)
                    hb = p2_hops(t0 + 1)
                    for k in range(max(len(ha), len(hb))):
                        if k < len(ha):
                            ha[k]()
                        if k < len(hb):
                            hb[k]()

            if stop_after == "mix_b":
                return
            kb.barrier()
            ar.reset(mark)
            kb.op('dve', lambda e: e.tensor_scalar_mul(out=U[:, :, 0:16], in0=UHR[:, :, :], scalar1=pcol(OFF_FLAG)),
                  reads=["UHR", "PC"], writes=["UH"])
            S_A = ar.alloc([P, 528], F32)
            S_B = ar.alloc([P, 528], F32)
            YP = [ar.alloc([P, 512], BF16), ar.alloc([P, 512], BF16)]
            it = 0
            for g in range(4):
                w = 2 ** (g + 1)
                for n in range(2):
                    a0 = U[:, g, n * 512:n * 512 + 528]
                    rk = [("U", g, n), "UH"] + ([("U", g, 0)] if n == 1 else [])
                    kb.op('dve', lambda e: e.tensor_tensor(out=S_A[:, 1:528], in0=a0[:, 1:528], in1=a0[:, 0:527],
                                                           op=ALU.add), reads=rk, writes=["S_A"])
                    cur, curk, oth, othk = S_A, "S_A", S_B, "S_B"
                    sh = 1
                    for step in range(g):
                        sh2 = sh * 2
                        lo = 2 * sh2 - 1
                        kb.op('dve', lambda e: e.tensor_tensor(out=oth[:, lo:528], in0=cur[:, lo:528],
                                                               in1=cur[:, lo - sh2:528 - sh2], op=ALU.add),
                              reads=[curk], writes=[othk])
                        cur, curk, oth, othk = oth, othk, cur, curk
                        sh = sh2
                    if n == 0:
                        kb.op('dve', lambda e: e.tensor_tensor(out=cur[:, 16:32], in0=cur[:, 16:32],
                                                               in1=PC[:, OFF_PCORR + g * 16:OFF_PCORR + (g + 1) * 16],
                                                               op=ALU.mult),
                              reads=[curk, "PC"], writes=[curk])
                    yp = YP[it % 2]
                    ypk = ("YP", it % 2)
                    it += 1
                    kb.op('dve', lambda e: e.scalar_tensor_tensor(out=yp[:, :], in0=cur[:, 16:528], scalar=1.0 / w,
                                                                  in1=a0[:, 16:528], op0=ALU.mult, op1=ALU.subtract),
                          reads=[curk] + rk, writes=[ypk])
                    b = bank()
                    kb.op('pe', lambda e: e.matmul(PB[b][:, :], lhsT=PW[:, g, :], rhs=yp[:, :], start=True, stop=True),
                          reads=["PW", ypk], writes=[("PB", b)])
                    kb.op('act', lambda e: e.activation(out=MT[:, g, n * 512:(n + 1) * 512], in_=PB[b][:, :],
                                                        func=AF.Copy, scale=pcol(OFF_PSC + e_idx * 4 + g)),
                          reads=[("PB", b), "PC"], writes=[("MT", g, t) for t in range(n * 4, n * 4 + 4)])
            if stop_after == "mix_c":
                return
            out_proj([wout[:, cb * 512:(cb + 1) * 512] for cb in range(4)], KC, key=("wo", "e", e_idx))

        def odd_mixer(o_idx, layer_idx):
            kb.barrier()
            ar.reset()
            win = w_in_odd[o_idx]
            wout = w_out_odd[o_idx]
            J0, J1, J2 = 2048, 3072, 4096
            Hb = H[:, :, :].rearrange("p a b -> p (a b)").bitcast(BF16)
            KT = Hb[:, 0:8192].rearrange("p (a b) -> p a b", a=8)
            V1 = Hb[:, 8192:8192 + 8224].rearrange("p (a b c) -> p a b c", a=8, b=4)
            QT = Hb[:, 16416:16416 + 8192].rearrange("p (a b) -> p a b", a=8)
            Hf = H[:, :, :].rearrange("p a b -> p (a b)")
            XTb = XT[:, :, :].rearrange("p a b -> p (a b)")
            KTP = XTb[:, 0:8192].rearrange("p (a b) -> p a b", a=8)
            V1P = XTb[:, 8192:8192 + 8224].rearrange("p (a b c) -> p a b c", a=8, b=4)
            lam_init = 0.8 - 0.6 * math.exp(-0.3 * layer_idx)

            BIAS = ar.alloc([P, 4, 2, 128], F32)
            ED = ar.alloc([P, 4, 2, 128], BF16)
            LAMV = ar.alloc([P, 4, 128], F32)
            LT = ar.alloc([P, 2, 128], F32)
            LS = ar.alloc([P, 8], F32)
            CB = ar.alloc([P, 8], F32)
            mark0 = ar.off
            kb.dma('sp', [(BIAS[:, :, :, :], p_biasT[:, :, :, :])], writes=["BIAS"])
            kb.dma('sp', [(LAMV[:, :, :], p_lam[:, o_idx, :, :])], writes=["LAMV"])
            kb.op('act', lambda e: e.activation(out=ED[:, :, :, :], in_=BIAS[:, :, :, :], func=AF.Exp),
                  reads=["BIAS"], writes=["ED"])
            for hh in range(4):
                kb.op('dve', lambda e: e.tensor_tensor(out=ED[:, hh, 0, :], in0=ED[:, hh, 0, :], in1=MASK[:, :],
                                                       op=ALU.mult), reads=["ED", "MASK"], writes=["ED"])
            kb.op('dve', lambda e: e.tensor_tensor(out=LT[:, 0, :], in0=LAMV[:, 0, :], in1=LAMV[:, 1, :], op=ALU.mult),
                  reads=["LAMV"], writes=["LT0"])
            kb.op('dve', lambda e: e.tensor_tensor(out=LT[:, 1, :], in0=LAMV[:, 2, :], in1=LAMV[:, 3, :], op=ALU.mult),
                  reads=["LAMV"], writes=["LT1"])
            kb.op('dve', lambda e: e.reduce_sum(out=LS[:, 0:2], in_=LT[:, :, :], axis=mybir.AxisListType.X),
                  reads=["LT0", "LT1"], writes=["LS"])
            kb.op('act', lambda e: e.activation(out=LS[:, 2:4], in_=LS[:, 0:2], func=AF.Exp), reads=["LS"], writes=["LS2"])
            kb.op('dve', lambda e: e.tensor_tensor(out=LS[:, 4:5], in0=LS[:, 2:3], in1=LS[:, 3:4], op=ALU.subtract),
                  reads=["LS2"], writes=["LS4"])
            kb.op('dve', lambda e: e.tensor_scalar(out=LS[:, 5:6], in0=LS[:, 4:5], scalar1=lam_init, scalar2=-1.0,
                                                   op0=ALU.add, op1=ALU.mult), reads=["LS4"], writes=["LAM"])
            kb.op('dve', lambda e: e.tensor_copy(out=CB[:, 0:4], in_=PC[:, OFF_C31:OFF_C31 + 4]), reads=["PC"], writes=["CB0"])
            kb.op('dve', lambda e: e.tensor_scalar_add(out=CB[:, 4:8], in0=PC[:, OFF_C31:OFF_C31 + 4], scalar1=pcol(OFF_NEGB)),
                  reads=["PC"], writes=["CB1"])

            def proj_fm(col0, dst, nm):
                for c in range(2):
                    s = load_w([(0, win[:, col0 + c * 512:col0 + (c + 1) * 512])], key=("ofm", o_idx, nm, c))
                    for f in range(4):
                        for n in range(2):
                            b = bank()
                            for kc in range(KC):
                                kb.op('pe', lambda e: e.matmul(PB[b][:, :], lhsT=WB[s][:, kc, f * 128:(f + 1) * 128],
                                                               rhs=XT[:, kc, 2 + n * 512:2 + (n + 1) * 512],
                                                               start=(kc == 0), stop=(kc == KC - 1)),
                                      reads=[("XT", t) for t in range(n * 4, n * 4 + 4)] + [("WB", s)],
                                      writes=[("PB", b)], signal=(kc == KC - 1))
                            kb.op('act', lambda e: e.activation(out=dst[:, c * 4 + f, n * 512:(n + 1) * 512],
                                                                in_=PB[b][:, :], func=AF.Copy),
                                  reads=[("PB", b)], writes=[(nm, c * 4 + f, n)])

            proj_fm(J1, KT, "KT")
            kb.op('dve', lambda e: e.memset(V1[:, :, :, 256:257], 1.0), writes=["V1one"])
            for c in range(2):
                s = load_w([(0, win[:, J2 + c * 512:J2 + (c + 1) * 512])])
                for t in range(NT):
                    b = bank()
                    for kc in range(KC):
                        kb.op('pe', lambda e: e.matmul(PB[b][:, :], lhsT=XT[:, kc, 2 + t * 128:2 + (t + 1) * 128],
                                                       rhs=WB[s][:, kc, :], start=(kc == 0), stop=(kc == KC - 1)),
                              reads=[("XT", t), ("WB", s)], writes=[("PB", b)], signal=(kc == KC - 1))
                    kb.op('act', lambda e: e.activation(out=V1[:, t, 2 * c:2 * c + 2, 0:256],
                                                        in_=PB[b][:, :].rearrange("p (a b) -> p a b", a=2), func=AF.Copy),
                          reads=[("PB", b)], writes=[("V1", t, c)])
            chk("o_a")
            kvkeys = [("KT", i, n) for i in range(8) for n in range(2)] + [("V1", t, c) for t in range(8) for c in range(2)] + ["V1one"]
            for c in range(2):
                prefetch(("ofm", o_idx, "QT", c), [(0, win[:, J0 + c * 512:J0 + (c + 1) * 512])])
            for i in range(4):
                kb.dma('sp', [(kv_src[i][:, :], Hf[:, i * KVQ:(i + 1) * KVQ])], reads=kvkeys, writes=[("kv_src", i)])
            for i in range(2):
                kb.allgather(kv_src[i][:, :], kv_dst[i][:, :], reads=[("kv_src", i)], writes=[("kv_dst", i)])
            chk("o_b")
            proj_fm(J0, QT, "QT")
            chk("o_c")

            ar2 = Arena(Hf[:, 12304:16384], 4080 * 4)
            SLN = ar2.alloc([P, 1024], F32)
            WSF = ar2.alloc([P, 8, 128], F32)
            ZV = ar2.alloc([P, 1024], F32)
            ZU = ar2.alloc([P, 512], F32)
            WST = ar.alloc([P, 8, 128], BF16)
            SV = ar.alloc([P, 8, 1024], BF16)
            VN = ar.alloc([P, 1024], BF16)
            CO = ar.alloc([P, 512], BF16)
            kb.dma('sp', [(SLN[:, :], p_sln[:, o_idx, :])], writes=["SLN"])
            kb.dma('sp', [(WSF[:, :, :], p_sguT[o_idx])], writes=["WSF"])
            kb.op('dve', lambda e: e.tensor_tensor(out=WST[:, :, :], in0=WSF[:, :, :],
                                                   in1=MASK[:, :].unsqueeze(1).to_broadcast([P, 8, 128]), op=ALU.mult),
                  reads=["WSF", "MASK"], writes=["WST"])
            s0 = load_w([(0, win[:, 1024:1536])])
            s1 = load_w([(0, win[:, 1536:2048])])
            for i in range(2, 4):
                kb.allgather(kv_src[i][:, :], kv_dst[i][:, :], reads=[("kv_src", i)], writes=[("kv_dst", i)])
            VNs = [VN, ar.alloc([P, 1024], BF16)]
            COs = [CO, ar.alloc([P, 512], BF16)]

            def sgu_A(t):
                bb = [bank(), bank()]
                for c, s_ in enumerate((s0, s1)):
                    for kc in range(KC):
                        kb.op('pe', lambda e: e.matmul(PB[bb[c]][:, :], lhsT=XT[:, kc, 2 + t * 128:2 + (t + 1) * 128],
                                                       rhs=WB[s_][:, kc, :], start=(kc == 0), stop=(kc == KC - 1)),
                              reads=[("XT", t), ("WB", s_)], writes=[("PB", bb[c])], signal=(kc == KC - 1))
                    kb.op('act', lambda e: e.activation(out=ZV[:, c * 512:(c + 1) * 512], in_=PB[bb[c]][:, :], func=GELU),
                          reads=[("PB", bb[c])], writes=[("ZV", c)])
                vn = VNs[t % 2]
                kb.op('dve', lambda e: e.bn_stats(out=SS[:, 24:30], in_=ZV[:, 0:512]), reads=[("ZV", 0)], writes=["BN0"])
                kb.op('dve', lambda e: e.bn_stats(out=SS[:, 30:36], in_=ZV[:, 512:1024]), reads=[("ZV", 1)], writes=["BN1"])
                kb.op('dve', lambda e: e.bn_aggr(out=LS[:, 6:8], in_=SS[:, 24:36].rearrange("p (a b) -> p a b", a=2)),
                      reads=["BN0", "BN1"], writes=["MVZ"])
                kb.op('act', lambda e: e.activation(out=SS[:, 36:37], in_=LS[:, 7:8],
                                                    func=AF.Sqrt, bias=EPS), reads=["MVZ"], writes=["SDZ"])
                kb.op('dve', lambda e: e.reciprocal(out=SS[:, 37:38], in_=SS[:, 36:37]), reads=["SDZ"], writes=["RSZ"])
                kb.op('dve', lambda e: e.tensor_scalar(out=ZV[:, :], in0=ZV[:, :], scalar1=LS[:, 6:7], scalar2=SS[:, 37:38],
                                                       op0=ALU.subtract, op1=ALU.mult),
                      reads=[("ZV", 0), ("ZV", 1), "MVZ", "RSZ"], writes=[("ZV", 0), ("ZV", 1)])
                kb.op('dve', lambda e: e.tensor_tensor(out=vn[:, :], in0=ZV[:, :], in1=SLN[:, :], op=ALU.mult),
                      reads=[("ZV", 0), ("ZV", 1), "SLN"], writes=[("VN", t % 2)])

            def sgu_B(t):
                vn = VNs[t % 2]
                bs2 = [bank(), bank()]
                for g in range(8):
                    b = bs2[g // 4]
                    kb.op('pe', lambda e: e.matmul(PB[b][:, (g % 4) * 128:(g % 4 + 1) * 128], lhsT=WST[:, g, :],
                                                   rhs=vn[:, g * 128:(g + 1) * 128], start=True, stop=True),
                          reads=["WST", ("VN", t % 2)], writes=[("PB", b)], signal=(g % 4 == 3))
                for g in range(8):
                    b = bs2[g // 4]
                    bcol = pcol(OFF_SB + o_idx * 8 + g)
                    kb.op('act', lambda e: e.activation(out=SV[:, t, g * 128:(g + 1) * 128],
                                                        in_=PB[b][:, (g % 4) * 128:(g % 4 + 1) * 128],
                                                        func=AF.Identity, bias=bcol),
                          reads=[("PB", b), "PC"], writes=[("SV", t)])

            for i in range(NT + 1):
                if i < NT:
                    sgu_A(i)
                if i - 1 >= 0:
                    sgu_B(i - 1)

            zitems = [(c, t) for c in range(2) for t in range(NT)]
            zslots = {}

            def zu_A(k):
                c, t = zitems[k]
                if t == 0:
                    zslots[c] = load_w([(0, win[:, c * 512:(c + 1) * 512])])
                s_ = zslots[c]
                b = bank()
                for kc in range(KC):
                    kb.op('pe', lambda e: e.matmul(PB[b][:, :], lhsT=XT[:, kc, 2 + t * 128:2 + (t + 1) * 128],
                                                   rhs=WB[s_][:, kc, :], start=(kc == 0), stop=(kc == KC - 1)),
                          reads=[("XT", t), ("WB", s_)], writes=[("PB", b)], signal=(kc == KC - 1))
                co = COs[k % 2]
                kb.op('act', lambda e: e.activation(out=ZU[:, :], in_=PB[b][:, :], func=GELU),
                      reads=[("PB", b)], writes=["ZU"])
                kb.op('dve', lambda e: e.tensor_tensor(out=co[:, :], in0=ZU[:, :], in1=SV[:, t, c * 512:(c + 1) * 512],
                                                       op=ALU.mult), reads=["ZU", ("SV", t)], writes=[("CO", k % 2)])

            def zu_B(k):
                c, t = zitems[k]
                co = COs[k % 2]
                bt = bank()
                pbt = PB[bt][:, :].bitcast(BF16).rearrange("p (a b) -> p a b", a=8)
                for i in range(4):
                    kb.op('pe', lambda e: e.transpose(out=pbt[:, i, :], in_=co[:, i * 128:(i + 1) * 128], identity=IDB[:, :]),
                          reads=[("CO", k % 2), "IDB"], writes=[("PB", bt)], signal=(i == 3))
                for i in range(4):
                    kc = c * 4 + i
                    dst = MT[:, kc, t * 128:(t + 1) * 128]
                    kb.op('dve', lambda e: e.tensor_copy(out=dst, in_=pbt[:, i, :]),
                          reads=[("PB", bt)], writes=[("MT", kc, t)])

            for k in range(len(zitems) + 1):
                if k < len(zitems):
                    zu_A(k)
                if k - 1 >= 0:
                    zu_B(k - 1)
            chk("o_d")
            kb.barrier()
            XTf = XTb.bitcast(F32)
            kb.dma('sp', [(XTf[:, i * KVQ:(i + 1) * KVQ], kv_dst[i][0:128, :]) for i in range(4)],
                   reads=[("kv_dst", i) for i in range(4)], writes=["KVP"])
            prefetch((("wo", "o", o_idx), 0), [(0, wout[:, 0:512])])
            prefetch((("wo", "o", o_idx), 1), [(0, wout[:, 512:1024])])
            ar.reset(mark0)
            PT = [ar.alloc([P, 512], BF16) for _ in range(6)]
            OA = ar.alloc([P, 2, 260], F32)
            RD = ar.alloc([P, 4], F32)
            DO = ar.alloc([P, 256], F32)
            DJ = ar.alloc([P, 256], F32)
            DBs = [ar.alloc([P, 256], BF16), ar.alloc([P, 256], BF16)]
            scale = 128.0 ** -0.5
            LOOK = 2
            steps = []
            gid = 0
            for hh in range(4):
                for qp in range(4):
                    qt2 = [qp * 2, qp * 2 + 1]
                    klist = [("p", kt) for kt in range(8)] + [("o", kt) for kt in range(qt2[1] + 1)]
                    gsteps = []
                    for (kind, kt) in klist:
                        vis = qt2 if kind == "p" else [qt for qt in qt2 if qt >= kt]
                        gsteps.append(dict(g=gid, hh=hh, qt2=qt2, kind=kind, kt=kt, vis=vis))
                    seen = set()
                    for st in gsteps:
                        st["first"] = {}
                        for qt in st["vis"]:
                            st["first"][qt] = qt not in seen
                            seen.add(qt)
                    seen = set()
                    for st in reversed(gsteps):
                        st["last"] = {}
                        for qt in st["vis"]:
                            st["last"][qt] = qt not in seen
                            seen.add(qt)
                    gsteps[-1]["gend"] = True
                    steps.extend(gsteps)
                    gid += 1

            def emit_S(i, st):
                hh, kind, kt, vis = st["hh"], st["kind"], st["kt"], st["vis"]
                nq = len(vis) * 128
                ksrc = KTP if kind == "p" else KT
                bsx = 4 + (i % 3)
                for j in range(2):
                    kkey = ["KVP"] if kind == "p" else [("KT", hh * 2 + j, kt // 4)]
                    kb.op('pe', lambda e: e.matmul(PB[bsx][:, j * 256:j * 256 + nq],
                                                   lhsT=ksrc[:, hh * 2 + j, kt * 128:(kt + 1) * 128],
                                                   rhs=QT[:, hh * 2 + j, vis[0] * 128:(vis[-1] + 1) * 128],
                                                   start=True, stop=True),
                          reads=kkey + [("QT", hh * 2 + j, vis[0] // 4)], writes=[("PB", bsx)], signal=(j == 1))
                pt = PT[i % 6]
                ptk = ("PT", i % 6)
                pt3 = pt[:, :].rearrange("p (j c) -> p j c", j=2)
                ps3 = PB[bsx][:, :].rearrange("p (j c) -> p j c", j=2)
                bcol = CB[:, 4 + hh:5 + hh] if kind == "p" else CB[:, hh:hh + 1]
                bkey = "CB1" if kind == "p" else "CB0"
                near = {}
                for qt in vis:
                    if kind == "o" and kt == qt:
                        near[qt] = 0
                    elif kind == "o" and kt == qt - 1:
                        near[qt] = 1
                    elif kind == "p" and kt == 7 and qt == 0:
                        near[qt] = 1
                far = [qt for qt in vis if qt not in near]
                if far:
                    o0 = (far[0] - vis[0]) * 128
                    o1 = (far[-1] - vis[0] + 1) * 128
                    kb.op('act', lambda e: e.activation(out=pt3[:, :, o0:o1], in_=ps3[:, :, o0:o1], func=AF.Exp,
                                                        scale=scale, bias=bcol),
                          reads=[("PB", bsx), bkey], writes=[ptk])
                for qt, which in near.items():
                    o0 = (qt - vis[0]) * 128
                    if kind == "p":
                        kb.op('act', lambda e: e.activation(out=pt3[:, :, o0:o0 + 128], in_=ps3[:, :, o0:o0 + 128],
                                                            func=AF.Exp, scale=scale, bias=pcol(OFF_NEGB)),
                              reads=[("PB", bsx), "PC"], writes=[ptk])
                    else:
                        kb.op('act', lambda e: e.activation(out=pt3[:, :, o0:o0 + 128], in_=ps3[:, :, o0:o0 + 128],
                                                            func=AF.Exp, scale=scale),
                              reads=[("PB", bsx)], writes=[ptk])
                    kb.op('dve', lambda e: e.tensor_tensor(out=pt3[:, :, o0:o0 + 128], in0=pt3[:, :, o0:o0 + 128],
                                                           in1=ED[:, hh, which, :].unsqueeze(1).to_broadcast([P, 2, 128]),
                                                           op=ALU.mult),
                          reads=[ptk, "ED"], writes=[ptk])

            def emit_PV(i, st):
                hh, kind, kt, vis, qt2 = st["hh"], st["kind"], st["kt"], st["vis"], st["qt2"]
                pt = PT[i % 6]
                ptk = ("PT", i % 6)
                vsrc = V1P if kind == "p" else V1
                vkey = ["KVP"] if kind == "p" else [("V1", kt, hh // 2), "V1one"]
                for j in range(2):
                    for qt in vis:
                        o0 = j * 256 + (qt - vis[0]) * 128
                        ab = (qt - qt2[0]) * 2 + j
                        kb.op('pe', lambda e: e.matmul(PB[ab][:, 0:257], lhsT=pt[:, o0:o0 + 128], rhs=vsrc[:, kt, hh, :],
                                                       start=st["first"][qt], stop=st["last"][qt]),
                              reads=[ptk] + vkey, writes=[("PB", ab)], signal=True)

            fin_cnt = [0]

            def emit_fin_a(st):
                hh, qt2 = st["hh"], st["qt2"]
                deferred = []
                for qt in qt2:
                    for j in range(2):
                        ab = (qt - qt2[0]) * 2 + j
                        kb.op('dve', lambda e: e.tensor_copy(out=OA[:, j, 0:257], in_=PB[ab][:, 0:257]),
                              reads=[("PB", ab)], writes=[("OA", j)])
                    db = DBs[fin_cnt[0] % 2]
                    dbk = ("DB", fin_cnt[0] % 2)
                    fin_cnt[0] += 1
                    kb.op('dve', lambda e: e.reciprocal(out=RD[:, 0:1], in_=OA[:, 0, 256:257]), reads=[("OA", 0)], writes=["RD0"])
                    kb.op('dve', lambda e: e.reciprocal(out=RD[:, 1:2], in_=OA[:, 1, 256:257]), reads=[("OA", 1)], writes=["RD1"])
                    kb.op('dve', lambda e: e.tensor_tensor(out=RD[:, 2:3], in0=RD[:, 1:2], in1=LS[:, 5:6], op=ALU.mult),
                          reads=["RD1", "LAM"], writes=["RD2"])
                    kb.op('dve', lambda e: e.tensor_scalar_mul(out=DO[:, :], in0=OA[:, 0, 0:256], scalar1=RD[:, 0:1]),
                          reads=[("OA", 0), "RD0"], writes=["DO"])
                    kb.op('dve', lambda e: e.scalar_tensor_tensor(out=DO[:, :], in0=OA[:, 1, 0:256], scalar=RD[:, 2:3],
                                                                  in1=DO[:, :], op0=ALU.mult, op1=ALU.add),
                          reads=[("OA", 1), "RD2", "DO"], writes=["DO"])
                    kb.op('act', lambda e: e.activation(out=DJ[:, :], in_=DO[:, :], func=AF.Square, accum_out=RD[:, 3:4]),
                          reads=["DO"], writes=["DJ", "RD3"])
                    kb.op('act', lambda e: e.activation(out=SS[:, 38:39], in_=RD[:, 3:4], func=AF.Sqrt, scale=1.0 / 256, bias=EPS),
                          reads=["RD3"], writes=["SD1"])
                    kb.op('dve', lambda e: e.reciprocal(out=SS[:, 39:40], in_=SS[:, 38:39]), reads=["SD1"], writes=["SD2"])
                    kb.op('dve', lambda e: e.tensor_scalar(out=db[:, :], in0=DO[:, :], scalar1=SS[:, 39:40],
                                                           scalar2=(1.0 - lam_init), op0=ALU.mult, op1=ALU.mult),
                          reads=["DO", "SD2"], writes=[dbk])
                    deferred.append((hh, qt, db, dbk))
                return deferred

            def emit_fin_b(hh, qt, db, dbk):
                bt = 7
                pbt = PB[bt][:, :].bitcast(BF16).rearrange("p (a b) -> p a b", a=8)
                kb.op('pe', lambda e: e.transpose(out=pbt[:, 0, :], in_=db[:, 0:128], identity=IDB[:, :]),
                      reads=[dbk, "IDB"], writes=[("PB", bt)], signal=False)
                kb.op('pe', lambda e: e.transpose(out=pbt[:, 1, :], in_=db[:, 128:256], identity=IDB[:, :]),
                      reads=[dbk, "IDB"], writes=[("PB", bt)])
                for i2 in range(2):
                    kc = 8 + hh * 2 + i2
                    gcol = pcol(OFF_SUB + o_idx * 2 + i2)
                    dst = MT[:, kc, qt * 128:(qt + 1) * 128]
                    kb.op('dve', lambda e: e.tensor_scalar_mul(out=dst, in0=pbt[:, i2, :], scalar1=gcol),
                          reads=[("PB", bt), "PC"], writes=[("MT", kc, qt)])

            pending_fin = []
            nsteps = len(steps)
            for idx in range(nsteps + LOOK):
                if idx < nsteps:
                    emit_S(idx, steps[idx])
                while pending_fin and pending_fin[0][0] <= idx:
                    emit_fin_b(*pending_fin.pop(0)[1])
                pi_ = idx - LOOK
                if pi_ >= 0:
                    emit_PV(pi_, steps[pi_])
                    if steps[pi_].get("gend"):
                        for k2, args in enumerate(emit_fin_a(steps[pi_])):
                            pending_fin.append((idx + 4 + 2 * k2, args))
            while pending_fin:
                emit_fin_b(*pending_fin.pop(0)[1])
            chk("o_e")
            kb.barrier()
            for t in range(NT):
                kb.dma('sp', [(H[:, t, :], hspill[t * 128:(t + 1) * 128, :])], reads=[("HSP", t)], writes=[("H", t)])
            out_proj([wout[:, cb * 512:(cb + 1) * 512] for cb in range(4)], KC, key=("wo", "o", o_idx))

        order_last_first = [NT - 1] + list(range(NT - 1))
        for l in range(n_layers):
          try:
            last = (l == n_layers - 1)
            if l % 2 == 0:
                prefetch(("eu", l // 2), [(0, w_in_even[l // 2][:, 0:512])])
            else:
                for t in range(NT):
                    kb.dma('sp', [(hspill[t * 128:(t + 1) * 128, :], H[:, t, :])], reads=[("H", t)], writes=[("HSP", t)])
                for c in range(2):
                    prefetch(("ofm", l // 2, "KT", c), [(0, w_in_odd[l // 2][:, 3072 + c * 512:3072 + (c + 1) * 512])])
            norm_phase(l * 2, list(range(NT)), halo=False, first=(l == 0))
            if last and stop_after == "norm0":
                break
            if l % 2 == 0:
                even_mixer(l // 2)
            else:
                odd_mixer(l // 2, l)
            if last and stop_after and stop_after.startswith("mix"):
                break
            prefetch(("up", l, 0), [(0, w_up[l][:, 0:256]), (256, w_up[l][:, DFF:DFF + 256])])
            norm_phase(l * 2 + 1, order_last_first, halo=True)
            if last and stop_after == "norm1":
                break
            ffn(l)
          except StopBuild:
            break

        kb.barrier()
        ar.reset()
        FG = ar.alloc([P, D], F32)
        JK = ar.alloc([P, D], BF16)
        YO = [ar.alloc([P, D], F32), ar.alloc([P, D], F32)]
        kb.dma('sp', [(FG[:, :], p_fng[:, :])], writes=["FG"])
        for t in range(NT):
            yo = YO[t % 2]
            yk = ("YO", t % 2)
            if dbg_h:
                kb.dma('sp', [(out_d[t * 128:(t + 1) * 128, :], H[:, t, :])], reads=[("H", t)], writes=[("OUT", t)])
                continue
            kb.op('act', lambda e: e.activation(out=JK[:, :], in_=H[:, t, :], func=AF.Square, accum_out=SS[:, t:t + 1]),
                  reads=[("H", t)], writes=["JK", ("SS", t)])
            kb.op('act', lambda e: e.activation(out=SS[:, 8 + t:9 + t], in_=SS[:, t:t + 1], func=AF.Sqrt, scale=1.0 / D, bias=EPS),
                  reads=[("SS", t)], writes=[("SS", 8 + t)])
            kb.op('dve', lambda e: e.reciprocal(out=SS[:, 16 + t:17 + t], in_=SS[:, 8 + t:9 + t]),
                  reads=[("SS", 8 + t)], writes=[("SS", 16 + t)])
            kb.op('dve', lambda e: e.scalar_tensor_tensor(out=yo[:, :], in0=H[:, t, :], scalar=SS[:, 16 + t:17 + t],
                                                          in1=FG[:, :], op0=ALU.mult, op1=ALU.mult),
                  reads=[("H", t), ("SS", 16 + t), "FG"], writes=[yk])
            kb.dma('sp', [(out_d[t * 128:(t + 1) * 128, :], yo[:, :])], reads=[yk], writes=[("OUT", t)])
        kb.barrier()
        print("ops emitted:", kb.nops, "sems:", len(kb.sems), flush=True)
    return nc


def _t5_bucket(n):
    n = np.maximum(n, 0)
    max_exact = 16
    nn = np.maximum(n, 1).astype(np.float32)
    large = max_exact + (np.log(nn / max_exact) / math.log(128 / max_exact) * (32 - max_exact)).astype(np.int32)
    large = np.minimum(large, 31)
    return np.where(n < max_exact, n, large)


def _consts(half):
    c = {}
    c["c_ident"] = np.eye(128, dtype=np.float32)
    half_d = 64
    inv = 1.0 / (10000.0 ** (np.arange(half_d, dtype=np.float32) / half_d))
    pos = (half * T + np.arange(T, dtype=np.float32))
    ang = pos[:, None] * inv[None, :]
    cos = np.cos(ang).astype(np.float32).reshape(NT, 128, 64).transpose(1, 0, 2)
    sin = np.sin(ang).astype(np.float32).reshape(NT, 128, 64).transpose(1, 0, 2)
    c["c_rot"] = np.ascontiguousarray(np.stack([cos, sin], axis=1))
    log_g = np.log(1.0 - 2.0 ** (-5.0 - np.arange(6, dtype=np.float32))).astype(np.float32)
    idx = np.arange(128, dtype=np.float32)
    diff = idx[:, None] - idx[None, :]
    intra = np.where(diff >= 0, np.exp(log_g[:, None, None] * np.maximum(diff, 0.0)), 0.0)
    intraT = intra.transpose(2, 0, 1) * (128.0 ** -0.5)
    c["c_intra"] = np.ascontiguousarray(intraT.astype(np.float32))
    c["qdec"] = np.exp(log_g[None, :] * (idx[:, None] + 1.0)).astype(np.float32)
    c["kdec"] = (np.exp(log_g[None, :] * (127.0 - idx[:, None])) * (128.0 ** -0.5)).astype(np.float32)
    p = np.arange(128)
    c["c_mask"] = (p[None, :] >= p[:, None]).astype(np.float32)
    pcorr = np.ones((4, 16), np.float32)
    if half == 0:
        for g in range(4):
            w = 2 ** (g + 1)
            tt = np.arange(16) + 1
            pcorr[g] = w / np.minimum(tt, w)
    c["pcorr"] = pcorr
    return c


def _prep(inputs):
    f = lambda a: np.ascontiguousarray(np.asarray(a, dtype=np.float32))
    ins = {k: f(v) for k, v in inputs.items()}

    def cols(v):
        return v.reshape(-1, 128).T

    shared = {}
    for k in ("w_in_even", "w_out_even", "pool_w", "w_in_odd", "w_out_odd", "w_up", "w_down"):
        shared[k] = ins[k]
    pc = np.zeros((128, NCOLS), np.float32)
    for l in range(4):
        pc[:, OFF_NG + (2 * l) * 16:OFF_NG + (2 * l + 1) * 16] = cols(ins["mix_norm_g"][l])
        pc[:, OFF_NG + (2 * l + 1) * 16:OFF_NG + (2 * l + 2) * 16] = cols(ins["ffn_norm_g"][l])
        for r in range(3):
            pc[:, OFF_CW + l * 264 + r * 88:OFF_CW + l * 264 + (r + 1) * 88] = cols(ins["conv_w"][l, r])
        pc[:, OFF_CB + l * 88:OFF_CB + (l + 1) * 88] = cols(ins["conv_b"][l])
    for e in range(2):
        pc[:, OFF_PSC + e * 4:OFF_PSC + (e + 1) * 4] = cols(ins["pool_scale"][e])
        pc[:, OFF_GNG + e * 12:OFF_GNG + (e + 1) * 12] = cols(ins["ret_gn_g"][e])
        pc[:, OFF_SUB + e * 2:OFF_SUB + (e + 1) * 2] = cols(ins["diff_subln_g"][e])
        pc[:, OFF_SB + e * 8:OFF_SB + (e + 1) * 8] = ins["sgu_b"][e].T
    pc[:, OFF_C31:OFF_C31 + 4] = np.broadcast_to(ins["rel_bias"][31][None, :], (128, 4))
    shared["p_fng"] = np.ascontiguousarray(np.broadcast_to(ins["final_norm_g"][None, :], (128, D)))
    shared["p_sln"] = np.ascontiguousarray(np.broadcast_to(ins["sgu_ln_g"][None, :, :], (128, 2, 1024)))
    lam = np.stack([ins["lam_q1"], ins["lam_k1"], ins["lam_q2"], ins["lam_k2"]], axis=1)
    shared["p_lam"] = np.ascontiguousarray(np.broadcast_to(lam[None], (128, 2, 4, 128)))
    j = np.arange(128)[:, None]
    i = np.arange(128)[None, :]
    b0 = _t5_bucket(i - j)
    b1 = _t5_bucket(128 + i - j)
    rb = ins["rel_bias"]
    biasT = np.stack([rb[b0], rb[b1]], axis=0)
    shared["p_biasT"] = np.ascontiguousarray(biasT.transpose(1, 3, 0, 2))
    shared["p_sguT"] = np.ascontiguousarray(ins["sgu_w"].transpose(0, 3, 1, 2))
    maps = []
    for core in range(NCORES):
        b, half = core // 2, core % 2
        c = _consts(half)
        m = dict(shared)
        m["x"] = np.ascontiguousarray(ins["x"][b, half * T:(half + 1) * T, :])
        pcc = pc.copy()
        pcc[:, OFF_QDEC:OFF_QDEC + 6] = c["qdec"]
        pcc[:, OFF_KDEC:OFF_KDEC + 6] = c["kdec"]
        pcc[:, OFF_FLAG] = float(half)
        pcc[:, OFF_NEGB] = 0.0 if half == 1 else NEGBIG
        pcc[:, OFF_PCORR:OFF_PCORR + 64] = c["pcorr"].reshape(1, 64)
        m["p_cols"] = pcc
        for k in ("c_ident", "c_rot", "c_intra", "c_mask"):
            m[k] = c[k]
        maps.append(m)
    return maps


_NC_CACHE = {}


def kernel(**inputs):
    maps = _prep(inputs)
    if "nc" not in _NC_CACHE:
        _NC_CACHE["nc"] = build()
    nc = _NC_CACHE["nc"]
    res = run_bass_kernel_spmd(nc, maps, core_ids=list(range(NCORES)))
    out = np.zeros((4, 2 * T, D), np.float32)
    for core in range(NCORES):
        b, half = core // 2, core % 2
        out[b, half * T:(half + 1) * T, :] = np.asarray(res.results[core]["out"], dtype=np.float32)
    return out
```

```python
import math
from contextlib import ExitStack
import numpy as np
import concourse.bass as bass
import concourse.mybir as mybir
from concourse.bass_utils import run_bass_kernel_spmd

F32 = mybir.dt.float32
BF16 = mybir.dt.bfloat16
AF = mybir.ActivationFunctionType
ALU = mybir.AluOpType

P = 128
T = 1024
NT = 8
D = 2048
KC = 16
DFF = 5632
NFF = 44
EPS = 1e-6
NCORES = 8
PAIRS = [[0, 1], [2, 3], [4, 5], [6, 7]]
NEGBIG = -30000.0
GELU = AF.Gelu

OFF_NG = 0
OFF_PSC = OFF_NG + 128
OFF_GNG = OFF_PSC + 8
OFF_SUB = OFF_GNG + 24
OFF_CW = OFF_SUB + 4
OFF_CB = OFF_CW + 1056
OFF_SB = OFF_CB + 352
OFF_C31 = OFF_SB + 16
OFF_QDEC = OFF_C31 + 4
OFF_KDEC = OFF_QDEC + 6
OFF_FLAG = OFF_KDEC + 6
OFF_NEGB = OFF_FLAG + 1
OFF_PCORR = OFF_NEGB + 1
NCOLS = OFF_PCORR + 64


class KB:
    def __init__(self, nc):
        self.nc = nc
        self.eng = {'pe': nc.tensor, 'act': nc.scalar, 'dve': nc.vector, 'pool': nc.gpsimd, 'sp': nc.sync}
        self.sems = []
        self.cur = {}
        self.cnt = {}
        self.waited = {e: {} for e in self.eng}
        self.own = {e: set() for e in self.eng}
        self.track = {}
        self.pending = {e: ([], []) for e in self.eng}
        self.dpool = {}
        self.didx = {}
        self.ccsid = None
        self.cctot = 0
        self.nops = 0

    def new_sem(self, name):
        h = self.nc.alloc_semaphore(name=name)
        self.sems.append(h)
        return len(self.sems) - 1

    def _deps(self, reads, writes, e=None):
        evs = []
        own = self.own.get(e, ())
        for k in reads:
            t = self.track.get(k)
            if t and t[0]:
                evs.append(t[0])
            if t and isinstance(k, tuple) and k[0] == "PB":
                for sid, val in t[1].items():
                    if sid not in own:
                        evs.append((sid, val))
        for k in writes:
            t = self.track.get(k)
            if t:
                if t[0]:
                    evs.append(t[0])
                evs.extend(t[1].items())
        return evs

    def _wait(self, e, evs):
        need = {}
        for sid, val in evs:
            if need.get(sid, 0) < val:
                need[sid] = val
        w = self.waited[e]
        for sid, val in need.items():
            if w.get(sid, 0) < val:
                self.eng[e].wait_ge(self.sems[sid], val)
                w[sid] = val

    def _commit(self, e, ev):
        pr, pw = self.pending[e]
        for k in pr:
            t = self.track.setdefault(k, [None, {}])
            if t[1].get(ev[0], 0) < ev[1]:
                t[1][ev[0]] = ev[1]
        for k in pw:
            self.track[k] = [ev, {}]
        pr.clear()
        pw.clear()

    def op(self, e, fn, reads=(), writes=(), signal=True):
        self._wait(e, self._deps(reads, writes, e))
        inst = fn(self.eng[e])
        self.nops += 1
        pr, pw = self.pending[e]
        pr.extend(reads)
        pw.extend(writes)
        if not signal:
            return None
        if e not in self.cur or self.cnt[e] >= 6000:
            self.cur[e] = self.new_sem("s_%s_%d" % (e, len(self.sems)))
            self.own[e].add(self.cur[e])
            self.cnt[e] = 0
        self.cnt[e] += 1
        inst.then_inc(self.sems[self.cur[e]], 1)
        ev = (self.cur[e], self.cnt[e])
        self._commit(e, ev)
        return ev

    def dma(self, e, pairs, reads=(), writes=()):
        if e not in self.dpool:
            self.dpool[e] = [[self.new_sem("d_%s_%d" % (e, i)), 0] for i in range(6)]
            self.didx[e] = 0
        slot = self.dpool[e][self.didx[e] % len(self.dpool[e])]
        self.didx[e] += 1
        evs = self._deps(reads, writes)
        if slot[1] > 0:
            evs.append((slot[0], slot[1]))
        self._wait(e, evs)
        for (o, i) in pairs:
            self.eng[e].dma_start(out=o, in_=i).then_inc(self.sems[slot[0]], 16)
            slot[1] += 16
            self.nops += 1
        ev = (slot[0], slot[1])
        pr, pw = self.pending[e]
        for k in reads:
            t = self.track.setdefault(k, [None, {}])
            if t[1].get(ev[0], 0) < ev[1]:
                t[1][ev[0]] = ev[1]
        for k in writes:
            self.track[k] = [ev, {}]
        return ev

    def allgather(self, src, dst, reads=(), writes=()):
        e = 'pool'
        if self.ccsid is None:
            self.ccsid = self.new_sem("ccsem")
        evs = self._deps(reads, writes)
        if self.cctot > 0:
            evs.append((self.ccsid, self.cctot))
        self._wait(e, evs)
        inst = self.nc.gpsimd.collective_compute("AllGather", ALU.bypass, replica_groups=self.pairs,
                                                 ins=[src], outs=[dst])
        inst.then_inc(self.sems[self.ccsid])
        self.cctot += 1
        ev = (self.ccsid, self.cctot)
        for k in reads:
            t = self.track.setdefault(k, [None, {}])
            t[1][ev[0]] = ev[1]
        for k in writes:
            self.track[k] = [ev, {}]
        return ev

    def barrier(self):
        evs = []
        for e2 in self.cur:
            evs.append((self.cur[e2], self.cnt[e2]))
        for e2 in self.dpool:
            if e2 == 'pool':
                continue
            for sid, tot in self.dpool[e2]:
                if tot:
                    evs.append((sid, tot))
        if self.cctot:
            evs.append((self.ccsid, self.cctot))
        for e in self.eng:
            self._wait(e, evs)


class StopBuild(Exception):
    pass


class Arena:
    def __init__(self, ap_f32, nbytes):
        self.ap = ap_f32
        self.n = nbytes
        self.off = 0

    def reset(self, off=0):
        self.off = off

    def alloc(self, shape, dt):
        esz = 4 if dt == F32 else 2
        free = 1
        for s in shape[1:]:
            free *= s
        nb = (free * esz + 31) // 32 * 32
        assert self.off + nb <= self.n, ("arena overflow", self.off, nb, self.n)
        v = self.ap[:, self.off // 4:(self.off + nb) // 4]
        self.off += nb
        if dt != F32:
            v = v.bitcast(dt)
        v = v[:, 0:free]
        if len(shape) == 3:
            v = v.rearrange("p (a b) -> p a b", a=shape[1])
        elif len(shape) == 4:
            v = v.rearrange("p (a b c) -> p a b c", a=shape[1], b=shape[2])
        return v


def build(n_layers=4, dbg_h=False, stop_after=None, ncores=NCORES):
    nc = bass.Bass("TRN2", target_bir_lowering=False)

    def din(name, shape):
        return nc.dram_tensor(name, list(shape), F32, kind="ExternalInput").ap()

    x_d = din("x", [T, D])
    n_ev = max(1, (n_layers + 1) // 2)
    n_od = max(1, n_layers // 2)
    n_ff = max(1, n_layers)
    w_in_even = din("w_in_even", [n_ev, D, 5120])
    w_out_even = din("w_out_even", [n_ev, D, D])
    pool_w = din("pool_w", [2, 4, 128, 128])
    w_in_odd = din("w_in_odd", [n_od, D, 5120])
    w_out_odd = din("w_out_odd", [n_od, D, D])
    w_up = din("w_up", [n_ff, D, 2 * DFF])
    w_down = din("w_down", [n_ff, DFF, D])
    c_ident = din("c_ident", [128, 128])
    c_rot = din("c_rot", [128, 2, 8, 64])
    c_intra = din("c_intra", [128, 6, 128])
    c_mask = din("c_mask", [128, 128])
    p_cols = din("p_cols", [128, NCOLS])
    p_fng = din("p_fng", [128, D])
    p_sln = din("p_sln", [128, 2, 1024])
    p_lam = din("p_lam", [128, 2, 4, 128])
    p_biasT = din("p_biasT", [128, 4, 2, 128])
    p_sguT = din("p_sguT", [2, 128, 8, 128])
    out_d = nc.dram_tensor("out", [T, D], F32, kind="ExternalOutput").ap()

    hspill = nc.dram_tensor("hspill", [T, D], F32).ap()
    xu_src = nc.dram_tensor("xu_src", [128, 64], F32).ap()
    xu_dst = nc.dram_tensor("xu_dst", [256, 64], F32).ap()
    st_src = nc.dram_tensor("st_src", [128, 256], F32).ap()
    st_dst = nc.dram_tensor("st_dst", [256, 256], F32).ap()
    xh_src = nc.dram_tensor("xh_src", [128, 16], F32).ap()
    xh_dst = nc.dram_tensor("xh_dst", [256, 16], F32).ap()
    KVW = 4096 + 4112
    KVQ = KVW // 4
    kv_src = [nc.dram_tensor("kv_src%d" % i, [128, KVQ], F32).ap() for i in range(4)]
    kv_dst = [nc.dram_tensor("kv_dst%d" % i, [256, KVQ], F32).ap() for i in range(4)]

    ARENA_BYTES = 40 * 1024
    with ExitStack() as es:
        def sb(name, shape, dt):
            return es.enter_context(nc.sbuf_tensor(name, shape, dt))

        H = sb("H", [P, NT, D], F32)
        XT = sb("XT", [P, KC, T + 2], BF16)
        MT = sb("MT", [P, KC, T], BF16)
        WB = [sb("WB0", [P, KC, 512], BF16), sb("WB1", [P, KC, 512], BF16)]
        PC = sb("PC", [P, NCOLS], F32)
        IDB = sb("IDB", [P, 128], BF16)
        MASK = sb("MASK", [P, 128], F32)
        SS = sb("SS", [P, 48], F32)
        ARN = sb("ARN", [P, ARENA_BYTES // 4], F32)
        PB = [es.enter_context(nc.psum_tensor("pb%d" % i, [P, 512], F32)) for i in range(8)]
        ar = Arena(ARN, ARENA_BYTES)
        kb = KB(nc)
        kb.pairs = [[2 * i, 2 * i + 1] for i in range(ncores // 2)]

        def chk(name):
            if stop_after == name:
                raise StopBuild()

        def pcol(off, n=1):
            return PC[:, off:off + n]

        kb.dma('sp', [(PC[:, :], p_cols[:, :])], writes=["PC"])
        kb.dma('sp', [(MASK[:, :], c_mask[:, :])], writes=["MASK"])
        kb.dma('pool', [(IDB[:, :], c_ident[:, :])], writes=["IDB"])
        for t in range(NT):
            kb.dma('sp', [(H[:, t, :], x_d[t * 128:(t + 1) * 128, :])], writes=[("H", t)])

        bank_rr = [0]

        def bank():
            b = bank_rr[0] % 8
            bank_rr[0] += 1
            return b

        wslot = [0]

        preloaded = {}

        def prefetch(key, pieces, nk=KC):
            preloaded[key] = load_w(pieces, nk)

        def load_w(pieces, nk=KC, key=None):
            if key is not None and key in preloaded:
                return preloaded.pop(key)
            s = wslot[0] % 2
            wslot[0] += 1
            pairs = []
            for (co, src) in pieces:
                ncols = src.shape[1]
                v = src.rearrange("(kc p) c -> p kc c", p=128)
                step = 4 if ncols * 4 >= 2048 else 8
                for k0 in range(0, nk, step):
                    k1 = min(nk, k0 + step)
                    pairs.append((WB[s][:, k0:k1, co:co + ncols], v[:, k0:k1, :]))
            kb.dma('pool', pairs, writes=[("WB", s)])
            return s

        def norm_phase(gidx, order, halo, first=False):
            if not first:
                kb.barrier()
            ar.reset()
            JK = ar.alloc([P, D], BF16)
            HN = [ar.alloc([P, D], BF16) for _ in range(3)]
            XH = ar.alloc([P, 32], BF16)
            XHF = ar.alloc([P, 16], F32)

            def stA_act(oi, t):
                kb.op('act', lambda e: e.activation(out=JK[:, :], in_=H[:, t, :], func=AF.Square,
                                                    accum_out=SS[:, t:t + 1]),
                      reads=[("H", t)], writes=["JK", ("SS", t)])
                kb.op('act', lambda e: e.activation(out=SS[:, 8 + t:9 + t], in_=SS[:, t:t + 1], func=AF.Sqrt,
                                                    scale=1.0 / D, bias=EPS),
                      reads=[("SS", t)], writes=[("SS", 8 + t)])

            def stA_dve(oi, t):
                hn = HN[oi % 3]
                kb.op('dve', lambda e: e.reciprocal(out=SS[:, 16 + t:17 + t], in_=SS[:, 8 + t:9 + t]),
                      reads=[("SS", 8 + t)], writes=[("SS", 16 + t)])
                kb.op('dve', lambda e: e.tensor_scalar_mul(out=hn[:, :], in0=H[:, t, :],
                                                           scalar1=SS[:, 16 + t:17 + t]),
                      reads=[("H", t), ("SS", 16 + t)], writes=[("HN", oi % 3)])

            def stB(oi, t):
                hn = HN[oi % 3]
                hk = ("HN", oi % 3)
                for half in range(2):
                    b = bank()
                    pbb = PB[b][:, :].bitcast(BF16).rearrange("p (a b) -> p a b", a=8)
                    for i in range(8):
                        kc = half * 8 + i
                        kb.op('pe', lambda e: e.transpose(out=pbb[:, i, :], in_=hn[:, kc * 128:(kc + 1) * 128],
                                                          identity=IDB[:, :]),
                              reads=[hk, "IDB"], writes=[("PB", b)], signal=(i == 7))
                    for i in range(8):
                        kc = half * 8 + i
                        g = pcol(OFF_NG + gidx * 16 + kc)
                        dst = XT[:, kc, 2 + t * 128:2 + (t + 1) * 128]
                        if half == 0:
                            kb.op('act', lambda e: e.activation(out=dst, in_=pbb[:, i, :], func=AF.Copy, scale=g),
                                  reads=[("PB", b), "PC"], writes=[("XT", t)])
                        else:
                            kb.op('dve', lambda e: e.tensor_scalar_mul(out=dst, in0=pbb[:, i, :], scalar1=g),
                                  reads=[("PB", b), "PC"], writes=[("XT", t)])

            n = len(order)
            stA_act(0, order[0])
            stA_dve(0, order[0])
            for oi, t in enumerate(order):
                if oi + 1 < n:
                    stA_act(oi + 1, order[oi + 1])
                stB(oi, t)
                if oi + 1 < n:
                    stA_dve(oi + 1, order[oi + 1])
                if halo and t == NT - 1:
                    kb.op('act', lambda e: e.activation(out=XH[:, :].rearrange("p (a b) -> p a b", a=16),
                                                        in_=XT[:, :, T:T + 2], func=AF.Copy),
                          reads=[("XT", t)], writes=["XH"])
                    kb.dma('sp', [(xh_src[:, :], XH[:, :].bitcast(F32))], reads=["XH"], writes=["xh_src"])
                    kb.allgather(xh_src[:, :], xh_dst[:, :], reads=["xh_src"], writes=["xh_dst"])
                    kb.dma('sp', [(XHF[:, :], xh_dst[0:128, :])], reads=["xh_dst"], writes=["XHF"])
            if halo:
                kb.op('dve', lambda e: e.tensor_scalar_mul(
                    out=XT[:, :, 0:2], in0=XHF[:, :].bitcast(BF16).rearrange("p (a b) -> p a b", a=16),
                    scalar1=pcol(OFF_FLAG)), reads=["XHF", "PC"], writes=["XTH"])

        def out_proj(chunks, nk, key=None):
            slots = [load_w([(0, chunks[0])], nk, key=(key, 0) if key else None)]
            for cb in range(4):
                if cb + 1 < 4:
                    slots.append(load_w([(0, chunks[cb + 1])], nk, key=(key, cb + 1) if key else None))
                s = slots[cb]
                for t in range(NT):
                    b = bank()
                    for kc in range(nk):
                        kb.op('pe', lambda e: e.matmul(PB[b][:, :], lhsT=MT[:, kc, t * 128:(t + 1) * 128],
                                                       rhs=WB[s][:, kc, :], start=(kc == 0), stop=(kc == nk - 1)),
                              reads=[("MT", kc, t), ("WB", s)], writes=[("PB", b)], signal=(kc == nk - 1))
                    hs = H[:, t, cb * 512:(cb + 1) * 512]
                    kb.op('dve', lambda e: e.tensor_tensor(out=hs, in0=hs, in1=PB[b][:, :], op=ALU.add),
                          reads=[("PB", b), ("H", t)], writes=[("H", t)])

        def ffn(l):
            kb.barrier()
            ar.reset()
            AG = [ar.alloc([P, 514], F32), ar.alloc([P, 514], F32)]
            AV = [ar.alloc([P, 514], F32), ar.alloc([P, 514], F32)]
            CG = [ar.alloc([P, 512], F32), ar.alloc([P, 512], F32)]
            CV = [ar.alloc([P, 512], F32), ar.alloc([P, 512], F32)]
            SG = [ar.alloc([P, 512], F32), ar.alloc([P, 512], F32)]
            wup = w_up[l]
            wdn = w_down[l]
            groups = [(0, 12), (12, 12), (24, 10), (34, 10)]
            PW3 = 342

            def up_pieces(j0):
                return [(0, wup[:, j0 * 128:(j0 + 2) * 128]), (256, wup[:, DFF + j0 * 128:DFF + (j0 + 2) * 128])]

            pair_list = []
            for (g0, gn) in groups:
                for j0 in range(g0, g0 + gn, 2):
                    pair_list.append(j0)
            nxt = None
            it = 0
            for (g0, gn) in groups:
                for j0 in range(g0, g0 + gn, 2):
                    s = nxt if nxt is not None else load_w(up_pieces(j0), key=("up", l, j0))
                    nxt = None
                    if j0 + 2 < g0 + gn:
                        nxt = load_w(up_pieces(j0 + 2))
                    for jj in range(2):
                        j = j0 + jj
                        jl = j - g0
                        wg = lambda kc: WB[s][:, kc, jj * 128:(jj + 1) * 128]
                        wv = lambda kc: WB[s][:, kc, 256 + jj * 128:256 + (jj + 1) * 128]
                        cw = lambda r: pcol(OFF_CW + l * 264 + r * 88 + j)
                        cbias = pcol(OFF_CB + l * 88 + j)
                        for n in range(3):
                            q = it % 2
                            it += 1
                            bg = bank()
                            bv = bank()
                            c0 = n * PW3
                            tlo = max(0, c0 - 2) // 128
                            thi = (c0 + PW3 - 3) // 128
                            xt_keys = [("XT", t) for t in range(tlo, thi + 1)] + (["XTH"] if n == 0 else [])
                            for kc in range(KC):
                                kb.op('pe', lambda e: e.matmul(PB[bg][:, 0:PW3], lhsT=wg(kc), rhs=XT[:, kc, c0:c0 + PW3],
                                                               start=(kc == 0), stop=(kc == KC - 1)),
                                      reads=xt_keys + [("WB", s)], writes=[("PB", bg)], signal=(kc == KC - 1))
                            for kc in range(KC):
                                kb.op('pe', lambda e: e.matmul(PB[bv][:, 0:PW3], lhsT=wv(kc), rhs=XT[:, kc, c0:c0 + PW3],
                                                               start=(kc == 0), stop=(kc == KC - 1)),
                                      reads=xt_keys + [("WB", s)], writes=[("PB", bv)], signal=(kc == KC - 1))
                            ag, av, cg, cv, sg = AG[q], AV[q], CG[q], CV[q], SG[q]
                            kag, kav, kcg, kcv, ksg = ("AG", q), ("AV", q), ("CG", q), ("CV", q), ("SG", q)
                            lo = 4 if n == 0 else 2
                            wo = PW3 + 2 - lo
                            tok0 = 0 if n == 0 else n * PW3 - 2
                            kb.op('act', lambda e: e.activation(out=ag[:, 2:2 + PW3], in_=PB[bg][:, 0:PW3], func=AF.Copy),
                                  reads=[("PB", bg)], writes=[kag])
                            kb.op('act', lambda e: e.activation(out=av[:, 2:2 + PW3], in_=PB[bv][:, 0:PW3], func=AF.Copy),
                                  reads=[("PB", bv)], writes=[kav])
                            if n > 0:
                                pq = 1 - q
                                kb.op('dve', lambda e: e.tensor_copy(out=ag[:, 0:2], in_=AG[pq][:, PW3:PW3 + 2]),
                                      reads=[("AG", pq)], writes=[kag])
                                kb.op('dve', lambda e: e.tensor_copy(out=av[:, 0:2], in_=AV[pq][:, PW3:PW3 + 2]),
                                      reads=[("AV", pq)], writes=[kav])
                            kb.op('act', lambda e: e.activation(out=cg[:, 0:wo], in_=PB[bg][:, lo - 2:lo - 2 + wo], func=AF.Identity,
                                                                scale=cw(2), bias=cbias),
                                  reads=[("PB", bg), "PC"], writes=[kcg])
                            kb.op('act', lambda e: e.activation(out=cv[:, 0:wo], in_=PB[bv][:, lo - 2:lo - 2 + wo], func=AF.Identity,
                                                                scale=CWV(l, 2, j), bias=CBV(l, j)),
                                  reads=[("PB", bv), "PC"], writes=[kcv])
                            kb.op('dve', lambda e: e.scalar_tensor_tensor(out=cg[:, 0:wo], in0=ag[:, lo - 1:lo - 1 + wo], scalar=cw(1),
                                                                          in1=cg[:, 0:wo], op0=ALU.mult, op1=ALU.add),
                                  reads=[kag, "PC", kcg], writes=[kcg])
                            kb.op('dve', lambda e: e.scalar_tensor_tensor(out=cg[:, 0:wo], in0=ag[:, lo - 2:lo - 2 + wo], scalar=cw(0),
                                                                          in1=cg[:, 0:wo], op0=ALU.mult, op1=ALU.add),
                                  reads=[kag, "PC", kcg], writes=[kcg])
                            kb.op('dve', lambda e: e.scalar_tensor_tensor(out=cv[:, 0:wo], in0=av[:, lo - 1:lo - 1 + wo],
                                                                          scalar=CWV(l, 1, j), in1=cv[:, 0:wo],
                                                                          op0=ALU.mult, op1=ALU.add),
                                  reads=[kav, "PC", kcv], writes=[kcv])
                            kb.op('dve', lambda e: e.scalar_tensor_tensor(out=cv[:, 0:wo], in0=av[:, lo - 2:lo - 2 + wo],
                                                                          scalar=CWV(l, 0, j), in1=cv[:, 0:wo],
                                                                          op0=ALU.mult, op1=ALU.add),
                                  reads=[kav, "PC", kcv], writes=[kcv])
                            kb.op('act', lambda e: e.activation(out=sg[:, 0:wo], in_=cg[:, 0:wo], func=AF.Silu),
                                  reads=[kcg], writes=[ksg])
                            kb.op('dve', lambda e: e.tensor_tensor(out=MT[:, jl, tok0:tok0 + wo], in0=sg[:, 0:wo],
                                                                   in1=cv[:, 0:wo], op=ALU.mult),
                                  reads=[ksg, kcv],
                                  writes=[("MT", jl, t) for t in range(tok0 // 128, (tok0 + wo - 1) // 128 + 1)])
                out_proj([wdn[g0 * 128:(g0 + gn) * 128, cb * 512:(cb + 1) * 512] for cb in range(4)], gn)
                if (g0, gn) != groups[-1]:
                    pass

        def CWV(l, r, j):
            return pcol(OFF_CW + l * 264 + r * 88 + NFF + j)

        def CBV(l, j):
            return pcol(OFF_CB + l * 88 + NFF + j)

        def even_mixer(e_idx):
            kb.barrier()
            ar.reset()
            win = w_in_even[e_idx]
            wout = w_out_even[e_idx]
            I0, I1, I2, I3 = 512, 512 + 768, 512 + 1536, 512 + 1536 + 1536
            ROT = ar.alloc([P, 2, 8, 64], F32)
            INTRA = ar.alloc([P, 6, 128], F32)
            U = ar.alloc([P, 4, 16 + T], BF16)
            PW = ar.alloc([P, 4, 128], BF16)
            UHS = ar.alloc([P, 4, 16], F32)
            UHR = ar.alloc([P, 4, 16], F32)
            mark = ar.off
            kb.dma('sp', [(ROT[:, :, :, :], c_rot[:, :, :, :])], writes=["ROT"])
            kb.dma('sp', [(INTRA[:, :, :], c_intra[:, :, :])], writes=["INTRA"])
            kb.dma('pool', [(PW[:, :, :], pool_w[e_idx].rearrange("g c d -> c g d"))], writes=["PW"])

            s = load_w([(0, win[:, 0:512])], key=("eu", e_idx))
            for g in range(4):
                for n in range(2):
                    b = bank()
                    for kc in range(KC):
                        kb.op('pe', lambda e: e.matmul(PB[b][:, :], lhsT=WB[s][:, kc, g * 128:(g + 1) * 128],
                                                       rhs=XT[:, kc, 2 + n * 512:2 + (n + 1) * 512],
                                                       start=(kc == 0), stop=(kc == KC - 1)),
                              reads=[("XT", t) for t in range(n * 4, n * 4 + 4)] + [("WB", s)],
                              writes=[("PB", b)], signal=(kc == KC - 1))
                    kb.op('act', lambda e: e.activation(out=U[:, g, 16 + n * 512:16 + (n + 1) * 512], in_=PB[b][:, :],
                                                        func=AF.Copy),
                          reads=[("PB", b)], writes=[("U", g, n)])
            kb.op('act', lambda e: e.activation(out=UHS[:, :, :], in_=U[:, :, T:T + 16], func=AF.Copy),
                  reads=[("U", g, 1) for g in range(4)], writes=["UHS"])
            kb.dma('sp', [(xu_src[:, :], UHS[:, :, :].rearrange("p a b -> p (a b)"))], reads=["UHS"], writes=["xu_src"])
            kb.allgather(xu_src[:, :], xu_dst[:, :], reads=["xu_src"], writes=["xu_dst"])
            kb.dma('sp', [(UHR[:, :, :].rearrange("p a b -> p (a b)"), xu_dst[0:128, :])], reads=["xu_dst"],
                   writes=["UHR"])

            if stop_after == "mix_a":
                return
            GS = ar.alloc([P, 8, 256], BF16)
            RSB = ar.alloc([P, 8, 256], BF16)
            QDT = ar.alloc([P, 8, 128], BF16)
            STATE = ar.alloc([P, 256], F32)
            STATEB = ar.alloc([P, 256], BF16)
            mark_r = ar.off
            NB3 = 4
            for h in range(6):
                dec = 1.0 - 2.0 ** (-5.0 - h)
                chunk_dec = dec ** 128
                kb.barrier()
                ar.reset(mark_r)
                QK = [ar.alloc([P, 2, 2, 64], BF16) for _ in range(NB3)]
                VB = [ar.alloc([P, 256], BF16) for _ in range(NB3)]
                QD = [ar.alloc([P, 128], BF16) for _ in range(NB3)]
                KD = [ar.alloc([P, 128], BF16) for _ in range(NB3)]
                QKT = [ar.alloc([P, 2, 128], BF16) for _ in range(2)]
                ST = [ar.alloc([P, 128], BF16) for _ in range(2)]
                R1 = ar.alloc([P, 2, 64], F32)
                R2 = ar.alloc([P, 2, 64], F32)
                R3 = ar.alloc([P, 2, 64], F32)
                R4 = ar.alloc([P, 2, 64], F32)
                def g_pieces(hx):
                    return [(0, win[:, I3 + hx * 256:I3 + (hx + 1) * 256])]

                def qkv_pieces(hx):
                    return [(0, win[:, I0 + hx * 128:I0 + (hx + 1) * 128]),
                            (128, win[:, I1 + hx * 128:I1 + (hx + 1) * 128]),
                            (256, win[:, I2 + hx * 256:I2 + (hx + 1) * 256])]

                sg_ = load_w(g_pieces(h), key=("eg", e_idx, h))
                sq_ = load_w(qkv_pieces(h), key=("eq", e_idx, h))
                kb.op('dve', lambda e: e.memset(STATE[:, :], 0.0), writes=["STATE"])
                kb.op('dve', lambda e: e.memset(STATEB[:, :], 0.0), writes=["STATEB"])

                def st_P(t):
                    b = t % 2
                    for kc in range(KC):
                        kb.op('pe', lambda e: e.matmul(PB[b][:, :], lhsT=XT[:, kc, 2 + t * 128:2 + (t + 1) * 128],
                                                       rhs=WB[sq_][:, kc, :], start=(kc == 0), stop=(kc == KC - 1)),
                              reads=[("XT", t), ("WB", sq_)], writes=[("PB", b)], signal=(kc == KC - 1))
                def st_ROT(t):
                    b = t % 2
                    q3 = t % NB3
                    qk, vb, qd, kd = QK[q3], VB[q3], QD[q3], KD[q3]
                    z4 = PB[b][:, 0:256].rearrange("p (a b c) -> p a b c", a=2, b=2)
                    x1 = z4[:, :, 0, :]
                    x2 = z4[:, :, 1, :]
                    cosb = ROT[:, 0, t, :].unsqueeze(1).to_broadcast([P, 2, 64])
                    sinb = ROT[:, 1, t, :].unsqueeze(1).to_broadcast([P, 2, 64])
                    kb.op('dve', lambda e: e.tensor_tensor(out=R1[:, :, :], in0=x1, in1=cosb, op=ALU.mult),
                          reads=[("PB", b), "ROT"], writes=["R1"])
                    kb.op('dve', lambda e: e.tensor_tensor(out=R2[:, :, :], in0=x2, in1=sinb, op=ALU.mult),
                          reads=[("PB", b), "ROT"], writes=["R2"])
                    kb.op('dve', lambda e: e.tensor_tensor(out=qk[:, :, 0, :], in0=R1[:, :, :], in1=R2[:, :, :],
                                                           op=ALU.subtract),
                          reads=["R1", "R2"], writes=[("QK0", q3)])
                    kb.op('dve', lambda e: e.tensor_tensor(out=R3[:, :, :], in0=x1, in1=sinb, op=ALU.mult),
                          reads=[("PB", b), "ROT"], writes=["R3"])
                    kb.op('dve', lambda e: e.tensor_tensor(out=R4[:, :, :], in0=x2, in1=cosb, op=ALU.mult),
                          reads=[("PB", b), "ROT"], writes=["R4"])
                    kb.op('dve', lambda e: e.tensor_tensor(out=qk[:, :, 1, :], in0=R3[:, :, :], in1=R4[:, :, :],
                                                           op=ALU.add),
                          reads=["R3", "R4"], writes=[("QK1", q3)])
                    kb.op('act', lambda e: e.activation(out=vb[:, :], in_=PB[b][:, 256:512], func=AF.Copy),
                          reads=[("PB", b)], writes=[("VB", q3)])
                    qf = qk[:, 0, :, :].rearrange("p a b -> p (a b)")
                    kf = qk[:, 1, :, :].rearrange("p a b -> p (a b)")
                    kb.op('act', lambda e: e.activation(out=qd[:, :], in_=qf, func=AF.Copy, scale=pcol(OFF_QDEC + h)),
                          reads=[("QK0", q3), ("QK1", q3), "PC"], writes=[("QD", q3)])
                    kb.op('act', lambda e: e.activation(out=kd[:, :], in_=kf, func=AF.Copy, scale=pcol(OFF_KDEC + h)),
                          reads=[("QK0", q3), ("QK1", q3), "PC"], writes=[("KD", q3)])

                def st_TR(t):
                    q3 = t % NB3
                    qk, qd = QK[q3], QD[q3]
                    qf = qk[:, 0, :, :].rearrange("p a b -> p (a b)")
                    kf = qk[:, 1, :, :].rearrange("p a b -> p (a b)")
                    bt = 2 + (t % 2)
                    pbt = PB[bt][:, :].bitcast(BF16).rearrange("p (a b) -> p a b", a=8)
                    kb.op('pe', lambda e: e.transpose(out=pbt[:, 0, :], in_=qf, identity=IDB[:, :]),
                          reads=[("QK0", q3), ("QK1", q3), "IDB"], writes=[("PB", bt)], signal=False)
                    kb.op('pe', lambda e: e.transpose(out=pbt[:, 1, :], in_=kf, identity=IDB[:, :]),
                          reads=[("QK0", q3), ("QK1", q3), "IDB"], writes=[("PB", bt)], signal=False)
                    kb.op('pe', lambda e: e.transpose(out=pbt[:, 2, :], in_=qd[:, :], identity=IDB[:, :]),
                          reads=[("QD", q3), "IDB"], writes=[("PB", bt)])
                    kb.op('dve', lambda e: e.tensor_copy(out=QKT[t % 2][:, :, :], in_=pbt[:, 0:2, :]),
                          reads=[("PB", bt)], writes=[("QKT", t % 2)])
                    kb.op('dve', lambda e: e.tensor_copy(out=QDT[:, t, :], in_=pbt[:, 2, :]),
                          reads=[("PB", bt)], writes=[("QDT", t)])

                def st_S(t):
                    bs = 4
                    qkt = QKT[t % 2]
                    kb.op('pe', lambda e: e.matmul(PB[bs][:, 0:128], lhsT=qkt[:, 1, :], rhs=qkt[:, 0, :],
                                                   start=True, stop=True),
                          reads=[("QKT", t % 2)], writes=[("PB", bs)])
                    kb.op('dve', lambda e: e.tensor_tensor(out=ST[t % 2][:, :], in0=PB[bs][:, 0:128], in1=INTRA[:, h, :],
                                                           op=ALU.mult),
                          reads=[("PB", bs), "INTRA"], writes=[("ST", t % 2)])

                def st_R(t):
                    q3 = t % NB3
                    vb, kd = VB[q3], KD[q3]
                    br, bk = 5, 6
                    kb.op('pe', lambda e: e.matmul(PB[br][:, 0:256], lhsT=ST[t % 2][:, :], rhs=vb[:, :], start=True, stop=False),
                          reads=[("ST", t % 2), ("VB", q3)], writes=[("PB", br)], signal=False)
                    kb.op('pe', lambda e: e.matmul(PB[br][:, 0:256], lhsT=QDT[:, t, :], rhs=STATEB[:, :],
                                                   start=False, stop=True),
                          reads=[("QDT", t), "STATEB"], writes=[("PB", br)])
                    kb.op('pe', lambda e: e.matmul(PB[bk][:, 0:256], lhsT=kd[:, :], rhs=vb[:, :], start=True, stop=True),
                          reads=[("KD", q3), ("VB", q3)], writes=[("PB", bk)])
                    kb.op('act', lambda e: e.activation(out=RSB[:, t, :], in_=PB[br][:, 0:256], func=AF.Copy),
                          reads=[("PB", br)], writes=[("RSB", t)])
                    kb.op('dve', lambda e: e.scalar_tensor_tensor(out=STATE[:, :], in0=STATE[:, :], scalar=chunk_dec,
                                                                  in1=PB[bk][:, 0:256], op0=ALU.mult, op1=ALU.add),
                          reads=["STATE", ("PB", bk)], writes=["STATE"])
                    kb.op('act', lambda e: e.activation(out=STATEB[:, :], in_=STATE[:, :], func=AF.Copy),
                          reads=["STATE"], writes=["STATEB"])

                for i in range(NT + 3):
                    if i < NT:
                        st_P(i)
                    if 0 <= i - 1 < NT:
                        st_TR(i - 1)
                    if 0 <= i - 2 < NT:
                        st_S(i - 2)
                    if 0 <= i - 3 < NT:
                        st_R(i - 3)
                    if i < NT:
                        st_ROT(i)
                if stop_after == "mix_b1":
                    return
                kb.dma('sp', [(st_src[:, :], STATE[:, :])], reads=["STATE"], writes=["st_src"])
                kb.allgather(st_src[:, :], st_dst[:, :], reads=["st_src"], writes=["st_dst"])
                nxt_q = None
                for t in range(NT):
                    b = t % 2
                    for kc in range(KC):
                        kb.op('pe', lambda e: e.matmul(PB[b][:, 0:256], lhsT=XT[:, kc, 2 + t * 128:2 + (t + 1) * 128],
                                                       rhs=WB[sg_][:, kc, 0:256], start=(kc == 0), stop=(kc == KC - 1)),
                              reads=[("XT", t), ("WB", sg_)], writes=[("PB", b)], signal=(kc == KC - 1))
                    kb.op('act', lambda e: e.activation(out=GS[:, t, :], in_=PB[b][:, 0:256], func=AF.Silu),
                          reads=[("PB", b)], writes=[("GS", t)])
                if h + 1 < 6:
                    prefetch(("eg", e_idx, h + 1), g_pieces(h + 1))
                    prefetch(("eq", e_idx, h + 1), qkv_pieces(h + 1))
                else:
                    prefetch((("wo", "e", e_idx), 0), [(0, wout[:, 0:512])])
                    prefetch((("wo", "e", e_idx), 1), [(0, wout[:, 512:1024])])
                kb.barrier()
                ar.reset(mark_r)
                SA = ar.alloc([P, 256], F32)
                SC = ar.alloc([P, 256], F32)
                SCB = [ar.alloc([P, 256], BF16) for _ in range(2)]
                RT = [ar.alloc([P, 256], F32) for _ in range(2)]
                YF = [ar.alloc([P, 256], F32) for _ in range(2)]
                YG = [ar.alloc([P, 256], BF16) for _ in range(2)]
                BNS = [ar.alloc([P, 8], F32) for _ in range(2)]
                MV = [ar.alloc([P, 4], F32) for _ in range(2)]
                kb.dma('sp', [(SA[:, :], st_dst[0:128, :])], reads=["st_dst"], writes=["SA"])
                kb.op('dve', lambda e: e.tensor_scalar_mul(out=SC[:, :], in0=SA[:, :], scalar1=pcol(OFF_FLAG)),
                      reads=["SA", "PC"], writes=["SC"])

                def p2_hops(t):
                    q = t % 2
                    bc = 4 + q
                    hops = []
                    def hop0():
                        kb.op('act', lambda e: e.activation(out=SCB[q][:, :], in_=SC[:, :], func=AF.Copy),
                              reads=["SC"], writes=[("SCB", q)])
                        if t < NT - 1:
                            kb.op('dve', lambda e: e.tensor_scalar_mul(out=SC[:, :], in0=SC[:, :], scalar1=chunk_dec),
                                  reads=["SC"], writes=["SC"])
                    hops.append(hop0)
                    hops.append(lambda: kb.op('pe', lambda e: e.matmul(PB[bc][:, 0:256], lhsT=QDT[:, t, :], rhs=SCB[q][:, :],
                                                                       start=True, stop=True),
                                              reads=[("QDT", t), ("SCB", q)], writes=[("PB", bc)]))
                    hops.append(lambda: kb.op('dve', lambda e: e.tensor_tensor(out=RT[q][:, :], in0=RSB[:, t, :], in1=PB[bc][:, 0:256],
                                                                               op=ALU.add),
                                              reads=[("RSB", t), ("PB", bc)], writes=[("RT", q)]))
                    hops.append(lambda: kb.op('dve', lambda e: e.bn_stats(out=BNS[q][:, 0:6], in_=RT[q][:, :]),
                                              reads=[("RT", q)], writes=[("BNS", q)]))
                    hops.append(lambda: kb.op('dve', lambda e: e.bn_aggr(out=MV[q][:, 0:2], in_=BNS[q][:, 0:6]),
                                              reads=[("BNS", q)], writes=[("MV", q)]))
                    hops.append(lambda: kb.op('act', lambda e: e.activation(out=MV[q][:, 2:3], in_=MV[q][:, 1:2], func=AF.Sqrt, bias=EPS),
                                              reads=[("MV", q)], writes=[("MV2", q)]))
                    hops.append(lambda: kb.op('dve', lambda e: e.reciprocal(out=MV[q][:, 3:4], in_=MV[q][:, 2:3]),
                                              reads=[("MV2", q)], writes=[("MV3", q)]))
                    hops.append(lambda: kb.op('dve', lambda e: e.tensor_scalar(out=YF[q][:, :], in0=RT[q][:, :], scalar1=MV[q][:, 0:1],
                                                                               scalar2=MV[q][:, 3:4], op0=ALU.subtract, op1=ALU.mult),
                                              reads=[("RT", q), ("MV", q), ("MV3", q)], writes=[("YF", q)]))
                    hops.append(lambda: kb.op('pool', lambda e: e.tensor_tensor(out=YG[q][:, :], in0=YF[q][:, :], in1=GS[:, t, :], op=ALU.mult),
                                              reads=[("YF", q), ("GS", t)], writes=[("YG", q)]))
                    hops.append(lambda: p2_B(t))
                    return hops

                def p2_B(t):
                    q = t % 2
                    by = 6 + q
                    pby = PB[by][:, :].bitcast(BF16).rearrange("p (a b) -> p a b", a=8)
                    kb.op('pe', lambda e: e.transpose(out=pby[:, 0, :], in_=YG[q][:, 0:128], identity=IDB[:, :]),
                          reads=[("YG", q), "IDB"], writes=[("PB", by)], signal=False)
                    kb.op('pe', lambda e: e.transpose(out=pby[:, 1, :], in_=YG[q][:, 128:256], identity=IDB[:, :]),
                          reads=[("YG", q), "IDB"], writes=[("PB", by)])
                    for j in range(2):
                        kc = 4 + 2 * h + j
                        gcol = pcol(OFF_GNG + e_idx * 12 + 2 * h + j)
                        dst = MT[:, kc, t * 128:(t + 1) * 128]
                        kb.op('act', lambda e: e.activation(out=dst, in_=pby[:, j, :], func=AF.Copy, scale=gcol),
                              reads=[("PB", by), "PC"], writes=[("MT", kc, t)])

                for t0 in range(0, NT, 2):
                    ha = p2_hops(t0)
                    hb = p2_hops(t0 + 1)
                    for k in range(max(len(ha), len(hb))):
                        if k < len(ha):
                            ha[k]()
                        if k < len(hb):
                            hb[k]()

            if stop_after == "mix_b":
                return
            kb.barrier()
            ar.reset(mark)
            kb.op('dve', lambda e: e.tensor_scalar_mul(out=U[:, :, 0:16], in0=UHR[:, :, :], scalar1=pcol(OFF_FLAG)),
                  reads=["UHR", "PC"], writes=["UH"])
            S_A = ar.alloc([P, 528], F32)
            S_B = ar.alloc([P, 528], F32)
            YP = [ar.alloc([P, 512], BF16), ar.alloc([P, 512], BF16)]
            it = 0
            for g in range(4):
                w = 2 ** (g + 1)
                for n in range(2):
                    a0 = U[:, g, n * 512:n * 512 + 528]
                    rk = [("U", g, n), "UH"] + ([("U", g, 0)] if n == 1 else [])
                    kb.op('dve', lambda e: e.tensor_tensor(out=S_A[:, 1:528], in0=a0[:, 1:528], in1=a0[:, 0:527],
                                                           op=ALU.add), reads=rk, writes=["S_A"])
                    cur, curk, oth, othk = S_A, "S_A", S_B, "S_B"
                    sh = 1
                    for step in range(g):
                        sh2 = sh * 2
                        lo = 2 * sh2 - 1
                        kb.op('dve', lambda e: e.tensor_tensor(out=oth[:, lo:528], in0=cur[:, lo:528],
                                                               in1=cur[:, lo - sh2:528 - sh2], op=ALU.add),
                              reads=[curk], writes=[othk])
                        cur, curk, oth, othk = oth, othk, cur, curk
                        sh = sh2
                    if n == 0:
                        kb.op('dve', lambda e: e.tensor_tensor(out=cur[:, 16:32], in0=cur[:, 16:32],
                                                               in1=PC[:, OFF_PCORR + g * 16:OFF_PCORR + (g + 1) * 16],
                                                               op=ALU.mult),
                              reads=[curk, "PC"], writes=[curk])
                    yp = YP[it % 2]
                    ypk = ("YP", it % 2)
                    it += 1
                    kb.op('dve', lambda e: e.scalar_tensor_tensor(out=yp[:, :], in0=cur[:, 16:528], scalar=1.0 / w,
                                                                  in1=a0[:, 16:528], op0=ALU.mult, op1=ALU.subtract),
                          reads=[curk] + rk, writes=[ypk])
                    b = bank()
                    kb.op('pe', lambda e: e.matmul(PB[b][:, :], lhsT=PW[:, g, :], rhs=yp[:, :], start=True, stop=True),
                          reads=["PW", ypk], writes=[("PB", b)])
                    kb.op('act', lambda e: e.activation(out=MT[:, g, n * 512:(n + 1) * 512], in_=PB[b][:, :],
                                                        func=AF.Copy, scale=pcol(OFF_PSC + e_idx * 4 + g)),
                          reads=[("PB", b), "PC"], writes=[("MT", g, t) for t in range(n * 4, n * 4 + 4)])
            if stop_after == "mix_c":
                return
            out_proj([wout[:, cb * 512:(cb + 1) * 512] for cb in range(4)], KC, key=("wo", "e", e_idx))

        def odd_mixer(o_idx, layer_idx):
            kb.barrier()
            ar.reset()
            win = w_in_odd[o_idx]
            wout = w_out_odd[o_idx]
            J0, J1, J2 = 2048, 3072, 4096
            Hb = H[:, :, :].rearrange("p a b -> p (a b)").bitcast(BF16)
            KT = Hb[:, 0:8192].rearrange("p (a b) -> p a b", a=8)
            V1 = Hb[:, 8192:8192 + 8224].rearrange("p (a b c) -> p a b c", a=8, b=4)
            QT = Hb[:, 16416:16416 + 8192].rearrange("p (a b) -> p a b", a=8)
            Hf = H[:, :, :].rearrange("p a b -> p (a b)")
            XTb = XT[:, :, :].rearrange("p a b -> p (a b)")
            KTP = XTb[:, 0:8192].rearrange("p (a b) -> p a b", a=8)
            V1P = XTb[:, 8192:8192 + 8224].rearrange("p (a b c) -> p a b c", a=8, b=4)
            lam_init = 0.8 - 0.6 * math.exp(-0.3 * layer_idx)

            BIAS = ar.alloc([P, 4, 2, 128], F32)
            ED = ar.alloc([P, 4, 2, 128], BF16)
            LAMV = ar.alloc([P, 4, 128], F32)
            LT = ar.alloc([P, 2, 128], F32)
            LS = ar.alloc([P, 8], F32)
            CB = ar.alloc([P, 8], F32)
            mark0 = ar.off
            kb.dma('sp', [(BIAS[:, :, :, :], p_biasT[:, :, :, :])], writes=["BIAS"])
            kb.dma('sp', [(LAMV[:, :, :], p_lam[:, o_idx, :, :])], writes=["LAMV"])
            kb.op('act', lambda e: e.activation(out=ED[:, :, :, :], in_=BIAS[:, :, :, :], func=AF.Exp),
                  reads=["BIAS"], writes=["ED"])
            for hh in range(4):
                kb.op('dve', lambda e: e.tensor_tensor(out=ED[:, hh, 0, :], in0=ED[:, hh, 0, :], in1=MASK[:, :],
                                                       op=ALU.mult), reads=["ED", "MASK"], writes=["ED"])
            kb.op('dve', lambda e: e.tensor_tensor(out=LT[:, 0, :], in0=LAMV[:, 0, :], in1=LAMV[:, 1, :], op=ALU.mult),
                  reads=["LAMV"], writes=["LT0"])
            kb.op('dve', lambda e: e.tensor_tensor(out=LT[:, 1, :], in0=LAMV[:, 2, :], in1=LAMV[:, 3, :], op=ALU.mult),
                  reads=["LAMV"], writes=["LT1"])
            kb.op('dve', lambda e: e.reduce_sum(out=LS[:, 0:2], in_=LT[:, :, :], axis=mybir.AxisListType.X),
                  reads=["LT0", "LT1"], writes=["LS"])
            kb.op('act', lambda e: e.activation(out=LS[:, 2:4], in_=LS[:, 0:2], func=AF.Exp), reads=["LS"], writes=["LS2"])
            kb.op('dve', lambda e: e.tensor_tensor(out=LS[:, 4:5], in0=LS[:, 2:3], in1=LS[:, 3:4], op=ALU.subtract),
                  reads=["LS2"], writes=["LS4"])
            kb.op('dve', lambda e: e.tensor_scalar(out=LS[:, 5:6], in0=LS[:, 4:5], scalar1=lam_init, scalar2=-1.0,
                                                   op0=ALU.add, op1=ALU.mult), reads=["LS4"], writes=["LAM"])
            kb.op('dve', lambda e: e.tensor_copy(out=CB[:, 0:4], in_=PC[:, OFF_C31:OFF_C31 + 4]), reads=["PC"], writes=["CB0"])
            kb.op('dve', lambda e: e.tensor_scalar_add(out=CB[:, 4:8], in0=PC[:, OFF_C31:OFF_C31 + 4], scalar1=pcol(OFF_NEGB)),
                  reads=["PC"], writes=["CB1"])

            def proj_fm(col0, dst, nm):
                for c in range(2):
                    s = load_w([(0, win[:, col0 + c * 512:col0 + (c + 1) * 512])], key=("ofm", o_idx, nm, c))
                    for f in range(4):
                        for n in range(2):
                            b = bank()
                            for kc in range(KC):
                                kb.op('pe', lambda e: e.matmul(PB[b][:, :], lhsT=WB[s][:, kc, f * 128:(f + 1) * 128],
                                                               rhs=XT[:, kc, 2 + n * 512:2 + (n + 1) * 512],
                                                               start=(kc == 0), stop=(kc == KC - 1)),
                                      reads=[("XT", t) for t in range(n * 4, n * 4 + 4)] + [("WB", s)],
                                      writes=[("PB", b)], signal=(kc == KC - 1))
                            kb.op('act', lambda e: e.activation(out=dst[:, c * 4 + f, n * 512:(n + 1) * 512],
                                                                in_=PB[b][:, :], func=AF.Copy),
                                  reads=[("PB", b)], writes=[(nm, c * 4 + f, n)])

            proj_fm(J1, KT, "KT")
            kb.op('dve', lambda e: e.memset(V1[:, :, :, 256:257], 1.0), writes=["V1one"])
            for c in range(2):
                s = load_w([(0, win[:, J2 + c * 512:J2 + (c + 1) * 512])])
                for t in range(NT):
                    b = bank()
                    for kc in range(KC):
                        kb.op('pe', lambda e: e.matmul(PB[b][:, :], lhsT=XT[:, kc, 2 + t * 128:2 + (t + 1) * 128],
                                                       rhs=WB[s][:, kc, :], start=(kc == 0), stop=(kc == KC - 1)),
                              reads=[("XT", t), ("WB", s)], writes=[("PB", b)], signal=(kc == KC - 1))
                    kb.op('act', lambda e: e.activation(out=V1[:, t, 2 * c:2 * c + 2, 0:256],
                                                        in_=PB[b][:, :].rearrange("p (a b) -> p a b", a=2), func=AF.Copy),
                          reads=[("PB", b)], writes=[("V1", t, c)])
            chk("o_a")
            kvkeys = [("KT", i, n) for i in range(8) for n in range(2)] + [("V1", t, c) for t in range(8) for c in range(2)] + ["V1one"]
            for c in range(2):
                prefetch(("ofm", o_idx, "QT", c), [(0, win[:, J0 + c * 512:J0 + (c + 1) * 512])])
            for i in range(4):
                kb.dma('sp', [(kv_src[i][:, :], Hf[:, i * KVQ:(i + 1) * KVQ])], reads=kvkeys, writes=[("kv_src", i)])
            for i in range(2):
                kb.allgather(kv_src[i][:, :], kv_dst[i][:, :], reads=[("kv_src", i)], writes=[("kv_dst", i)])
            chk("o_b")
            proj_fm(J0, QT, "QT")
            chk("o_c")

            ar2 = Arena(Hf[:, 12304:16384], 4080 * 4)
            SLN = ar2.alloc([P, 1024], F32)
            WSF = ar2.alloc([P, 8, 128], F32)
            ZV = ar2.alloc([P, 1024], F32)
            ZU = ar2.alloc([P, 512], F32)
            WST = ar.alloc([P, 8, 128], BF16)
            SV = ar.alloc([P, 8, 1024], BF16)
            VN = ar.alloc([P, 1024], BF16)
            CO = ar.alloc([P, 512], BF16)
            kb.dma('sp', [(SLN[:, :], p_sln[:, o_idx, :])], writes=["SLN"])
            kb.dma('sp', [(WSF[:, :, :], p_sguT[o_idx])], writes=["WSF"])
            kb.op('dve', lambda e: e.tensor_tensor(out=WST[:, :, :], in0=WSF[:, :, :],
                                                   in1=MASK[:, :].unsqueeze(1).to_broadcast([P, 8, 128]), op=ALU.mult),
                  reads=["WSF", "MASK"], writes=["WST"])
            s0 = load_w([(0, win[:, 1024:1536])])
            s1 = load_w([(0, win[:, 1536:2048])])
            for i in range(2, 4):
                kb.allgather(kv_src[i][:, :], kv_dst[i][:, :], reads=[("kv_src", i)], writes=[("kv_dst", i)])
            VNs = [VN, ar.alloc([P, 1024], BF16)]
            COs = [CO, ar.alloc([P, 512], BF16)]

            def sgu_A(t):
                bb = [bank(), bank()]
                for c, s_ in enumerate((s0, s1)):
                    for kc in range(KC):
                        kb.op('pe', lambda e: e.matmul(PB[bb[c]][:, :], lhsT=XT[:, kc, 2 + t * 128:2 + (t + 1) * 128],
                                                       rhs=WB[s_][:, kc, :], start=(kc == 0), stop=(kc == KC - 1)),
                              reads=[("XT", t), ("WB", s_)], writes=[("PB", bb[c])], signal=(kc == KC - 1))
                    kb.op('act', lambda e: e.activation(out=ZV[:, c * 512:(c + 1) * 512], in_=PB[bb[c]][:, :], func=GELU),
                          reads=[("PB", bb[c])], writes=[("ZV", c)])
                vn = VNs[t % 2]
                kb.op('dve', lambda e: e.bn_stats(out=SS[:, 24:30], in_=ZV[:, 0:512]), reads=[("ZV", 0)], writes=["BN0"])
                kb.op('dve', lambda e: e.bn_stats(out=SS[:, 30:36], in_=ZV[:, 512:1024]), reads=[("ZV", 1)], writes=["BN1"])
                kb.op('dve', lambda e: e.bn_aggr(out=LS[:, 6:8], in_=SS[:, 24:36].rearrange("p (a b) -> p a b", a=2)),
                      reads=["BN0", "BN1"], writes=["MVZ"])
                kb.op('act', lambda e: e.activation(out=SS[:, 36:37], in_=LS[:, 7:8],
                                                    func=AF.Sqrt, bias=EPS), reads=["MVZ"], writes=["SDZ"])
                kb.op('dve', lambda e: e.reciprocal(out=SS[:, 37:38], in_=SS[:, 36:37]), reads=["SDZ"], writes=["RSZ"])
                kb.op('dve', lambda e: e.tensor_scalar(out=ZV[:, :], in0=ZV[:, :], scalar1=LS[:, 6:7], scalar2=SS[:, 37:38],
                                                       op0=ALU.subtract, op1=ALU.mult),
                      reads=[("ZV", 0), ("ZV", 1), "MVZ", "RSZ"], writes=[("ZV", 0), ("ZV", 1)])
                kb.op('dve', lambda e: e.tensor_tensor(out=vn[:, :], in0=ZV[:, :], in1=SLN[:, :], op=ALU.mult),
                      reads=[("ZV", 0), ("ZV", 1), "SLN"], writes=[("VN", t % 2)])

            def sgu_B(t):
                vn = VNs[t % 2]
                bs2 = [bank(), bank()]
                for g in range(8):
                    b = bs2[g // 4]
                    kb.op('pe', lambda e: e.matmul(PB[b][:, (g % 4) * 128:(g % 4 + 1) * 128], lhsT=WST[:, g, :],
                                                   rhs=vn[:, g * 128:(g + 1) * 128], start=True, stop=True),
                          reads=["WST", ("VN", t % 2)], writes=[("PB", b)], signal=(g % 4 == 3))
                for g in range(8):
                    b = bs2[g // 4]
                    bcol = pcol(OFF_SB + o_idx * 8 + g)
                    kb.op('act', lambda e: e.activation(out=SV[:, t, g * 128:(g + 1) * 128],
                                                        in_=PB[b][:, (g % 4) * 128:(g % 4 + 1) * 128],
                                                        func=AF.Identity, bias=bcol),
                          reads=[("PB", b), "PC"], writes=[("SV", t)])

            for i in range(NT + 1):
                if i < NT:
                    sgu_A(i)
                if i - 1 >= 0:
                    sgu_B(i - 1)

            zitems = [(c, t) for c in range(2) for t in range(NT)]
            zslots = {}

            def zu_A(k):
                c, t = zitems[k]
                if t == 0:
                    zslots[c] = load_w([(0, win[:, c * 512:(c + 1) * 512])])
                s_ = zslots[c]
                b = bank()
                for kc in range(KC):
                    kb.op('pe', lambda e: e.matmul(PB[b][:, :], lhsT=XT[:, kc, 2 + t * 128:2 + (t + 1) * 128],
                                                   rhs=WB[s_][:, kc, :], start=(kc == 0), stop=(kc == KC - 1)),
                          reads=[("XT", t), ("WB", s_)], writes=[("PB", b)], signal=(kc == KC - 1))
                co = COs[k % 2]
                kb.op('act', lambda e: e.activation(out=ZU[:, :], in_=PB[b][:, :], func=GELU),
                      reads=[("PB", b)], writes=["ZU"])
                kb.op('dve', lambda e: e.tensor_tensor(out=co[:, :], in0=ZU[:, :], in1=SV[:, t, c * 512:(c + 1) * 512],
                                                       op=ALU.mult), reads=["ZU", ("SV", t)], writes=[("CO", k % 2)])

            def zu_B(k):
                c, t = zitems[k]
                co = COs[k % 2]
                bt = bank()
                pbt = PB[bt][:, :].bitcast(BF16).rearrange("p (a b) -> p a b", a=8)
                for i in range(4):
                    kb.op('pe', lambda e: e.transpose(out=pbt[:, i, :], in_=co[:, i * 128:(i + 1) * 128], identity=IDB[:, :]),
                          reads=[("CO", k % 2), "IDB"], writes=[("PB", bt)], signal=(i == 3))
                for i in range(4):
                    kc = c * 4 + i
                    dst = MT[:, kc, t * 128:(t + 1) * 128]
                    kb.op('dve', lambda e: e.tensor_copy(out=dst, in_=pbt[:, i, :]),
                          reads=[("PB", bt)], writes=[("MT", kc, t)])

            for k in range(len(zitems) + 1):
                if k < len(zitems):
                    zu_A(k)
                if k - 1 >= 0:
                    zu_B(k - 1)
            chk("o_d")
            kb.barrier()
            XTf = XTb.bitcast(F32)
            kb.dma('sp', [(XTf[:, i * KVQ:(i + 1) * KVQ], kv_dst[i][0:128, :]) for i in range(4)],
                   reads=[("kv_dst", i) for i in range(4)], writes=["KVP"])
            prefetch((("wo", "o", o_idx), 0), [(0, wout[:, 0:512])])
            prefetch((("wo", "o", o_idx), 1), [(0, wout[:, 512:1024])])
            ar.reset(mark0)
            PT = [ar.alloc([P, 512], BF16) for _ in range(6)]
            OA = ar.alloc([P, 2, 260], F32)
            RD = ar.alloc([P, 4], F32)
            DO = ar.alloc([P, 256], F32)
            DJ = ar.alloc([P, 256], F32)
            DBs = [ar.alloc([P, 256], BF16), ar.alloc([P, 256], BF16)]
            scale = 128.0 ** -0.5
            LOOK = 2
            steps = []
            gid = 0
            for hh in range(4):
                for qp in range(4):
                    qt2 = [qp * 2, qp * 2 + 1]
                    klist = [("p", kt) for kt in range(8)] + [("o", kt) for kt in range(qt2[1] + 1)]
                    gsteps = []
                    for (kind, kt) in klist:
                        vis = qt2 if kind == "p" else [qt for qt in qt2 if qt >= kt]
                        gsteps.append(dict(g=gid, hh=hh, qt2=qt2, kind=kind, kt=kt, vis=vis))
                    seen = set()
                    for st in gsteps:
                        st["first"] = {}
                        for qt in st["vis"]:
                            st["first"][qt] = qt not in seen
                            seen.add(qt)
                    seen = set()
                    for st in reversed(gsteps):
                        st["last"] = {}
                        for qt in st["vis"]:
                            st["last"][qt] = qt not in seen
                            seen.add(qt)
                    gsteps[-1]["gend"] = True
                    steps.extend(gsteps)
                    gid += 1

            def emit_S(i, st):
                hh, kind, kt, vis = st["hh"], st["kind"], st["kt"], st["vis"]
                nq = len(vis) * 128
                ksrc = KTP if kind == "p" else KT
                bsx = 4 + (i % 3)
                for j in range(2):
                    kkey = ["KVP"] if kind == "p" else [("KT", hh * 2 + j, kt // 4)]
                    kb.op('pe', lambda e: e.matmul(PB[bsx][:, j * 256:j * 256 + nq],
                                                   lhsT=ksrc[:, hh * 2 + j, kt * 128:(kt + 1) * 128],
                                                   rhs=QT[:, hh * 2 + j, vis[0] * 128:(vis[-1] + 1) * 128],
                                                   start=True, stop=True),
                          reads=kkey + [("QT", hh * 2 + j, vis[0] // 4)], writes=[("PB", bsx)], signal=(j == 1))
                pt = PT[i % 6]
                ptk = ("PT", i % 6)
                pt3 = pt[:, :].rearrange("p (j c) -> p j c", j=2)
                ps3 = PB[bsx][:, :].rearrange("p (j c) -> p j c", j=2)
                bcol = CB[:, 4 + hh:5 + hh] if kind == "p" else CB[:, hh:hh + 1]
                bkey = "CB1" if kind == "p" else "CB0"
                near = {}
                for qt in vis:
                    if kind == "o" and kt == qt:
                        near[qt] = 0
                    elif kind == "o" and kt == qt - 1:
                        near[qt] = 1
                    elif kind == "p" and kt == 7 and qt == 0:
                        near[qt] = 1
                far = [qt for qt in vis if qt not in near]
                if far:
                    o0 = (far[0] - vis[0]) * 128
                    o1 = (far[-1] - vis[0] + 1) * 128
                    kb.op('act', lambda e: e.activation(out=pt3[:, :, o0:o1], in_=ps3[:, :, o0:o1], func=AF.Exp,
                                                        scale=scale, bias=bcol),
                          reads=[("PB", bsx), bkey], writes=[ptk])
                for qt, which in near.items():
                    o0 = (qt - vis[0]) * 128
                    if kind == "p":
                        kb.op('act', lambda e: e.activation(out=pt3[:, :, o0:o0 + 128], in_=ps3[:, :, o0:o0 + 128],
                                                            func=AF.Exp, scale=scale, bias=pcol(OFF_NEGB)),
                              reads=[("PB", bsx), "PC"], writes=[ptk])
                    else:
                        kb.op('act', lambda e: e.activation(out=pt3[:, :, o0:o0 + 128], in_=ps3[:, :, o0:o0 + 128],
                                                            func=AF.Exp, scale=scale),
                              reads=[("PB", bsx)], writes=[ptk])
                    kb.op('dve', lambda e: e.tensor_tensor(out=pt3[:, :, o0:o0 + 128], in0=pt3[:, :, o0:o0 + 128],
                                                           in1=ED[:, hh, which, :].unsqueeze(1).to_broadcast([P, 2, 128]),
                                                           op=ALU.mult),
                          reads=[ptk, "ED"], writes=[ptk])

            def emit_PV(i, st):
                hh, kind, kt, vis, qt2 = st["hh"], st["kind"], st["kt"], st["vis"], st["qt2"]
                pt = PT[i % 6]
                ptk = ("PT", i % 6)
                vsrc = V1P if kind == "p" else V1
                vkey = ["KVP"] if kind == "p" else [("V1", kt, hh // 2), "V1one"]
                for j in range(2):
                    for qt in vis:
                        o0 = j * 256 + (qt - vis[0]) * 128
                        ab = (qt - qt2[0]) * 2 + j
                        kb.op('pe', lambda e: e.matmul(PB[ab][:, 0:257], lhsT=pt[:, o0:o0 + 128], rhs=vsrc[:, kt, hh, :],
                                                       start=st["first"][qt], stop=st["last"][qt]),
                              reads=[ptk] + vkey, writes=[("PB", ab)], signal=True)

            fin_cnt = [0]

            def emit_fin_a(st):
                hh, qt2 = st["hh"], st["qt2"]
                deferred = []
                for qt in qt2:
                    for j in range(2):
                        ab = (qt - qt2[0]) * 2 + j
                        kb.op('dve', lambda e: e.tensor_copy(out=OA[:, j, 0:257], in_=PB[ab][:, 0:257]),
                              reads=[("PB", ab)], writes=[("OA", j)])
                    db = DBs[fin_cnt[0] % 2]
                    dbk = ("DB", fin_cnt[0] % 2)
                    fin_cnt[0] += 1
                    kb.op('dve', lambda e: e.reciprocal(out=RD[:, 0:1], in_=OA[:, 0, 256:257]), reads=[("OA", 0)], writes=["RD0"])
                    kb.op('dve', lambda e: e.reciprocal(out=RD[:, 1:2], in_=OA[:, 1, 256:257]), reads=[("OA", 1)], writes=["RD1"])
                    kb.op('dve', lambda e: e.tensor_tensor(out=RD[:, 2:3], in0=RD[:, 1:2], in1=LS[:, 5:6], op=ALU.mult),
                          reads=["RD1", "LAM"], writes=["RD2"])
                    kb.op('dve', lambda e: e.tensor_scalar_mul(out=DO[:, :], in0=OA[:, 0, 0:256], scalar1=RD[:, 0:1]),
                          reads=[("OA", 0), "RD0"], writes=["DO"])
                    kb.op('dve', lambda e: e.scalar_tensor_tensor(out=DO[:, :], in0=OA[:, 1, 0:256], scalar=RD[:, 2:3],
                                                                  in1=DO[:, :], op0=ALU.mult, op1=ALU.add),
                          reads=[("OA", 1), "RD2", "DO"], writes=["DO"])
                    kb.op('act', lambda e: e.activation(out=DJ[:, :], in_=DO[:, :], func=AF.Square, accum_out=RD[:, 3:4]),
                          reads=["DO"], writes=["DJ", "RD3"])
                    kb.op('act', lambda e: e.activation(out=SS[:, 38:39], in_=RD[:, 3:4], func=AF.Sqrt, scale=1.0 / 256, bias=EPS),
                          reads=["RD3"], writes=["SD1"])
                    kb.op('dve', lambda e: e.reciprocal(out=SS[:, 39:40], in_=SS[:, 38:39]), reads=["SD1"], writes=["SD2"])
                    kb.op('dve', lambda e: e.tensor_scalar(out=db[:, :], in0=DO[:, :], scalar1=SS[:, 39:40],
                                                           scalar2=(1.0 - lam_init), op0=ALU.mult, op1=ALU.mult),
                          reads=["DO", "SD2"], writes=[dbk])
                    deferred.append((hh, qt, db, dbk))
                return deferred

            def emit_fin_b(hh, qt, db, dbk):
                bt = 7
                pbt = PB[bt][:, :].bitcast(BF16).rearrange("p (a b) -> p a b", a=8)
                kb.op('pe', lambda e: e.transpose(out=pbt[:, 0, :], in_=db[:, 0:128], identity=IDB[:, :]),
                      reads=[dbk, "IDB"], writes=[("PB", bt)], signal=False)
                kb.op('pe', lambda e: e.transpose(out=pbt[:, 1, :], in_=db[:, 128:256], identity=IDB[:, :]),
                      reads=[dbk, "IDB"], writes=[("PB", bt)])
                for i2 in range(2):
                    kc = 8 + hh * 2 + i2
                    gcol = pcol(OFF_SUB + o_idx * 2 + i2)
                    dst = MT[:, kc, qt * 128:(qt + 1) * 128]
                    kb.op('dve', lambda e: e.tensor_scalar_mul(out=dst, in0=pbt[:, i2, :], scalar1=gcol),
                          reads=[("PB", bt), "PC"], writes=[("MT", kc, qt)])

            pending_fin = []
            nsteps = len(steps)
            for idx in range(nsteps + LOOK):
                if idx < nsteps:
                    emit_S(idx, steps[idx])
                while pending_fin and pending_fin[0][0] <= idx:
                    emit_fin_b(*pending_fin.pop(0)[1])
                pi_ = idx - LOOK
                if pi_ >= 0:
                    emit_PV(pi_, steps[pi_])
                    if steps[pi_].get("gend"):
                        for k2, args in enumerate(emit_fin_a(steps[pi_])):
                            pending_fin.append((idx + 4 + 2 * k2, args))
            while pending_fin:
                emit_fin_b(*pending_fin.pop(0)[1])
            chk("o_e")
            kb.barrier()
            for t in range(NT):
                kb.dma('sp', [(H[:, t, :], hspill[t * 128:(t + 1) * 128, :])], reads=[("HSP", t)], writes=[("H", t)])
            out_proj([wout[:, cb * 512:(cb + 1) * 512] for cb in range(4)], KC, key=("wo", "o", o_idx))

        order_last_first = [NT - 1] + list(range(NT - 1))
        for l in range(n_layers):
          try:
            last = (l == n_layers - 1)
            if l % 2 == 0:
                prefetch(("eu", l // 2), [(0, w_in_even[l // 2][:, 0:512])])
            else:
                for t in range(NT):
                    kb.dma('sp', [(hspill[t * 128:(t + 1) * 128, :], H[:, t, :])], reads=[("H", t)], writes=[("HSP", t)])
                for c in range(2):
                    prefetch(("ofm", l // 2, "KT", c), [(0, w_in_odd[l // 2][:, 3072 + c * 512:3072 + (c + 1) * 512])])
            norm_phase(l * 2, list(range(NT)), halo=False, first=(l == 0))
            if last and stop_after == "norm0":
                break
            if l % 2 == 0:
                even_mixer(l // 2)
            else:
                odd_mixer(l // 2, l)
            if last and stop_after and stop_after.startswith("mix"):
                break
            prefetch(("up", l, 0), [(0, w_up[l][:, 0:256]), (256, w_up[l][:, DFF:DFF + 256])])
            norm_phase(l * 2 + 1, order_last_first, halo=True)
            if last and stop_after == "norm1":
                break
            ffn(l)
          except StopBuild:
            break

        kb.barrier()
        ar.reset()
        FG = ar.alloc([P, D], F32)
        JK = ar.alloc([P, D], BF16)
        YO = [ar.alloc([P, D], F32), ar.alloc([P, D], F32)]
        kb.dma('sp', [(FG[:, :], p_fng[:, :])], writes=["FG"])
        for t in range(NT):
            yo = YO[t % 2]
            yk = ("YO", t % 2)
            if dbg_h:
                kb.dma('sp', [(out_d[t * 128:(t + 1) * 128, :], H[:, t, :])], reads=[("H", t)], writes=[("OUT", t)])
                continue
            kb.op('act', lambda e: e.activation(out=JK[:, :], in_=H[:, t, :], func=AF.Square, accum_out=SS[:, t:t + 1]),
                  reads=[("H", t)], writes=["JK", ("SS", t)])
            kb.op('act', lambda e: e.activation(out=SS[:, 8 + t:9 + t], in_=SS[:, t:t + 1], func=AF.Sqrt, scale=1.0 / D, bias=EPS),
                  reads=[("SS", t)], writes=[("SS", 8 + t)])
            kb.op('dve', lambda e: e.reciprocal(out=SS[:, 16 + t:17 + t], in_=SS[:, 8 + t:9 + t]),
                  reads=[("SS", 8 + t)], writes=[("SS", 16 + t)])
            kb.op('dve', lambda e: e.scalar_tensor_tensor(out=yo[:, :], in0=H[:, t, :], scalar=SS[:, 16 + t:17 + t],
                                                          in1=FG[:, :], op0=ALU.mult, op1=ALU.mult),
                  reads=[("H", t), ("SS", 16 + t), "FG"], writes=[yk])
            kb.dma('sp', [(out_d[t * 128:(t + 1) * 128, :], yo[:, :])], reads=[yk], writes=[("OUT", t)])
        kb.barrier()
        print("ops emitted:", kb.nops, "sems:", len(kb.sems), flush=True)
    return nc


def _t5_bucket(n):
    n = np.maximum(n, 0)
    max_exact = 16
    nn = np.maximum(n, 1).astype(np.float32)
    large = max_exact + (np.log(nn / max_exact) / math.log(128 / max_exact) * (32 - max_exact)).astype(np.int32)
    large = np.minimum(large, 31)
    return np.where(n < max_exact, n, large)


def _consts(half):
    c = {}
    c["c_ident"] = np.eye(128, dtype=np.float32)
    half_d = 64
    inv = 1.0 / (10000.0 ** (np.arange(half_d, dtype=np.float32) / half_d))
    pos = (half * T + np.arange(T, dtype=np.float32))
    ang = pos[:, None] * inv[None, :]
    cos = np.cos(ang).astype(np.float32).reshape(NT, 128, 64).transpose(1, 0, 2)
    sin = np.sin(ang).astype(np.float32).reshape(NT, 128, 64).transpose(1, 0, 2)
    c["c_rot"] = np.ascontiguousarray(np.stack([cos, sin], axis=1))
    log_g = np.log(1.0 - 2.0 ** (-5.0 - np.arange(6, dtype=np.float32))).astype(np.float32)
    idx = np.arange(128, dtype=np.float32)
    diff = idx[:, None] - idx[None, :]
    intra = np.where(diff >= 0, np.exp(log_g[:, None, None] * np.maximum(diff, 0.0)), 0.0)
    intraT = intra.transpose(2, 0, 1) * (128.0 ** -0.5)
    c["c_intra"] = np.ascontiguousarray(intraT.astype(np.float32))
    c["qdec"] = np.exp(log_g[None, :] * (idx[:, None] + 1.0)).astype(np.float32)
    c["kdec"] = (np.exp(log_g[None, :] * (127.0 - idx[:, None])) * (128.0 ** -0.5)).astype(np.float32)
    p = np.arange(128)
    c["c_mask"] = (p[None, :] >= p[:, None]).astype(np.float32)
    pcorr = np.ones((4, 16), np.float32)
    if half == 0:
        for g in range(4):
            w = 2 ** (g + 1)
            tt = np.arange(16) + 1
            pcorr[g] = w / np.minimum(tt, w)
    c["pcorr"] = pcorr
    return c


def _prep(inputs):
    f = lambda a: np.ascontiguousarray(np.asarray(a, dtype=np.float32))
    ins = {k: f(v) for k, v in inputs.items()}

    def cols(v):
        return v.reshape(-1, 128).T

    shared = {}
    for k in ("w_in_even", "w_out_even", "pool_w", "w_in_odd", "w_out_odd", "w_up", "w_down"):
        shared[k] = ins[k]
    pc = np.zeros((128, NCOLS), np.float32)
    for l in range(4):
        pc[:, OFF_NG + (2 * l) * 16:OFF_NG + (2 * l + 1) * 16] = cols(ins["mix_norm_g"][l])
        pc[:, OFF_NG + (2 * l + 1) * 16:OFF_NG + (2 * l + 2) * 16] = cols(ins["ffn_norm_g"][l])
        for r in range(3):
            pc[:, OFF_CW + l * 264 + r * 88:OFF_CW + l * 264 + (r + 1) * 88] = cols(ins["conv_w"][l, r])
        pc[:, OFF_CB + l * 88:OFF_CB + (l + 1) * 88] = cols(ins["conv_b"][l])
    for e in range(2):
        pc[:, OFF_PSC + e * 4:OFF_PSC + (e + 1) * 4] = cols(ins["pool_scale"][e])
        pc[:, OFF_GNG + e * 12:OFF_GNG + (e + 1) * 12] = cols(ins["ret_gn_g"][e])
        pc[:, OFF_SUB + e * 2:OFF_SUB + (e + 1) * 2] = cols(ins["diff_subln_g"][e])
        pc[:, OFF_SB + e * 8:OFF_SB + (e + 1) * 8] = ins["sgu_b"][e].T
    pc[:, OFF_C31:OFF_C31 + 4] = np.broadcast_to(ins["rel_bias"][31][None, :], (128, 4))
    shared["p_fng"] = np.ascontiguousarray(np.broadcast_to(ins["final_norm_g"][None, :], (128, D)))
    shared["p_sln"] = np.ascontiguousarray(np.broadcast_to(ins["sgu_ln_g"][None, :, :], (128, 2, 1024)))
    lam = np.stack([ins["lam_q1"], ins["lam_k1"], ins["lam_q2"], ins["lam_k2"]], axis=1)
    shared["p_lam"] = np.ascontiguousarray(np.broadcast_to(lam[None], (128, 2, 4, 128)))
    j = np.arange(128)[:, None]
    i = np.arange(128)[None, :]
    b0 = _t5_bucket(i - j)
    b1 = _t5_bucket(128 + i - j)
    rb = ins["rel_bias"]
    biasT = np.stack([rb[b0], rb[b1]], axis=0)
    shared["p_biasT"] = np.ascontiguousarray(biasT.transpose(1, 3, 0, 2))
    shared["p_sguT"] = np.ascontiguousarray(ins["sgu_w"].transpose(0, 3, 1, 2))
    maps = []
    for core in range(NCORES):
        b, half = core // 2, core % 2
        c = _consts(half)
        m = dict(shared)
        m["x"] = np.ascontiguousarray(ins["x"][b, half * T:(half + 1) * T, :])
        pcc = pc.copy()
        pcc[:, OFF_QDEC:OFF_QDEC + 6] = c["qdec"]
        pcc[:, OFF_KDEC:OFF_KDEC + 6] = c["kdec"]
        pcc[:, OFF_FLAG] = float(half)
        pcc[:, OFF_NEGB] = 0.0 if half == 1 else NEGBIG
        pcc[:, OFF_PCORR:OFF_PCORR + 64] = c["pcorr"].reshape(1, 64)
        m["p_cols"] = pcc
        for k in ("c_ident", "c_rot", "c_intra", "c_mask"):
            m[k] = c[k]
        maps.append(m)
    return maps


_NC_CACHE = {}


def kernel(**inputs):
    maps = _prep(inputs)
    if "nc" not in _NC_CACHE:
        _NC_CACHE["nc"] = build()
    nc = _NC_CACHE["nc"]
    res = run_bass_kernel_spmd(nc, maps, core_ids=list(range(NCORES)))
    out = np.zeros((4, 2 * T, D), np.float32)
    for core in range(NCORES):
        b, half = core // 2, core % 2
        out[b, half * T:(half + 1) * T, :] = np.asarray(res.results[core]["out"], dtype=np.float32)
    return out
```

```python
import math
from contextlib import ExitStack
import numpy as np
import concourse.bass as bass
import concourse.mybir as mybir
from concourse.bass_utils import run_bass_kernel_spmd

F32 = mybir.dt.float32
BF16 = mybir.dt.bfloat16
AF = mybir.ActivationFunctionType
ALU = mybir.AluOpType

P = 128
T = 1024
NT = 8
D = 2048
KC = 16
DFF = 5632
NFF = 44
EPS = 1e-6
NCORES = 8
PAIRS = [[0, 1], [2, 3], [4, 5], [6, 7]]
NEGBIG = -30000.0
GELU = AF.Gelu

OFF_NG = 0
OFF_PSC = OFF_NG + 128
OFF_GNG = OFF_PSC + 8
OFF_SUB = OFF_GNG + 24
OFF_CW = OFF_SUB + 4
OFF_CB = OFF_CW + 1056
OFF_SB = OFF_CB + 352
OFF_C31 = OFF_SB + 16
OFF_QDEC = OFF_C31 + 4
OFF_KDEC = OFF_QDEC + 6
OFF_FLAG = OFF_KDEC + 6
OFF_NEGB = OFF_FLAG + 1
OFF_PCORR = OFF_NEGB + 1
NCOLS = OFF_PCORR + 64


class KB:
    def __init__(self, nc):
        self.nc = nc
        self.eng = {'pe': nc.tensor, 'act': nc.scalar, 'dve': nc.vector, 'pool': nc.gpsimd, 'sp': nc.sync}
        self.sems = []
        self.cur = {}
        self.cnt = {}
        self.waited = {e: {} for e in self.eng}
        self.own = {e: set() for e in self.eng}
        self.track = {}
        self.pending = {e: ([], []) for e in self.eng}
        self.dpool = {}
        self.didx = {}
        self.ccsid = None
        self.cctot = 0
        self.nops = 0

    def new_sem(self, name):
        h = self.nc.alloc_semaphore(name=name)
        self.sems.append(h)
        return len(self.sems) - 1

    def _deps(self, reads, writes, e=None):
        evs = []
        own = self.own.get(e, ())
        for k in reads:
            t = self.track.get(k)
            if t and t[0]:
                evs.append(t[0])
            if t and isinstance(k, tuple) and k[0] == "PB":
                for sid, val in t[1].items():
                    if sid not in own:
                        evs.append((sid, val))
        for k in writes:
            t = self.track.get(k)
            if t:
                if t[0]:
                    evs.append(t[0])
                evs.extend(t[1].items())
        return evs

    def _wait(self, e, evs):
        need = {}
        for sid, val in evs:
            if need.get(sid, 0) < val:
                need[sid] = val
        w = self.waited[e]
        for sid, val in need.items():
            if w.get(sid, 0) < val:
                self.eng[e].wait_ge(self.sems[sid], val)
                w[sid] = val

    def _commit(self, e, ev):
        pr, pw = self.pending[e]
        for k in pr:
            t = self.track.setdefault(k, [None, {}])
            if t[1].get(ev[0], 0) < ev[1]:
                t[1][ev[0]] = ev[1]
        for k in pw:
            self.track[k] = [ev, {}]
        pr.clear()
        pw.clear()

    def op(self, e, fn, reads=(), writes=(), signal=True):
        self._wait(e, self._deps(reads, writes, e))
        inst = fn(self.eng[e])
        self.nops += 1
        pr, pw = self.pending[e]
        pr.extend(reads)
        pw.extend(writes)
        if not signal:
            return None
        if e not in self.cur or self.cnt[e] >= 6000:
            self.cur[e] = self.new_sem("s_%s_%d" % (e, len(self.sems)))
            self.own[e].add(self.cur[e])
            self.cnt[e] = 0
        self.cnt[e] += 1
        inst.then_inc(self.sems[self.cur[e]], 1)
        ev = (self.cur[e], self.cnt[e])
        self._commit(e, ev)
        return ev

    def dma(self, e, pairs, reads=(), writes=()):
        if e not in self.dpool:
            self.dpool[e] = [[self.new_sem("d_%s_%d" % (e, i)), 0] for i in range(6)]
            self.didx[e] = 0
        slot = self.dpool[e][self.didx[e] % len(self.dpool[e])]
        self.didx[e] += 1
        evs = self._deps(reads, writes)
        if slot[1] > 0:
            evs.append((slot[0], slot[1]))
        self._wait(e, evs)
        for (o, i) in pairs:
            self.eng[e].dma_start(out=o, in_=i).then_inc(self.sems[slot[0]], 16)
            slot[1] += 16
            self.nops += 1
        ev = (slot[0], slot[1])
        pr, pw = self.pending[e]
        for k in reads:
            t = self.track.setdefault(k, [None, {}])
            if t[1].get(ev[0], 0) < ev[1]:
                t[1][ev[0]] = ev[1]
        for k in writes:
            self.track[k] = [ev, {}]
        return ev

    def allgather(self, src, dst, reads=(), writes=()):
        e = 'pool'
        if self.ccsid is None:
            self.ccsid = self.new_sem("ccsem")
        evs = self._deps(reads, writes)
        if self.cctot > 0:
            evs.append((self.ccsid, self.cctot))
        self._wait(e, evs)
        inst = self.nc.gpsimd.collective_compute("AllGather", ALU.bypass, replica_groups=self.pairs,
                                                 ins=[src], outs=[dst])
        inst.then_inc(self.sems[self.ccsid])
        self.cctot += 1
        ev = (self.ccsid, self.cctot)
        for k in reads:
            t = self.track.setdefault(k, [None, {}])
            t[1][ev[0]] = ev[1]
        for k in writes:
            self.track[k] = [ev, {}]
        return ev

    def barrier(self):
        evs = []
        for e2 in self.cur:
            evs.append((self.cur[e2], self.cnt[e2]))
        for e2 in self.dpool:
            if e2 == 'pool':
                continue
            for sid, tot in self.dpool[e2]:
                if tot:
                    evs.append((sid, tot))
        if self.cctot:
            evs.append((self.ccsid, self.cctot))
        for e in self.eng:
            self._wait(e, evs)


class StopBuild(Exception):
    pass


class Arena:
    def __init__(self, ap_f32, nbytes):
        self.ap = ap_f32
        self.n = nbytes
        self.off = 0

    def reset(self, off=0):
        self.off = off

    def alloc(self, shape, dt):
        esz = 4 if dt == F32 else 2
        free = 1
        for s in shape[1:]:
            free *= s
        nb = (free * esz + 31) // 32 * 32
        assert self.off + nb <= self.n, ("arena overflow", self.off, nb, self.n)
        v = self.ap[:, self.off // 4:(self.off + nb) // 4]
        self.off += nb
        if dt != F32:
            v = v.bitcast(dt)
        v = v[:, 0:free]
        if len(shape) == 3:
            v = v.rearrange("p (a b) -> p a b", a=shape[1])
        elif len(shape) == 4:
            v = v.rearrange("p (a b c) -> p a b c", a=shape[1], b=shape[2])
        return v


def build(n_layers=4, dbg_h=False, stop_after=None, ncores=NCORES):
    nc = bass.Bass("TRN2", target_bir_lowering=False)

    def din(name, shape):
        return nc.dram_tensor(name, list(shape), F32, kind="ExternalInput").ap()

    x_d = din("x", [T, D])
    n_ev = max(1, (n_layers + 1) // 2)
    n_od = max(1, n_layers // 2)
    n_ff = max(1, n_layers)
    w_in_even = din("w_in_even", [n_ev, D, 5120])
    w_out_even = din("w_out_even", [n_ev, D, D])
    pool_w = din("pool_w", [2, 4, 128, 128])
    w_in_odd = din("w_in_odd", [n_od, D, 5120])
    w_out_odd = din("w_out_odd", [n_od, D, D])
    w_up = din("w_up", [n_ff, D, 2 * DFF])
    w_down = din("w_down", [n_ff, DFF, D])
    c_ident = din("c_ident", [128, 128])
    c_rot = din("c_rot", [128, 2, 8, 64])
    c_intra = din("c_intra", [128, 6, 128])
    c_mask = din("c_mask", [128, 128])
    p_cols = din("p_cols", [128, NCOLS])
    p_fng = din("p_fng", [128, D])
    p_sln = din("p_sln", [128, 2, 1024])
    p_lam = din("p_lam", [128, 2, 4, 128])
    p_biasT = din("p_biasT", [128, 4, 2, 128])
    p_sguT = din("p_sguT", [2, 128, 8, 128])
    out_d = nc.dram_tensor("out", [T, D], F32, kind="ExternalOutput").ap()

    hspill = nc.dram_tensor("hspill", [T, D], F32).ap()
    xu_src = nc.dram_tensor("xu_src", [128, 64], F32).ap()
    xu_dst = nc.dram_tensor("xu_dst", [256, 64], F32).ap()
    st_src = nc.dram_tensor("st_src", [128, 256], F32).ap()
    st_dst = nc.dram_tensor("st_dst", [256, 256], F32).ap()
    xh_src = nc.dram_tensor("xh_src", [128, 16], F32).ap()
    xh_dst = nc.dram_tensor("xh_dst", [256, 16], F32).ap()
    KVW = 4096 + 4112
    KVQ = KVW // 4
    kv_src = [nc.dram_tensor("kv_src%d" % i, [128, KVQ], F32).ap() for i in range(4)]
    kv_dst = [nc.dram_tensor("kv_dst%d" % i, [256, KVQ], F32).ap() for i in range(4)]

    ARENA_BYTES = 40 * 1024
    with ExitStack() as es:
        def sb(name, shape, dt):
            return es.enter_context(nc.sbuf_tensor(name, shape, dt))

        H = sb("H", [P, NT, D], F32)
        XT = sb("XT", [P, KC, T + 2], BF16)
        MT = sb("MT", [P, KC, T], BF16)
        WB = [sb("WB0", [P, KC, 512], BF16), sb("WB1", [P, KC, 512], BF16)]
        PC = sb("PC", [P, NCOLS], F32)
        IDB = sb("IDB", [P, 128], BF16)
        MASK = sb("MASK", [P, 128], F32)
        SS = sb("SS", [P, 48], F32)
        ARN = sb("ARN", [P, ARENA_BYTES // 4], F32)
        PB = [es.enter_context(nc.psum_tensor("pb%d" % i, [P, 512], F32)) for i in range(8)]
        ar = Arena(ARN, ARENA_BYTES)
        kb = KB(nc)
        kb.pairs = [[2 * i, 2 * i + 1] for i in range(ncores // 2)]

        def chk(name):
            if stop_after == name:
                raise StopBuild()

        def pcol(off, n=1):
            return PC[:, off:off + n]

        kb.dma('sp', [(PC[:, :], p_cols[:, :])], writes=["PC"])
        kb.dma('sp', [(MASK[:, :], c_mask[:, :])], writes=["MASK"])
        kb.dma('pool', [(IDB[:, :], c_ident[:, :])], writes=["IDB"])
        for t in range(NT):
            kb.dma('sp', [(H[:, t, :], x_d[t * 128:(t + 1) * 128, :])], writes=[("H", t)])

        bank_rr = [0]

        def bank():
            b = bank_rr[0] % 8
            bank_rr[0] += 1
            return b

        wslot = [0]

        preloaded = {}

        def prefetch(key, pieces, nk=KC):
            preloaded[key] = load_w(pieces, nk)

        def load_w(pieces, nk=KC, key=None):
            if key is not None and key in preloaded:
                return preloaded.pop(key)
            s = wslot[0] % 2
            wslot[0] += 1
            pairs = []
            for (co, src) in pieces:
                ncols = src.shape[1]
                v = src.rearrange("(kc p) c -> p kc c", p=128)
                step = 4 if ncols * 4 >= 2048 else 8
                for k0 in range(0, nk, step):
                    k1 = min(nk, k0 + step)
                    pairs.append((WB[s][:, k0:k1, co:co + ncols], v[:, k0:k1, :]))
            kb.dma('pool', pairs, writes=[("WB", s)])
            return s

        def norm_phase(gidx, order, halo, first=False):
            ar.reset(27 * 1024)
            JK = ar.alloc([P, D], BF16)
            HN = [ar.alloc([P, D], BF16) for _ in range(2)]
            XH = ar.alloc([P, 32], BF16)
            XHF = ar.alloc([P, 16], F32)

            def stA_act(oi, t):
                kb.op('act', lambda e: e.activation(out=JK[:, :], in_=H[:, t, :], func=AF.Square,
                                                    accum_out=SS[:, t:t + 1]),
                      reads=[("H", t)], writes=["JK", ("SS", t)])
                kb.op('act', lambda e: e.activation(out=SS[:, 8 + t:9 + t], in_=SS[:, t:t + 1], func=AF.Sqrt,
                                                    scale=1.0 / D, bias=EPS),
                      reads=[("SS", t)], writes=[("SS", 8 + t)])

            def stA_dve(oi, t):
                hn = HN[oi % 2]
                kb.op('dve', lambda e: e.reciprocal(out=SS[:, 16 + t:17 + t], in_=SS[:, 8 + t:9 + t]),
                      reads=[("SS", 8 + t)], writes=[("SS", 16 + t)])
                kb.op('dve', lambda e: e.tensor_scalar_mul(out=hn[:, :], in0=H[:, t, :],
                                                           scalar1=SS[:, 16 + t:17 + t]),
                      reads=[("H", t), ("SS", 16 + t)], writes=[("HN", oi % 2)])

            def stB(oi, t):
                hn = HN[oi % 2]
                hk = ("HN", oi % 2)
                for half in range(2):
                    b = bank()
                    pbb = PB[b][:, :].bitcast(BF16).rearrange("p (a b) -> p a b", a=8)
                    for i in range(8):
                        kc = half * 8 + i
                        kb.op('pe', lambda e: e.transpose(out=pbb[:, i, :], in_=hn[:, kc * 128:(kc + 1) * 128],
                                                          identity=IDB[:, :]),
                              reads=[hk, "IDB"], writes=[("PB", b)], signal=(i == 7))
                    for i in range(8):
                        kc = half * 8 + i
                        g = pcol(OFF_NG + gidx * 16 + kc)
                        dst = XT[:, kc, 2 + t * 128:2 + (t + 1) * 128]
                        if half == 0:
                            kb.op('act', lambda e: e.activation(out=dst, in_=pbb[:, i, :], func=AF.Copy, scale=g),
                                  reads=[("PB", b), "PC"], writes=[("XT", t)])
                        else:
                            kb.op('dve', lambda e: e.tensor_scalar_mul(out=dst, in0=pbb[:, i, :], scalar1=g),
                                  reads=[("PB", b), "PC"], writes=[("XT", t)])

            n = len(order)
            stA_act(0, order[0])
            stA_dve(0, order[0])
            for oi, t in enumerate(order):
                if oi + 1 < n:
                    stA_act(oi + 1, order[oi + 1])
                stB(oi, t)
                if oi + 1 < n:
                    stA_dve(oi + 1, order[oi + 1])
                if halo and t == NT - 1:
                    kb.op('act', lambda e: e.activation(out=XH[:, :].rearrange("p (a b) -> p a b", a=16),
                                                        in_=XT[:, :, T:T + 2], func=AF.Copy),
                          reads=[("XT", t)], writes=["XH"])
                    kb.dma('sp', [(xh_src[:, :], XH[:, :].bitcast(F32))], reads=["XH"], writes=["xh_src"])
                    kb.allgather(xh_src[:, :], xh_dst[:, :], reads=["xh_src"], writes=["xh_dst"])
                    kb.dma('sp', [(XHF[:, :], xh_dst[0:128, :])], reads=["xh_dst"], writes=["XHF"])
            if halo:
                kb.op('dve', lambda e: e.tensor_scalar_mul(
                    out=XT[:, :, 0:2], in0=XHF[:, :].bitcast(BF16).rearrange("p (a b) -> p a b", a=16),
                    scalar1=pcol(OFF_FLAG)), reads=["XHF", "PC"], writes=["XTH"])

        def out_proj(chunks, nk, key=None):
            slots = [load_w([(0, chunks[0])], nk, key=(key, 0) if key else None)]
            for cb in range(4):
                if cb + 1 < 4:
                    slots.append(load_w([(0, chunks[cb + 1])], nk, key=(key, cb + 1) if key else None))
                s = slots[cb]
                for t in range(NT):
                    b = bank()
                    for kc in range(nk):
                        kb.op('pe', lambda e: e.matmul(PB[b][:, :], lhsT=MT[:, kc, t * 128:(t + 1) * 128],
                                                       rhs=WB[s][:, kc, :], start=(kc == 0), stop=(kc == nk - 1)),
                              reads=[("MT", kc, t), ("WB", s)], writes=[("PB", b)], signal=(kc == nk - 1))
                    hs = H[:, t, cb * 512:(cb + 1) * 512]
                    kb.op('dve', lambda e: e.tensor_tensor(out=hs, in0=hs, in1=PB[b][:, :], op=ALU.add),
                          reads=[("PB", b), ("H", t)], writes=[("H", t)])

        def ffn(l):
            ar.reset()
            AG = [ar.alloc([P, 514], F32), ar.alloc([P, 514], F32)]
            AV = [ar.alloc([P, 514], F32), ar.alloc([P, 514], F32)]
            CG = [ar.alloc([P, 512], F32), ar.alloc([P, 512], F32)]
            CV = [ar.alloc([P, 512], F32), ar.alloc([P, 512], F32)]
            SG = [ar.alloc([P, 512], F32), ar.alloc([P, 512], F32)]
            wup = w_up[l]
            wdn = w_down[l]
            groups = [(0, 12), (12, 12), (24, 10), (34, 10)]
            PW3 = 342

            def up_pieces(j0):
                return [(0, wup[:, j0 * 128:(j0 + 2) * 128]), (256, wup[:, DFF + j0 * 128:DFF + (j0 + 2) * 128])]

            pair_list = []
            for (g0, gn) in groups:
                for j0 in range(g0, g0 + gn, 2):
                    pair_list.append(j0)
            nxt = None
            it = 0
            for (g0, gn) in groups:
                for j0 in range(g0, g0 + gn, 2):
                    s = nxt if nxt is not None else load_w(up_pieces(j0), key=("up", l, j0))
                    nxt = None
                    if j0 + 2 < g0 + gn:
                        nxt = load_w(up_pieces(j0 + 2))
                    for jj in range(2):
                        j = j0 + jj
                        jl = j - g0
                        wg = lambda kc: WB[s][:, kc, jj * 128:(jj + 1) * 128]
                        wv = lambda kc: WB[s][:, kc, 256 + jj * 128:256 + (jj + 1) * 128]
                        cw = lambda r: pcol(OFF_CW + l * 264 + r * 88 + j)
                        cbias = pcol(OFF_CB + l * 88 + j)
                        for n in range(3):
                            q = it % 2
                            it += 1
                            bg = bank()
                            bv = bank()
                            c0 = n * PW3
                            tlo = max(0, c0 - 2) // 128
                            thi = (c0 + PW3 - 3) // 128
                            xt_keys = [("XT", t) for t in range(tlo, thi + 1)] + (["XTH"] if n == 0 else [])
                            for kc in range(KC):
                                kb.op('pe', lambda e: e.matmul(PB[bg][:, 0:PW3], lhsT=wg(kc), rhs=XT[:, kc, c0:c0 + PW3],
                                                               start=(kc == 0), stop=(kc == KC - 1)),
                                      reads=xt_keys + [("WB", s)], writes=[("PB", bg)], signal=(kc == KC - 1))
                            for kc in range(KC):
                                kb.op('pe', lambda e: e.matmul(PB[bv][:, 0:PW3], lhsT=wv(kc), rhs=XT[:, kc, c0:c0 + PW3],
                                                               start=(kc == 0), stop=(kc == KC - 1)),
                                      reads=xt_keys + [("WB", s)], writes=[("PB", bv)], signal=(kc == KC - 1))
                            ag, av, cg, cv, sg = AG[q], AV[q], CG[q], CV[q], SG[q]
                            kag, kav, kcg, kcv, ksg = ("AG", q), ("AV", q), ("CG", q), ("CV", q), ("SG", q)
                            lo = 4 if n == 0 else 2
                            wo = PW3 + 2 - lo
                            tok0 = 0 if n == 0 else n * PW3 - 2
                            kb.op('act', lambda e: e.activation(out=ag[:, 2:2 + PW3], in_=PB[bg][:, 0:PW3], func=AF.Copy),
                                  reads=[("PB", bg)], writes=[kag])
                            kb.op('act', lambda e: e.activation(out=av[:, 2:2 + PW3], in_=PB[bv][:, 0:PW3], func=AF.Copy),
                                  reads=[("PB", bv)], writes=[kav])
                            if n > 0:
                                pq = 1 - q
                                kb.op('dve', lambda e: e.tensor_copy(out=ag[:, 0:2], in_=AG[pq][:, PW3:PW3 + 2]),
                                      reads=[("AG", pq)], writes=[kag])
                                kb.op('dve', lambda e: e.tensor_copy(out=av[:, 0:2], in_=AV[pq][:, PW3:PW3 + 2]),
                                      reads=[("AV", pq)], writes=[kav])
                            kb.op('act', lambda e: e.activation(out=cg[:, 0:wo], in_=PB[bg][:, lo - 2:lo - 2 + wo], func=AF.Identity,
                                                                scale=cw(2), bias=cbias),
                                  reads=[("PB", bg), "PC"], writes=[kcg])
                            kb.op('act', lambda e: e.activation(out=cv[:, 0:wo], in_=PB[bv][:, lo - 2:lo - 2 + wo], func=AF.Identity,
                                                                scale=CWV(l, 2, j), bias=CBV(l, j)),
                                  reads=[("PB", bv), "PC"], writes=[kcv])
                            kb.op('dve', lambda e: e.scalar_tensor_tensor(out=cg[:, 0:wo], in0=ag[:, lo - 1:lo - 1 + wo], scalar=cw(1),
                                                                          in1=cg[:, 0:wo], op0=ALU.mult, op1=ALU.add),
                                  reads=[kag, "PC", kcg], writes=[kcg])
                            kb.op('dve', lambda e: e.scalar_tensor_tensor(out=cg[:, 0:wo], in0=ag[:, lo - 2:lo - 2 + wo], scalar=cw(0),
                                                                          in1=cg[:, 0:wo], op0=ALU.mult, op1=ALU.add),
                                  reads=[kag, "PC", kcg], writes=[kcg])
                            kb.op('dve', lambda e: e.scalar_tensor_tensor(out=cv[:, 0:wo], in0=av[:, lo - 1:lo - 1 + wo],
                                                                          scalar=CWV(l, 1, j), in1=cv[:, 0:wo],
                                                                          op0=ALU.mult, op1=ALU.add),
                                  reads=[kav, "PC", kcv], writes=[kcv])
                            kb.op('dve', lambda e: e.scalar_tensor_tensor(out=cv[:, 0:wo], in0=av[:, lo - 2:lo - 2 + wo],
                                                                          scalar=CWV(l, 0, j), in1=cv[:, 0:wo],
                                                                          op0=ALU.mult, op1=ALU.add),
                                  reads=[kav, "PC", kcv], writes=[kcv])
                            kb.op('act', lambda e: e.activation(out=sg[:, 0:wo], in_=cg[:, 0:wo], func=AF.Silu),
                                  reads=[kcg], writes=[ksg])
                            kb.op('dve', lambda e: e.tensor_tensor(out=MT[:, jl, tok0:tok0 + wo], in0=sg[:, 0:wo],
                                                                   in1=cv[:, 0:wo], op=ALU.mult),
                                  reads=[ksg, kcv],
                                  writes=[("MT", jl, t) for t in range(tok0 // 128, (tok0 + wo - 1) // 128 + 1)])
                out_proj([wdn[g0 * 128:(g0 + gn) * 128, cb * 512:(cb + 1) * 512] for cb in range(4)], gn)
                if (g0, gn) != groups[-1]:
                    pass

        def CWV(l, r, j):
            return pcol(OFF_CW + l * 264 + r * 88 + NFF + j)

        def CBV(l, j):
            return pcol(OFF_CB + l * 88 + NFF + j)

        def even_mixer(e_idx):
            kb.barrier()
            ar.reset()
            win = w_in_even[e_idx]
            wout = w_out_even[e_idx]
            I0, I1, I2, I3 = 512, 512 + 768, 512 + 1536, 512 + 1536 + 1536
            ROT = ar.alloc([P, 2, 8, 64], F32)
            INTRA = ar.alloc([P, 6, 128], F32)
            U = ar.alloc([P, 4, 16 + T], BF16)
            PW = ar.alloc([P, 4, 128], BF16)
            UHS = ar.alloc([P, 4, 16], F32)
            UHR = ar.alloc([P, 4, 16], F32)
            mark = ar.off
            kb.dma('sp', [(ROT[:, :, :, :], c_rot[:, :, :, :])], writes=["ROT"])
            kb.dma('sp', [(INTRA[:, :, :], c_intra[:, :, :])], writes=["INTRA"])
            kb.dma('pool', [(PW[:, :, :], pool_w[e_idx].rearrange("g c d -> c g d"))], writes=["PW"])

            s = load_w([(0, win[:, 0:512])], key=("eu", e_idx))
            for g in range(4):
                for n in range(2):
                    b = bank()
                    for kc in range(KC):
                        kb.op('pe', lambda e: e.matmul(PB[b][:, :], lhsT=WB[s][:, kc, g * 128:(g + 1) * 128],
                                                       rhs=XT[:, kc, 2 + n * 512:2 + (n + 1) * 512],
                                                       start=(kc == 0), stop=(kc == KC - 1)),
                              reads=[("XT", t) for t in range(n * 4, n * 4 + 4)] + [("WB", s)],
                              writes=[("PB", b)], signal=(kc == KC - 1))
                    kb.op('act', lambda e: e.activation(out=U[:, g, 16 + n * 512:16 + (n + 1) * 512], in_=PB[b][:, :],
                                                        func=AF.Copy),
                          reads=[("PB", b)], writes=[("U", g, n)])
            kb.op('act', lambda e: e.activation(out=UHS[:, :, :], in_=U[:, :, T:T + 16], func=AF.Copy),
                  reads=[("U", g, 1) for g in range(4)], writes=["UHS"])
            kb.dma('sp', [(xu_src[:, :], UHS[:, :, :].rearrange("p a b -> p (a b)"))], reads=["UHS"], writes=["xu_src"])
            kb.allgather(xu_src[:, :], xu_dst[:, :], reads=["xu_src"], writes=["xu_dst"])
            kb.dma('sp', [(UHR[:, :, :].rearrange("p a b -> p (a b)"), xu_dst[0:128, :])], reads=["xu_dst"],
                   writes=["UHR"])

            if stop_after == "mix_a":
                return
            GS = ar.alloc([P, 8, 256], BF16)
            RSB = ar.alloc([P, 8, 256], BF16)
            QDT = ar.alloc([P, 8, 128], BF16)
            STATE = ar.alloc([P, 256], F32)
            STATEB = ar.alloc([P, 256], BF16)
            mark_r = ar.off
            NB3 = 4
            for h in range(6):
                dec = 1.0 - 2.0 ** (-5.0 - h)
                chunk_dec = dec ** 128
                kb.barrier()
                ar.reset(mark_r)
                QK = [ar.alloc([P, 2, 2, 64], BF16) for _ in range(NB3)]
                VB = [ar.alloc([P, 256], BF16) for _ in range(NB3)]
                QD = [ar.alloc([P, 128], BF16) for _ in range(NB3)]
                KD = [ar.alloc([P, 128], BF16) for _ in range(NB3)]
                QKT = [ar.alloc([P, 2, 128], BF16) for _ in range(2)]
                ST = [ar.alloc([P, 128], BF16) for _ in range(2)]
                R1 = ar.alloc([P, 2, 64], F32)
                R2 = ar.alloc([P, 2, 64], F32)
                R3 = ar.alloc([P, 2, 64], F32)
                R4 = ar.alloc([P, 2, 64], F32)
                def g_pieces(hx):
                    return [(0, win[:, I3 + hx * 256:I3 + (hx + 1) * 256])]

                def qkv_pieces(hx):
                    return [(0, win[:, I0 + hx * 128:I0 + (hx + 1) * 128]),
                            (128, win[:, I1 + hx * 128:I1 + (hx + 1) * 128]),
                            (256, win[:, I2 + hx * 256:I2 + (hx + 1) * 256])]

                sg_ = load_w(g_pieces(h), key=("eg", e_idx, h))
                sq_ = load_w(qkv_pieces(h), key=("eq", e_idx, h))
                kb.op('dve', lambda e: e.memset(STATE[:, :], 0.0), writes=["STATE"])
                kb.op('dve', lambda e: e.memset(STATEB[:, :], 0.0), writes=["STATEB"])

                def st_P(t):
                    b = t % 2
                    for kc in range(KC):
                        kb.op('pe', lambda e: e.matmul(PB[b][:, :], lhsT=XT[:, kc, 2 + t * 128:2 + (t + 1) * 128],
                                                       rhs=WB[sq_][:, kc, :], start=(kc == 0), stop=(kc == KC - 1)),
                              reads=[("XT", t), ("WB", sq_)], writes=[("PB", b)], signal=(kc == KC - 1))
                def st_ROT(t):
                    b = t % 2
                    q3 = t % NB3
                    qk, vb, qd, kd = QK[q3], VB[q3], QD[q3], KD[q3]
                    z4 = PB[b][:, 0:256].rearrange("p (a b c) -> p a b c", a=2, b=2)
                    x1 = z4[:, :, 0, :]
                    x2 = z4[:, :, 1, :]
                    cosb = ROT[:, 0, t, :].unsqueeze(1).to_broadcast([P, 2, 64])
                    sinb = ROT[:, 1, t, :].unsqueeze(1).to_broadcast([P, 2, 64])
                    kb.op('dve', lambda e: e.tensor_tensor(out=R1[:, :, :], in0=x1, in1=cosb, op=ALU.mult),
                          reads=[("PB", b), "ROT"], writes=["R1"])
                    kb.op('dve', lambda e: e.tensor_tensor(out=R2[:, :, :], in0=x2, in1=sinb, op=ALU.mult),
                          reads=[("PB", b), "ROT"], writes=["R2"])
                    kb.op('dve', lambda e: e.tensor_tensor(out=qk[:, :, 0, :], in0=R1[:, :, :], in1=R2[:, :, :],
                                                           op=ALU.subtract),
                          reads=["R1", "R2"], writes=[("QK0", q3)])
                    kb.op('dve', lambda e: e.tensor_tensor(out=R3[:, :, :], in0=x1, in1=sinb, op=ALU.mult),
                          reads=[("PB", b), "ROT"], writes=["R3"])
                    kb.op('dve', lambda e: e.tensor_tensor(out=R4[:, :, :], in0=x2, in1=cosb, op=ALU.mult),
                          reads=[("PB", b), "ROT"], writes=["R4"])
                    kb.op('dve', lambda e: e.tensor_tensor(out=qk[:, :, 1, :], in0=R3[:, :, :], in1=R4[:, :, :],
                                                           op=ALU.add),
                          reads=["R3", "R4"], writes=[("QK1", q3)])
                    kb.op('act', lambda e: e.activation(out=vb[:, :], in_=PB[b][:, 256:512], func=AF.Copy),
                          reads=[("PB", b)], writes=[("VB", q3)])
                    qf = qk[:, 0, :, :].rearrange("p a b -> p (a b)")
                    kf = qk[:, 1, :, :].rearrange("p a b -> p (a b)")
                    kb.op('act', lambda e: e.activation(out=qd[:, :], in_=qf, func=AF.Copy, scale=pcol(OFF_QDEC + h)),
                          reads=[("QK0", q3), ("QK1", q3), "PC"], writes=[("QD", q3)])
                    kb.op('act', lambda e: e.activation(out=kd[:, :], in_=kf, func=AF.Copy, scale=pcol(OFF_KDEC + h)),
                          reads=[("QK0", q3), ("QK1", q3), "PC"], writes=[("KD", q3)])

                def st_TR(t):
                    q3 = t % NB3
                    qk, qd = QK[q3], QD[q3]
                    qf = qk[:, 0, :, :].rearrange("p a b -> p (a b)")
                    kf = qk[:, 1, :, :].rearrange("p a b -> p (a b)")
                    bt = 2 + (t % 2)
                    pbt = PB[bt][:, :].bitcast(BF16).rearrange("p (a b) -> p a b", a=8)
                    kb.op('pe', lambda e: e.transpose(out=pbt[:, 0, :], in_=qf, identity=IDB[:, :]),
                          reads=[("QK0", q3), ("QK1", q3), "IDB"], writes=[("PB", bt)], signal=False)
                    kb.op('pe', lambda e: e.transpose(out=pbt[:, 1, :], in_=kf, identity=IDB[:, :]),
                          reads=[("QK0", q3), ("QK1", q3), "IDB"], writes=[("PB", bt)], signal=False)
                    kb.op('pe', lambda e: e.transpose(out=pbt[:, 2, :], in_=qd[:, :], identity=IDB[:, :]),
                          reads=[("QD", q3), "IDB"], writes=[("PB", bt)])
                    kb.op('dve', lambda e: e.tensor_copy(out=QKT[t % 2][:, :, :], in_=pbt[:, 0:2, :]),
                          reads=[("PB", bt)], writes=[("QKT", t % 2)])
                    kb.op('dve', lambda e: e.tensor_copy(out=QDT[:, t, :], in_=pbt[:, 2, :]),
                          reads=[("PB", bt)], writes=[("QDT", t)])

                def st_S(t):
                    bs = 4
                    qkt = QKT[t % 2]
                    kb.op('pe', lambda e: e.matmul(PB[bs][:, 0:128], lhsT=qkt[:, 1, :], rhs=qkt[:, 0, :],
                                                   start=True, stop=True),
                          reads=[("QKT", t % 2)], writes=[("PB", bs)])
                    kb.op('dve', lambda e: e.tensor_tensor(out=ST[t % 2][:, :], in0=PB[bs][:, 0:128], in1=INTRA[:, h, :],
                                                           op=ALU.mult),
                          reads=[("PB", bs), "INTRA"], writes=[("ST", t % 2)])

                def st_R(t):
                    q3 = t % NB3
                    vb, kd = VB[q3], KD[q3]
                    br, bk = 5, 6
                    kb.op('pe', lambda e: e.matmul(PB[br][:, 0:256], lhsT=ST[t % 2][:, :], rhs=vb[:, :], start=True, stop=False),
                          reads=[("ST", t % 2), ("VB", q3)], writes=[("PB", br)], signal=False)
                    kb.op('pe', lambda e: e.matmul(PB[br][:, 0:256], lhsT=QDT[:, t, :], rhs=STATEB[:, :],
                                                   start=False, stop=True),
                          reads=[("QDT", t), "STATEB"], writes=[("PB", br)])
                    kb.op('pe', lambda e: e.matmul(PB[bk][:, 0:256], lhsT=kd[:, :], rhs=vb[:, :], start=True, stop=True),
                          reads=[("KD", q3), ("VB", q3)], writes=[("PB", bk)])
                    kb.op('act', lambda e: e.activation(out=RSB[:, t, :], in_=PB[br][:, 0:256], func=AF.Copy),
                          reads=[("PB", br)], writes=[("RSB", t)])
                    kb.op('dve', lambda e: e.scalar_tensor_tensor(out=STATE[:, :], in0=STATE[:, :], scalar=chunk_dec,
                                                                  in1=PB[bk][:, 0:256], op0=ALU.mult, op1=ALU.add),
                          reads=["STATE", ("PB", bk)], writes=["STATE"])
                    kb.op('act', lambda e: e.activation(out=STATEB[:, :], in_=STATE[:, :], func=AF.Copy),
                          reads=["STATE"], writes=["STATEB"])

                for i in range(NT + 3):
                    if i < NT:
                        st_P(i)
                    if 0 <= i - 1 < NT:
                        st_TR(i - 1)
                    if 0 <= i - 2 < NT:
                        st_S(i - 2)
                    if 0 <= i - 3 < NT:
                        st_R(i - 3)
                    if i < NT:
                        st_ROT(i)
                if stop_after == "mix_b1":
                    return
                kb.dma('sp', [(st_src[:, :], STATE[:, :])], reads=["STATE"], writes=["st_src"])
                kb.allgather(st_src[:, :], st_dst[:, :], reads=["st_src"], writes=["st_dst"])
                nxt_q = None
                for t in range(NT):
                    b = t % 2
                    for kc in range(KC):
                        kb.op('pe', lambda e: e.matmul(PB[b][:, 0:256], lhsT=XT[:, kc, 2 + t * 128:2 + (t + 1) * 128],
                                                       rhs=WB[sg_][:, kc, 0:256], start=(kc == 0), stop=(kc == KC - 1)),
                              reads=[("XT", t), ("WB", sg_)], writes=[("PB", b)], signal=(kc == KC - 1))
                    kb.op('act', lambda e: e.activation(out=GS[:, t, :], in_=PB[b][:, 0:256], func=AF.Silu),
                          reads=[("PB", b)], writes=[("GS", t)])
                if h + 1 < 6:
                    prefetch(("eg", e_idx, h + 1), g_pieces(h + 1))
                    prefetch(("eq", e_idx, h + 1), qkv_pieces(h + 1))
                else:
                    prefetch((("wo", "e", e_idx), 0), [(0, wout[:, 0:512])])
                    prefetch((("wo", "e", e_idx), 1), [(0, wout[:, 512:1024])])
                kb.barrier()
                ar.reset(mark_r)
                SA = ar.alloc([P, 256], F32)
                SC = ar.alloc([P, 256], F32)
                SCB = [ar.alloc([P, 256], BF16) for _ in range(2)]
                RT = [ar.alloc([P, 256], F32) for _ in range(2)]
                YF = [ar.alloc([P, 256], F32) for _ in range(2)]
                YG = [ar.alloc([P, 256], BF16) for _ in range(2)]
                BNS = [ar.alloc([P, 8], F32) for _ in range(2)]
                MV = [ar.alloc([P, 4], F32) for _ in range(2)]
                kb.dma('sp', [(SA[:, :], st_dst[0:128, :])], reads=["st_dst"], writes=["SA"])
                kb.op('dve', lambda e: e.tensor_scalar_mul(out=SC[:, :], in0=SA[:, :], scalar1=pcol(OFF_FLAG)),
                      reads=["SA", "PC"], writes=["SC"])

                def p2_hops(t):
                    q = t % 2
                    bc = 4 + q
                    hops = []
                    def hop0():
                        kb.op('act', lambda e: e.activation(out=SCB[q][:, :], in_=SC[:, :], func=AF.Copy),
                              reads=["SC"], writes=[("SCB", q)])
                        if t < NT - 1:
                            kb.op('dve', lambda e: e.tensor_scalar_mul(out=SC[:, :], in0=SC[:, :], scalar1=chunk_dec),
                                  reads=["SC"], writes=["SC"])
                    hops.append(hop0)
                    hops.append(lambda: kb.op('pe', lambda e: e.matmul(PB[bc][:, 0:256], lhsT=QDT[:, t, :], rhs=SCB[q][:, :],
                                                                       start=True, stop=True),
                                              reads=[("QDT", t), ("SCB", q)], writes=[("PB", bc)]))
                    hops.append(lambda: kb.op('dve', lambda e: e.tensor_tensor(out=RT[q][:, :], in0=RSB[:, t, :], in1=PB[bc][:, 0:256],
                                                                               op=ALU.add),
                                              reads=[("RSB", t), ("PB", bc)], writes=[("RT", q)]))
                    hops.append(lambda: kb.op('dve', lambda e: e.bn_stats(out=BNS[q][:, 0:6], in_=RT[q][:, :]),
                                              reads=[("RT", q)], writes=[("BNS", q)]))
                    hops.append(lambda: kb.op('dve', lambda e: e.bn_aggr(out=MV[q][:, 0:2], in_=BNS[q][:, 0:6]),
                                              reads=[("BNS", q)], writes=[("MV", q)]))
                    hops.append(lambda: kb.op('act', lambda e: e.activation(out=MV[q][:, 2:3], in_=MV[q][:, 1:2], func=AF.Sqrt, bias=EPS),
                                              reads=[("MV", q)], writes=[("MV2", q)]))
                    hops.append(lambda: kb.op('dve', lambda e: e.reciprocal(out=MV[q][:, 3:4], in_=MV[q][:, 2:3]),
                                              reads=[("MV2", q)], writes=[("MV3", q)]))
                    hops.append(lambda: kb.op('dve', lambda e: e.tensor_scalar(out=YF[q][:, :], in0=RT[q][:, :], scalar1=MV[q][:, 0:1],
                                                                               scalar2=MV[q][:, 3:4], op0=ALU.subtract, op1=ALU.mult),
                                              reads=[("RT", q), ("MV", q), ("MV3", q)], writes=[("YF", q)]))
                    hops.append(lambda: kb.op('pool', lambda e: e.tensor_tensor(out=YG[q][:, :], in0=YF[q][:, :], in1=GS[:, t, :], op=ALU.mult),
                                              reads=[("YF", q), ("GS", t)], writes=[("YG", q)]))
                    hops.append(lambda: p2_B(t))
                    return hops

                def p2_B(t):
                    q = t % 2
                    by = 6 + q
                    pby = PB[by][:, :].bitcast(BF16).rearrange("p (a b) -> p a b", a=8)
                    kb.op('pe', lambda e: e.transpose(out=pby[:, 0, :], in_=YG[q][:, 0:128], identity=IDB[:, :]),
                          reads=[("YG", q), "IDB"], writes=[("PB", by)], signal=False)
                    kb.op('pe', lambda e: e.transpose(out=pby[:, 1, :], in_=YG[q][:, 128:256], identity=IDB[:, :]),
                          reads=[("YG", q), "IDB"], writes=[("PB", by)])
                    for j in range(2):
                        kc = 4 + 2 * h + j
                        gcol = pcol(OFF_GNG + e_idx * 12 + 2 * h + j)
                        dst = MT[:, kc, t * 128:(t + 1) * 128]
                        kb.op('act', lambda e: e.activation(out=dst, in_=pby[:, j, :], func=AF.Copy, scale=gcol),
                              reads=[("PB", by), "PC"], writes=[("MT", kc, t)])

                for t0 in range(0, NT, 2):
                    ha = p2_hops(t0)
                    hb = p2_hops(t0 + 1)
                    for k in range(max(len(ha), len(hb))):
                        if k < len(ha):
                            ha[k]()
                        if k < len(hb):
                            hb[k]()

            if stop_after == "mix_b":
                return
            kb.barrier()
            ar.reset(mark)
            kb.op('dve', lambda e: e.tensor_scalar_mul(out=U[:, :, 0:16], in0=UHR[:, :, :], scalar1=pcol(OFF_FLAG)),
                  reads=["UHR", "PC"], writes=["UH"])
            S_A = ar.alloc([P, 528], F32)
            S_B = ar.alloc([P, 528], F32)
            YP = [ar.alloc([P, 512], BF16), ar.alloc([P, 512], BF16)]
            it = 0
            for g in range(4):
                w = 2 ** (g + 1)
                for n in range(2):
                    a0 = U[:, g, n * 512:n * 512 + 528]
                    rk = [("U", g, n), "UH"] + ([("U", g, 0)] if n == 1 else [])
                    kb.op('dve', lambda e: e.tensor_tensor(out=S_A[:, 1:528], in0=a0[:, 1:528], in1=a0[:, 0:527],
                                                           op=ALU.add), reads=rk, writes=["S_A"])
                    cur, curk, oth, othk = S_A, "S_A", S_B, "S_B"
                    sh = 1
                    for step in range(g):
                        sh2 = sh * 2
                        lo = 2 * sh2 - 1
                        kb.op('dve', lambda e: e.tensor_tensor(out=oth[:, lo:528], in0=cur[:, lo:528],
                                                               in1=cur[:, lo - sh2:528 - sh2], op=ALU.add),
                              reads=[curk], writes=[othk])
                        cur, curk, oth, othk = oth, othk, cur, curk
                        sh = sh2
                    if n == 0:
                        kb.op('dve', lambda e: e.tensor_tensor(out=cur[:, 16:32], in0=cur[:, 16:32],
                                                               in1=PC[:, OFF_PCORR + g * 16:OFF_PCORR + (g + 1) * 16],
                                                               op=ALU.mult),
                              reads=[curk, "PC"], writes=[curk])
                    yp = YP[it % 2]
                    ypk = ("YP", it % 2)
                    it += 1
                    kb.op('dve', lambda e: e.scalar_tensor_tensor(out=yp[:, :], in0=cur[:, 16:528], scalar=1.0 / w,
                                                                  in1=a0[:, 16:528], op0=ALU.mult, op1=ALU.subtract),
                          reads=[curk] + rk, writes=[ypk])
                    b = bank()
                    kb.op('pe', lambda e: e.matmul(PB[b][:, :], lhsT=PW[:, g, :], rhs=yp[:, :], start=True, stop=True),
                          reads=["PW", ypk], writes=[("PB", b)])
                    kb.op('act', lambda e: e.activation(out=MT[:, g, n * 512:(n + 1) * 512], in_=PB[b][:, :],
                                                        func=AF.Copy, scale=pcol(OFF_PSC + e_idx * 4 + g)),
                          reads=[("PB", b), "PC"], writes=[("MT", g, t) for t in range(n * 4, n * 4 + 4)])
            if stop_after == "mix_c":
                return
            out_proj([wout[:, cb * 512:(cb + 1) * 512] for cb in range(4)], KC, key=("wo", "e", e_idx))

        def odd_mixer(o_idx, layer_idx):
            kb.barrier()
            ar.reset()
            win = w_in_odd[o_idx]
            wout = w_out_odd[o_idx]
            J0, J1, J2 = 2048, 3072, 4096
            Hb = H[:, :, :].rearrange("p a b -> p (a b)").bitcast(BF16)
            KT = Hb[:, 0:8192].rearrange("p (a b) -> p a b", a=8)
            V1 = Hb[:, 8192:8192 + 8224].rearrange("p (a b c) -> p a b c", a=8, b=4)
            QT = Hb[:, 16416:16416 + 8192].rearrange("p (a b) -> p a b", a=8)
            Hf = H[:, :, :].rearrange("p a b -> p (a b)")
            XTb = XT[:, :, :].rearrange("p a b -> p (a b)")
            KTP = XTb[:, 0:8192].rearrange("p (a b) -> p a b", a=8)
            V1P = XTb[:, 8192:8192 + 8224].rearrange("p (a b c) -> p a b c", a=8, b=4)
            lam_init = 0.8 - 0.6 * math.exp(-0.3 * layer_idx)

            BIAS = ar.alloc([P, 4, 2, 128], F32)
            ED = ar.alloc([P, 4, 2, 128], BF16)
            LAMV = ar.alloc([P, 4, 128], F32)
            LT = ar.alloc([P, 2, 128], F32)
            LS = ar.alloc([P, 8], F32)
            CB = ar.alloc([P, 8], F32)
            mark0 = ar.off
            kb.dma('sp', [(BIAS[:, :, :, :], p_biasT[:, :, :, :])], writes=["BIAS"])
            kb.dma('sp', [(LAMV[:, :, :], p_lam[:, o_idx, :, :])], writes=["LAMV"])
            kb.op('act', lambda e: e.activation(out=ED[:, :, :, :], in_=BIAS[:, :, :, :], func=AF.Exp),
                  reads=["BIAS"], writes=["ED"])
            for hh in range(4):
                kb.op('dve', lambda e: e.tensor_tensor(out=ED[:, hh, 0, :], in0=ED[:, hh, 0, :], in1=MASK[:, :],
                                                       op=ALU.mult), reads=["ED", "MASK"], writes=["ED"])
            kb.op('dve', lambda e: e.tensor_tensor(out=LT[:, 0, :], in0=LAMV[:, 0, :], in1=LAMV[:, 1, :], op=ALU.mult),
                  reads=["LAMV"], writes=["LT0"])
            kb.op('dve', lambda e: e.tensor_tensor(out=LT[:, 1, :], in0=LAMV[:, 2, :], in1=LAMV[:, 3, :], op=ALU.mult),
                  reads=["LAMV"], writes=["LT1"])
            kb.op('dve', lambda e: e.reduce_sum(out=LS[:, 0:2], in_=LT[:, :, :], axis=mybir.AxisListType.X),
                  reads=["LT0", "LT1"], writes=["LS"])
            kb.op('act', lambda e: e.activation(out=LS[:, 2:4], in_=LS[:, 0:2], func=AF.Exp), reads=["LS"], writes=["LS2"])
            kb.op('dve', lambda e: e.tensor_tensor(out=LS[:, 4:5], in0=LS[:, 2:3], in1=LS[:, 3:4], op=ALU.subtract),
                  reads=["LS2"], writes=["LS4"])
            kb.op('dve', lambda e: e.tensor_scalar(out=LS[:, 5:6], in0=LS[:, 4:5], scalar1=lam_init, scalar2=-1.0,
                                                   op0=ALU.add, op1=ALU.mult), reads=["LS4"], writes=["LAM"])
            kb.op('dve', lambda e: e.tensor_copy(out=CB[:, 0:4], in_=PC[:, OFF_C31:OFF_C31 + 4]), reads=["PC"], writes=["CB0"])
            kb.op('dve', lambda e: e.tensor_scalar_add(out=CB[:, 4:8], in0=PC[:, OFF_C31:OFF_C31 + 4], scalar1=pcol(OFF_NEGB)),
                  reads=["PC"], writes=["CB1"])

            def proj_fm(col0, dst, nm):
                for c in range(2):
                    s = load_w([(0, win[:, col0 + c * 512:col0 + (c + 1) * 512])], key=("ofm", o_idx, nm, c))
                    for f in range(4):
                        for n in range(2):
                            b = bank()
                            for kc in range(KC):
                                kb.op('pe', lambda e: e.matmul(PB[b][:, :], lhsT=WB[s][:, kc, f * 128:(f + 1) * 128],
                                                               rhs=XT[:, kc, 2 + n * 512:2 + (n + 1) * 512],
                                                               start=(kc == 0), stop=(kc == KC - 1)),
                                      reads=[("XT", t) for t in range(n * 4, n * 4 + 4)] + [("WB", s)],
                                      writes=[("PB", b)], signal=(kc == KC - 1))
                            kb.op('act', lambda e: e.activation(out=dst[:, c * 4 + f, n * 512:(n + 1) * 512],
                                                                in_=PB[b][:, :], func=AF.Copy),
                                  reads=[("PB", b)], writes=[(nm, c * 4 + f, n)])

            proj_fm(J1, KT, "KT")
            kb.op('dve', lambda e: e.memset(V1[:, :, :, 256:257], 1.0), writes=["V1one"])
            for c in range(2):
                s = load_w([(0, win[:, J2 + c * 512:J2 + (c + 1) * 512])])
                for t in range(NT):
                    b = bank()
                    for kc in range(KC):
                        kb.op('pe', lambda e: e.matmul(PB[b][:, :], lhsT=XT[:, kc, 2 + t * 128:2 + (t + 1) * 128],
                                                       rhs=WB[s][:, kc, :], start=(kc == 0), stop=(kc == KC - 1)),
                              reads=[("XT", t), ("WB", s)], writes=[("PB", b)], signal=(kc == KC - 1))
                    kb.op('act', lambda e: e.activation(out=V1[:, t, 2 * c:2 * c + 2, 0:256],
                                                        in_=PB[b][:, :].rearrange("p (a b) -> p a b", a=2), func=AF.Copy),
                          reads=[("PB", b)], writes=[("V1", t, c)])
            chk("o_a")
            kvkeys = [("KT", i, n) for i in range(8) for n in range(2)] + [("V1", t, c) for t in range(8) for c in range(2)] + ["V1one"]
            for c in range(2):
                prefetch(("ofm", o_idx, "QT", c), [(0, win[:, J0 + c * 512:J0 + (c + 1) * 512])])
            for i in range(4):
                kb.dma('sp', [(kv_src[i][:, :], Hf[:, i * KVQ:(i + 1) * KVQ])], reads=kvkeys, writes=[("kv_src", i)])
            for i in range(2):
                kb.allgather(kv_src[i][:, :], kv_dst[i][:, :], reads=[("kv_src", i)], writes=[("kv_dst", i)])
            chk("o_b")
            proj_fm(J0, QT, "QT")
            chk("o_c")

            ar2 = Arena(Hf[:, 12304:16384], 4080 * 4)
            SLN = ar2.alloc([P, 1024], F32)
            WSF = ar2.alloc([P, 8, 128], F32)
            ZV = ar2.alloc([P, 1024], F32)
            ZU = ar2.alloc([P, 512], F32)
            WST = ar.alloc([P, 8, 128], BF16)
            SV = ar.alloc([P, 8, 1024], BF16)
            VN = ar.alloc([P, 1024], BF16)
            CO = ar.alloc([P, 512], BF16)
            kb.dma('sp', [(SLN[:, :], p_sln[:, o_idx, :])], writes=["SLN"])
            kb.dma('sp', [(WSF[:, :, :], p_sguT[o_idx])], writes=["WSF"])
            kb.op('dve', lambda e: e.tensor_tensor(out=WST[:, :, :], in0=WSF[:, :, :],
                                                   in1=MASK[:, :].unsqueeze(1).to_broadcast([P, 8, 128]), op=ALU.mult),
                  reads=["WSF", "MASK"], writes=["WST"])
            s0 = load_w([(0, win[:, 1024:1536])])
            s1 = load_w([(0, win[:, 1536:2048])])
            for i in range(2, 4):
                kb.allgather(kv_src[i][:, :], kv_dst[i][:, :], reads=[("kv_src", i)], writes=[("kv_dst", i)])
            VNs = [VN, ar.alloc([P, 1024], BF16)]
            COs = [CO, ar.alloc([P, 512], BF16)]

            def sgu_A(t):
                bb = [bank(), bank()]
                for c, s_ in enumerate((s0, s1)):
                    for kc in range(KC):
                        kb.op('pe', lambda e: e.matmul(PB[bb[c]][:, :], lhsT=XT[:, kc, 2 + t * 128:2 + (t + 1) * 128],
                                                       rhs=WB[s_][:, kc, :], start=(kc == 0), stop=(kc == KC - 1)),
                              reads=[("XT", t), ("WB", s_)], writes=[("PB", bb[c])], signal=(kc == KC - 1))
                    kb.op('act', lambda e: e.activation(out=ZV[:, c * 512:(c + 1) * 512], in_=PB[bb[c]][:, :], func=GELU),
                          reads=[("PB", bb[c])], writes=[("ZV", c)])
                vn = VNs[t % 2]
                kb.op('dve', lambda e: e.bn_stats(out=SS[:, 24:30], in_=ZV[:, 0:512]), reads=[("ZV", 0)], writes=["BN0"])
                kb.op('dve', lambda e: e.bn_stats(out=SS[:, 30:36], in_=ZV[:, 512:1024]), reads=[("ZV", 1)], writes=["BN1"])
                kb.op('dve', lambda e: e.bn_aggr(out=LS[:, 6:8], in_=SS[:, 24:36].rearrange("p (a b) -> p a b", a=2)),
                      reads=["BN0", "BN1"], writes=["MVZ"])
                kb.op('act', lambda e: e.activation(out=SS[:, 36:37], in_=LS[:, 7:8],
                                                    func=AF.Sqrt, bias=EPS), reads=["MVZ"], writes=["SDZ"])
                kb.op('dve', lambda e: e.reciprocal(out=SS[:, 37:38], in_=SS[:, 36:37]), reads=["SDZ"], writes=["RSZ"])
                kb.op('dve', lambda e: e.tensor_scalar(out=ZV[:, :], in0=ZV[:, :], scalar1=LS[:, 6:7], scalar2=SS[:, 37:38],
                                                       op0=ALU.subtract, op1=ALU.mult),
                      reads=[("ZV", 0), ("ZV", 1), "MVZ", "RSZ"], writes=[("ZV", 0), ("ZV", 1)])
                kb.op('dve', lambda e: e.tensor_tensor(out=vn[:, :], in0=ZV[:, :], in1=SLN[:, :], op=ALU.mult),
                      reads=[("ZV", 0), ("ZV", 1), "SLN"], writes=[("VN", t % 2)])

            def sgu_B(t):
                vn = VNs[t % 2]
                bs2 = [bank(), bank()]
                for g in range(8):
                    b = bs2[g // 4]
                    kb.op('pe', lambda e: e.matmul(PB[b][:, (g % 4) * 128:(g % 4 + 1) * 128], lhsT=WST[:, g, :],
                                                   rhs=vn[:, g * 128:(g + 1) * 128], start=True, stop=True),
                          reads=["WST", ("VN", t % 2)], writes=[("PB", b)], signal=(g % 4 == 3))
                for g in range(8):
                    b = bs2[g // 4]
                    bcol = pcol(OFF_SB + o_idx * 8 + g)
                    kb.op('act', lambda e: e.activation(out=SV[:, t, g * 128:(g + 1) * 128],
                                                        in_=PB[b][:, (g % 4) * 128:(g % 4 + 1) * 128],
                                                        func=AF.Identity, bias=bcol),
                          reads=[("PB", b), "PC"], writes=[("SV", t)])

            for i in range(NT + 1):
                if i < NT:
                    sgu_A(i)
                if i - 1 >= 0:
                    sgu_B(i - 1)

            zitems = [(c, t) for c in range(2) for t in range(NT)]
            zslots = {}

            def zu_A(k):
                c, t = zitems[k]
                if t == 0:
                    zslots[c] = load_w([(0, win[:, c * 512:(c + 1) * 512])])
                s_ = zslots[c]
                b = bank()
                for kc in range(KC):
                    kb.op('pe', lambda e: e.matmul(PB[b][:, :], lhsT=XT[:, kc, 2 + t * 128:2 + (t + 1) * 128],
                                                   rhs=WB[s_][:, kc, :], start=(kc == 0), stop=(kc == KC - 1)),
                          reads=[("XT", t), ("WB", s_)], writes=[("PB", b)], signal=(kc == KC - 1))
                co = COs[k % 2]
                kb.op('act', lambda e: e.activation(out=ZU[:, :], in_=PB[b][:, :], func=GELU),
                      reads=[("PB", b)], writes=["ZU"])
                kb.op('dve', lambda e: e.tensor_tensor(out=co[:, :], in0=ZU[:, :], in1=SV[:, t, c * 512:(c + 1) * 512],
                                                       op=ALU.mult), reads=["ZU", ("SV", t)], writes=[("CO", k % 2)])

            def zu_B(k):
                c, t = zitems[k]
                co = COs[k % 2]
                bt = bank()
                pbt = PB[bt][:, :].bitcast(BF16).rearrange("p (a b) -> p a b", a=8)
                for i in range(4):
                    kb.op('pe', lambda e: e.transpose(out=pbt[:, i, :], in_=co[:, i * 128:(i + 1) * 128], identity=IDB[:, :]),
                          reads=[("CO", k % 2), "IDB"], writes=[("PB", bt)], signal=(i == 3))
                for i in range(4):
                    kc = c * 4 + i
                    dst = MT[:, kc, t * 128:(t + 1) * 128]
                    kb.op('dve', lambda e: e.tensor_copy(out=dst, in_=pbt[:, i, :]),
                          reads=[("PB", bt)], writes=[("MT", kc, t)])

            for k in range(len(zitems) + 1):
                if k < len(zitems):
                    zu_A(k)
                if k - 1 >= 0:
                    zu_B(k - 1)
            chk("o_d")
            kb.barrier()
            XTf = XTb.bitcast(F32)
            kb.dma('sp', [(XTf[:, i * KVQ:(i + 1) * KVQ], kv_dst[i][0:128, :]) for i in range(4)],
                   reads=[("kv_dst", i) for i in range(4)], writes=["KVP"])
            prefetch((("wo", "o", o_idx), 0), [(0, wout[:, 0:512])])
            prefetch((("wo", "o", o_idx), 1), [(0, wout[:, 512:1024])])
            ar.reset(mark0)
            PT = [ar.alloc([P, 512], BF16) for _ in range(6)]
            OA = ar.alloc([P, 2, 260], F32)
            RD = ar.alloc([P, 4], F32)
            DO = ar.alloc([P, 256], F32)
            DJ = ar.alloc([P, 256], F32)
            DBs = [ar.alloc([P, 256], BF16), ar.alloc([P, 256], BF16)]
            scale = 128.0 ** -0.5
            LOOK = 2
            steps = []
            gid = 0
            for hh in range(4):
                for qp in range(4):
                    qt2 = [qp * 2, qp * 2 + 1]
                    klist = [("p", kt) for kt in range(8)] + [("o", kt) for kt in range(qt2[1] + 1)]
                    gsteps = []
                    for (kind, kt) in klist:
                        vis = qt2 if kind == "p" else [qt for qt in qt2 if qt >= kt]
                        gsteps.append(dict(g=gid, hh=hh, qt2=qt2, kind=kind, kt=kt, vis=vis))
                    seen = set()
                    for st in gsteps:
                        st["first"] = {}
                        for qt in st["vis"]:
                            st["first"][qt] = qt not in seen
                            seen.add(qt)
                    seen = set()
                    for st in reversed(gsteps):
                        st["last"] = {}
                        for qt in st["vis"]:
                            st["last"][qt] = qt not in seen
                            seen.add(qt)
                    gsteps[-1]["gend"] = True
                    steps.extend(gsteps)
                    gid += 1

            def emit_S(i, st):
                hh, kind, kt, vis = st["hh"], st["kind"], st["kt"], st["vis"]
                nq = len(vis) * 128
                ksrc = KTP if kind == "p" else KT
                bsx = 4 + (i % 3)
                for j in range(2):
                    kkey = ["KVP"] if kind == "p" else [("KT", hh * 2 + j, kt // 4)]
                    kb.op('pe', lambda e: e.matmul(PB[bsx][:, j * 256:j * 256 + nq],
                                                   lhsT=ksrc[:, hh * 2 + j, kt * 128:(kt + 1) * 128],
                                                   rhs=QT[:, hh * 2 + j, vis[0] * 128:(vis[-1] + 1) * 128],
                                                   start=True, stop=True),
                          reads=kkey + [("QT", hh * 2 + j, vis[0] // 4)], writes=[("PB", bsx)], signal=(j == 1))
                pt = PT[i % 6]
                ptk = ("PT", i % 6)
                pt3 = pt[:, :].rearrange("p (j c) -> p j c", j=2)
                ps3 = PB[bsx][:, :].rearrange("p (j c) -> p j c", j=2)
                bcol = CB[:, 4 + hh:5 + hh] if kind == "p" else CB[:, hh:hh + 1]
                bkey = "CB1" if kind == "p" else "CB0"
                near = {}
                for qt in vis:
                    if kind == "o" and kt == qt:
                        near[qt] = 0
                    elif kind == "o" and kt == qt - 1:
                        near[qt] = 1
                    elif kind == "p" and kt == 7 and qt == 0:
                        near[qt] = 1
                far = [qt for qt in vis if qt not in near]
                if far:
                    o0 = (far[0] - vis[0]) * 128
                    o1 = (far[-1] - vis[0] + 1) * 128
                    kb.op('act', lambda e: e.activation(out=pt3[:, :, o0:o1], in_=ps3[:, :, o0:o1], func=AF.Exp,
                                                        scale=scale, bias=bcol),
                          reads=[("PB", bsx), bkey], writes=[ptk])
                for qt, which in near.items():
                    o0 = (qt - vis[0]) * 128
                    if kind == "p":
                        kb.op('act', lambda e: e.activation(out=pt3[:, :, o0:o0 + 128], in_=ps3[:, :, o0:o0 + 128],
                                                            func=AF.Exp, scale=scale, bias=pcol(OFF_NEGB)),
                              reads=[("PB", bsx), "PC"], writes=[ptk])
                    else:
                        kb.op('act', lambda e: e.activation(out=pt3[:, :, o0:o0 + 128], in_=ps3[:, :, o0:o0 + 128],
                                                            func=AF.Exp, scale=scale),
                              reads=[("PB", bsx)], writes=[ptk])
                    kb.op('dve', lambda e: e.tensor_tensor(out=pt3[:, :, o0:o0 + 128], in0=pt3[:, :, o0:o0 + 128],
                                                           in1=ED[:, hh, which, :].unsqueeze(1).to_broadcast([P, 2, 128]),
                                                           op=ALU.mult),
                          reads=[ptk, "ED"], writes=[ptk])

            def emit_PV(i, st):
                hh, kind, kt, vis, qt2 = st["hh"], st["kind"], st["kt"], st["vis"], st["qt2"]
                pt = PT[i % 6]
                ptk = ("PT", i % 6)
                vsrc = V1P if kind == "p" else V1
                vkey = ["KVP"] if kind == "p" else [("V1", kt, hh // 2), "V1one"]
                for j in range(2):
                    for qt in vis:
                        o0 = j * 256 + (qt - vis[0]) * 128
                        ab = (qt - qt2[0]) * 2 + j
                        kb.op('pe', lambda e: e.matmul(PB[ab][:, 0:257], lhsT=pt[:, o0:o0 + 128], rhs=vsrc[:, kt, hh, :],
                                                       start=st["first"][qt], stop=st["last"][qt]),
                              reads=[ptk] + vkey, writes=[("PB", ab)], signal=True)

            fin_cnt = [0]

            def emit_fin_a(st):
                hh, qt2 = st["hh"], st["qt2"]
                deferred = []
                for qt in qt2:
                    for j in range(2):
                        ab = (qt - qt2[0]) * 2 + j
                        kb.op('dve', lambda e: e.tensor_copy(out=OA[:, j, 0:257], in_=PB[ab][:, 0:257]),
                              reads=[("PB", ab)], writes=[("OA", j)])
                    db = DBs[fin_cnt[0] % 2]
                    dbk = ("DB", fin_cnt[0] % 2)
                    fin_cnt[0] += 1
                    kb.op('dve', lambda e: e.reciprocal(out=RD[:, 0:1], in_=OA[:, 0, 256:257]), reads=[("OA", 0)], writes=["RD0"])
                    kb.op('dve', lambda e: e.reciprocal(out=RD[:, 1:2], in_=OA[:, 1, 256:257]), reads=[("OA", 1)], writes=["RD1"])
                    kb.op('dve', lambda e: e.tensor_tensor(out=RD[:, 2:3], in0=RD[:, 1:2], in1=LS[:, 5:6], op=ALU.mult),
                          reads=["RD1", "LAM"], writes=["RD2"])
                    kb.op('dve', lambda e: e.tensor_scalar_mul(out=DO[:, :], in0=OA[:, 0, 0:256], scalar1=RD[:, 0:1]),
                          reads=[("OA", 0), "RD0"], writes=["DO"])
                    kb.op('dve', lambda e: e.scalar_tensor_tensor(out=DO[:, :], in0=OA[:, 1, 0:256], scalar=RD[:, 2:3],
                                                                  in1=DO[:, :], op0=ALU.mult, op1=ALU.add),
                          reads=[("OA", 1), "RD2", "DO"], writes=["DO"])
                    kb.op('act', lambda e: e.activation(out=DJ[:, :], in_=DO[:, :], func=AF.Square, accum_out=RD[:, 3:4]),
                          reads=["DO"], writes=["DJ", "RD3"])
                    kb.op('act', lambda e: e.activation(out=SS[:, 38:39], in_=RD[:, 3:4], func=AF.Sqrt, scale=1.0 / 256, bias=EPS),
                          reads=["RD3"], writes=["SD1"])
                    kb.op('dve', lambda e: e.reciprocal(out=SS[:, 39:40], in_=SS[:, 38:39]), reads=["SD1"], writes=["SD2"])
                    kb.op('dve', lambda e: e.tensor_scalar(out=db[:, :], in0=DO[:, :], scalar1=SS[:, 39:40],
                                                           scalar2=(1.0 - lam_init), op0=ALU.mult, op1=ALU.mult),
                          reads=["DO", "SD2"], writes=[dbk])
                    deferred.append((hh, qt, db, dbk))
                return deferred

            def emit_fin_b(hh, qt, db, dbk):
                bt = 7
                pbt = PB[bt][:, :].bitcast(BF16).rearrange("p (a b) -> p a b", a=8)
                kb.op('pe', lambda e: e.transpose(out=pbt[:, 0, :], in_=db[:, 0:128], identity=IDB[:, :]),
                      reads=[dbk, "IDB"], writes=[("PB", bt)], signal=False)
                kb.op('pe', lambda e: e.transpose(out=pbt[:, 1, :], in_=db[:, 128:256], identity=IDB[:, :]),
                      reads=[dbk, "IDB"], writes=[("PB", bt)])
                for i2 in range(2):
                    kc = 8 + hh * 2 + i2
                    gcol = pcol(OFF_SUB + o_idx * 2 + i2)
                    dst = MT[:, kc, qt * 128:(qt + 1) * 128]
                    kb.op('dve', lambda e: e.tensor_scalar_mul(out=dst, in0=pbt[:, i2, :], scalar1=gcol),
                          reads=[("PB", bt), "PC"], writes=[("MT", kc, qt)])

            pending_fin = []
            nsteps = len(steps)
            for idx in range(nsteps + LOOK):
                if idx < nsteps:
                    emit_S(idx, steps[idx])
                while pending_fin and pending_fin[0][0] <= idx:
                    emit_fin_b(*pending_fin.pop(0)[1])
                pi_ = idx - LOOK
                if pi_ >= 0:
                    emit_PV(pi_, steps[pi_])
                    if steps[pi_].get("gend"):
                        for k2, args in enumerate(emit_fin_a(steps[pi_])):
                            pending_fin.append((idx + 4 + 2 * k2, args))
            while pending_fin:
                emit_fin_b(*pending_fin.pop(0)[1])
            chk("o_e")
            kb.barrier()
            for t in range(NT):
                kb.dma('sp', [(H[:, t, :], hspill[t * 128:(t + 1) * 128, :])], reads=[("HSP", t)], writes=[("H", t)])
            out_proj([wout[:, cb * 512:(cb + 1) * 512] for cb in range(4)], KC, key=("wo", "o", o_idx))

        order_last_first = [NT - 1] + list(range(NT - 1))
        for l in range(n_layers):
          try:
            last = (l == n_layers - 1)
            if l % 2 == 0:
                prefetch(("eu", l // 2), [(0, w_in_even[l // 2][:, 0:512])])
            else:
                for t in range(NT):
                    kb.dma('sp', [(hspill[t * 128:(t + 1) * 128, :], H[:, t, :])], reads=[("H", t)], writes=[("HSP", t)])
                for c in range(2):
                    prefetch(("ofm", l // 2, "KT", c), [(0, w_in_odd[l // 2][:, 3072 + c * 512:3072 + (c + 1) * 512])])
            norm_phase(l * 2, list(range(NT)), halo=False, first=(l == 0))
            if last and stop_after == "norm0":
                break
            if l % 2 == 0:
                even_mixer(l // 2)
            else:
                odd_mixer(l // 2, l)
            if last and stop_after and stop_after.startswith("mix"):
                break
            prefetch(("up", l, 0), [(0, w_up[l][:, 0:256]), (256, w_up[l][:, DFF:DFF + 256])])
            norm_phase(l * 2 + 1, order_last_first, halo=True)
            if last and stop_after == "norm1":
                break
            ffn(l)
          except StopBuild:
            break

        kb.barrier()
        ar.reset()
        FG = ar.alloc([P, D], F32)
        JK = ar.alloc([P, D], BF16)
        YO = [ar.alloc([P, D], F32), ar.alloc([P, D], F32)]
        kb.dma('sp', [(FG[:, :], p_fng[:, :])], writes=["FG"])
        for t in range(NT):
            yo = YO[t % 2]
            yk = ("YO", t % 2)
            if dbg_h:
                kb.dma('sp', [(out_d[t * 128:(t + 1) * 128, :], H[:, t, :])], reads=[("H", t)], writes=[("OUT", t)])
                continue
            kb.op('act', lambda e: e.activation(out=JK[:, :], in_=H[:, t, :], func=AF.Square, accum_out=SS[:, t:t + 1]),
                  reads=[("H", t)], writes=["JK", ("SS", t)])
            kb.op('act', lambda e: e.activation(out=SS[:, 8 + t:9 + t], in_=SS[:, t:t + 1], func=AF.Sqrt, scale=1.0 / D, bias=EPS),
                  reads=[("SS", t)], writes=[("SS", 8 + t)])
            kb.op('dve', lambda e: e.reciprocal(out=SS[:, 16 + t:17 + t], in_=SS[:, 8 + t:9 + t]),
                  reads=[("SS", 8 + t)], writes=[("SS", 16 + t)])
            kb.op('dve', lambda e: e.scalar_tensor_tensor(out=yo[:, :], in0=H[:, t, :], scalar=SS[:, 16 + t:17 + t],
                                                          in1=FG[:, :], op0=ALU.mult, op1=ALU.mult),
                  reads=[("H", t), ("SS", 16 + t), "FG"], writes=[yk])
            kb.dma('sp', [(out_d[t * 128:(t + 1) * 128, :], yo[:, :])], reads=[yk], writes=[("OUT", t)])
        kb.barrier()
        print("ops emitted:", kb.nops, "sems:", len(kb.sems), flush=True)
    return nc


def _t5_bucket(n):
    n = np.maximum(n, 0)
    max_exact = 16
    nn = np.maximum(n, 1).astype(np.float32)
    large = max_exact + (np.log(nn / max_exact) / math.log(128 / max_exact) * (32 - max_exact)).astype(np.int32)
    large = np.minimum(large, 31)
    return np.where(n < max_exact, n, large)


def _consts(half):
    c = {}
    c["c_ident"] = np.eye(128, dtype=np.float32)
    half_d = 64
    inv = 1.0 / (10000.0 ** (np.arange(half_d, dtype=np.float32) / half_d))
    pos = (half * T + np.arange(T, dtype=np.float32))
    ang = pos[:, None] * inv[None, :]
    cos = np.cos(ang).astype(np.float32).reshape(NT, 128, 64).transpose(1, 0, 2)
    sin = np.sin(ang).astype(np.float32).reshape(NT, 128, 64).transpose(1, 0, 2)
    c["c_rot"] = np.ascontiguousarray(np.stack([cos, sin], axis=1))
    log_g = np.log(1.0 - 2.0 ** (-5.0 - np.arange(6, dtype=np.float32))).astype(np.float32)
    idx = np.arange(128, dtype=np.float32)
    diff = idx[:, None] - idx[None, :]
    intra = np.where(diff >= 0, np.exp(log_g[:, None, None] * np.maximum(diff, 0.0)), 0.0)
    intraT = intra.transpose(2, 0, 1) * (128.0 ** -0.5)
    c["c_intra"] = np.ascontiguousarray(intraT.astype(np.float32))
    c["qdec"] = np.exp(log_g[None, :] * (idx[:, None] + 1.0)).astype(np.float32)
    c["kdec"] = (np.exp(log_g[None, :] * (127.0 - idx[:, None])) * (128.0 ** -0.5)).astype(np.float32)
    p = np.arange(128)
    c["c_mask"] = (p[None, :] >= p[:, None]).astype(np.float32)
    pcorr = np.ones((4, 16), np.float32)
    if half == 0:
        for g in range(4):
            w = 2 ** (g + 1)
            tt = np.arange(16) + 1
            pcorr[g] = w / np.minimum(tt, w)
    c["pcorr"] = pcorr
    return c


def _prep(inputs):
    f = lambda a: np.ascontiguousarray(np.asarray(a, dtype=np.float32))
    ins = {k: f(v) for k, v in inputs.items()}

    def cols(v):
        return v.reshape(-1, 128).T

    shared = {}
    for k in ("w_in_even", "w_out_even", "pool_w", "w_in_odd", "w_out_odd", "w_up", "w_down"):
        shared[k] = ins[k]
    pc = np.zeros((128, NCOLS), np.float32)
    for l in range(4):
        pc[:, OFF_NG + (2 * l) * 16:OFF_NG + (2 * l + 1) * 16] = cols(ins["mix_norm_g"][l])
        pc[:, OFF_NG + (2 * l + 1) * 16:OFF_NG + (2 * l + 2) * 16] = cols(ins["ffn_norm_g"][l])
        for r in range(3):
            pc[:, OFF_CW + l * 264 + r * 88:OFF_CW + l * 264 + (r + 1) * 88] = cols(ins["conv_w"][l, r])
        pc[:, OFF_CB + l * 88:OFF_CB + (l + 1) * 88] = cols(ins["conv_b"][l])
    for e in range(2):
        pc[:, OFF_PSC + e * 4:OFF_PSC + (e + 1) * 4] = cols(ins["pool_scale"][e])
        pc[:, OFF_GNG + e * 12:OFF_GNG + (e + 1) * 12] = cols(ins["ret_gn_g"][e])
        pc[:, OFF_SUB + e * 2:OFF_SUB + (e + 1) * 2] = cols(ins["diff_subln_g"][e])
        pc[:, OFF_SB + e * 8:OFF_SB + (e + 1) * 8] = ins["sgu_b"][e].T
    pc[:, OFF_C31:OFF_C31 + 4] = np.broadcast_to(ins["rel_bias"][31][None, :], (128, 4))
    shared["p_fng"] = np.ascontiguousarray(np.broadcast_to(ins["final_norm_g"][None, :], (128, D)))
    shared["p_sln"] = np.ascontiguousarray(np.broadcast_to(ins["sgu_ln_g"][None, :, :], (128, 2, 1024)))
    lam = np.stack([ins["lam_q1"], ins["lam_k1"], ins["lam_q2"], ins["lam_k2"]], axis=1)
    shared["p_lam"] = np.ascontiguousarray(np.broadcast_to(lam[None], (128, 2, 4, 128)))
    j = np.arange(128)[:, None]
    i = np.arange(128)[None, :]
    b0 = _t5_bucket(i - j)
    b1 = _t5_bucket(128 + i - j)
    rb = ins["rel_bias"]
    biasT = np.stack([rb[b0], rb[b1]], axis=0)
    shared["p_biasT"] = np.ascontiguousarray(biasT.transpose(1, 3, 0, 2))
    shared["p_sguT"] = np.ascontiguousarray(ins["sgu_w"].transpose(0, 3, 1, 2))
    maps = []
    for core in range(NCORES):
        b, half = core // 2, core % 2
        c = _consts(half)
        m = dict(shared)
        m["x"] = np.ascontiguousarray(ins["x"][b, half * T:(half + 1) * T, :])
        pcc = pc.copy()
        pcc[:, OFF_QDEC:OFF_QDEC + 6] = c["qdec"]
        pcc[:, OFF_KDEC:OFF_KDEC + 6] = c["kdec"]
        pcc[:, OFF_FLAG] = float(half)
        pcc[:, OFF_NEGB] = 0.0 if half == 1 else NEGBIG
        pcc[:, OFF_PCORR:OFF_PCORR + 64] = c["pcorr"].reshape(1, 64)
        m["p_cols"] = pcc
        for k in ("c_ident", "c_rot", "c_intra", "c_mask"):
            m[k] = c[k]
        maps.append(m)
    return maps


_NC_CACHE = {}


def kernel(**inputs):
    maps = _prep(inputs)
    if "nc" not in _NC_CACHE:
        _NC_CACHE["nc"] = build()
    nc = _NC_CACHE["nc"]
    res = run_bass_kernel_spmd(nc, maps, core_ids=list(range(NCORES)))
    out = np.zeros((4, 2 * T, D), np.float32)
    for core in range(NCORES):
        b, half = core // 2, core % 2
        out[b, half * T:(half + 1) * T, :] = np.asarray(res.results[core]["out"], dtype=np.float32)
    return out
```
